# Optimizing a Trainium2 kernel written in Bass

```python
import math
import jax, jax.numpy as jnp
from jax import lax
import numpy as np

D_MODEL = 1024
BATCH = 4
SEQ = 8192
DEPTH = 4

GRID_W = 64
CTX_LEN = 256
N_MOD = 6
MLP_HIDDEN = 4 * D_MODEL
NORM_EPS = 1e-6

S5_WIDTH = D_MODEL // 2
S5_GROUP_CH = 16
S5_GROUPS = S5_WIDTH // S5_GROUP_CH
S5_STATE = 64

M2_INNER = D_MODEL
M2_HEAD_DIM = 64
M2_HEADS = M2_INNER // M2_HEAD_DIM
M2_GROUPS = 4
M2_STATE = 128
M2_BC = M2_GROUPS * M2_STATE
M2_XBC = M2_INNER + 2 * M2_BC
M2_CONV = 3
SSD_CHUNK = 128

EVEN_IN = S5_WIDTH + M2_INNER + M2_XBC + 2 * M2_HEADS
EVEN_MIX = S5_WIDTH + M2_INNER

HY_WIDTH = D_MODEL
HY_ORDER = 2
HY_CONV = 3
HY_BANDS = 16
HY_EMB = 2 * HY_BANDS + 1
HY_HIDDEN = 64
HY_MIN_DECAY = math.log(1e-2) / 1.5
HY_MAX_DECAY = math.log(1e-2) / 0.3

N_EVEN = (DEPTH + 1) // 2
N_ODD = DEPTH // 2

kernel_name = 'hybrid_s5_ssd_hyena_dit_block'


def rmsnorm(x, g):
    xf = x.astype(jnp.float32)
    y = xf * lax.rsqrt(jnp.mean(xf * xf, axis=-1, keepdims=True) + NORM_EPS)
    return (y * g.astype(jnp.float32)).astype(x.dtype)


def adaln(x, g, shift, scale):
    return rmsnorm(x, g) * (1 + scale) + shift


def sq_relu_mlp(h, w1, w2):
    return jnp.square(jax.nn.relu(h @ w1)) @ w2


def dwconv_centred(u, w, b):
    k = w.shape[0]
    y = lax.conv_general_dilated(u, w[:, None, :].astype(u.dtype), window_strides=(1,),
                                 padding=[(k // 2, k // 2)], dimension_numbers=('NWC', 'WIO', 'NWC'),
                                 feature_group_count=u.shape[-1])
    return y + b.astype(u.dtype)


def grid_sincos(n, dm):
    rows = n // GRID_W
    quarter = dm // 4
    omega = 1.0 / (10000.0 ** (jnp.arange(quarter, dtype=jnp.float32) / quarter))
    ang_r = jnp.arange(rows, dtype=jnp.float32)[:, None] * omega
    ang_c = jnp.arange(GRID_W, dtype=jnp.float32)[:, None] * omega
    emb_r = jnp.concatenate([jnp.sin(ang_r), jnp.cos(ang_r)], axis=-1)
    emb_c = jnp.concatenate([jnp.sin(ang_c), jnp.cos(ang_c)], axis=-1)
    half = emb_r.shape[-1]
    pe = jnp.concatenate([jnp.broadcast_to(emb_r[:, None, :], (rows, GRID_W, half)),
                          jnp.broadcast_to(emb_c[None, :, :], (rows, GRID_W, half))], axis=-1)
    return pe.reshape(rows * GRID_W, 2 * half)


def _complex_affine_combine(e1, e2):
    a1r, a1i, b1r, b1i = e1
    a2r, a2i, b2r, b2i = e2
    return (a2r * a1r - a2i * a1i,
            a2r * a1i + a2i * a1r,
            a2r * b1r - a2i * b1i + b2r,
            a2r * b1i + a2i * b1r + b2i)


def s5_direction(u, lam_re, lam_im, log_dt, b_re, b_im, c_re, c_im, h0):
    f32 = jnp.float32
    lam_re = lam_re.astype(f32)
    lam_im = lam_im.astype(f32)
    b_re = b_re.astype(f32)
    b_im = b_im.astype(f32)
    c_re = c_re.astype(f32)
    c_im = c_im.astype(f32)
    step = jnp.exp(log_dt.astype(f32))[:, None]
    mag = jnp.exp(lam_re * step)
    ar = mag * jnp.cos(lam_im * step)
    ai = mag * jnp.sin(lam_im * step)
    den = lam_re * lam_re + lam_im * lam_im
    fr = ((ar - 1.0) * lam_re + ai * lam_im) / den
    fi = (ai * lam_re - (ar - 1.0) * lam_im) / den
    bbr = fr[..., None] * b_re - fi[..., None] * b_im
    bbi = fr[..., None] * b_im + fi[..., None] * b_re
    bu_r = jnp.einsum('gpk,blgk->blgp', bbr, u)
    bu_i = jnp.einsum('gpk,blgk->blgp', bbi, u)
    l = u.shape[1]
    a_r = jnp.broadcast_to(ar[None, None], (1, l) + ar.shape)
    a_i = jnp.broadcast_to(ai[None, None], (1, l) + ai.shape)
    cum_r, cum_i, h_r, h_i = lax.associative_scan(_complex_affine_combine, (a_r, a_i, bu_r, bu_i), axis=1)
    h0_r = h0[0][:, None]
    h0_i = h0[1][:, None]
    h_r, h_i = h_r + cum_r * h0_r - cum_i * h0_i, h_i + cum_r * h0_i + cum_i * h0_r
    y = jnp.einsum('gkp,blgp->blgk', c_re, h_r) - jnp.einsum('gkp,blgp->blgk', c_im, h_i)
    return y, (h_r[:, -1], h_i[:, -1])


def s5_mixer(u, lam_re, lam_im, log_dt, b_re, b_im, c_re, c_im, d_skip, glu_w, glu_b, h0_f, h0_b):
    f32 = jnp.float32
    bsz, l, _ = u.shape
    uf = u.astype(f32)
    ug = uf.reshape(bsz, l, S5_GROUPS, S5_GROUP_CH)
    y_f, s_f = s5_direction(ug, lam_re[0], lam_im[0], log_dt[0], b_re, b_im, c_re, c_im, h0_f)
    y_b, s_b = s5_direction(ug[:, ::-1], lam_re[1], lam_im[1], log_dt[1], b_re, b_im, c_re, c_im, h0_b)
    y = (y_f + y_b[:, ::-1]).reshape(bsz, l, S5_WIDTH) + d_skip.astype(f32) * uf
    y = jax.nn.gelu(y)
    y = y * jax.nn.sigmoid(y @ glu_w.astype(f32) + glu_b.astype(f32))
    return y.astype(u.dtype), s_f, s_b


def segsum(a):
    cs = jnp.cumsum(a, axis=-1)
    d = cs[..., :, None] - cs[..., None, :]
    t = a.shape[-1]
    mask = jnp.tril(jnp.ones((t, t), dtype=bool))
    return jnp.where(mask, d, -jnp.inf)


def ssd_scan(x, dt, a_coef, bm, cm, h0):
    bsz, l, nh, hp = x.shape
    ng, ns = bm.shape[2], bm.shape[3]
    nr = nh // ng
    nc = l // SSD_CHUNK
    xc = x.reshape(bsz, nc, SSD_CHUNK, ng, nr, hp)
    dtc = dt.reshape(bsz, nc, SSD_CHUNK, ng, nr)
    bc = bm.reshape(bsz, nc, SSD_CHUNK, ng, ns)
    cc = cm.reshape(bsz, nc, SSD_CHUNK, ng, ns)
    a = jnp.transpose(dtc * a_coef.reshape(ng, nr), (0, 3, 4, 1, 2))
    a_cum = jnp.cumsum(a, axis=-1)
    xdt = xc * dtc[..., None]
    lmat = jnp.exp(segsum(a))
    cb = jnp.einsum('bcqgn,bcsgn->bgcqs', cc, bc)
    y_diag = jnp.einsum('bgcqs,bgrcqs,bcsgrp->bcqgrp', cb, lmat, xdt)
    decay_states = jnp.exp(a_cum[..., -1:] - a_cum)
    states = jnp.einsum('bcqgn,bgrcq,bcqgrp->bcgrpn', bc, decay_states, xdt)
    states = jnp.concatenate([h0.reshape(bsz, 1, ng, nr, hp, ns), states], axis=1)
    chunk_a = jnp.pad(a_cum[..., -1], ((0, 0), (0, 0), (0, 0), (1, 0)))
    decay_chunk = jnp.exp(segsum(chunk_a))
    new_states = jnp.einsum('bgrzc,bcgrpn->bzgrpn', decay_chunk, states)
    prev_states = new_states[:, :-1]
    final = new_states[:, -1]
    y_off = jnp.einsum('bcqgn,bcgrpn,bgrcq->bcqgrp', cc, prev_states, jnp.exp(a_cum))
    y = (y_diag + y_off).reshape(bsz, l, nh, hp)
    return y, final.reshape(bsz, nh, hp, ns)


def mamba2_mixer(z, xbc, dt_raw, conv_w, conv_b, dt_bias, a_log, d_skip, norm_g, h0_f, h0_b):
    f32 = jnp.float32
    bsz, l, _ = z.shape
    xbc = jax.nn.silu(dwconv_centred(xbc, conv_w, conv_b)).astype(f32)
    xs, bm, cm = jnp.split(xbc, [M2_INNER, M2_INNER + M2_BC], axis=-1)
    xs = xs.reshape(bsz, l, M2_HEADS, M2_HEAD_DIM)
    bm = bm.reshape(bsz, l, M2_GROUPS, M2_STATE)
    cm = cm.reshape(bsz, l, M2_GROUPS, M2_STATE)
    dt = jax.nn.softplus(dt_raw.astype(f32).reshape(bsz, l, 2, M2_HEADS) + dt_bias.astype(f32))
    a_coef = -jnp.exp(a_log.astype(f32))
    y_f, h_f = ssd_scan(xs, dt[:, :, 0], a_coef[0], bm, cm, h0_f)
    y_b, h_b = ssd_scan(xs[:, ::-1], dt[:, ::-1, 1], a_coef[1], bm[:, ::-1], cm[:, ::-1], h0_b)
    y = y_f + y_b[:, ::-1] + d_skip.astype(f32)[:, None] * xs
    y = y.reshape(bsz, l, M2_INNER) * jax.nn.silu(z.astype(f32))
    y = rmsnorm(y, norm_g)
    return y.astype(z.dtype), h_f, h_b


def even_mixer(h, in_w, out_w, lam_re, lam_im, log_dt, b_re, b_im, c_re, c_im, s5_d, glu_w, glu_b,
               conv_w, conv_b, dt_bias, a_log, m_d, m_norm_g, s5_h0_f, s5_h0_b, m_h0_f, m_h0_b):
    proj = h @ in_w
    u, z, xbc, dt_raw = jnp.split(proj, [S5_WIDTH, S5_WIDTH + M2_INNER, S5_WIDTH + M2_INNER + M2_XBC], axis=-1)
    y_s5, s5_f, s5_b = s5_mixer(u, lam_re, lam_im, log_dt, b_re, b_im, c_re, c_im, s5_d, glu_w, glu_b,
                                s5_h0_f, s5_h0_b)
    y_m2, m_f, m_b = mamba2_mixer(z, xbc, dt_raw, conv_w, conv_b, dt_bias, a_log, m_d, m_norm_g, m_h0_f, m_h0_b)
    out = jnp.concatenate([y_s5, y_m2], axis=-1) @ out_w
    return out, (s5_f, s5_b, m_f, m_b)


def hyena_filters(l, f_w1, f_b1, f_freq1, f_w2, f_b2, f_freq2, f_w3):
    f32 = jnp.float32
    t = jnp.linspace(0.0, 1.0, l, dtype=f32)[:, None]
    bands = jnp.linspace(1e-4, HY_BANDS - 1, HY_BANDS, dtype=f32)
    ang = (2.0 * math.pi / l) * jnp.arange(l, dtype=f32)[:, None] * bands
    feats = jnp.concatenate([t, jnp.cos(ang), -jnp.sin(ang)], axis=-1)
    hid = jnp.sin(f_freq1.astype(f32) * (feats @ f_w1.astype(f32) + f_b1.astype(f32)))
    hid = jnp.sin(f_freq2.astype(f32) * (hid @ f_w2.astype(f32) + f_b2.astype(f32)))
    hf = (hid @ f_w3.astype(f32)).reshape(l, HY_ORDER, 2, HY_WIDTH)
    deltas = jnp.abs(jnp.linspace(HY_MIN_DECAY, HY_MAX_DECAY, HY_WIDTH, dtype=f32))
    hf = hf * jnp.exp(-t[:, :, None, None] * deltas)
    fwd = hf[:, :, 0]
    bwd = hf[:, :, 1]
    filt = jnp.concatenate([fwd, jnp.zeros((1, HY_ORDER, HY_WIDTH), f32), bwd[:0:-1]], axis=0)
    filt = filt / (jnp.sum(jnp.abs(filt), axis=0, keepdims=True) + 1e-6)
    return jnp.fft.rfft(filt, axis=0)


def long_conv(u, filt_f, d_skip):
    l = u.shape[1]
    uf = jnp.fft.rfft(u, n=2 * l, axis=1)
    y = jnp.fft.irfft(uf * filt_f[None], n=2 * l, axis=1)[:, :l]
    return y + d_skip * u


def hyena_mixer(h, in_w, in_b, conv_w, conv_b, f_w1, f_b1, f_freq1, f_w2, f_b2, f_freq2, f_w3, d_skip,
                out_w, out_b):
    f32 = jnp.float32
    l = h.shape[1]
    zz = dwconv_centred(h @ in_w + in_b, conv_w, conv_b).astype(f32)
    v, x1, x2 = jnp.split(zz, 3, axis=-1)
    filt_f = hyena_filters(l, f_w1, f_b1, f_freq1, f_w2, f_b2, f_freq2, f_w3)
    d = d_skip.astype(f32)
    y = x1 * long_conv(v, filt_f[:, 0], d[0])
    y = x2 * long_conv(y, filt_f[:, 1], d[1])
    return y.astype(h.dtype) @ out_w + out_b


def setup_inputs(seed: int = 0) -> dict:
    key = jax.random.key(seed)
    ks = list(jax.random.split(key, 48))
    f32 = jnp.float32
    D = D_MODEL

    def nrm(shape, scale):
        return scale * jax.random.normal(ks.pop(), shape, f32)

    def unif(shape, lo, hi):
        return jax.random.uniform(ks.pop(), shape, f32, lo, hi)

    x = nrm((BATCH, SEQ, D), 1.0)
    c = nrm((BATCH, D), 1.0)
    ctx = nrm((BATCH, CTX_LEN, D), 1.0)
    c_ctx = nrm((D,), 1.0)
    mod_w = nrm((DEPTH, D, N_MOD * D), 0.5 * D ** -0.5)
    mod_b = nrm((DEPTH, N_MOD * D), 0.02)
    norm_mix_g = 1.0 + nrm((DEPTH, D), 0.02)
    norm_mlp_g = 1.0 + nrm((DEPTH, D), 0.02)
    mlp_w1 = nrm((DEPTH, D, MLP_HIDDEN), D ** -0.5)
    mlp_w2 = nrm((DEPTH, MLP_HIDDEN, D), MLP_HIDDEN ** -0.5)
    final_norm_g = 1.0 + nrm((D,), 0.02)
    ev_in_w = nrm((N_EVEN, D, EVEN_IN), D ** -0.5)
    ev_out_w = nrm((N_EVEN, EVEN_MIX, D), EVEN_MIX ** -0.5)
    s5_lam_re = -0.5 + nrm((N_EVEN, 2, S5_GROUPS, S5_STATE), 0.01)
    s5_lam_im = jnp.pi * jnp.arange(S5_STATE, dtype=f32) + nrm((N_EVEN, 2, S5_GROUPS, S5_STATE), 0.01)
    s5_log_dt = unif((N_EVEN, 2, S5_GROUPS), math.log(1e-3), math.log(1e-1))
    s5_b_re = nrm((N_EVEN, S5_GROUPS, S5_STATE, S5_GROUP_CH), (2 * S5_GROUP_CH) ** -0.5)
    s5_b_im = nrm((N_EVEN, S5_GROUPS, S5_STATE, S5_GROUP_CH), (2 * S5_GROUP_CH) ** -0.5)
    s5_c_re = nrm((N_EVEN, S5_GROUPS, S5_GROUP_CH, S5_STATE), (2 * S5_STATE) ** -0.5)
    s5_c_im = nrm((N_EVEN, S5_GROUPS, S5_GROUP_CH, S5_STATE), (2 * S5_STATE) ** -0.5)
    s5_d = nrm((N_EVEN, S5_WIDTH), 1.0)
    s5_glu_w = nrm((N_EVEN, S5_WIDTH, S5_WIDTH), S5_WIDTH ** -0.5)
    s5_glu_b = nrm((N_EVEN, S5_WIDTH), 0.02)
    m2_conv_w = nrm((N_EVEN, M2_CONV, M2_XBC), 0.5)
    m2_conv_b = nrm((N_EVEN, M2_XBC), 0.02)
    dt0 = jnp.exp(unif((N_EVEN, 2, M2_HEADS), math.log(1e-3), math.log(1e-1)))
    m2_dt_bias = dt0 + jnp.log(-jnp.expm1(-dt0))
    m2_a_log = jnp.log(unif((N_EVEN, 2, M2_HEADS), 1.0, 16.0))
    m2_d = 1.0 + nrm((N_EVEN, M2_HEADS), 0.1)
    m2_norm_g = 1.0 + nrm((N_EVEN, M2_INNER), 0.02)
    hy_in_w = nrm((N_ODD, D, 3 * HY_WIDTH), D ** -0.5)
    hy_in_b = nrm((N_ODD, 3 * HY_WIDTH), 0.02)
    hy_conv_w = nrm((N_ODD, HY_CONV, 3 * HY_WIDTH), 0.5)
    hy_conv_b = nrm((N_ODD, 3 * HY_WIDTH), 0.02)
    hy_f_w1 = nrm((N_ODD, HY_EMB, HY_HIDDEN), HY_EMB ** -0.5)
    hy_f_b1 = nrm((N_ODD, HY_HIDDEN), 0.1)
    hy_f_freq1 = 1.0 + nrm((N_ODD, HY_HIDDEN), 0.01)
    hy_f_w2 = nrm((N_ODD, HY_HIDDEN, HY_HIDDEN), HY_HIDDEN ** -0.5)
    hy_f_b2 = nrm((N_ODD, HY_HIDDEN), 0.1)
    hy_f_freq2 = 1.0 + nrm((N_ODD, HY_HIDDEN), 0.01)
    hy_f_w3 = nrm((N_ODD, HY_HIDDEN, HY_ORDER * 2 * HY_WIDTH), HY_HIDDEN ** -0.5)
    hy_d = nrm((N_ODD, HY_ORDER, HY_WIDTH), 1.0)
    hy_out_w = nrm((N_ODD, HY_WIDTH, D), HY_WIDTH ** -0.5)
    hy_out_b = nrm((N_ODD, D), 0.02)
    return {'x': x, 'c': c, 'ctx': ctx, 'c_ctx': c_ctx, 'mod_w': mod_w, 'mod_b': mod_b,
            'norm_mix_g': norm_mix_g, 'norm_mlp_g': norm_mlp_g, 'mlp_w1': mlp_w1, 'mlp_w2': mlp_w2,
            'final_norm_g': final_norm_g, 'ev_in_w': ev_in_w, 'ev_out_w': ev_out_w,
            's5_lam_re': s5_lam_re, 's5_lam_im': s5_lam_im, 's5_log_dt': s5_log_dt,
            's5_b_re': s5_b_re, 's5_b_im': s5_b_im, 's5_c_re': s5_c_re, 's5_c_im': s5_c_im,
            's5_d': s5_d, 's5_glu_w': s5_glu_w, 's5_glu_b': s5_glu_b,
            'm2_conv_w': m2_conv_w, 'm2_conv_b': m2_conv_b, 'm2_dt_bias': m2_dt_bias, 'm2_a_log': m2_a_log,
            'm2_d': m2_d, 'm2_norm_g': m2_norm_g,
            'hy_in_w': hy_in_w, 'hy_in_b': hy_in_b, 'hy_conv_w': hy_conv_w, 'hy_conv_b': hy_conv_b,
            'hy_f_w1': hy_f_w1, 'hy_f_b1': hy_f_b1, 'hy_f_freq1': hy_f_freq1, 'hy_f_w2': hy_f_w2,
            'hy_f_b2': hy_f_b2, 'hy_f_freq2': hy_f_freq2, 'hy_f_w3': hy_f_w3, 'hy_d': hy_d,
            'hy_out_w': hy_out_w, 'hy_out_b': hy_out_b}


def reference(x, c, ctx, c_ctx, mod_w, mod_b, norm_mix_g, norm_mlp_g, mlp_w1, mlp_w2, final_norm_g,
              ev_in_w, ev_out_w, s5_lam_re, s5_lam_im, s5_log_dt, s5_b_re, s5_b_im, s5_c_re, s5_c_im,
              s5_d, s5_glu_w, s5_glu_b, m2_conv_w, m2_conv_b, m2_dt_bias, m2_a_log, m2_d, m2_norm_g,
              hy_in_w, hy_in_b, hy_conv_w, hy_conv_b, hy_f_w1, hy_f_b1, hy_f_freq1, hy_f_w2, hy_f_b2,
              hy_f_freq2, hy_f_w3, hy_d, hy_out_w, hy_out_b):
    f32 = jnp.float32
    bsz, n, dm = x.shape
    x = x + grid_sincos(n, dm).astype(x.dtype)
    for i in range(DEPTH):
        j = i // 2
        m_x = jax.nn.silu(c) @ mod_w[i] + mod_b[i]
        sh_a, sc_a, g_a, sh_f, sc_f, g_f = jnp.split(m_x[:, None, :], N_MOD, axis=-1)
        m_c = jax.nn.silu(c_ctx) @ mod_w[i] + mod_b[i]
        ch_a, cs_a, cg_a, ch_f, cs_f, cg_f = jnp.split(m_c, N_MOD, axis=-1)
        ctx_later = any(k % 2 == 0 for k in range(i + 1, DEPTH))
        hx = adaln(x, norm_mix_g[i], sh_a, sc_a)
        if i % 2 == 0:
            ev = (ev_in_w[j], ev_out_w[j], s5_lam_re[j], s5_lam_im[j], s5_log_dt[j], s5_b_re[j], s5_b_im[j],
                  s5_c_re[j], s5_c_im[j], s5_d[j], s5_glu_w[j], s5_glu_b[j], m2_conv_w[j], m2_conv_b[j],
                  m2_dt_bias[j], m2_a_log[j], m2_d[j], m2_norm_g[j])
            bc = ctx.shape[0]
            z_s5 = (jnp.zeros((bc, S5_GROUPS, S5_STATE), f32), jnp.zeros((bc, S5_GROUPS, S5_STATE), f32))
            z_m2 = jnp.zeros((bc, M2_HEADS, M2_HEAD_DIM, M2_STATE), f32)
            hc = adaln(ctx, norm_mix_g[i], ch_a, cs_a)
            out_c, ctx_states = even_mixer(hc, *ev, z_s5, z_s5, z_m2, z_m2)
            out_x, _ = even_mixer(hx, *ev, *ctx_states)
        else:
            od = (hy_in_w[j], hy_in_b[j], hy_conv_w[j], hy_conv_b[j], hy_f_w1[j], hy_f_b1[j], hy_f_freq1[j],
                  hy_f_w2[j], hy_f_b2[j], hy_f_freq2[j], hy_f_w3[j], hy_d[j], hy_out_w[j], hy_out_b[j])
            out_x = hyena_mixer(hx, *od)
            if ctx_later:
                out_c = hyena_mixer(adaln(ctx, norm_mix_g[i], ch_a, cs_a), *od)
        x = x + g_a * out_x
        x = x + g_f * sq_relu_mlp(adaln(x, norm_mlp_g[i], sh_f, sc_f), mlp_w1[i], mlp_w2[i])
        if ctx_later:
            ctx = ctx + cg_a * out_c
            ctx = ctx + cg_f * sq_relu_mlp(adaln(ctx, norm_mlp_g[i], ch_f, cs_f), mlp_w1[i], mlp_w2[i])
    return rmsnorm(x, final_norm_g)
```

```python
import contextlib
import math
import numpy as np
import concourse.bass as bass
import concourse.mybir as mybir
from concourse.bass_utils import run_bass_kernel_spmd


F32 = mybir.dt.float32
BF16 = mybir.dt.bfloat16
I32 = mybir.dt.int32
AF = mybir.ActivationFunctionType
ALU = mybir.AluOpType
AX = mybir.AxisListType


class Prog:
    COMPUTE = ("pe", "dve", "act", "pool")
    ENGS = ("pe", "dve", "act", "pool", "sp")
    QS = ("sp", "act", "pool")
    NDS = 8

    def __init__(self):
        self.nc = bass.Bass("TRN2", target_bir_lowering=False)
        nc = self.nc
        self.top = contextlib.ExitStack()
        self.sems = {e: self.top.enter_context(nc.semaphore(f"s_{e}")) for e in self.COMPUTE}
        self.dsems = {q: [self.top.enter_context(nc.semaphore(f"d_{q}{i}")) for i in range(self.NDS)] for q in self.QS}
        self.cnt = {e: 0 for e in self.COMPUTE}
        self.dval = {q: [0] * self.NDS for q in self.QS}
        self.dnext = {q: 0 for q in self.QS}
        self.n_sb = 0
        self.stats = {e: 0 for e in self.ENGS}
        self.stack = None
        self.barrier_tok = []
        self.begin()

    def begin(self):
        self.ops = []
        self.lastw = {}
        self.readers = {}
        self.stack = contextlib.ExitStack()

    def dram(self, name, shape, dt, kind="Internal"):
        return self.nc.dram_tensor(name, list(shape), dt, kind=kind).ap()

    def sb(self, shape, dt=F32, name=None):
        self.n_sb += 1
        return self.stack.enter_context(self.nc.sbuf_tensor(f"{name or 'sb'}_{self.n_sb}", list(shape), dt))

    def ps(self, shape, dt=F32, name=None):
        self.n_sb += 1
        return self.stack.enter_context(self.nc.psum_tensor(f"{name or 'ps'}_{self.n_sb}", list(shape), dt))

    def op(self, eng, fn, reads=(), writes=(), dma=False):
        deps = set()
        for k in reads:
            if k in self.lastw:
                deps.add(self.lastw[k])
        for k in writes:
            if k in self.lastw:
                deps.add(self.lastw[k])
            for r in self.readers.get(k, ()):
                deps.add(r)
        idx = len(self.ops)
        self.ops.append(dict(eng=eng, fn=fn, deps=deps, dma=dma, sig=False))
        for k in reads:
            rl = self.readers.setdefault(k, [])
            if not dma:
                rl[:] = [r for r in rl if self.ops[r]["dma"] or self.ops[r]["eng"] != eng]
            rl.append(idx)
        for k in writes:
            self.lastw[k] = idx
            self.readers[k] = []
        return idx

    def dma(self, q, out, in_, reads=(), writes=(), **kw):
        return self.op(q, lambda e: e.dma_start(out=out, in_=in_, **kw), reads, writes, dma=True)

    def end(self):
        nc = self.nc
        ops = self.ops
        for o in ops:
            for d in o["deps"]:
                ops[d]["sig"] = True
        per = {e: [] for e in self.ENGS}
        for o in ops:
            per[o["eng"]].append(o)
        for e in self.COMPUTE:
            for o in reversed(per[e]):
                if not o["dma"]:
                    o["sig"] = True
                    break
        for o in ops:
            if o["dma"]:
                q = o["eng"]
                k = self.dnext[q]
                self.dnext[q] = (k + 1) % self.NDS
                o["prev"] = (q, k, self.dval[q][k])
                self.dval[q][k] += 16
                o["tok"] = ("d", q, k, self.dval[q][k])
            elif o["sig"]:
                self.cnt[o["eng"]] += 1
                o["tok"] = ("c", o["eng"], self.cnt[o["eng"]])
        for e in self.ENGS:
            self.stats[e] += len(per[e])
        start_tok = self.barrier_tok
        end_tok = [("c", e, self.cnt[e]) for e in self.COMPUTE if self.cnt[e] > 0]
        for q in self.QS:
            for k in range(self.NDS):
                if self.dval[q][k] > 0:
                    end_tok.append(("d", q, k, self.dval[q][k]))
        self.barrier_tok = end_tok
        sems, dsems = self.sems, self.dsems

        def emit(e_name, eng):
            waited = {}

            def wait(t):
                key = t[:-1]
                if waited.get(key, 0) >= t[-1]:
                    return
                waited[key] = t[-1]
                sem = sems[t[1]] if t[0] == "c" else dsems[t[1]][t[2]]
                eng.wait_ge(sem, t[-1])

            for t in start_tok:
                wait(t)
            for o in per[e_name]:
                if o["dma"]:
                    q, k, pv = o["prev"]
                    if pv > 0:
                        wait(("d", q, k, pv))
                for d in sorted(o["deps"]):
                    t = ops[d]["tok"]
                    if t[0] == "c" and t[1] == e_name and e_name == "pe":
                        continue
                    wait(t)
                ins = o["fn"](eng)
                if o["dma"]:
                    t = o["tok"]
                    ins.then_inc(dsems[t[1]][t[2]], 16)
                elif o["sig"]:
                    ins.then_inc(sems[e_name], 1)

        with nc.Block() as block:
            @block.tensor
            def _(eng):
                emit("pe", eng)

            @block.vector
            def _(eng):
                emit("dve", eng)

            @block.scalar
            def _(eng):
                emit("act", eng)

            @block.gpsimd
            def _(eng):
                emit("pool", eng)

            @block.sync
            def _(eng):
                emit("sp", eng)
        self.stack.close()
        self.begin()

    def build(self):
        nc = self.nc
        toks = self.barrier_tok
        sems, dsems = self.sems, self.dsems
        with nc.Block() as block:
            def fin(eng):
                for t in toks:
                    sem = sems[t[1]] if t[0] == "c" else dsems[t[1]][t[2]]
                    eng.wait_ge(sem, t[-1])

            @block.sync
            def _(eng):
                fin(eng)

            @block.gpsimd
            def _(eng):
                fin(eng)
        self.top.close()
        return nc


def run(p, in_maps, n=None, trace=False):
    nc = p.build()
    n = n or len(in_maps)
    return run_bass_kernel_spmd(nc, in_maps, core_ids=list(range(n)), trace=trace)


EPS = 1e-6
D = 1024


def col(ap1d):
    return ap1d.rearrange("(c p) -> p c", p=128)


def load_cols(p, dst, src1d, key, q="sp"):
    p.dma(q, dst, col(src1d), writes=[key], allow_slow_non_contiguous=True)


def load_weight_bf16(p, w_d, K, F, name, q="sp", chunk=1024):
    kc = K // 128
    wb = p.sb([128, kc, F], BF16, name=name)
    stg = [p.sb([128, chunk], F32, name=f"{name}_stg{i}") for i in range(2)]
    i = 0
    for c in range(kc):
        for f0 in range(0, F, chunk):
            fn = min(chunk, F - f0)
            s = stg[i % 2]
            p.dma(q, s[:, :fn], w_d[c * 128:(c + 1) * 128, f0:f0 + fn], writes=[f"{name}_stg{i%2}"])
            if i % 2 == 0:
                p.op("pool", lambda e, s=s, c=c, f0=f0, fn=fn: e.tensor_copy(out=wb[:, c, f0:f0 + fn], in_=s[:, :fn]),
                     reads=[f"{name}_stg{i%2}"], writes=[name])
            else:
                p.op("act", lambda e, s=s, c=c, f0=f0, fn=fn: e.copy(out=wb[:, c, f0:f0 + fn], in_=s[:, :fn]),
                     reads=[f"{name}_stg{i%2}"], writes=[name])
            i += 1
    return wb


class Norm:
    def __init__(self, p, NMAX, nmod, g_d, mods):
        self.p = p
        self.ones = p.sb([128, 128], F32, name="ones")
        p.op("pool", lambda e: e.memset(self.ones[:, :], 1.0), writes=["ones"])
        self.eps = p.sb([128, 1], F32, name="eps")
        p.op("pool", lambda e: e.memset(self.eps[:, :], EPS), writes=["eps"])
        self.sq = p.sb([128, 8, NMAX], F32, name="sq")
        self.tmp = p.sb([128, 8, NMAX], F32, name="ntmp")
        self.rstd = p.sb([128, NMAX], F32, name="rstd")
        self.ps_ss = p.ps([128, 512], name="ps_ss")
        g_sb = p.sb([128, 8], F32, name="g_sb")
        load_cols(p, g_sb[:, :], g_d, "g_sb")
        self.gs = p.sb([128, nmod, 8], F32, name="gs_sb")
        self.sh = p.sb([128, nmod, 8], F32, name="sh_sb")
        sc_sb = p.sb([128, nmod, 8], F32, name="sc_sb")
        for m, (sc_d, sh_d) in enumerate(mods):
            if sc_d is None:
                p.op("pool", lambda e, m=m: e.memset(sc_sb[:, m, :], 0.0), writes=["sc_sb"])
                p.op("pool", lambda e, m=m: e.memset(self.sh[:, m, :], 0.0), writes=["mods"])
            else:
                load_cols(p, sc_sb[:, m, :], sc_d, "sc_sb")
                load_cols(p, self.sh[:, m, :], sh_d, "mods")
        for m in range(nmod):
            p.op("dve", lambda e, m=m: e.scalar_tensor_tensor(out=self.gs[:, m, :], in0=sc_sb[:, m, :], scalar=1.0, in1=g_sb[:, :],
                                                               op0=ALU.add, op1=ALU.mult),
                 reads=["sc_sb", "g_sb"], writes=["mods"])

    def apply(self, x_sb, xk, N, mi, ht, hk):
        p = self.p
        sq, ones, ps_ss, rstd, tmp, eps_t = self.sq, self.ones, self.ps_ss, self.rstd, self.tmp, self.eps
        gs, sh = self.gs[:, mi, :], self.sh[:, mi, :]
        p.op("act", lambda e: e.activation(out=sq[:, :, :N], in_=x_sb[:, :, :N], func=AF.Square), reads=[xk], writes=["sq"])
        for c in range(8):
            p.op("pe", lambda e, c=c: e.matmul(ps_ss[:, :N], lhsT=ones[:, :], rhs=sq[:, c, :N], start=(c == 0), stop=(c == 7)),
                 reads=["sq", "ones"], writes=["ps_ss"])
        p.op("act", lambda e: e.activation(out=rstd[:, :N], in_=ps_ss[:, :N], func=AF.Sqrt, scale=1.0 / D, bias=eps_t[:, 0:1]),
             reads=["ps_ss", "eps"], writes=["rstd"])
        p.op("dve", lambda e: e.reciprocal(out=rstd[:, :N], in_=rstd[:, :N]), reads=["rstd"], writes=["rstd"])
        for c in range(8):
            p.op("dve", lambda e, c=c: e.tensor_tensor(out=tmp[:, c, :N], in0=x_sb[:, c, :N], in1=rstd[:, :N], op=ALU.mult),
                 reads=[xk, "rstd"], writes=["ntmp" + str(c)])
            p.op("act", lambda e, c=c: e.activation(out=ht[:, c, :N], in_=tmp[:, c, :N], func=AF.Identity,
                                                   scale=gs[:, c:c + 1], bias=sh[:, c:c + 1]),
                 reads=["ntmp" + str(c), "mods"], writes=[hk])


def fm(ap2d, t0, N):
    return ap2d[:, t0:t0 + N].rearrange("(c p) n -> p c n", p=128)


def ph_tok_in(p, XT, W, F, g_d, mods, tiles, PROJ, bias=None):
    nmod = len(mods)
    nfc = (F + 127) // 128
    nrm = Norm(p, 512, nmod, g_d, mods)
    if bias is not None:
        b_sb = p.sb([128, nfc], F32, name="b_sb")
        p.op("pool", lambda e: e.memset(b_sb[:, :], 0.0), writes=["b_sb"])
        nfull = F // 128
        p.dma("sp", b_sb[:, :nfull], col(bias[:nfull * 128]), writes=["b_sb"], allow_slow_non_contiguous=True)
        if F % 128:
            p.dma("sp", b_sb[:F % 128, nfull:nfull + 1], bias[nfull * 128:F].rearrange("(p o) -> p o", o=1), writes=["b_sb"],
                  allow_slow_non_contiguous=True)
    wb = load_weight_bf16(p, W, D, F, "wb")
    xs = [p.sb([128, 8, 512], F32, name=f"xt{i}") for i in range(2)]
    hts = [p.sb([128, 8, 512], BF16, name=f"ht{i}") for i in range(2)]
    pss = [p.ps([128, 512], name=f"psm{i}") for i in range(4)]
    outs = [p.sb([128, 512], F32, name=f"o{i}") for i in range(4)]
    oi = 0
    for ti, (t0, N, mi) in enumerate(tiles):
        x_sb, xk, ht, hk = xs[ti % 2], f"xt{ti%2}", hts[ti % 2], f"ht{ti%2}"
        p.dma("sp", x_sb[:, :, :N], fm(XT, t0, N), reads=["XT"], writes=[xk])
        nrm.apply(x_sb, xk, N, mi, ht, hk)
        for fc in range(nfc):
            M = min(128, F - fc * 128)
            ps, pk, o, ok = pss[oi % 4], f"psm{oi%4}", outs[oi % 4], f"o{oi%4}"
            for c in range(8):
                p.op("pe", lambda e, ps=ps, c=c, fc=fc, M=M, ht=ht, N=N: e.matmul(ps[:M, :N], lhsT=wb[:, c, fc * 128:fc * 128 + M],
                                                                                  rhs=ht[:, c, :N], start=(c == 0), stop=(c == 7)),
                     reads=["wb", hk], writes=[pk])
            if bias is not None:
                p.op("act", lambda e, ps=ps, o=o, M=M, N=N, fc=fc: e.activation(out=o[:M, :N], in_=ps[:M, :N], func=AF.Identity,
                                                                                 bias=b_sb[:M, fc:fc + 1], scale=1.0),
                     reads=[pk, "b_sb"], writes=[ok])
            elif oi % 2 == 0:
                p.op("act", lambda e, ps=ps, o=o, M=M, N=N: e.copy(out=o[:M, :N], in_=ps[:M, :N]), reads=[pk], writes=[ok])
            else:
                p.op("dve", lambda e, ps=ps, o=o, M=M, N=N: e.tensor_copy(out=o[:M, :N], in_=ps[:M, :N]), reads=[pk], writes=[ok])
            p.dma("pool", PROJ[fc * 128:fc * 128 + M, t0:t0 + N], o[:M, :N], reads=[ok], writes=["PROJ"])
            oi += 1


def ph_outproj(p, XT, Y, CM, W, ga_list, tiles, bias=None):
    kc = CM // 128
    nmod = len(ga_list)
    ga = p.sb([128, nmod, 8], F32, name="ga")
    for m, gd in enumerate(ga_list):
        load_cols(p, ga[:, m, :], gd, "ga")
    if bias is not None:
        b_sb = p.sb([128, 8], F32, name="ob_sb")
        load_cols(p, b_sb[:, :], bias, "ob")
        gb = p.sb([128, nmod, 8], F32, name="gb")
        for m in range(nmod):
            p.op("dve", lambda e, m=m: e.tensor_tensor(out=gb[:, m, :], in0=ga[:, m, :], in1=b_sb[:, :], op=ALU.mult),
                 reads=["ga", "ob"], writes=["gb"])
    wb = load_weight_bf16(p, W, CM, D, "wo")
    xs = [p.sb([128, 8, 512], F32, name=f"xo{i}") for i in range(2)]
    ys = [p.sb([128, kc, 512], BF16, name=f"yo{i}") for i in range(2)]
    pss = [p.ps([128, 512], name=f"pso{i}") for i in range(4)]
    oi = 0
    for ti, (t0, N, mi) in enumerate(tiles):
        x_sb, xk, y_sb, yk = xs[ti % 2], f"xo{ti%2}", ys[ti % 2], f"yo{ti%2}"
        p.dma("sp", x_sb[:, :, :N], fm(XT, t0, N), reads=["XT"], writes=[xk])
        p.dma("sp", y_sb[:, :, :N], fm(Y[0:CM, :], t0, N), reads=["Y"], writes=[yk])
        if bias is not None:
            for c in range(8):
                p.op("act", lambda e, c=c, x_sb=x_sb, N=N, mi=mi: e.activation(out=x_sb[:, c, :N], in_=x_sb[:, c, :N], func=AF.Identity,
                                                                                 bias=gb[:, mi, c:c + 1], scale=1.0),
                     reads=[xk, "gb"], writes=[xk])
        for m in range(8):
            ps, pk = pss[oi % 4], f"pso{oi%4}"
            for c in range(kc):
                p.op("pe", lambda e, ps=ps, c=c, m=m, y_sb=y_sb, N=N: e.matmul(ps[:, :N], lhsT=wb[:, c, m * 128:(m + 1) * 128],
                                                                                rhs=y_sb[:, c, :N], start=(c == 0), stop=(c == kc - 1)),
                     reads=["wo", yk], writes=[pk])
            p.op("dve", lambda e, ps=ps, m=m, x_sb=x_sb, N=N, mi=mi: e.scalar_tensor_tensor(
                out=x_sb[:, m, :N], in0=ps[:, :N], scalar=ga[:, mi, m:m + 1], in1=x_sb[:, m, :N], op0=ALU.mult, op1=ALU.add),
                 reads=[pk, xk, "ga"], writes=[xk])
            oi += 1
        p.dma("pool", fm(XT, t0, N), x_sb[:, :, :N], reads=[xk], writes=["XT"])


def ph_mlp(p, XT, W1, W2, g_d, mods, gf_list, tiles):
    nmod = len(mods)
    NT = 256
    nrm = Norm(p, NT, nmod, g_d, mods)
    gf = p.sb([128, nmod, 8], F32, name="gf")
    for m, gd in enumerate(gf_list):
        load_cols(p, gf[:, m, :], gd, "gf")
    w1 = load_weight_bf16(p, W1, D, 4096, "w1")
    w2 = load_weight_bf16(p, W2, 4096, D, "w2")
    xs = [p.sb([128, 8, NT], F32, name=f"xm{i}") for i in range(2)]
    ht = p.sb([128, 8, NT], BF16, name="htm")
    h1 = p.sb([128, 32, NT], BF16, name="h1")
    rs = [p.sb([128, NT], F32, name=f"r{i}") for i in range(2)]
    pss = [p.ps([128, 512], name=f"psx{i}") for i in range(4)]
    oi = 0
    for ti, (t0, N, mi) in enumerate(tiles):
        x_sb, xk = xs[ti % 2], f"xm{ti%2}"
        p.dma("sp", x_sb[:, :, :N], fm(XT, t0, N), reads=["XT"], writes=[xk])
        nrm.apply(x_sb, xk, N, mi, ht, "htm")
        for hc in range(32):
            ps, pk, r, rk = pss[oi % 4], f"psx{oi%4}", rs[oi % 2], f"r{oi%2}"
            for c in range(8):
                p.op("pe", lambda e, ps=ps, c=c, hc=hc, N=N: e.matmul(ps[:, :N], lhsT=w1[:, c, hc * 128:(hc + 1) * 128],
                                                                       rhs=ht[:, c, :N], start=(c == 0), stop=(c == 7)),
                     reads=["w1", "htm"], writes=[pk])
            p.op("act", lambda e, ps=ps, r=r, N=N: e.activation(out=r[:, :N], in_=ps[:, :N], func=AF.Relu), reads=[pk], writes=[rk])
            p.op("pool", lambda e, r=r, hc=hc, N=N: e.tensor_tensor(out=h1[:, hc, :N], in0=r[:, :N], in1=r[:, :N], op=ALU.mult),
                 reads=[rk], writes=["h1_" + str(hc)])
            oi += 1
        for m in range(8):
            ps, pk = pss[oi % 4], f"psx{oi%4}"
            for hc in range(32):
                p.op("pe", lambda e, ps=ps, hc=hc, m=m, N=N: e.matmul(ps[:, :N], lhsT=w2[:, hc, m * 128:(m + 1) * 128],
                                                                       rhs=h1[:, hc, :N], start=(hc == 0), stop=(hc == 31)),
                     reads=["w2", "h1_" + str(hc)], writes=[pk])
            p.op("dve", lambda e, ps=ps, m=m, x_sb=x_sb, N=N, mi=mi: e.scalar_tensor_tensor(
                out=x_sb[:, m, :N], in0=ps[:, :N], scalar=gf[:, mi, m:m + 1], in1=x_sb[:, m, :N], op0=ALU.mult, op1=ALU.add),
                 reads=[pk, xk, "gf"], writes=[xk])
            oi += 1
        p.dma("pool", fm(XT, t0, N), x_sb[:, :, :N], reads=[xk], writes=["XT"])


def ph_final(p, XT, g_d, tiles, OUT, t_off, IDENT):
    nrm = Norm(p, 512, 1, g_d, [(None, None)])
    ident = p.sb([128, 128], F32, name="ident")
    p.dma("sp", ident[:, :], IDENT, writes=["ident"])
    xs = [p.sb([128, 8, 512], F32, name=f"xf{i}") for i in range(2)]
    hf = p.sb([128, 8, 512], F32, name="hf")
    pst = [p.ps([128, 512], name=f"pst{i}") for i in range(4)]
    ot = [p.sb([128, 1024], F32, name=f"ot{i}") for i in range(2)]
    oi = 0
    k = 0
    for ti, (t0, N, mi) in enumerate(tiles):
        x_sb, xk = xs[ti % 2], f"xf{ti%2}"
        p.dma("sp", x_sb[:, :, :N], fm(XT, t0, N), reads=["XT"], writes=[xk])
        nrm.apply(x_sb, xk, N, 0, hf, "hf")
        for j in range(N // 128):
            o, ok = ot[k % 2], f"ot{k%2}"
            for half in range(2):
                ps, pk = pst[oi % 4], f"pst{oi%4}"
                for cc in range(4):
                    c = half * 4 + cc
                    p.op("pe", lambda e, ps=ps, c=c, cc=cc, j=j: e.transpose(out=ps[:, cc * 128:(cc + 1) * 128],
                                                                            in_=hf[:, c, j * 128:(j + 1) * 128], identity=ident[:, :]),
                         reads=["hf", "ident"], writes=[pk])
                if half == 0:
                    p.op("act", lambda e, ps=ps, o=o: e.copy(out=o[:, 0:512], in_=ps[:, :]), reads=[pk], writes=[ok])
                else:
                    p.op("dve", lambda e, ps=ps, o=o: e.tensor_copy(out=o[:, 512:1024], in_=ps[:, :]), reads=[pk], writes=[ok])
                oi += 1
            tt = t0 + j * 128 - t_off
            p.dma("pool", OUT[tt:tt + 128, :], o[:, :], reads=[ok], writes=["OUT"])
            k += 1


TWO_PI = 2.0 * math.pi


def sin_tmps(p, shape, tag):
    return (p.sb(shape, F32, name=tag + "_a"), p.sb(shape, I32, name=tag + "_i"), p.sb(shape, F32, name=tag + "_f"),
            p.sb(shape, F32, name=tag + "_r"), p.sb(shape, F32, name=tag + "_m"))


def sin_rr(p, out, x, shape, tag, shift=0.0, tmps=None):
    a, ai, af, r, m = tmps if tmps is not None else sin_tmps(p, shape, tag)
    sl = tuple(slice(None) for _ in shape)
    k = tag
    p.op("dve", lambda e: e.tensor_scalar(out=a[sl], in0=x, scalar1=shift, scalar2=1.0 / TWO_PI, op0=ALU.add, op1=ALU.mult),
         reads=[k + "x"], writes=[k + "a"])
    p.op("dve", lambda e: e.tensor_copy(out=ai[sl], in_=a[sl]), reads=[k + "a"], writes=[k + "i"])
    p.op("dve", lambda e: e.tensor_copy(out=af[sl], in_=ai[sl]), reads=[k + "i"], writes=[k + "f"])
    p.op("dve", lambda e: e.tensor_scalar(out=r[sl], in0=x, scalar1=shift, scalar2=None, op0=ALU.add), reads=[k + "x"], writes=[k + "r"])
    p.op("dve", lambda e: e.scalar_tensor_tensor(out=r[sl], in0=af[sl], scalar=-TWO_PI, in1=r[sl], op0=ALU.mult, op1=ALU.add),
         reads=[k + "f", k + "r"], writes=[k + "r"])
    p.op("dve", lambda e: e.tensor_scalar(out=m[sl], in0=r[sl], scalar1=math.pi, scalar2=None, op0=ALU.is_gt), reads=[k + "r"], writes=[k + "m"])
    p.op("dve", lambda e: e.scalar_tensor_tensor(out=r[sl], in0=m[sl], scalar=-TWO_PI, in1=r[sl], op0=ALU.mult, op1=ALU.add),
         reads=[k + "m", k + "r"], writes=[k + "r"])
    p.op("dve", lambda e: e.tensor_scalar(out=m[sl], in0=r[sl], scalar1=-math.pi, scalar2=None, op0=ALU.is_lt), reads=[k + "r"], writes=[k + "m"])
    p.op("dve", lambda e: e.scalar_tensor_tensor(out=r[sl], in0=m[sl], scalar=TWO_PI, in1=r[sl], op0=ALU.mult, op1=ALU.add),
         reads=[k + "m", k + "r"], writes=[k + "r"])
    p.op("dve", lambda e: e.tensor_scalar(out=r[sl], in0=r[sl], scalar1=3.1415925, scalar2=-3.1415925, op0=ALU.min, op1=ALU.max),
         reads=[k + "r"], writes=[k + "r"])
    p.op("act", lambda e: e.activation(out=out, in_=r[sl], func=AF.Sin), reads=[k + "r"], writes=[k + "o"])


def s5_params(p, src3, shape, tag):
    sl = tuple(slice(None) for _ in shape)
    t = {n: p.sb(shape, F32, name=f"{tag}_{n}") for n in
         ("lre", "lim", "ldt", "step", "r", "th", "c", "s", "ar", "ai", "den", "fr", "fi", "t1", "t2")}
    k = tag
    p.dma("sp", t["lre"][sl], src3[0], writes=[k])
    p.dma("sp", t["lim"][sl], src3[1], writes=[k])
    p.dma("sp", t["ldt"][sl], src3[2], writes=[k])

    def dve(fn):
        p.op("dve", fn, reads=[k], writes=[k])

    p.op("act", lambda e: e.activation(out=t["step"][sl], in_=t["ldt"][sl], func=AF.Exp), reads=[k], writes=[k])
    dve(lambda e: e.tensor_tensor(out=t["t1"][sl], in0=t["lre"][sl], in1=t["step"][sl], op=ALU.mult))
    p.op("act", lambda e: e.activation(out=t["r"][sl], in_=t["t1"][sl], func=AF.Exp), reads=[k], writes=[k])
    dve(lambda e: e.tensor_tensor(out=t["th"][sl], in0=t["lim"][sl], in1=t["step"][sl], op=ALU.mult))
    p.op("dve", lambda e: e.tensor_copy(out=t["t2"][sl], in_=t["th"][sl]), reads=[k], writes=[k + "sx", k + "cx"])
    sin_rr(p, t["s"][sl], t["th"][sl], shape, k + "s")
    sin_rr(p, t["c"][sl], t["th"][sl], shape, k + "c", shift=math.pi / 2)
    p.op("dve", lambda e: e.tensor_tensor(out=t["ar"][sl], in0=t["r"][sl], in1=t["c"][sl], op=ALU.mult), reads=[k, k + "co"], writes=[k])
    p.op("dve", lambda e: e.tensor_tensor(out=t["ai"][sl], in0=t["r"][sl], in1=t["s"][sl], op=ALU.mult), reads=[k, k + "so"], writes=[k])
    dve(lambda e: e.tensor_tensor(out=t["den"][sl], in0=t["lre"][sl], in1=t["lre"][sl], op=ALU.mult))
    dve(lambda e: e.tensor_tensor(out=t["t1"][sl], in0=t["lim"][sl], in1=t["lim"][sl], op=ALU.mult))
    dve(lambda e: e.tensor_tensor(out=t["den"][sl], in0=t["den"][sl], in1=t["t1"][sl], op=ALU.add))
    dve(lambda e: e.reciprocal(out=t["den"][sl], in_=t["den"][sl]))
    dve(lambda e: e.tensor_scalar(out=t["t1"][sl], in0=t["ar"][sl], scalar1=-1.0, scalar2=None, op0=ALU.add))
    dve(lambda e: e.tensor_tensor(out=t["fr"][sl], in0=t["t1"][sl], in1=t["lre"][sl], op=ALU.mult))
    dve(lambda e: e.tensor_tensor(out=t["t2"][sl], in0=t["ai"][sl], in1=t["lim"][sl], op=ALU.mult))
    dve(lambda e: e.tensor_tensor(out=t["fr"][sl], in0=t["fr"][sl], in1=t["t2"][sl], op=ALU.add))
    dve(lambda e: e.tensor_tensor(out=t["fr"][sl], in0=t["fr"][sl], in1=t["den"][sl], op=ALU.mult))
    dve(lambda e: e.tensor_tensor(out=t["fi"][sl], in0=t["ai"][sl], in1=t["lre"][sl], op=ALU.mult))
    dve(lambda e: e.tensor_tensor(out=t["t2"][sl], in0=t["t1"][sl], in1=t["lim"][sl], op=ALU.mult))
    dve(lambda e: e.tensor_tensor(out=t["fi"][sl], in0=t["fi"][sl], in1=t["t2"][sl], op=ALU.subtract))
    dve(lambda e: e.tensor_tensor(out=t["fi"][sl], in0=t["fi"][sl], in1=t["den"][sl], op=ALU.mult))
    return t


def windows(T, NCTX, W=512):
    fw_ = [(0, NCTX)] + [(t, min(t + W, T)) for t in range(NCTX, T, W)]
    bw = [(0, NCTX)] + [(max(t - W, NCTX), t) for t in range(T, NCTX, -W)]
    return fw_, bw


def ph_s5_scan(p, PROJ, T, NCTX, LS, BT, CT, DSK, YPRE):
    W = 512
    ls = s5_params(p, LS, [128, 32], "ls")
    bt = p.sb([32, 2, 16, 128], F32, name="bt")
    p.dma("sp", bt[:, 0], BT[0], writes=["bt"])
    p.dma("sp", bt[:, 1], BT[1], writes=["bt"])
    ct = p.sb([128, 2, 16, 32], F32, name="ct")
    p.dma("sp", ct[:, 0], CT[0], writes=["ct"])
    p.dma("sp", ct[:, 1], CT[1], writes=["ct"])
    p.op("dve", lambda e: e.tensor_scalar(out=ct[:, 1], in0=ct[:, 1], scalar1=-1.0, scalar2=None, op0=ALU.mult), reads=["ct"], writes=["ct"])
    dsk = p.sb([32, 16], F32, name="dsk")
    p.dma("sp", dsk[:, :], DSK.rearrange("(g q) -> q g", q=32), writes=["dsk"], allow_slow_non_contiguous=True)
    nsc = p.sb([128, 32], F32, name="nsc")
    p.op("dve", lambda e: e.tensor_scalar(out=nsc[:, :], in0=ls["s"][:, :], scalar1=-1.0, scalar2=None, op0=ALU.mult), reads=["ls", "lsso"], writes=["nsc"])

    F32R = mybir.dt.float32r
    ctr = p.sb([128, 3, 16, 32], F32, name="ctr")
    p.op("dve", lambda e: e.tensor_copy(out=ctr[:, 0].bitcast(F32R), in_=ct[:, 0]), reads=["ct"], writes=["ctr"])
    p.op("dve", lambda e: e.tensor_scalar(out=ctr[:, 1].bitcast(F32R), in0=ct[:, 0], scalar1=-1.0, scalar2=None, op0=ALU.mult), reads=["ct"], writes=["ctr"])
    p.op("dve", lambda e: e.tensor_copy(out=ctr[:, 2].bitcast(F32R), in_=ct[:, 1]), reads=["ct"], writes=["ctr"])
    ones = p.sb([128, W], F32, name="s5ones")
    p.op("pool", lambda e: e.memset(ones[:, :], 1.0), writes=["s5ones"])
    mkd = lambda n: [p.sb([128, W], F32, name=f"{n}{i}") for i in range(2)]
    Ec, Es, rt, Mc, Ms = mkd("Ec"), mkd("Es"), mkd("rt"), mkd("Mc"), mkd("Ms")
    wv = p.sb([128, 4], F32, name="wv")
    wN = [p.sb([128, 2, 3], F32, name=f"wN{i}") for i in range(2)]
    tt = p.sb([128, W], F32, name="ttab")
    u_sb = p.sb([32, T], F32, name="u_sb")
    ysum = p.sb([32, T], F32, name="ysum")
    psA = [p.ps([128, 512], name=f"psA{i}") for i in range(2)]
    psB = [p.ps([128, 512], name=f"psB{i}") for i in range(2)]
    psY = [p.ps([128, 512], name=f"psY{i}") for i in range(2)]
    ta, tb_, tc, td = mkd("ta"), mkd("tb"), mkd("tc"), mkd("td")
    mre, mim = mkd("mre"), mkd("mim")
    gre = [[p.sb([128, W], F32, name=f"gre{i}{j}") for j in range(2)] for i in range(2)]
    gim = [[p.sb([128, W], F32, name=f"gim{i}{j}") for j in range(2)] for i in range(2)]
    q1, q2, q3, q4 = mkd("q1"), mkd("q2"), mkd("q3"), mkd("q4")
    init = [p.sb([128, 4], F32, name=f"init{i}") for i in range(2)]
    fwin, bwin = windows(T, NCTX, W)
    assert len(fwin) == len(bwin)
    for gp in range(16):
        p.dma("sp", u_sb[:, :], PROJ[gp * 32:(gp + 1) * 32, :], reads=["PROJ"], writes=["u_sb"])
        p.op("pool", lambda e: e.memset(ysum[:, :], 0.0), writes=["ysum"])
        for d in range(2):
            cix = d * 16 + gp
            cs, sn, rr = ls["c"][:, cix:cix + 1], ls["s"][:, cix:cix + 1], ls["r"][:, cix:cix + 1]
            EC, ES, MC, MS, RT, WN, ek, mk_ = Ec[d], Es[d], Mc[d], Ms[d], rt[d], wN[d], f"E{d}", f"M{d}"
            p.op("dve", lambda e, EC=EC: e.memset(EC[:, 0:1], 1.0), writes=[ek])
            p.op("dve", lambda e, ES=ES: e.memset(ES[:, 0:1], 0.0), writes=[ek])
            p.op("dve", lambda e, cs=cs: e.tensor_copy(out=wv[:, 0:1], in_=cs), reads=["ls", "lsco"], writes=["wv"])
            p.op("dve", lambda e, sn=sn: e.tensor_copy(out=wv[:, 1:2], in_=sn), reads=["ls", "lsso"], writes=["wv"])
            m = 1
            while m < W:
                p.op("dve", lambda e: e.tensor_scalar(out=wv[:, 2:3], in0=wv[:, 1:2], scalar1=-1.0, scalar2=None, op0=ALU.mult), reads=["wv"], writes=["wv"])
                if m == 256:
                    p.op("dve", lambda e, WN=WN: e.tensor_copy(out=WN[:, 0, :], in_=wv[:, 0:3]), reads=["wv"], writes=[f"wN{d}"])
                p.op("dve", lambda e, m=m, EC=EC: e.tensor_scalar(out=tt[:, 0:m], in0=EC[:, 0:m], scalar1=wv[:, 0:1], scalar2=None, op0=ALU.mult), reads=[ek, "wv"], writes=["ttab"])
                p.op("dve", lambda e, m=m, EC=EC, ES=ES: e.scalar_tensor_tensor(out=EC[:, m:2 * m], in0=ES[:, 0:m], scalar=wv[:, 2:3], in1=tt[:, 0:m], op0=ALU.mult, op1=ALU.add),
                     reads=[ek, "wv", "ttab"], writes=[ek])
                p.op("dve", lambda e, m=m, EC=EC: e.tensor_scalar(out=tt[:, 0:m], in0=EC[:, 0:m], scalar1=wv[:, 1:2], scalar2=None, op0=ALU.mult), reads=[ek, "wv"], writes=["ttab"])
                p.op("dve", lambda e, m=m, ES=ES: e.scalar_tensor_tensor(out=ES[:, m:2 * m], in0=ES[:, 0:m], scalar=wv[:, 0:1], in1=tt[:, 0:m], op0=ALU.mult, op1=ALU.add),
                     reads=[ek, "wv", "ttab"], writes=[ek])
                p.op("dve", lambda e: e.tensor_tensor(out=wv[:, 3:4], in0=wv[:, 1:2], in1=wv[:, 1:2], op=ALU.mult), reads=["wv"], writes=["wv"])
                p.op("dve", lambda e: e.scalar_tensor_tensor(out=wv[:, 1:2], in0=wv[:, 1:2], scalar=2.0, in1=wv[:, 0:1], op0=ALU.mult, op1=ALU.mult), reads=["wv"], writes=["wv"])
                p.op("dve", lambda e: e.tensor_tensor(out=wv[:, 0:1], in0=wv[:, 0:1], in1=wv[:, 0:1], op=ALU.mult), reads=["wv"], writes=["wv"])
                p.op("dve", lambda e: e.tensor_tensor(out=wv[:, 0:1], in0=wv[:, 0:1], in1=wv[:, 3:4], op=ALU.subtract), reads=["wv"], writes=["wv"])
                m *= 2
            p.op("dve", lambda e: e.tensor_scalar(out=wv[:, 2:3], in0=wv[:, 1:2], scalar1=-1.0, scalar2=None, op0=ALU.mult), reads=["wv"], writes=["wv"])
            p.op("dve", lambda e, WN=WN: e.tensor_copy(out=WN[:, 1, :], in_=wv[:, 0:3]), reads=["wv"], writes=[f"wN{d}"])
            frc, fic = ls["fr"][:, cix:cix + 1], ls["fi"][:, cix:cix + 1]
            p.op("dve", lambda e, fic=fic, ES=ES: e.tensor_scalar(out=tt[:, :], in0=ES[:, :], scalar1=fic, scalar2=None, op0=ALU.mult), reads=[ek, "ls"], writes=["ttab"])
            p.op("dve", lambda e, frc=frc, EC=EC, MC=MC: e.scalar_tensor_tensor(out=MC[:, :], in0=EC[:, :], scalar=frc, in1=tt[:, :], op0=ALU.mult, op1=ALU.add), reads=[ek, "ls", "ttab"], writes=[mk_])
            p.op("dve", lambda e, fic=fic, EC=EC: e.tensor_scalar(out=tt[:, :], in0=EC[:, :], scalar1=fic, scalar2=None, op0=ALU.mult), reads=[ek, "ls", mk_], writes=["ttab"])
            p.op("dve", lambda e, frc=frc, ES=ES, MS=MS: e.scalar_tensor_tensor(out=MS[:, :], in0=ES[:, :], scalar=frc, in1=tt[:, :], op0=ALU.mult, op1=ALU.subtract), reads=[ek, "ls", "ttab"], writes=[mk_])
            p.op("dve", lambda e, rr=rr, RT=RT: e.tensor_scalar(out=RT[:, :], in0=ones[:, :], scalar1=rr, scalar2=None, op0=ALU.mult), reads=["ls", "s5ones"], writes=[f"rt{d}"])
        prevs = [None, None]
        for wi_ in range(len(fwin)):
            for d in range(2):
                lo, hi = (fwin if d == 0 else bwin)[wi_]
                N = hi - lo
                sl = slice(lo, hi) if d == 0 else slice(hi - 1, lo - 1 if lo > 0 else None, -1)
                EC, ES, MC, MS, RT, WN, ek, mk_ = Ec[d], Es[d], Mc[d], Ms[d], rt[d], wN[d], f"E{d}", f"M{d}"
                b2 = d
                A, B, Yp = psA[b2], psB[b2], psY[b2]
                ak, bk, yk = f"psA{b2}", f"psB{b2}", f"psY{b2}"
                INIT, ik = init[d], f"init{d}"
                p.op("pe", lambda e, A=A, sl=sl, N=N, gp=gp: e.matmul(A[:, :N], lhsT=bt[:, 0, gp, :], rhs=u_sb[:, sl], start=True, stop=True),
                     reads=["bt", "u_sb"], writes=[ak])
                p.op("pe", lambda e, B=B, sl=sl, N=N, gp=gp: e.matmul(B[:, :N], lhsT=bt[:, 1, gp, :], rhs=u_sb[:, sl], start=True, stop=True),
                     reads=["bt", "u_sb"], writes=[bk])
                TA, TB, TC, TD = ta[b2], tb_[b2], tc[b2], td[b2]
                par = wi_ % 2
                MR, MI, GR, GI = mre[b2], mim[b2], gre[b2][par], gim[b2][par]
                PGR, PGI = gre[b2][1 - par], gim[b2][1 - par]
                gk, gik, pgk, pgik = f"gre{b2}{par}", f"gim{b2}{par}", f"gre{b2}{1-par}", f"gim{b2}{1-par}"
                prev = prevs[d]
                if prev is None:
                    p.op("dve", lambda e, INIT=INIT: e.memset(INIT[:, :], 0.0), writes=[ik])
                else:
                    pN = prev
                    wsel = 0 if pN == 256 else 1
                    assert pN in (256, 512)
                    p.op("dve", lambda e, GR=PGR, pN=pN, wsel=wsel, INIT=INIT, WN=WN: e.tensor_scalar(out=INIT[:, 2:3], in0=GR[:, pN - 1:pN], scalar1=WN[:, wsel, 0:1], scalar2=None, op0=ALU.mult),
                         reads=[pgk, f"wN{d}"], writes=[ik])
                    p.op("dve", lambda e, GI=PGI, pN=pN, wsel=wsel, INIT=INIT, WN=WN: e.scalar_tensor_tensor(out=INIT[:, 0:1], in0=GI[:, pN - 1:pN], scalar=WN[:, wsel, 2:3], in1=INIT[:, 2:3], op0=ALU.mult, op1=ALU.add),
                         reads=[pgik, f"wN{d}", ik], writes=[ik])
                    p.op("dve", lambda e, GR=PGR, pN=pN, wsel=wsel, INIT=INIT, WN=WN: e.tensor_scalar(out=INIT[:, 3:4], in0=GR[:, pN - 1:pN], scalar1=WN[:, wsel, 1:2], scalar2=None, op0=ALU.mult),
                         reads=[pgk, f"wN{d}"], writes=[ik])
                    p.op("dve", lambda e, GI=PGI, pN=pN, wsel=wsel, INIT=INIT, WN=WN: e.scalar_tensor_tensor(out=INIT[:, 1:2], in0=GI[:, pN - 1:pN], scalar=WN[:, wsel, 0:1], in1=INIT[:, 3:4], op0=ALU.mult, op1=ALU.add),
                         reads=[pgik, f"wN{d}", ik], writes=[ik])
                p.op("dve", lambda e, A=A, N=N, TA=TA, MC=MC: e.tensor_tensor(out=TA[:, :N], in0=A[:, :N], in1=MC[:, :N], op=ALU.mult), reads=[ak, mk_], writes=[f"ta{b2}"])
                p.op("dve", lambda e, B=B, N=N, TB=TB, MS=MS: e.tensor_tensor(out=TB[:, :N], in0=B[:, :N], in1=MS[:, :N], op=ALU.mult), reads=[bk, mk_], writes=[f"tb{b2}"])
                p.op("dve", lambda e, N=N, TA=TA, TB=TB, MR=MR: e.tensor_tensor(out=MR[:, :N], in0=TA[:, :N], in1=TB[:, :N], op=ALU.add),
                     reads=[f"ta{b2}", f"tb{b2}"], writes=[f"mre{b2}"])
                p.op("dve", lambda e, B=B, N=N, TC=TC, MC=MC: e.tensor_tensor(out=TC[:, :N], in0=B[:, :N], in1=MC[:, :N], op=ALU.mult), reads=[bk, mk_], writes=[f"tc{b2}"])
                p.op("dve", lambda e, A=A, N=N, TD=TD, MS=MS: e.tensor_tensor(out=TD[:, :N], in0=A[:, :N], in1=MS[:, :N], op=ALU.mult), reads=[ak, mk_], writes=[f"td{b2}"])
                p.op("dve", lambda e, N=N, TC=TC, TD=TD, MI=MI: e.tensor_tensor(out=MI[:, :N], in0=TC[:, :N], in1=TD[:, :N], op=ALU.subtract),
                     reads=[f"tc{b2}", f"td{b2}"], writes=[f"mim{b2}"])
                p.op("dve", lambda e, N=N, GR=GR, MR=MR, RT=RT, INIT=INIT: e.tensor_tensor_scan(out=GR[:, :N], data0=RT[:, :N], data1=MR[:, :N], initial=INIT[:, 0:1], op0=ALU.mult, op1=ALU.add),
                     reads=[f"rt{d}", f"mre{b2}", ik], writes=[gk])
                p.op("dve", lambda e, N=N, GI=GI, MI=MI, RT=RT, INIT=INIT: e.tensor_tensor_scan(out=GI[:, :N], data0=RT[:, :N], data1=MI[:, :N], initial=INIT[:, 1:2], op0=ALU.mult, op1=ALU.add),
                     reads=[f"rt{d}", f"mim{b2}", ik], writes=[gik])
                Q1, Q2, Q3, Q4 = q1[b2], q2[b2], q3[b2], q4[b2]
                for (Q, G, Et, qn, gkk) in ((Q1, GR, EC, "q1", gk), (Q2, GI, ES, "q2", gik), (Q3, GR, ES, "q3", gk), (Q4, GI, EC, "q4", gik)):
                    p.op("pool", lambda e, N=N, Q=Q, G=G, Et=Et: e.tensor_tensor(out=Q[:, :N].bitcast(F32R), in0=G[:, :N], in1=Et[:, :N], op=ALU.mult),
                         reads=[gkk, ek], writes=[f"{qn}{b2}"])
                for k, (Q, ci, qn) in enumerate(((Q1, 0, "q1"), (Q2, 1, "q2"), (Q3, 2, "q3"), (Q4, 2, "q4"))):
                    p.op("pe", lambda e, Yp=Yp, Q=Q, N=N, gp=gp, ci=ci, k=k: e.matmul(Yp[:32, :N], lhsT=ctr[:, ci, gp, :].bitcast(F32R), rhs=Q[:, :N].bitcast(F32R),
                                                                                   start=(k == 0), stop=(k == 3)), reads=["ctr", f"{qn}{b2}"], writes=[yk])
                p.op("act", lambda e, Yp=Yp, N=N, TA=TA: e.copy(out=TA[:32, :N], in_=Yp[:32, :N]), reads=[yk, f"ta{b2}"], writes=[f"ta{b2}"])
                p.op("pool", lambda e, TA=TA, sl=sl, N=N: e.tensor_tensor(out=ysum[:, sl], in0=TA[:32, :N], in1=ysum[:, sl], op=ALU.add),
                     reads=[f"ta{b2}", "ysum"], writes=["ysum"])
                prevs[d] = N
        p.op("dve", lambda e, gp=gp: e.scalar_tensor_tensor(out=ysum[:, :], in0=u_sb[:, :], scalar=dsk[:, gp:gp + 1], in1=ysum[:, :], op0=ALU.mult, op1=ALU.add),
             reads=["u_sb", "ysum", "dsk"], writes=["ysum"])
        p.dma("pool", YPRE[gp * 32:(gp + 1) * 32, :], ysum[:, :], reads=["ysum"], writes=["YPRE"])


def ph_s5_glu(p, YPRE, T, GW, GB, Y):
    N = 512
    K0 = 0.7978845608028654
    gw = load_weight_bf16(p, GW, 512, 512, "gw", chunk=512)
    gb = p.sb([128, 4], F32, name="gb")
    load_cols(p, gb[:, :], GB, "gb")
    ys = [p.sb([128, 4, N], F32, name=f"yp{i}") for i in range(2)]
    x2 = p.sb([128, 4, N], F32, name="x2")
    g = p.sb([128, 4, N], F32, name="gg")
    gbf = p.sb([128, 4, N], BF16, name="gbf")
    sg = p.sb([128, N], F32, name="sg")
    o = [p.sb([128, N], BF16, name=f"og{i}") for i in range(2)]
    pss = [p.ps([128, 512], name=f"psg{i}") for i in range(2)]
    oi = 0
    for ti, t0 in enumerate(range(0, T, N)):
        n = min(N, T - t0)
        y, yk = ys[ti % 2], f"yp{ti%2}"
        p.dma("sp", y[:, :, :n], fm(YPRE, t0, n), reads=["YPRE"], writes=[yk])
        p.op("pool", lambda e, y=y, n=n: e.tensor_tensor(out=x2[:, :, :n], in0=y[:, :, :n], in1=y[:, :, :n], op=ALU.mult), reads=[yk], writes=["x2"])
        p.op("dve", lambda e, n=n: e.tensor_scalar(out=x2[:, :, :n], in0=x2[:, :, :n], scalar1=0.044715, scalar2=1.0, op0=ALU.mult, op1=ALU.add), reads=["x2"], writes=["x2"])
        p.op("pool", lambda e, y=y, n=n: e.tensor_tensor(out=x2[:, :, :n], in0=x2[:, :, :n], in1=y[:, :, :n], op=ALU.mult), reads=[yk, "x2"], writes=["x2"])
        p.op("act", lambda e, n=n: e.activation(out=x2[:, :, :n], in_=x2[:, :, :n], func=AF.Sigmoid, scale=2.0 * K0), reads=["x2"], writes=["x2"])
        p.op("dve", lambda e, y=y, n=n: e.tensor_tensor(out=g[:, :, :n], in0=x2[:, :, :n], in1=y[:, :, :n], op=ALU.mult), reads=[yk, "x2"], writes=["gg"])
        p.op("act", lambda e, n=n: e.copy(out=gbf[:, :, :n], in_=g[:, :, :n]), reads=["gg"], writes=["gbf"])
        for j in range(4):
            ps, pk, oo, ok = pss[oi % 2], f"psg{oi%2}", o[oi % 2], f"og{oi%2}"
            for c in range(4):
                p.op("pe", lambda e, ps=ps, c=c, j=j, n=n: e.matmul(ps[:, :n], lhsT=gw[:, c, j * 128:(j + 1) * 128], rhs=gbf[:, c, :n], start=(c == 0), stop=(c == 3)),
                     reads=["gw", "gbf"], writes=[pk])
            p.op("act", lambda e, ps=ps, j=j, n=n: e.activation(out=sg[:, :n], in_=ps[:, :n], func=AF.Sigmoid, bias=gb[:, j:j + 1], scale=1.0),
                 reads=[pk, "gb"], writes=["sg"])
            p.op("dve", lambda e, oo=oo, j=j, n=n: e.tensor_tensor(out=oo[:, :n], in0=sg[:, :n], in1=g[:, j, :n], op=ALU.mult), reads=["sg", "gg"], writes=[ok])
            p.dma("pool", Y[j * 128:(j + 1) * 128, t0:t0 + n], oo[:, :n], reads=[ok], writes=["Y"])
            oi += 1


NH = 16


def ph_conv_silu(p, SRC, row0, nrows, T, segs, CW, CB, DST, silu=True):
    nch = nrows // 128
    w = p.sb([128, 3, nch], F32, name="cw")
    for k in range(3):
        load_cols(p, w[:, k, :], CW[k], "cw")
    b = p.sb([128, nch], F32, name="cb")
    load_cols(p, b[:, :], CB, "cw")
    xs = [p.sb([128, T], F32, name=f"cx{i}") for i in range(2)]
    acc = [p.sb([128, T], F32, name=f"ca{i}") for i in range(2)]
    for c in range(nch):
        x, xk, a, ak = xs[c % 2], f"cx{c%2}", acc[c % 2], f"ca{c%2}"
        p.dma("sp", x[:, :], SRC[row0 + c * 128:row0 + (c + 1) * 128, :], reads=["SRC"], writes=[xk])
        p.op("dve", lambda e, x=x, a=a, c=c: e.tensor_scalar(out=a[:, :], in0=x[:, :], scalar1=w[:, 1, c:c + 1], scalar2=b[:, c:c + 1], op0=ALU.mult, op1=ALU.add),
             reads=[xk, "cw"], writes=[ak])
        for (s0, s1) in segs:
            p.op("dve", lambda e, x=x, a=a, c=c, s0=s0, s1=s1: e.scalar_tensor_tensor(out=a[:, s0 + 1:s1], in0=x[:, s0:s1 - 1], scalar=w[:, 0, c:c + 1], in1=a[:, s0 + 1:s1],
                                                                                      op0=ALU.mult, op1=ALU.add), reads=[xk, "cw", ak], writes=[ak])
            p.op("dve", lambda e, x=x, a=a, c=c, s0=s0, s1=s1: e.scalar_tensor_tensor(out=a[:, s0:s1 - 1], in0=x[:, s0 + 1:s1], scalar=w[:, 2, c:c + 1], in1=a[:, s0:s1 - 1],
                                                                                      op0=ALU.mult, op1=ALU.add), reads=[xk, "cw", ak], writes=[ak])
        if silu:
            p.op("act", lambda e, a=a: e.activation(out=a[:, :], in_=a[:, :], func=AF.Silu), reads=[ak], writes=[ak])
        p.dma("pool", DST[c * 128:(c + 1) * 128, :], a[:, :], reads=[ak], writes=["DST"])


def ph_ssd_prep(p, PROJ, T, DTB, ALOG, SC, CF2, ET):
    TB = 1408
    NCB = TB // 128
    dtb = p.sb([32, 1], F32, name="dtb")
    p.dma("sp", dtb[:, :], DTB.rearrange("(p o) -> p o", o=1), writes=["dtb"], allow_slow_non_contiguous=True)
    al = p.sb([32, 1], F32, name="al")
    p.dma("sp", al[:, :], ALOG.rearrange("(p o) -> p o", o=1), writes=["al"], allow_slow_non_contiguous=True)
    p.op("act", lambda e: e.activation(out=al[:, :], in_=al[:, :], func=AF.Exp), reads=["al"], writes=["al"])
    p.op("dve", lambda e: e.tensor_scalar(out=al[:, :], in0=al[:, :], scalar1=-1.0, scalar2=None, op0=ALU.mult), reads=["al"], writes=["al"])
    names = ["dt", "a", "pin", "sin", "cx", "ncx", "npin", "e1", "e2", "e3", "e4", "m0", "m1"]
    t = {n: p.sb([32, TB], F32, name="pp_" + n) for n in names}
    et = p.sb([32, NCB], F32, name="et")

    def A(n):
        return t[n][:, :]

    p.op("pool", lambda e: e.memset(A("m0"), 1.0), writes=["m0"])
    p.op("pool", lambda e: e.memset(A("m1"), 1.0), writes=["m1"])
    m0v = A("m0").rearrange("p (c q) -> p c q", q=128)
    m1v = A("m1").rearrange("p (c q) -> p c q", q=128)
    p.op("pool", lambda e: e.memset(m0v[:, :, 0:1], 0.0), reads=["m0"], writes=["m0"])
    p.op("pool", lambda e: e.memset(m1v[:, :, 127:128], 0.0), reads=["m1"], writes=["m1"])
    for b0 in range(0, T, TB):
        ts = slice(b0, b0 + TB)
        p.dma("sp", A("dt"), PROJ[3584:3616, ts], reads=["PROJ"], writes=["dt"])
        p.op("act", lambda e: e.activation(out=A("dt"), in_=A("dt"), func=AF.Exp, bias=dtb[:, 0:1], scale=1.0), reads=["dt", "dtb"], writes=["dt"])
        p.op("act", lambda e: e.activation(out=A("dt"), in_=A("dt"), func=AF.Ln, bias=1.0, scale=1.0), reads=["dt"], writes=["dt"])
        p.op("dve", lambda e: e.tensor_scalar(out=A("a"), in0=A("dt"), scalar1=al[:, 0:1], scalar2=None, op0=ALU.mult), reads=["dt", "al"], writes=["a"])
        p.op("dve", lambda e: e.tensor_tensor_scan(out=A("pin"), data0=A("m0"), data1=A("a"), initial=0.0, op0=ALU.mult, op1=ALU.add),
             reads=["a", "m0"], writes=["pin"])
        p.op("dve", lambda e: e.tensor_tensor_scan(out=t["sin"][:, ::-1], data0=t["m1"][:, ::-1], data1=t["a"][:, ::-1], initial=0.0, op0=ALU.mult, op1=ALU.add),
             reads=["a", "m1"], writes=["sin"])
        p.dma("pool", SC[0:32, ts], A("dt"), reads=["dt"], writes=["SC"])
        p.op("dve", lambda e: e.tensor_tensor(out=A("e1"), in0=A("sin"), in1=A("a"), op=ALU.subtract), reads=["sin", "a"], writes=["e1"])
        p.op("act", lambda e: e.activation(out=A("e1"), in_=A("e1"), func=AF.Exp), reads=["e1"], writes=["e1"])
        p.op("dve", lambda e: e.tensor_tensor(out=A("e1"), in0=A("e1"), in1=A("dt"), op=ALU.mult), reads=["e1", "dt"], writes=["e1"])
        p.dma("pool", SC[32:48, ts], t["e1"][0:16, :], reads=["e1"], writes=["SC"])
        p.op("dve", lambda e: e.tensor_tensor(out=A("cx"), in0=A("pin"), in1=A("a"), op=ALU.subtract), reads=["pin", "a"], writes=["cx"])
        p.op("act", lambda e: e.activation(out=A("e2"), in_=A("cx"), func=AF.Exp), reads=["cx"], writes=["e2"])
        p.op("dve", lambda e: e.tensor_tensor(out=A("e2"), in0=A("e2"), in1=A("dt"), op=ALU.mult), reads=["e2", "dt"], writes=["e2"])
        p.dma("pool", SC[48:64, ts], t["e2"][16:32, :], reads=["e2"], writes=["SC"])
        p.op("act", lambda e: e.activation(out=A("e3"), in_=A("pin"), func=AF.Exp), reads=["pin"], writes=["e3"])
        p.dma("pool", SC[64:80, ts], t["e3"][0:16, :], reads=["e3"], writes=["SC"])
        p.op("act", lambda e: e.activation(out=A("e4"), in_=A("sin"), func=AF.Exp), reads=["sin"], writes=["e4"])
        p.dma("pool", SC[80:96, ts], t["e4"][16:32, :], reads=["e4"], writes=["SC"])
        p.op("dve", lambda e: e.tensor_scalar(out=A("ncx"), in0=A("cx"), scalar1=-1.0, scalar2=None, op0=ALU.mult), reads=["cx"], writes=["ncx"])
        p.op("dve", lambda e: e.tensor_scalar(out=A("npin"), in0=A("pin"), scalar1=-1.0, scalar2=None, op0=ALU.mult), reads=["pin"], writes=["npin"])
        p.dma("pool", CF2[0, 0:16, ts], t["pin"][0:16, :], reads=["pin"], writes=["CF2"])
        p.dma("pool", CF2[0, 16:32, ts], t["ncx"][16:32, :], reads=["ncx"], writes=["CF2"])
        p.dma("pool", CF2[1, 0:16, ts], t["npin"][0:16, :], reads=["npin"], writes=["CF2"])
        p.dma("pool", CF2[1, 16:32, ts], t["cx"][16:32, :], reads=["cx"], writes=["CF2"])
        pv = A("pin").rearrange("p (c q) -> p c q", q=128)
        p.op("act", lambda e, pv=pv: e.activation(out=et[:, :], in_=pv[:, :, 127], func=AF.Exp), reads=["pin"], writes=["et"])
        c0 = b0 // 128
        p.dma("pool", ET[c0:c0 + NCB, :].rearrange("c j -> j c"), et[:, :], reads=["et"], writes=["ET"], allow_slow_non_contiguous=True)


def ph_ssd_pass(p, d, XBC, T, NCTX, SC, CF2, ET, DSKIP_BC, IDENT, MASKS, YT, YM):
    NCH = T // 128
    nctx = NCTX // 128
    order = list(range(NCH)) if d == 0 else list(range(nctx - 1, -1, -1)) + list(range(NCH - 1, nctx - 1, -1))
    ident = p.sb([128, 128], F32, name="ident")
    p.dma("sp", ident[:, :], IDENT, writes=["ident"])
    identb = p.sb([128, 128], BF16, name="identb")
    p.op("dve", lambda e: e.tensor_copy(out=identb[:, :], in_=ident[:, :]), reads=["ident"], writes=["identb"])
    maskf = p.sb([128, 512], F32, name="maskf")
    p.dma("sp", maskf[:, :], MASKS[d], writes=["maskf"])
    maskb = p.sb([128, 512], BF16, name="maskb")
    p.op("dve", lambda e: e.tensor_copy(out=maskb[:, :], in_=maskf[:, :]), reads=["maskf"], writes=["maskb"])
    etb = p.sb([128, NCH, 32], F32, name="etb")
    p.dma("sp", etb[:, :, :], ET.partition_broadcast(128), reads=["ET"], writes=["etb"])
    dsk = p.sb([128, 16], F32, name="dskb")
    p.dma("sp", dsk[:, :], DSKIP_BC, writes=["dskb"])
    H = p.sb([128, 1024], F32, name="H")
    Hb = p.sb([128, 1024], BF16, name="Hb")
    p.op("pool", lambda e: e.memset(H[:, :], 0.0), writes=["H"])
    p.op("pool", lambda e: e.memset(Hb[:, :], 0.0), writes=["Hb"])
    xin = [p.sb([128, 16, 128], F32, name=f"xin{i}") for i in range(2)]
    sct = [p.sb([96, 128], F32, name=f"sct{i}") for i in range(2)]
    Lc = [p.sb([2, 32, 128], F32, name=f"Lc{i}") for i in range(2)]
    Rc = [p.sb([2, 32, 128], F32, name=f"Rc{i}") for i in range(2)]
    for i in range(2):
        p.op("pool", lambda e, i=i: e.memset(Lc[i][:, :, :], 1.0), writes=[f"Lc{i}"])
        p.op("pool", lambda e, i=i: e.memset(Rc[i][:, :, :], 1.0), writes=[f"Rc{i}"])
    xtok = p.sb([128, 1024], F32, name="xtok")
    btok = p.sb([128, 512], BF16, name="btok")
    bcT = p.sb([128, 8, 128], BF16, name="bcT")
    sc = p.sb([128, 96], F32, name="sc")
    xdt = p.sb([128, 1024], BF16, name="xdt")
    xw = p.sb([128, 1024], BF16, name="xw")
    E = p.sb([128, 512], BF16, name="E")
    GT = p.sb([128, 4, 128], BF16, name="GT")
    ysb = p.sb([128, 256], F32, name="ysb")
    yacc = p.sb([128, 1024], F32, name="yacc")
    yprev = p.sb([128, 1024], F32, name="yprev")
    yfm = [p.sb([128, 8, 128], F32, name=f"yfm{i}") for i in range(2)]
    psT = [p.ps([128, 512], name=f"psT{i}") for i in range(2)]
    psG = p.ps([128, 512], name="psG")
    psD = p.ps([128, 512], name="psD")
    psY = p.ps([128, 512], name="psY")
    psZ = p.ps([128, 512], name="psZ")
    psS = p.ps([128, 512], name="psS")
    ti = 0
    for it, c in enumerate(order):
        t0 = c * 128
        xi, xik = xin[it % 2], f"xin{it%2}"
        st, stk = sct[it % 2], f"sct{it%2}"
        L, Lk, R, Rk = Lc[it % 2], f"Lc{it%2}", Rc[it % 2], f"Rc{it%2}"
        p.dma("sp", xi[:, :, :], fm(XBC, t0, 128), reads=["XBC"], writes=[xik])
        p.dma("sp", st[:, :], SC[:, t0:t0 + 128], reads=["SC"], writes=[stk])
        p.dma("sp", R[0:1, :, :], CF2[0:1, :, t0:t0 + 128], reads=["CF2"], writes=[Rk])
        p.dma("sp", L[1:2, :, :], CF2[1:2, :, t0:t0 + 128], reads=["CF2"], writes=[Lk])
        if d == 1:
            p.dma("sp", yprev[:, :], YT[t0:t0 + 128, :], reads=["YT"], writes=["yprev"])
        for half in range(2):
            ps, pk = psT[ti % 2], f"psT{ti%2}"
            ti += 1
            for cc in range(4):
                p.op("pe", lambda e, ps=ps, cc=cc, half=half, xi=xi: e.transpose(out=ps[:, cc * 128:(cc + 1) * 128], in_=xi[:, half * 4 + cc, :], identity=ident[:, :]),
                     reads=[xik, "ident"], writes=[pk])
            p.op("act", lambda e, ps=ps, half=half: e.copy(out=xtok[:, half * 512:(half + 1) * 512], in_=ps[:, :]), reads=[pk], writes=["xtok"])
        ps, pk = psT[ti % 2], f"psT{ti%2}"
        ti += 1
        for cc in range(4):
            p.op("pe", lambda e, ps=ps, cc=cc, xi=xi: e.transpose(out=ps[:, cc * 128:(cc + 1) * 128], in_=xi[:, 8 + cc, :], identity=ident[:, :]),
                 reads=[xik, "ident"], writes=[pk])
        p.op("act", lambda e, ps=ps: e.copy(out=btok[:, :], in_=ps[:, :]), reads=[pk], writes=["btok"])
        ps, pk = psT[ti % 2], f"psT{ti%2}"
        ti += 1
        p.op("pe", lambda e, ps=ps, st=st: e.transpose(out=ps[:, 0:96], in_=st[:, :], identity=ident[:96, :96]), reads=[stk, "ident"], writes=[pk])
        p.op("dve", lambda e, ps=ps: e.tensor_copy(out=sc[:, :], in_=ps[:, 0:96]), reads=[pk], writes=["sc"])
        p.op("pool", lambda e, xi=xi: e.tensor_copy(out=bcT[:, :, :], in_=xi[:, 8:16, :]), reads=[xik], writes=["bcT"])
        x3 = xtok[:, :].rearrange("p (h q) -> p h q", q=64)
        p.op("dve", lambda e, x3=x3: e.tensor_tensor(out=xdt[:, :].rearrange("p (h q) -> p h q", q=64), in0=x3,
                                                     in1=sc[:, d * 16:d * 16 + 16].unsqueeze(2).to_broadcast([128, 16, 64]), op=ALU.mult),
             reads=["xtok", "sc"], writes=["xdt"])
        p.op("pool", lambda e, x3=x3: e.tensor_tensor(out=xw[:, :].rearrange("p (h q) -> p h q", q=64), in0=x3,
                                                      in1=sc[:, 32 + d * 16:32 + d * 16 + 16].unsqueeze(2).to_broadcast([128, 16, 64]), op=ALU.mult),
             reads=["xtok", "sc"], writes=["xw"])
        for g in range(4):
            p.op("pe", lambda e, g=g: e.matmul(psG[:, 0:128], lhsT=bcT[:, g, :], rhs=bcT[:, 4 + g, :], start=True, stop=True), reads=["bcT"], writes=["psG"])
            p.op("pe", lambda e: e.matmul(psD[:, :], lhsT=identb[:, :], rhs=maskb[:, :], start=True, stop=False), reads=["identb", "maskb"], writes=["psD"])
            for hh in range(4):
                j = d * 16 + g * 4 + hh
                p.op("pe", lambda e, hh=hh, j=j, L=L, R=R: e.matmul(psD[:, hh * 128:(hh + 1) * 128], lhsT=L[:, j, :], rhs=R[:, j, :], start=False, stop=(hh == 3)),
                     reads=[Lk, Rk], writes=["psD"])
            p.op("act", lambda e: e.activation(out=E[:, :], in_=psD[:, :], func=AF.Exp), reads=["psD"], writes=["E"])
            for hh in range(4):
                p.op("dve", lambda e, hh=hh: e.tensor_tensor(out=GT[:, hh, :], in0=psG[:, 0:128], in1=E[:, hh * 128:(hh + 1) * 128], op=ALU.mult),
                     reads=["psG", "E"], writes=["GT" + str(hh)])
            for hh in range(4):
                h = g * 4 + hh
                p.op("pe", lambda e, hh=hh, h=h: e.matmul(psY[:, hh * 64:(hh + 1) * 64], lhsT=GT[:, hh, :], rhs=xdt[:, h * 64:(h + 1) * 64], start=True, stop=True),
                     reads=["GT" + str(hh), "xdt"], writes=["psY"])
            p.op("pe", lambda e, g=g: e.matmul(psZ[:, 0:256], lhsT=bcT[:, 4 + g, :], rhs=Hb[:, g * 256:(g + 1) * 256], start=True, stop=True), reads=["bcT", "Hb"], writes=["psZ"])
            p.op("pe", lambda e, g=g: e.matmul(psS[:, 0:256], lhsT=btok[:, g * 128:(g + 1) * 128], rhs=xw[:, g * 256:(g + 1) * 256], start=True, stop=True),
                 reads=["btok", "xw"], writes=["psS"])
            p.op("act", lambda e: e.copy(out=ysb[:, :], in_=psY[:, 0:256]), reads=["psY"], writes=["ysb"])
            for hh in range(4):
                h = g * 4 + hh
                hs = slice(h * 64, (h + 1) * 64)
                rs = sc[:, 64 + d * 16 + h:64 + d * 16 + h + 1]
                p.op("dve", lambda e, hh=hh, hs=hs, rs=rs: e.scalar_tensor_tensor(out=yacc[:, hs], in0=psZ[:, hh * 64:(hh + 1) * 64], scalar=rs, in1=ysb[:, hh * 64:(hh + 1) * 64],
                                                                                  op0=ALU.mult, op1=ALU.add), reads=["psZ", "ysb", "sc"], writes=["yacc"])
                et = etb[:, c, d * 16 + h:d * 16 + h + 1]
                p.op("dve", lambda e, hh=hh, hs=hs, et=et: e.scalar_tensor_tensor(out=H[:, hs], in0=H[:, hs], scalar=et, in1=psS[:, hh * 64:(hh + 1) * 64], op0=ALU.mult, op1=ALU.add),
                     reads=["H", "psS", "etb"], writes=["H"])
            p.op("act", lambda e, g=g: e.copy(out=Hb[:, g * 256:(g + 1) * 256], in_=H[:, g * 256:(g + 1) * 256]), reads=["H"], writes=["Hb"])
        if d == 0:
            p.op("pool", lambda e, x3=x3: e.tensor_tensor(out=yprev[:, :].rearrange("p (h q) -> p h q", q=64), in0=x3,
                                                          in1=dsk[:, :].unsqueeze(2).to_broadcast([128, 16, 64]), op=ALU.mult),
                 reads=["xtok", "dskb"], writes=["yprev"])
            p.op("pool", lambda e: e.tensor_tensor(out=yacc[:, :], in0=yacc[:, :], in1=yprev[:, :], op=ALU.add), reads=["yacc", "yprev"], writes=["yacc"])
            p.dma("pool", YT[t0:t0 + 128, :], yacc[:, :], reads=["yacc"], writes=["YT"])
        else:
            p.op("pool", lambda e: e.tensor_tensor(out=yacc[:, :], in0=yacc[:, :], in1=yprev[:, :], op=ALU.add), reads=["yacc", "yprev"], writes=["yacc"])
            yf, yfk = yfm[it % 2], f"yfm{it%2}"
            for half in range(2):
                ps, pk = psT[ti % 2], f"psT{ti%2}"
                ti += 1
                for cc in range(4):
                    cch = half * 4 + cc
                    p.op("pe", lambda e, ps=ps, cc=cc, cch=cch: e.transpose(out=ps[:, cc * 128:(cc + 1) * 128], in_=yacc[:, cch * 128:(cch + 1) * 128], identity=ident[:, :]),
                         reads=["yacc", "ident"], writes=[pk])
                p.op("act", lambda e, ps=ps, half=half, yf=yf: e.copy(out=yf[:, half * 4:(half + 1) * 4, :], in_=ps[:, :].rearrange("p (c q) -> p c q", q=128)), reads=[pk], writes=[yfk])
            p.dma("pool", fm(YM, t0, 128), yf[:, :, :], reads=[yfk], writes=["YM"])


def ph_ssd_post(p, YM, PROJ, T, G, Y):
    N = 512
    nrm = Norm(p, N, 1, G, [(None, None)])
    ys = [p.sb([128, 8, N], F32, name=f"ym{i}") for i in range(2)]
    zs = [p.sb([128, 8, N], F32, name=f"zz{i}") for i in range(2)]
    ht = [p.sb([128, 8, N], BF16, name=f"hp{i}") for i in range(2)]
    for ti, t0 in enumerate(range(0, T, N)):
        n = min(N, T - t0)
        y, yk, z, zk, h, hk = ys[ti % 2], f"ym{ti%2}", zs[ti % 2], f"zz{ti%2}", ht[ti % 2], f"hp{ti%2}"
        p.dma("sp", y[:, :, :n], fm(YM, t0, n), reads=["YM"], writes=[yk])
        p.dma("sp", z[:, :, :n], fm(PROJ[512:1536, :], t0, n), reads=["PROJ"], writes=[zk])
        p.op("act", lambda e, z=z, n=n: e.activation(out=z[:, :, :n], in_=z[:, :, :n], func=AF.Silu), reads=[zk], writes=[zk])
        p.op("pool", lambda e, y=y, z=z, n=n: e.tensor_tensor(out=y[:, :, :n], in0=y[:, :, :n], in1=z[:, :, :n], op=ALU.mult), reads=[yk, zk], writes=[yk])
        nrm.apply(y, yk, n, 0, h, hk)
        p.dma("pool", fm(Y[512:1536, :], t0, n), h[:, :, :n], reads=[hk], writes=["Y"])


NFFT = 16384
S = 4


def hy_tables():
    a = np.arange(128, dtype=np.float64)
    ang = 2 * np.pi * np.outer(a, a) / 128.0
    C, Sn = np.cos(ang), np.sin(ang)
    angN = 2 * np.pi * np.outer(a, a) / NFFT
    tabs = np.concatenate([C, -Sn, C, Sn, -Sn, C, np.cos(angN), np.sin(angN), C / NFFT, -Sn / NFFT, -C, -C / NFFT], axis=1)
    return tabs.astype(np.float32)


class FFT:
    def __init__(self, p, TABS, NC=2):
        self.p = p
        F32R = mybir.dt.float32r
        self.tabs = p.sb([128, 1536], F32, name="tabs")
        p.dma("sp", self.tabs[:, :], TABS, writes=["tabs"])
        self.tabs_r = p.sb([128, 1536], F32, name="tabs_r")
        p.op("dve", lambda e: e.tensor_copy(out=self.tabs_r[:, :].bitcast(F32R), in_=self.tabs[:, :]), reads=["tabs"], writes=["tabs"])
        t = self.tabs_r[:, :].bitcast(F32R)
        self.F1 = t[:, 0:256]
        self.CS = t[:, 256:512]
        self.NSC = t[:, 512:768]
        self.C = t[:, 256:384]
        self.Sm = t[:, 384:512]
        self.NS = t[:, 512:640]
        self.TWC = self.tabs[:, 768:896]
        self.TWS = self.tabs[:, 896:1024]
        self.CN = t[:, 1024:1152]
        self.NSN = t[:, 1152:1280]
        self.NegC = t[:, 1280:1408]
        self.NegCN = t[:, 1408:1536]
        self.NC = NC
        self.psA = [p.ps([128, S, 256], name=f"psA{c}") for c in range(NC)]
        self.psXr = [p.ps([128, 512], name=f"psXr{c}") for c in range(NC)]
        self.psXi = [p.ps([128, 512], name=f"psXi{c}") for c in range(NC)]
        self.psy = self.psXr
        mk = lambda n: [p.sb([128, S, 128], F32, name=f"{n}{c}") for c in range(NC)]
        self.t1, self.t2, self.t3, self.t4 = mk("ft1"), mk("ft2"), mk("ft3"), mk("ft4")
        self.u1, self.u2, self.u3, self.u4 = mk("fu1"), mk("fu2"), mk("fu3"), mk("fu4")
        self.Br, self.Bi = mk("Br"), mk("Bi")
        self.Yr, self.Yi = mk("Yr"), mk("Yi")

    def cmul(self, c, ar, ai, ak, br, bi, bk, outr, outi, ok, conj_b=False):
        p = self.p
        F32R = mybir.dt.float32r
        t1, t2, t3, t4 = self.u1[c], self.u2[c], self.u3[c], self.u4[c]
        k1, k2, k3, k4 = f"fu1{c}", f"fu2{c}", f"fu3{c}", f"fu4{c}"
        p.op("dve", lambda e: e.tensor_tensor(out=t1[:, :, :], in0=ar, in1=br, op=ALU.mult), reads=ak + bk, writes=[k1])
        p.op("dve", lambda e: e.tensor_tensor(out=t2[:, :, :], in0=ai, in1=bi, op=ALU.mult), reads=ak + bk, writes=[k2])
        p.op("pool", lambda e: e.tensor_tensor(out=outr.bitcast(F32R), in0=t1[:, :, :], in1=t2[:, :, :], op=(ALU.add if conj_b else ALU.subtract)),
             reads=[k1, k2], writes=ok)
        p.op("dve", lambda e: e.tensor_tensor(out=t3[:, :, :], in0=ai, in1=br, op=ALU.mult), reads=ak + bk, writes=[k3])
        p.op("dve", lambda e: e.tensor_tensor(out=t4[:, :, :], in0=ar, in1=bi, op=ALU.mult), reads=ak + bk, writes=[k4])
        p.op("pool", lambda e: e.tensor_tensor(out=outi.bitcast(F32R), in0=t3[:, :, :], in1=t4[:, :, :], op=(ALU.subtract if conj_b else ALU.add)),
             reads=[k3, k4], writes=ok)

    def cmul_split(self, c, ar, ai, ak, br, bi, bk):
        p = self.p
        F32R = mybir.dt.float32r
        for (t, x, y, kk) in ((self.t1[c], ar, br, f"ft1{c}"), (self.t2[c], ai, bi, f"ft2{c}"), (self.t3[c], ai, br, f"ft3{c}"), (self.t4[c], ar, bi, f"ft4{c}")):
            p.op("dve", lambda e, t=t, x=x, y=y: e.tensor_tensor(out=t[:, :, :].bitcast(F32R), in0=x, in1=y, op=ALU.mult), reads=ak + bk, writes=[kk])

    def _tflat(self, c):
        F32R = mybir.dt.float32r
        return [t[c][:, :, :].rearrange("p s k -> p (s k)").bitcast(F32R) for t in (self.t1, self.t2, self.t3, self.t4)]

    def _tw(self):
        return (self.TWC.unsqueeze(1).to_broadcast([128, S, 128]), self.TWS.unsqueeze(1).to_broadcast([128, S, 128]))

    def _bflat(self, c):
        F32R = mybir.dt.float32r
        return (self.Br[c][:, :, :].rearrange("p s k -> p (s k)").bitcast(F32R), self.Bi[c][:, :, :].rearrange("p s k -> p (s k)").bitcast(F32R))

    def st_f1(self, c, x0, xk, K):
        p, A = self.p, self.psA[c]
        for s in range(S):
            p.op("pe", lambda e, s=s: e.matmul(A[:, s, :], lhsT=x0[:K, s, :].bitcast(mybir.dt.float32r), rhs=self.F1[:K, :], start=True, stop=True),
                 reads=[xk, "tabs"], writes=[f"psA{c}"])

    def st_tw1(self, c):
        A = self.psA[c]
        twc, tws = self._tw()
        self.cmul_split(c, A[:, :, 0:128], A[:, :, 128:256], [f"psA{c}"], twc, tws, ["tabs"])

    def st_f2(self, c):
        p = self.p
        T1, T2, T3, T4 = self._tflat(c)
        Xr, Xi = self.psXr[c], self.psXi[c]
        tk = [f"ft1{c}", f"ft2{c}", f"ft3{c}", f"ft4{c}"]
        for n, (w, t) in enumerate(((self.C, T1), (self.C, T2), (self.Sm, T3), (self.NS, T4))):
            p.op("pe", lambda e, w=w, t=t, n=n: e.matmul(Xr[:, :], lhsT=w, rhs=t, start=(n == 0), stop=(n == 3)), reads=["tabs"] + tk, writes=[f"psXr{c}"])
        for n, (w, t) in enumerate(((self.C, T3), (self.NegC, T4), (self.NS, T1), (self.NS, T2))):
            p.op("pe", lambda e, w=w, t=t, n=n: e.matmul(Xi[:, :], lhsT=w, rhs=t, start=(n == 0), stop=(n == 3)), reads=["tabs"] + tk, writes=[f"psXi{c}"])

    def st_filt(self, c, Hr, Hi, hk):
        Xr = self.psXr[c][:, :].rearrange("p (s k) -> p s k", k=128)
        Xi = self.psXi[c][:, :].rearrange("p (s k) -> p s k", k=128)
        self.cmul(c, Xr, Xi, [f"psXr{c}", f"psXi{c}"], Hr, Hi, [hk], self.Yr[c][:, :, :], self.Yi[c][:, :, :], [f"Y{c}"])

    def st_i1(self, c):
        p, A = self.p, self.psA[c]
        F32R = mybir.dt.float32r
        Yr, Yi = self.Yr[c], self.Yi[c]
        for s in range(S):
            p.op("pe", lambda e, s=s: e.matmul(A[:, s, :], lhsT=Yr[:, s, :].bitcast(F32R), rhs=self.CS, start=True, stop=False), reads=[f"Y{c}", "tabs"], writes=[f"psA{c}"])
            p.op("pe", lambda e, s=s: e.matmul(A[:, s, :], lhsT=Yi[:, s, :].bitcast(F32R), rhs=self.NSC, start=False, stop=True), reads=[f"Y{c}", "tabs"], writes=[f"psA{c}"])

    def st_tw2(self, c):
        A = self.psA[c]
        twc, tws = self._tw()
        self.cmul_split(c, A[:, :, 0:128], A[:, :, 128:256], [f"psA{c}"], twc, tws, ["tabs"])

    def st_i2(self, c, M):
        p = self.p
        T1, T2, T3, T4 = self._tflat(c)
        y = self.psy[c]
        tk = [f"ft1{c}", f"ft2{c}", f"ft3{c}", f"ft4{c}"]
        for n, (w, t) in enumerate(((self.CN, T1), (self.NegCN, T2), (self.NSN, T3), (self.NSN, T4))):
            p.op("pe", lambda e, w=w, t=t, n=n: e.matmul(y[:M, :], lhsT=w[:, :M], rhs=t, start=(n == 0), stop=(n == 3)), reads=["tabs"] + tk, writes=[f"psXr{c}"])


def ph_hy_filter_raw(p, l, FEATS, TL, W1, B1, FQ1, W2, B2, FQ2, W3, DELTAS, FRAW, RINV):
    NT = min(512, l)
    ntile = l // NT
    feats = p.sb([33, l], F32, name="feats")
    p.dma("sp", feats[:, :], FEATS, writes=["feats"])
    w1 = p.sb([33, 64], F32, name="fw1")
    p.dma("sp", w1[:, :], W1, writes=["fw1"])
    w2 = p.sb([64, 64], F32, name="fw2")
    p.dma("sp", w2[:, :], W2, writes=["fw2"])
    w3 = p.sb([64, 4096], F32, name="fw3")
    p.dma("sp", w3[:, :], W3, writes=["fw3"])
    cols = p.sb([64, 6], F32, name="fcols")
    for i, src in enumerate((B1, FQ1, B2, FQ2)):
        p.dma("sp", cols[:, i:i + 1], src.rearrange("(p o) -> p o", o=1), writes=["fcols"], allow_slow_non_contiguous=True)
    p.op("dve", lambda e: e.tensor_tensor(out=cols[:, 4:5], in0=cols[:, 0:1], in1=cols[:, 1:2], op=ALU.mult), reads=["fcols"], writes=["fcols"])
    p.op("dve", lambda e: e.tensor_tensor(out=cols[:, 5:6], in0=cols[:, 2:3], in1=cols[:, 3:4], op=ALU.mult), reads=["fcols"], writes=["fcols"])
    hid2 = p.sb([64, l], F32, name="hid2")
    arg = p.sb([64, NT], F32, name="farg")
    arg2 = p.sb([64, NT], F32, name="farg2")
    hid1 = p.sb([64, NT], F32, name="hid1")
    tm1 = sin_tmps(p, [64, NT], "s1")
    tm2 = sin_tmps(p, [64, NT], "s2")
    ps1 = p.ps([128, 512], name="psf1")
    for ti in range(ntile):
        ts = slice(ti * NT, (ti + 1) * NT)
        p.op("pe", lambda e, ts=ts: e.matmul(ps1[:64, :NT], lhsT=w1[:, :], rhs=feats[:, ts], start=True, stop=True), reads=["fw1", "feats"], writes=["psf1"])
        p.op("dve", lambda e: e.tensor_scalar(out=arg[:, :], in0=ps1[:64, :NT], scalar1=cols[:, 1:2], scalar2=cols[:, 4:5], op0=ALU.mult, op1=ALU.add),
             reads=["psf1", "fcols"], writes=["s1" + "x"])
        sin_rr(p, hid1[:, :], arg[:, :], [64, NT], "s1", tmps=tm1)
        p.op("pe", lambda e: e.matmul(ps1[:64, :NT], lhsT=w2[:, :], rhs=hid1[:, :], start=True, stop=True), reads=["fw2", "s1o"], writes=["psf1"])
        p.op("dve", lambda e: e.tensor_scalar(out=arg2[:, :], in0=ps1[:64, :NT], scalar1=cols[:, 3:4], scalar2=cols[:, 5:6], op0=ALU.mult, op1=ALU.add),
             reads=["psf1", "fcols"], writes=["s2" + "x"])
        sin_rr(p, hid2[:, ts], arg2[:, :], [64, NT], "s2", tmps=tm2)
    tl = p.sb([128, l], F32, name="tl")
    p.dma("sp", tl[:, :], TL.partition_broadcast(128), writes=["tl"])
    dl = p.sb([128, 8], F32, name="ndelta")
    load_cols(p, dl[:, :], DELTAS, "ndelta")
    p.op("dve", lambda e: e.tensor_scalar(out=dl[:, :], in0=dl[:, :], scalar1=-1.0, scalar2=None, op0=ALU.mult), reads=["ndelta"], writes=["ndelta"])
    dec = p.sb([128, l], F32, name="dec")
    sums = p.sb([128, 32, ntile], F32, name="fsums")
    junk = p.sb([128, NT], F32, name="fjunk")
    outs = [p.sb([128, NT], F32, name=f"fo{i}") for i in range(3)]
    pss = [p.ps([128, 512], name=f"psw{i}") for i in range(3)]
    oi = 0
    for cc in range(8):
        p.op("act", lambda e, cc=cc: e.activation(out=dec[:, :], in_=tl[:, :], func=AF.Exp, scale=dl[:, cc:cc + 1]), reads=["tl", "ndelta"], writes=["dec"])
        for od in range(4):
            m = od * 8 + cc
            for ti in range(ntile):
                ts = slice(ti * NT, (ti + 1) * NT)
                ps, pk, o, ok = pss[oi % 3], f"psw{oi%3}", outs[oi % 3], f"fo{oi%3}"
                p.op("pe", lambda e, ps=ps, m=m, ts=ts: e.matmul(ps[:, :NT], lhsT=w3[:, m * 128:(m + 1) * 128], rhs=hid2[:, ts], start=True, stop=True),
                     reads=["fw3", "s2o"], writes=[pk])
                p.op("dve", lambda e, ps=ps, o=o, ts=ts: e.tensor_tensor(out=o[:, :], in0=ps[:, :NT], in1=dec[:, ts], op=ALU.mult), reads=[pk, "dec"], writes=[ok])
                lo = 1 if (od % 2 == 1 and ti == 0) else 0
                p.op("act", lambda e, o=o, m=m, ti=ti, lo=lo: e.activation(out=junk[:, lo:], in_=o[:, lo:], func=AF.Abs, accum_out=sums[:, m, ti:ti + 1]),
                     reads=[ok], writes=["fsums", "fjunk"])
                p.dma("pool", FRAW[m * 128:(m + 1) * 128, ts], o[:, :], reads=[ok], writes=["FRAW"])
                oi += 1
    tot = p.sb([128, 32], F32, name="ftot")
    p.op("dve", lambda e: e.tensor_reduce(out=tot[:, :], in_=sums[:, :, :], axis=AX.X, op=ALU.add), reads=["fsums"], writes=["ftot"])
    rinv = p.sb([128, 2, 8], F32, name="rinv")
    t4 = tot[:, :].rearrange("p (o d c) -> p o d c", o=2, d=2)
    for o_ in range(2):
        p.op("dve", lambda e, o_=o_: e.tensor_tensor(out=rinv[:, o_, :], in0=t4[:, o_, 0, :], in1=t4[:, o_, 1, :], op=ALU.add), reads=["ftot"], writes=["rinv"])
    p.op("dve", lambda e: e.tensor_scalar(out=rinv[:, :, :], in0=rinv[:, :, :], scalar1=1e-6, scalar2=None, op0=ALU.add), reads=["rinv"], writes=["rinv"])
    p.op("dve", lambda e: e.reciprocal(out=rinv[:, :, :], in_=rinv[:, :, :]), reads=["rinv"], writes=["rinv"])
    p.dma("pool", RINV.rearrange("o (c q) -> q o c", q=128), rinv[:, :, :], reads=["rinv"], writes=["RINV"], allow_slow_non_contiguous=True)


def ph_hy_filter_asm(p, l, FRAW, RINV, FILT):
    rinv = p.sb([128, 2, 8], F32, name="rinv2")
    p.dma("sp", rinv[:, :, :], RINV.rearrange("o (c q) -> q o c", q=128), reads=["RINV"], writes=["rinv2"], allow_slow_non_contiguous=True)
    zero = p.sb([128, 4096], F32, name="fzero")
    p.op("pool", lambda e: e.memset(zero[:, :], 0.0), writes=["fzero"])
    f = [p.sb([128, l], F32, name=f"ff{i}") for i in range(2)]
    b = [p.sb([128, l], F32, name="fb0")] * 2
    r = [p.sb([128, l], F32, name="fr0")] * 2
    i = 0
    for o_ in range(2):
        for cc in range(8):
            ff, fk, bb, bk, rr, rk = f[i % 2], f"ff{i%2}", b[0], "fb0", r[0], "fr0"
            rows = slice((o_ * 8 + cc) * 128, (o_ * 8 + cc + 1) * 128)
            mf, mb = (o_ * 2 + 0) * 8 + cc, (o_ * 2 + 1) * 8 + cc
            p.dma("sp", ff[:, :], FRAW[mf * 128:(mf + 1) * 128, :], reads=["FRAW"], writes=[fk])
            p.dma("sp", bb[:, :], FRAW[mb * 128:(mb + 1) * 128, :], reads=["FRAW"], writes=[bk])
            sc = rinv[:, o_, cc:cc + 1]
            p.op("act", lambda e, ff=ff, sc=sc: e.activation(out=ff[:, :], in_=ff[:, :], func=AF.Identity, scale=sc), reads=[fk, "rinv2"], writes=[fk])
            p.op("dve", lambda e, bb=bb, rr=rr, sc=sc: e.tensor_scalar(out=rr[:, :], in0=bb[:, ::-1], scalar1=sc, scalar2=None, op0=ALU.mult), reads=[bk, "rinv2"], writes=[rk])
            p.dma("pool", FILT[rows, 0:l], ff[:, :], reads=[fk], writes=["FILT"])
            p.dma("pool", FILT[rows, NFFT - l + 1:NFFT], rr[:, 0:l - 1], reads=[rk], writes=["FILT"])
            for z0 in range(l, NFFT - l + 1, 4096):
                zn = min(4096, NFFT - l + 1 - z0)
                p.dma("pool", FILT[rows, z0:z0 + zn], zero[:, :zn], reads=["fzero"], writes=["FILT"], allow_slow_non_contiguous=True)
            i += 1


def ph_hy_filter_fft(p, FILT, TABS, HF, groups=None):
    NC = 2
    fft = FFT(p, TABS, NC)
    xs = [p.sb([128, S, 128], F32, name=f"fx{i}") for i in range(2 * NC)]
    hr = [p.sb([128, S, 128], F32, name=f"hro{i}") for i in range(2 * NC)]
    hi = [p.sb([128, S, 128], F32, name=f"hio{i}") for i in range(2 * NC)]
    gl = list(groups if groups is not None else range(2048 // S))
    for b0 in range(0, len(gl), NC):
        batch = gl[b0:b0 + NC]
        ctxs = []
        for c, gi in enumerate(batch):
            bi_ = ((b0 // NC) % 2) * NC + c
            x0, xk = xs[bi_], f"fx{bi_}"
            sig = slice(gi * S, (gi + 1) * S)
            p.dma("sp", x0[:, :, :], FILT[sig, :].rearrange("s (a b) -> a s b", b=128), reads=["FILT"], writes=[xk])
            p.op("act", lambda e, x0=x0: e.copy(out=x0[:, :, :].bitcast(mybir.dt.float32r), in_=x0[:, :, :]), reads=[xk], writes=[xk])
            ctxs.append((c, x0, xk, sig, bi_))
        for (c, x0, xk, sig, bi_) in ctxs:
            fft.st_f1(c, x0, xk, 128)
        for (c, x0, xk, sig, bi_) in ctxs:
            fft.st_tw1(c)
        for (c, x0, xk, sig, bi_) in ctxs:
            fft.st_f2(c)
        for (c, x0, xk, sig, bi_) in ctxs:
            a_, ak, b_, bk = hr[bi_], f"hro{bi_}", hi[bi_], f"hio{bi_}"
            p.op("act", lambda e, a_=a_, c=c: e.copy(out=a_[:, :, :].rearrange("p s k -> p (s k)"), in_=fft.psXr[c][:, :]), reads=[f"psXr{c}"], writes=[ak])
            p.op("act", lambda e, b_=b_, c=c: e.copy(out=b_[:, :, :].rearrange("p s k -> p (s k)"), in_=fft.psXi[c][:, :]), reads=[f"psXi{c}"], writes=[bk])
            p.dma("pool", HF[0, sig].rearrange("s a b -> a s b"), a_[:, :, :], reads=[ak], writes=["HF"])
            p.dma("pool", HF[1, sig].rearrange("s a b -> a s b"), b_[:, :, :], reads=[bk], writes=["HF"])


def ph_hy_conv(p, ZC, t_off, Lsig, HF, HYD, TABS, Y, groups=None):
    K = Lsig // 128
    NC = 2
    F32R = mybir.dt.float32r
    fft = FFT(p, TABS, NC)
    dsk = p.sb([128, 2, 1024], F32, name="hyd")
    p.dma("sp", dsk[:, :, :], HYD.partition_broadcast(128), writes=["hyd"])
    mk = lambda n, dt=F32: [p.sb([64, S, 128], dt, name=f"{n}{i}") for i in range(NC)]
    vs, x1s, x2s, vrs, y1s, tmps = mk("v"), mk("xa"), mk("xb"), mk("vr"), mk("y1"), mk("ctmp")
    yos = mk("yo", BF16)
    for i in range(NC):
        for tl_, nm in ((vrs[i], f"vr{i}"), (y1s[i], f"y1{i}")):
            p.op("pool", lambda e, tl_=tl_: e.memset(tl_[:, :, :], 0.0), writes=[nm])
            p.op("act", lambda e, tl_=tl_: e.copy(out=tl_[:, :, :].bitcast(F32R), in_=tl_[:, :, :]), reads=[nm], writes=[nm])
    H = [[[p.sb([128, S, 128], F32, name=f"H{c}{o}{ri}") for ri in range(2)] for o in range(2)] for c in range(NC)]

    def tb(ap2d):
        return ap2d.rearrange("s (a b) -> a s b", b=128)

    gl = list(groups if groups is not None else range(1024 // S))
    for b0 in range(0, len(gl), NC):
        batch = list(enumerate(gl[b0:b0 + NC]))
        for c, gi in batch:
            c0 = gi * S
            p.dma("sp", vs[c][:K, :, :], tb(ZC[c0:c0 + S, t_off:t_off + Lsig]), reads=["ZC"], writes=[f"v{c}"])
            p.dma("sp", x1s[c][:K, :, :], tb(ZC[1024 + c0:1024 + c0 + S, t_off:t_off + Lsig]), reads=["ZC"], writes=[f"xa{c}"])
            p.dma("sp", x2s[c][:K, :, :], tb(ZC[2048 + c0:2048 + c0 + S, t_off:t_off + Lsig]), reads=["ZC"], writes=[f"xb{c}"])
            for o in range(2):
                for ri in range(2):
                    p.dma("sp", H[c][o][ri][:, :, :], HF[ri, o * 1024 + c0:o * 1024 + c0 + S].rearrange("s a b -> a s b"), reads=["HF"], writes=[f"H{c}{o}"])
            p.op("act", lambda e, c=c: e.copy(out=vrs[c][:K, :, :].bitcast(F32R), in_=vs[c][:K, :, :]), reads=[f"v{c}"], writes=[f"vr{c}"])
        for o in range(2):
            for c, gi in batch:
                src, sk = (vrs[c], f"vr{c}") if o == 0 else (y1s[c], f"y1{c}")
                fft.st_f1(c, src, sk, 64)
            for c, gi in batch:
                fft.st_tw1(c)
            for c, gi in batch:
                fft.st_f2(c)
            for c, gi in batch:
                fft.st_filt(c, H[c][o][0][:, :, :], H[c][o][1][:, :, :], f"H{c}{o}")
            for c, gi in batch:
                fft.st_i1(c)
            for c, gi in batch:
                fft.st_tw2(c)
            for c, gi in batch:
                fft.st_i2(c, 64)
            for c, gi in batch:
                c0 = gi * S
                src, sk = (vs[c], f"v{c}") if o == 0 else (y1s[c], f"y1{c}")
                gate, gk = (x1s[c], f"xa{c}") if o == 0 else (x2s[c], f"xb{c}")
                tmp, tk = tmps[c], f"ctmp{c}"
                dbc = dsk[:64, o, c0:c0 + S].unsqueeze(2).to_broadcast([64, S, 128])
                p.op("dve", lambda e, src=src, dbc=dbc, tmp=tmp: e.tensor_tensor(out=tmp[:K, :, :], in0=src[:K, :, :], in1=dbc[:K], op=ALU.mult), reads=[sk, "hyd"], writes=[tk])
                psy3 = fft.psy[c][:, :].rearrange("p (s k) -> p s k", k=128)
                p.op("dve", lambda e, psy3=psy3, tmp=tmp: e.tensor_tensor(out=tmp[:K, :, :], in0=psy3[:K, :, :], in1=tmp[:K, :, :], op=ALU.add), reads=[f"psXr{c}", tk], writes=[tk])
                if o == 0:
                    p.op("pool", lambda e, gate=gate, tmp=tmp, c=c: e.tensor_tensor(out=y1s[c][:K, :, :].bitcast(F32R), in0=tmp[:K, :, :], in1=gate[:K, :, :], op=ALU.mult),
                         reads=[tk, gk], writes=[f"y1{c}"])
                else:
                    p.op("pool", lambda e, gate=gate, tmp=tmp, c=c: e.tensor_tensor(out=yos[c][:K, :, :], in0=tmp[:K, :, :], in1=gate[:K, :, :], op=ALU.mult),
                         reads=[tk, gk], writes=[f"yo{c}"])
                    p.dma("pool", tb(Y[c0:c0 + S, t_off:t_off + Lsig]), yos[c][:K, :, :], reads=[f"yo{c}"], writes=["Y"])


def hy_ctx_tables(l=256):
    N = 2 * l
    t = np.arange(l, dtype=np.float64)[:, None]
    k = np.arange(N, dtype=np.float64)[None, :]
    ang = 2 * np.pi * t * k / N
    c, s = np.cos(ang), np.sin(ang)
    cb, sb = c.copy(), s.copy()
    cb[0, :] = 0.0
    sb[0, :] = 0.0
    fw = np.stack([c, -s, cb, sb]).reshape(4, l // 128, 128, N // 128, 128).transpose(2, 0, 1, 3, 4)
    iv = np.stack([c.T / N, -s.T / N]).reshape(2, N // 128, 128, l // 128, 128).transpose(2, 0, 1, 3, 4)
    return np.ascontiguousarray(fw).astype(np.float32), np.ascontiguousarray(iv).astype(np.float32)


def ph_hy_ctx_dense(p, ZC, l, FRAW, RINV, HYD, FWT, IVT, IDENT, Y):
    TC, KC = l // 128, (2 * l) // 128
    ident = p.sb([128, 128], F32, name="cid")
    p.dma("sp", ident[:, :], IDENT, writes=["cid"])
    fw = p.sb([128, 4, TC, KC, 128], F32, name="cfw")
    p.dma("sp", fw[:, :, :, :, :], FWT, writes=["cfw"])
    iv = p.sb([128, 2, KC, TC, 128], F32, name="civ")
    p.dma("sp", iv[:, :, :, :, :], IVT, writes=["civ"])
    rinv = p.sb([128, 2, 1024], F32, name="crinv")
    p.dma("sp", rinv[:, :, :], RINV.partition_broadcast(128), reads=["RINV"], writes=["crinv"])
    dsk = p.sb([128, 2, 1024], F32, name="cdsk")
    p.dma("sp", dsk[:, :, :], HYD.partition_broadcast(128), writes=["cdsk"])
    pst = [p.ps([128, 512], name=f"cpt{i}") for i in range(2)]
    psm = [p.ps([128, 512], name=f"cpm{i}") for i in range(4)]
    stg = [p.sb([128, l], F32, name=f"cstg{i}") for i in range(4)]
    fT = p.sb([128, TC, 4096], F32, name="cfT")
    sT = p.sb([128, TC, 3072], F32, name="csT")
    k = 0
    for (SRC, nch_, dst, dk, rk) in ((FRAW, 32, fT, "cfT", "FRAW"), (ZC, 24, sT, "csT", "ZC")):
        for rc in range(0, nch_, 4):
            for tc in range(TC):
                ps, pk = pst[k % 2], f"cpt{k%2}"
                for j in range(4):
                    s_, sk = stg[j], f"cstg{j}"
                    if tc == 0:
                        p.dma("sp", s_[:, :], SRC[(rc + j) * 128:(rc + j + 1) * 128, 0:l], reads=[rk], writes=[sk])
                    p.op("pe", lambda e, ps=ps, j=j, s_=s_, tc=tc: e.transpose(out=ps[:, j * 128:(j + 1) * 128], in_=s_[:, tc * 128:(tc + 1) * 128], identity=ident[:, :]),
                         reads=[sk, "cid"], writes=[pk])
                p.op("act" if k % 2 else "dve", (lambda e, ps=ps, dst=dst, tc=tc, rc=rc: e.copy(out=dst[:, tc, rc * 128:(rc + 4) * 128], in_=ps[:, :])) if k % 2 else
                     (lambda e, ps=ps, dst=dst, tc=tc, rc=rc: e.tensor_copy(out=dst[:, tc, rc * 128:(rc + 4) * 128], in_=ps[:, :])), reads=[pk], writes=[dk])
                k += 1
    Xr = p.sb([128, KC, 1024], F32, name="cXr")
    Xi = p.sb([128, KC, 1024], F32, name="cXi")
    t1 = p.sb([128, 512], F32, name="ct1")
    t2 = p.sb([128, 512], F32, name="ct2")
    y1 = p.sb([128, TC, 1024], F32, name="cy1")
    y2 = p.sb([128, TC, 1024], F32, name="cy2")
    tmp = p.sb([128, 1024], F32, name="ctm")
    Hr = p.sb([128, KC, 1024], F32, name="cHr")
    Hi = p.sb([128, KC, 1024], F32, name="cHi")
    m = 0
    for o in range(2):
        for kc in range(KC):
            if True:
                for cb in range(2):
                    fcol = (o * 2 + 0) * 1024 + cb * 512
                    bcol = (o * 2 + 1) * 1024 + cb * 512
                    for part, (kf, kb, Hd, hk) in enumerate(((0, 2, Hr, "cHr"), (1, 3, Hi, "cHi"))):
                        ps, pk = psm[m % 4], f"cpm{m%4}"
                        n = 0
                        for tc in range(TC):
                            for (kind, col) in ((kf, fcol), (kb, bcol)):
                                p.op("pe", lambda e, ps=ps, kind=kind, tc=tc, kc=kc, col=col, n=n: e.matmul(ps[:, :], lhsT=fw[:, kind, tc, kc, :], rhs=fT[:, tc, col:col + 512],
                                                                                                         start=(n == 0), stop=(n == 2 * TC - 1)), reads=["cfw", "cfT"], writes=[pk])
                                n += 1
                        p.op("dve", lambda e, ps=ps, Hd=Hd, kc=kc, o=o, cb=cb: e.tensor_tensor(out=Hd[:, kc, cb * 512:(cb + 1) * 512], in0=ps[:, :],
                                                                                               in1=rinv[:, o, cb * 512:(cb + 1) * 512], op=ALU.mult), reads=[pk, "crinv"], writes=[hk])
                        m += 1
        sig = (lambda tc, c0: sT[:, tc, c0:c0 + 512]) if o == 0 else (lambda tc, c0: y1[:, tc, c0:c0 + 512])
        sigk = "csT" if o == 0 else "cy1"
        for kc in range(KC):
            for cb in range(2):
                c0 = cb * 512
                psr, prk, psi, pik = psm[0], "cpm0", psm[1], "cpm1"
                for tc in range(TC):
                    p.op("pe", lambda e, tc=tc, kc=kc, c0=c0, sg=sig: e.matmul(psr[:, :], lhsT=fw[:, 0, tc, kc, :], rhs=sg(tc, c0), start=(tc == 0), stop=(tc == TC - 1)),
                         reads=["cfw", sigk], writes=[prk])
                for tc in range(TC):
                    p.op("pe", lambda e, tc=tc, kc=kc, c0=c0, sg=sig: e.matmul(psi[:, :], lhsT=fw[:, 1, tc, kc, :], rhs=sg(tc, c0), start=(tc == 0), stop=(tc == TC - 1)),
                         reads=["cfw", sigk], writes=[pik])
                hs = slice(c0, c0 + 512)
                xs_ = slice(c0, c0 + 512)
                p.op("dve", lambda e, kc=kc, hs=hs: e.tensor_tensor(out=t1[:, :], in0=psr[:, :], in1=Hr[:, kc, hs], op=ALU.mult), reads=[prk, "cHr"], writes=["ct1"])
                p.op("dve", lambda e, kc=kc, hs=hs: e.tensor_tensor(out=t2[:, :], in0=psi[:, :], in1=Hi[:, kc, hs], op=ALU.mult), reads=[pik, "cHi"], writes=["ct2"])
                p.op("pool", lambda e, kc=kc, xs_=xs_: e.tensor_tensor(out=Xr[:, kc, xs_], in0=t1[:, :], in1=t2[:, :], op=ALU.subtract), reads=["ct1", "ct2"], writes=["cXr"])
                p.op("dve", lambda e, kc=kc, hs=hs: e.tensor_tensor(out=t1[:, :], in0=psr[:, :], in1=Hi[:, kc, hs], op=ALU.mult), reads=[prk, "cHi"], writes=["ct1"])
                p.op("dve", lambda e, kc=kc, hs=hs: e.tensor_tensor(out=t2[:, :], in0=psi[:, :], in1=Hr[:, kc, hs], op=ALU.mult), reads=[pik, "cHr"], writes=["ct2"])
                p.op("pool", lambda e, kc=kc, xs_=xs_: e.tensor_tensor(out=Xi[:, kc, xs_], in0=t1[:, :], in1=t2[:, :], op=ALU.add), reads=["ct1", "ct2"], writes=["cXi"])
        dst, dstk = (y1, "cy1") if o == 0 else (y2, "cy2")
        for tc in range(TC):
            for cb in range(2):
                c0 = cb * 512
                ps, pk = psm[2 + cb], f"cpm{2+cb}"
                n = 0
                for kc in range(KC):
                    for (kind, Xs, xk) in ((0, Xr, "cXr"), (1, Xi, "cXi")):
                        p.op("pe", lambda e, ps=ps, kind=kind, kc=kc, tc=tc, Xs=Xs, c0=c0, n=n: e.matmul(ps[:, :], lhsT=iv[:, kind, kc, tc, :], rhs=Xs[:, kc, c0:c0 + 512],
                                                                                                      start=(n == 0), stop=(n == 2 * KC - 1)), reads=["civ", xk], writes=[pk])
                        n += 1
                src = (lambda: sT[:, tc, c0:c0 + 512]) if o == 0 else (lambda: y1[:, tc, c0:c0 + 512])
                gate = sT[:, tc, (1 + o) * 1024 + c0:(1 + o) * 1024 + c0 + 512]
                srcap = src()
                p.op("dve", lambda e, srcap=srcap, o=o, c0=c0: e.tensor_tensor(out=tmp[:, c0:c0 + 512], in0=srcap, in1=dsk[:, o, c0:c0 + 512], op=ALU.mult),
                     reads=[sigk, "cdsk"], writes=["ctm"])
                p.op("dve", lambda e, ps=ps, c0=c0: e.tensor_tensor(out=tmp[:, c0:c0 + 512], in0=ps[:, :], in1=tmp[:, c0:c0 + 512], op=ALU.add), reads=[pk, "ctm"], writes=["ctm"])
                p.op("pool", lambda e, dst=dst, tc=tc, c0=c0, gate=gate: e.tensor_tensor(out=dst[:, tc, c0:c0 + 512], in0=tmp[:, c0:c0 + 512], in1=gate, op=ALU.mult),
                     reads=["ctm", "csT"], writes=[dstk])
    yo = [p.sb([128, l], BF16, name=f"cyo{i}") for i in range(2)]
    k = 0
    for cc in range(8):
        ps, pk = pst[cc % 2], f"cpt{cc%2}"
        for tc in range(TC):
            p.op("pe", lambda e, ps=ps, tc=tc, cc=cc: e.transpose(out=ps[:, tc * 128:(tc + 1) * 128], in_=y2[:, tc, cc * 128:(cc + 1) * 128], identity=ident[:, :]),
                 reads=["cy2", "cid"], writes=[pk])
        o_, ok = yo[cc % 2], f"cyo{cc%2}"
        p.op("act", lambda e, ps=ps, o_=o_: e.copy(out=o_[:, :], in_=ps[:, 0:l]), reads=[pk], writes=[ok])
        p.dma("pool", Y[cc * 128:(cc + 1) * 128, 0:l], o_[:, :], reads=[ok], writes=["Y"])


def s5_layouts(lam_re, lam_im, log_dt, b_re, b_im, c_re, c_im):
    def ls_of(a):
        return np.ascontiguousarray(a.reshape(2, 16, 2, 64).transpose(2, 3, 0, 1).reshape(128, 32))
    ldt = np.broadcast_to(log_dt[:, :, None], (2, 32, 64))
    LS = np.stack([ls_of(lam_re), ls_of(lam_im), ls_of(ldt)]).astype(np.float32)
    BT = np.zeros((2, 32, 16, 128), np.float32)
    CT = np.zeros((2, 128, 16, 32), np.float32)
    for i, (b, c) in enumerate(((b_re, c_re), (b_im, c_im))):
        for gs in range(2):
            bb = b.reshape(16, 2, 64, 16)[:, gs]
            BT[i, gs * 16:(gs + 1) * 16, :, gs * 64:(gs + 1) * 64] = bb.transpose(2, 0, 1)
            cc = c.reshape(16, 2, 16, 64)[:, gs]
            CT[i, gs * 64:(gs + 1) * 64, :, gs * 16:(gs + 1) * 16] = cc.transpose(2, 0, 1)
    return LS, BT, CT


SEQ = 8192
NCTX = 256
T = SEQ + NCTX
DEPTH = 4
NCORES = 4


def ph_init(p, XIN, CIN, PE, XT):
    a = [p.sb([128, 8, 512], F32, name=f"ia{i}") for i in range(2)]
    b = [p.sb([128, 8, 512], F32, name=f"ib{i}") for i in range(2)]
    p.dma("sp", a[0][:, :, :NCTX], fm(CIN, 0, NCTX), writes=["ia0"])
    p.dma("pool", fm(XT, 0, NCTX), a[0][:, :, :NCTX], reads=["ia0"], writes=["XT"])
    for i, t0 in enumerate(range(0, SEQ, 512)):
        k = (i + 1) % 2
        p.dma("sp", a[k][:, :, :], fm(XIN, t0, 512), writes=[f"ia{k}"])
        p.dma("sp", b[k][:, :, :], fm(PE, t0, 512), writes=[f"ib{k}"])
        p.op("dve" if i % 2 else "pool", lambda e, k=k: e.tensor_tensor(out=a[k][:, :, :], in0=a[k][:, :, :], in1=b[k][:, :, :], op=ALU.add),
             reads=[f"ia{k}", f"ib{k}"], writes=[f"ia{k}"])
        p.dma("pool", fm(XT, NCTX + t0, 512), a[k][:, :, :], reads=[f"ia{k}"], writes=["XT"])


def ph_mod(p, CV, MW, MB, M):
    cv = p.sb([128, 8, 2], F32, name="cv")
    for r in range(2):
        p.dma("sp", cv[:, :, r], CV[r].rearrange("(c p) -> p c", p=128), writes=["cv"], allow_slow_non_contiguous=True)
    p.op("act", lambda e: e.activation(out=cv[:, :, :], in_=cv[:, :, :], func=AF.Silu), reads=["cv"], writes=["cv"])
    ws = [p.sb([128, 8, 512], F32, name=f"mw{i}") for i in range(2)]
    bs = [p.sb([2, 512], F32, name=f"mb{i}") for i in range(2)]
    os_ = [p.sb([2, 512], F32, name=f"mo{i}") for i in range(2)]
    pss = [p.ps([128, 512], name=f"psm{i}") for i in range(2)]
    k = 0
    for i in range(DEPTH):
        for n0 in range(0, 6144, 512):
            w, wk, b, bk, o, ok, ps, pk = ws[k % 2], f"mw{k%2}", bs[k % 2], f"mb{k%2}", os_[k % 2], f"mo{k%2}", pss[k % 2], f"psm{k%2}"
            p.dma("sp", w[:, :, :], MW[i, :, n0:n0 + 512].rearrange("(c p) n -> p c n", p=128), writes=[wk])
            p.dma("sp", b[:, :], MB[i, n0:n0 + 512].partition_broadcast(2), writes=[bk])
            for c in range(8):
                p.op("pe", lambda e, ps=ps, w=w, c=c: e.matmul(ps[:2, :], lhsT=cv[:, c, :], rhs=w[:, c, :], start=(c == 0), stop=(c == 7)),
                     reads=["cv", wk], writes=[pk])
            p.op("dve", lambda e, ps=ps, b=b, o=o: e.tensor_tensor(out=o[:, :], in0=ps[:2, :], in1=b[:, :], op=ALU.add), reads=[pk, bk], writes=[ok])
            p.dma("pool", M[i, :, n0:n0 + 512], o[:, :], reads=[ok], writes=["M"])
            k += 1


def token_tiles(N, with_ctx):
    tl = [(0, NCTX, 1)] if with_ctx else []
    return tl + [(NCTX + t0, N, 0) for t0 in range(0, SEQ, N)]


def build_program():
    p = Prog()
    I = {}

    def inp(name, shape, dt=F32):
        I[name] = p.dram(name, shape, dt, "ExternalInput")
        return I[name]

    XIN = inp("xT", [1024, SEQ]); CIN = inp("ctxT", [1024, NCTX]); PE = inp("peT", [1024, SEQ]); CV = inp("cvec", [2, 1024])
    MW = inp("mod_w", [4, 1024, 6144]); MB = inp("mod_b", [4, 6144])
    NMG = inp("norm_mix_g", [4, 1024]); NLG = inp("norm_mlp_g", [4, 1024])
    W1 = inp("mlp_w1", [4, 1024, 4096]); W2 = inp("mlp_w2", [4, 4096, 1024]); FNG = inp("final_norm_g", [1024])
    EIW = inp("ev_in_w", [2, 1024, 3616]); EOW = inp("ev_out_w", [2, 1536, 1024])
    LS = inp("s5_ls", [2, 3, 128, 32]); BT = inp("s5_bt", [2, 2, 32, 16, 128]); CT = inp("s5_ct", [2, 2, 128, 16, 32])
    S5D = inp("s5_d", [2, 512]); GW = inp("s5_glu_w", [2, 512, 512]); GB = inp("s5_glu_b", [2, 512])
    MCW = inp("m2_conv_w", [2, 3, 2048]); MCB = inp("m2_conv_b", [2, 2048]); DTB = inp("m2_dt_bias", [2, 32]); ALOG = inp("m2_a_log", [2, 32])
    M2D = inp("m2_d", [2, 16]); M2G = inp("m2_norm_g", [2, 1024])
    HIW = inp("hy_in_w", [2, 1024, 3072]); HIB = inp("hy_in_b", [2, 3072]); HCW = inp("hy_conv_w", [2, 3, 3072]); HCB = inp("hy_conv_b", [2, 3072])
    HW1 = inp("hy_f_w1", [2, 33, 64]); HB1 = inp("hy_f_b1", [2, 64]); HQ1 = inp("hy_f_freq1", [2, 64])
    HW2 = inp("hy_f_w2", [2, 64, 64]); HB2 = inp("hy_f_b2", [2, 64]); HQ2 = inp("hy_f_freq2", [2, 64]); HW3 = inp("hy_f_w3", [2, 64, 4096])
    HYD = inp("hy_d", [2, 2, 1024]); HOW = inp("hy_out_w", [2, 1024, 1024]); HOB = inp("hy_out_b", [2, 1024])
    IDENT = inp("ident", [128, 128]); MASKS = inp("masks", [2, 128, 512]); TABS = inp("tabs", [128, 1536])
    FEAT_L = inp("feats_l", [33, SEQ]); TL_L = inp("tl_l", [SEQ]); FEAT_C = inp("feats_c", [33, NCTX]); TL_C = inp("tl_c", [NCTX]); DELTAS = inp("deltas", [1024])
    FWT = inp("ctx_fwt", [128, 4, 2, 4, 128]); IVT = inp("ctx_ivt", [128, 2, 4, 2, 128])
    OUT = p.dram("out", [SEQ, 1024], F32, "ExternalOutput")

    XT = p.dram("XT", [1024, T], F32); M = p.dram("M", [4, 2, 6144], F32)
    PROJ = p.dram("PROJ", [3616, T], F32); Y = p.dram("Y", [1536, T], BF16)
    YPRE = p.dram("YPRE", [512, T], F32); XBC = p.dram("XBC", [2048, T], F32)
    SC = p.dram("SC", [96, T], F32); CF2 = p.dram("CF2", [2, 32, T], F32); ET = p.dram("ET", [T // 128, 32], F32)
    YT = p.dram("YT", [T, 1024], F32); YM = p.dram("YM", [1024, T], F32)
    ZC = p.dram("ZC", [3072, T], F32); FRAW = p.dram("FRAW", [4096, SEQ], F32); RINV = p.dram("RINV", [2, 1024], F32)
    FILT = p.dram("FILT", [2048, NFFT], F32); HF = p.dram("HF", [2, 2048, 128, 128], F32)

    ph_init(p, XIN, CIN, PE, XT); p.end()
    ph_mod(p, CV, MW, MB, M); p.end()
    segs = [(0, NCTX), (NCTX, T)]
    for i in range(DEPTH):
        j = i // 2
        ctx_later = i < 2
        sl = lambda k: slice(k * 1024, (k + 1) * 1024)
        mods_a = [(M[i, r, sl(1)], M[i, r, sl(0)]) for r in range(2)]
        mods_f = [(M[i, r, sl(4)], M[i, r, sl(3)]) for r in range(2)]
        ga = [M[i, r, sl(2)] for r in range(2)]
        gf = [M[i, r, sl(5)] for r in range(2)]
        if i % 2 == 0:
            ph_tok_in(p, XT, EIW[j], 3616, NMG[i], mods_a, token_tiles(512, True), PROJ); p.end()
            ph_s5_scan(p, PROJ, T, NCTX, LS[j], BT[j], CT[j], S5D[j], YPRE); p.end()
            ph_s5_glu(p, YPRE, T, GW[j], GB[j], Y); p.end()
            ph_conv_silu(p, PROJ, 1536, 2048, T, segs, MCW[j], MCB[j], XBC); p.end()
            ph_ssd_prep(p, PROJ, T, DTB[j], ALOG[j], SC, CF2, ET); p.end()
            for d in range(2):
                ph_ssd_pass(p, d, XBC, T, NCTX, SC, CF2, ET, M2D[j].partition_broadcast(128), IDENT, MASKS, YT, YM); p.end()
            ph_ssd_post(p, YM, PROJ, T, M2G[j], Y); p.end()
            ph_outproj(p, XT, Y, 1536, EOW[j], ga, token_tiles(512, ctx_later)); p.end()
        else:
            ph_tok_in(p, XT, HIW[j], 3072, NMG[i], mods_a, token_tiles(512, ctx_later), PROJ, bias=HIB[j]); p.end()
            ph_conv_silu(p, PROJ, 0, 3072, T, segs, HCW[j], HCB[j], ZC, silu=False); p.end()
            fargs = (HW1[j], HB1[j], HQ1[j], HW2[j], HB2[j], HQ2[j], HW3[j], DELTAS)
            ph_hy_filter_raw(p, SEQ, FEAT_L, TL_L, *fargs, FRAW, RINV); p.end()
            ph_hy_filter_asm(p, SEQ, FRAW, RINV, FILT); p.end()
            ph_hy_filter_fft(p, FILT, TABS, HF); p.end()
            ph_hy_conv(p, ZC, NCTX, SEQ, HF, HYD[j], TABS, Y); p.end()
            if ctx_later:
                ph_hy_filter_raw(p, NCTX, FEAT_C, TL_C, *fargs, FRAW[:, 0:NCTX], RINV); p.end()
                ph_hy_ctx_dense(p, ZC, NCTX, FRAW[:, 0:NCTX], RINV, HYD[j], FWT, IVT, IDENT, Y); p.end()
            ph_outproj(p, XT, Y, 1024, HOW[j], ga, token_tiles(512, ctx_later), bias=HOB[j]); p.end()
        ph_mlp(p, XT, W1[i], W2[i], NLG[i], mods_f, gf, token_tiles(256, ctx_later)); p.end()
    ph_final(p, XT, FNG, token_tiles(512, False), OUT, NCTX, IDENT); p.end()
    return p


def grid_sincos_np(n, dm):
    rows = n // 64
    quarter = dm // 4
    omega = (1.0 / (np.float32(10000.0) ** (np.arange(quarter, dtype=np.float32) / np.float32(quarter)))).astype(np.float32)
    ang_r = np.arange(rows, dtype=np.float32)[:, None] * omega
    ang_c = np.arange(64, dtype=np.float32)[:, None] * omega
    emb_r = np.concatenate([np.sin(ang_r), np.cos(ang_r)], axis=-1)
    emb_c = np.concatenate([np.sin(ang_c), np.cos(ang_c)], axis=-1)
    half = emb_r.shape[-1]
    pe = np.concatenate([np.broadcast_to(emb_r[:, None, :], (rows, 64, half)), np.broadcast_to(emb_c[None, :, :], (rows, 64, half))], axis=-1)
    return pe.reshape(rows * 64, 2 * half).astype(np.float32)


def feats_tables(l):
    t = np.linspace(0.0, 1.0, l, dtype=np.float32)[:, None]
    bands = np.linspace(1e-4, 15, 16, dtype=np.float32)
    ang = (np.float32(2.0 * math.pi / l)) * np.arange(l, dtype=np.float32)[:, None] * bands
    feats = np.concatenate([t, np.cos(ang), -np.sin(ang)], axis=-1).astype(np.float32)
    return np.ascontiguousarray(feats.T), np.ascontiguousarray(t[:, 0])


def host_inputs(inputs):
    f = lambda a: np.ascontiguousarray(np.asarray(a, dtype=np.float32))
    g = {k: f(v) for k, v in inputs.items()}
    shared = {k: g[k] for k in ("mod_w", "mod_b", "norm_mix_g", "norm_mlp_g", "mlp_w1", "mlp_w2", "final_norm_g", "ev_in_w", "ev_out_w",
                                "s5_d", "s5_glu_w", "s5_glu_b", "m2_conv_w", "m2_conv_b", "m2_d", "m2_norm_g", "hy_in_w", "hy_in_b",
                                "hy_conv_w", "hy_conv_b", "hy_f_w1", "hy_f_b1", "hy_f_freq1", "hy_f_w2", "hy_f_b2", "hy_f_freq2", "hy_f_w3",
                                "hy_d", "hy_out_w", "hy_out_b")}
    shared["m2_dt_bias"] = g["m2_dt_bias"].reshape(2, 32)
    shared["m2_a_log"] = g["m2_a_log"].reshape(2, 32)
    lay = [s5_layouts(g["s5_lam_re"][j], g["s5_lam_im"][j], g["s5_log_dt"][j], g["s5_b_re"][j], g["s5_b_im"][j], g["s5_c_re"][j], g["s5_c_im"][j])
           for j in range(2)]
    shared["s5_ls"] = np.stack([l[0] for l in lay]); shared["s5_bt"] = np.stack([l[1] for l in lay]); shared["s5_ct"] = np.stack([l[2] for l in lay])
    shared["ident"] = np.eye(128, dtype=np.float32)
    sq = np.arange(128)
    mf = np.where(sq[None, :] >= sq[:, None], 0.0, -30000.0).astype(np.float32)
    mb = np.where(sq[None, :] <= sq[:, None], 0.0, -30000.0).astype(np.float32)
    shared["masks"] = np.stack([np.tile(mf, (1, 4)), np.tile(mb, (1, 4))])
    shared["tabs"] = hy_tables()
    shared["ctx_fwt"], shared["ctx_ivt"] = hy_ctx_tables(NCTX)
    shared["feats_l"], shared["tl_l"] = feats_tables(SEQ)
    shared["feats_c"], shared["tl_c"] = feats_tables(NCTX)
    shared["deltas"] = np.abs(np.linspace(math.log(1e-2) / 1.5, math.log(1e-2) / 0.3, 1024, dtype=np.float32)).astype(np.float32)
    shared["peT"] = np.ascontiguousarray(grid_sincos_np(SEQ, 1024).T)
    maps = []
    for b in range(NCORES):
        m = dict(shared)
        m["xT"] = np.ascontiguousarray(g["x"][b].T)
        m["ctxT"] = np.ascontiguousarray(g["ctx"][b].T)
        m["cvec"] = np.stack([g["c"][b], g["c_ctx"]])
        maps.append(m)
    return maps


def kernel(**inputs):
    p = build_program()
    nc = p.build()
    maps = host_inputs(inputs)
    res = run_bass_kernel_spmd(nc, maps, core_ids=list(range(NCORES)))
    return np.stack([np.asarray(r["out"], dtype=np.float32) for r in res.results], axis=0)
```

```python
import contextlib
import math
import numpy as np
import concourse.bass as bass
import concourse.mybir as mybir
from concourse.bass_utils import run_bass_kernel_spmd


F32 = mybir.dt.float32
BF16 = mybir.dt.bfloat16
I32 = mybir.dt.int32
AF = mybir.ActivationFunctionType
ALU = mybir.AluOpType
AX = mybir.AxisListType


class Prog:
    COMPUTE = ("pe", "dve", "act", "pool")
    ENGS = ("pe", "dve", "act", "pool", "sp")
    QS = ("sp", "act", "pool")
    NDS = 8

    def __init__(self):
        self.nc = bass.Bass("TRN2", target_bir_lowering=False)
        nc = self.nc
        self.top = contextlib.ExitStack()
        self.sems = {e: self.top.enter_context(nc.semaphore(f"s_{e}")) for e in self.COMPUTE}
        self.dsems = {q: [self.top.enter_context(nc.semaphore(f"d_{q}{i}")) for i in range(self.NDS)] for q in self.QS}
        self.cnt = {e: 0 for e in self.COMPUTE}
        self.dval = {q: [0] * self.NDS for q in self.QS}
        self.dnext = {q: 0 for q in self.QS}
        self.n_sb = 0
        self.stats = {e: 0 for e in self.ENGS}
        self.stack = None
        self.barrier_tok = []
        self.begin()

    def begin(self):
        self.ops = []
        self.lastw = {}
        self.readers = {}
        self.stack = contextlib.ExitStack()

    def dram(self, name, shape, dt, kind="Internal"):
        return self.nc.dram_tensor(name, list(shape), dt, kind=kind).ap()

    def sb(self, shape, dt=F32, name=None):
        self.n_sb += 1
        return self.stack.enter_context(self.nc.sbuf_tensor(f"{name or 'sb'}_{self.n_sb}", list(shape), dt))

    def ps(self, shape, dt=F32, name=None):
        self.n_sb += 1
        return self.stack.enter_context(self.nc.psum_tensor(f"{name or 'ps'}_{self.n_sb}", list(shape), dt))

    def op(self, eng, fn, reads=(), writes=(), dma=False):
        deps = set()
        for k in reads:
            if k in self.lastw:
                deps.add(self.lastw[k])
        for k in writes:
            if k in self.lastw:
                deps.add(self.lastw[k])
            for r in self.readers.get(k, ()):
                deps.add(r)
        idx = len(self.ops)
        self.ops.append(dict(eng=eng, fn=fn, deps=deps, dma=dma, sig=False))
        for k in reads:
            rl = self.readers.setdefault(k, [])
            if not dma:
                rl[:] = [r for r in rl if self.ops[r]["dma"] or self.ops[r]["eng"] != eng]
            rl.append(idx)
        for k in writes:
            self.lastw[k] = idx
            self.readers[k] = []
        return idx

    def dma(self, q, out, in_, reads=(), writes=(), **kw):
        return self.op(q, lambda e: e.dma_start(out=out, in_=in_, **kw), reads, writes, dma=True)

    def end(self):
        nc = self.nc
        ops = self.ops
        for o in ops:
            for d in o["deps"]:
                ops[d]["sig"] = True
        per = {e: [] for e in self.ENGS}
        for o in ops:
            per[o["eng"]].append(o)
        for e in self.COMPUTE:
            for o in reversed(per[e]):
                if not o["dma"]:
                    o["sig"] = True
                    break
        for o in ops:
            if o["dma"]:
                q = o["eng"]
                k = self.dnext[q]
                self.dnext[q] = (k + 1) % self.NDS
                o["prev"] = (q, k, self.dval[q][k])
                self.dval[q][k] += 16
                o["tok"] = ("d", q, k, self.dval[q][k])
            elif o["sig"]:
                self.cnt[o["eng"]] += 1
                o["tok"] = ("c", o["eng"], self.cnt[o["eng"]])
        for e in self.ENGS:
            self.stats[e] += len(per[e])
        start_tok = self.barrier_tok
        end_tok = [("c", e, self.cnt[e]) for e in self.COMPUTE if self.cnt[e] > 0]
        for q in self.QS:
            for k in range(self.NDS):
                if self.dval[q][k] > 0:
                    end_tok.append(("d", q, k, self.dval[q][k]))
        self.barrier_tok = end_tok
        sems, dsems = self.sems, self.dsems

        def emit(e_name, eng):
            waited = {}

            def wait(t):
                key = t[:-1]
                if waited.get(key, 0) >= t[-1]:
                    return
                waited[key] = t[-1]
                sem = sems[t[1]] if t[0] == "c" else dsems[t[1]][t[2]]
                eng.wait_ge(sem, t[-1])

            for t in start_tok:
                wait(t)
            for o in per[e_name]:
                if o["dma"]:
                    q, k, pv = o["prev"]
                    if pv > 0:
                        wait(("d", q, k, pv))
                for d in sorted(o["deps"]):
                    t = ops[d]["tok"]
                    if t[0] == "c" and t[1] == e_name and e_name == "pe":
                        continue
                    wait(t)
                ins = o["fn"](eng)
                if o["dma"]:
                    t = o["tok"]
                    ins.then_inc(dsems[t[1]][t[2]], 16)
                elif o["sig"]:
                    ins.then_inc(sems[e_name], 1)

        with nc.Block() as block:
            @block.tensor
            def _(eng):
                emit("pe", eng)

            @block.vector
            def _(eng):
                emit("dve", eng)

            @block.scalar
            def _(eng):
                emit("act", eng)

            @block.gpsimd
            def _(eng):
                emit("pool", eng)

            @block.sync
            def _(eng):
                emit("sp", eng)
        self.stack.close()
        self.begin()

    def build(self):
        nc = self.nc
        toks = self.barrier_tok
        sems, dsems = self.sems, self.dsems
        with nc.Block() as block:
            def fin(eng):
                for t in toks:
                    sem = sems[t[1]] if t[0] == "c" else dsems[t[1]][t[2]]
                    eng.wait_ge(sem, t[-1])

            @block.sync
            def _(eng):
                fin(eng)

            @block.gpsimd
            def _(eng):
                fin(eng)
        self.top.close()
        return nc


def run(p, in_maps, n=None, trace=False):
    nc = p.build()
    n = n or len(in_maps)
    return run_bass_kernel_spmd(nc, in_maps, core_ids=list(range(n)), trace=trace)


EPS = 1e-6
D = 1024


def col(ap1d):
    return ap1d.rearrange("(c p) -> p c", p=128)


def load_cols(p, dst, src1d, key, q="sp"):
    p.dma(q, dst, col(src1d), writes=[key], allow_slow_non_contiguous=True)


def load_weight_bf16(p, w_d, K, F, name, q="sp", chunk=1024):
    kc = K // 128
    wb = p.sb([128, kc, F], BF16, name=name)
    stg = [p.sb([128, chunk], F32, name=f"{name}_stg{i}") for i in range(2)]
    i = 0
    for c in range(kc):
        for f0 in range(0, F, chunk):
            fn = min(chunk, F - f0)
            s = stg[i % 2]
            p.dma(q, s[:, :fn], w_d[c * 128:(c + 1) * 128, f0:f0 + fn], writes=[f"{name}_stg{i%2}"])
            if i % 2 == 0:
                p.op("pool", lambda e, s=s, c=c, f0=f0, fn=fn: e.tensor_copy(out=wb[:, c, f0:f0 + fn], in_=s[:, :fn]),
                     reads=[f"{name}_stg{i%2}"], writes=[name])
            else:
                p.op("act", lambda e, s=s, c=c, f0=f0, fn=fn: e.copy(out=wb[:, c, f0:f0 + fn], in_=s[:, :fn]),
                     reads=[f"{name}_stg{i%2}"], writes=[name])
            i += 1
    return wb


class Norm:
    def __init__(self, p, NMAX, nmod, g_d, mods):
        self.p = p
        self.ones = p.sb([128, 128], F32, name="ones")
        p.op("pool", lambda e: e.memset(self.ones[:, :], 1.0), writes=["ones"])
        self.eps = p.sb([128, 1], F32, name="eps")
        p.op("pool", lambda e: e.memset(self.eps[:, :], EPS), writes=["eps"])
        self.sq = p.sb([128, 8, NMAX], F32, name="sq")
        self.tmp = p.sb([128, 8, NMAX], F32, name="ntmp")
        self.rstd = p.sb([128, NMAX], F32, name="rstd")
        self.ps_ss = p.ps([128, 512], name="ps_ss")
        g_sb = p.sb([128, 8], F32, name="g_sb")
        load_cols(p, g_sb[:, :], g_d, "g_sb")
        self.gs = p.sb([128, nmod, 8], F32, name="gs_sb")
        self.sh = p.sb([128, nmod, 8], F32, name="sh_sb")
        sc_sb = p.sb([128, nmod, 8], F32, name="sc_sb")
        for m, (sc_d, sh_d) in enumerate(mods):
            if sc_d is None:
                p.op("pool", lambda e, m=m: e.memset(sc_sb[:, m, :], 0.0), writes=["sc_sb"])
                p.op("pool", lambda e, m=m: e.memset(self.sh[:, m, :], 0.0), writes=["mods"])
            else:
                load_cols(p, sc_sb[:, m, :], sc_d, "sc_sb")
                load_cols(p, self.sh[:, m, :], sh_d, "mods")
        for m in range(nmod):
            p.op("dve", lambda e, m=m: e.scalar_tensor_tensor(out=self.gs[:, m, :], in0=sc_sb[:, m, :], scalar=1.0, in1=g_sb[:, :],
                                                               op0=ALU.add, op1=ALU.mult),
                 reads=["sc_sb", "g_sb"], writes=["mods"])

    def apply(self, x_sb, xk, N, mi, ht, hk):
        p = self.p
        sq, ones, ps_ss, rstd, tmp, eps_t = self.sq, self.ones, self.ps_ss, self.rstd, self.tmp, self.eps
        gs, sh = self.gs[:, mi, :], self.sh[:, mi, :]
        p.op("act", lambda e: e.activation(out=sq[:, :, :N], in_=x_sb[:, :, :N], func=AF.Square), reads=[xk], writes=["sq"])
        for c in range(8):
            p.op("pe", lambda e, c=c: e.matmul(ps_ss[:, :N], lhsT=ones[:, :], rhs=sq[:, c, :N], start=(c == 0), stop=(c == 7)),
                 reads=["sq", "ones"], writes=["ps_ss"])
        p.op("act", lambda e: e.activation(out=rstd[:, :N], in_=ps_ss[:, :N], func=AF.Sqrt, scale=1.0 / D, bias=eps_t[:, 0:1]),
             reads=["ps_ss", "eps"], writes=["rstd"])
        p.op("dve", lambda e: e.reciprocal(out=rstd[:, :N], in_=rstd[:, :N]), reads=["rstd"], writes=["rstd"])
        for c in range(8):
            p.op("dve", lambda e, c=c: e.tensor_tensor(out=tmp[:, c, :N], in0=x_sb[:, c, :N], in1=rstd[:, :N], op=ALU.mult),
                 reads=[xk, "rstd"], writes=["ntmp" + str(c)])
            p.op("act", lambda e, c=c: e.activation(out=ht[:, c, :N], in_=tmp[:, c, :N], func=AF.Identity,
                                                   scale=gs[:, c:c + 1], bias=sh[:, c:c + 1]),
                 reads=["ntmp" + str(c), "mods"], writes=[hk])


def fm(ap2d, t0, N):
    return ap2d[:, t0:t0 + N].rearrange("(c p) n -> p c n", p=128)


def ph_tok_in(p, XT, W, F, g_d, mods, tiles, PROJ, bias=None):
    nmod = len(mods)
    nfc = (F + 127) // 128
    nrm = Norm(p, 512, nmod, g_d, mods)
    if bias is not None:
        b_sb = p.sb([128, nfc], F32, name="b_sb")
        p.op("pool", lambda e: e.memset(b_sb[:, :], 0.0), writes=["b_sb"])
        nfull = F // 128
        p.dma("sp", b_sb[:, :nfull], col(bias[:nfull * 128]), writes=["b_sb"], allow_slow_non_contiguous=True)
        if F % 128:
            p.dma("sp", b_sb[:F % 128, nfull:nfull + 1], bias[nfull * 128:F].rearrange("(p o) -> p o", o=1), writes=["b_sb"],
                  allow_slow_non_contiguous=True)
    wb = load_weight_bf16(p, W, D, F, "wb")
    xs = [p.sb([128, 8, 512], F32, name=f"xt{i}") for i in range(2)]
    hts = [p.sb([128, 8, 512], BF16, name=f"ht{i}") for i in range(2)]
    pss = [p.ps([128, 512], name=f"psm{i}") for i in range(4)]
    outs = [p.sb([128, 512], F32, name=f"o{i}") for i in range(4)]
    oi = 0
    for ti, (t0, N, mi) in enumerate(tiles):
        x_sb, xk, ht, hk = xs[ti % 2], f"xt{ti%2}", hts[ti % 2], f"ht{ti%2}"
        p.dma("sp", x_sb[:, :, :N], fm(XT, t0, N), reads=["XT"], writes=[xk])
        nrm.apply(x_sb, xk, N, mi, ht, hk)
        for fc in range(nfc):
            M = min(128, F - fc * 128)
            ps, pk, o, ok = pss[oi % 4], f"psm{oi%4}", outs[oi % 4], f"o{oi%4}"
            for c in range(8):
                p.op("pe", lambda e, ps=ps, c=c, fc=fc, M=M, ht=ht, N=N: e.matmul(ps[:M, :N], lhsT=wb[:, c, fc * 128:fc * 128 + M],
                                                                                  rhs=ht[:, c, :N], start=(c == 0), stop=(c == 7)),
                     reads=["wb", hk], writes=[pk])
            if bias is not None:
                p.op("act", lambda e, ps=ps, o=o, M=M, N=N, fc=fc: e.activation(out=o[:M, :N], in_=ps[:M, :N], func=AF.Identity,
                                                                                 bias=b_sb[:M, fc:fc + 1], scale=1.0),
                     reads=[pk, "b_sb"], writes=[ok])
            elif oi % 2 == 0:
                p.op("act", lambda e, ps=ps, o=o, M=M, N=N: e.copy(out=o[:M, :N], in_=ps[:M, :N]), reads=[pk], writes=[ok])
            else:
                p.op("dve", lambda e, ps=ps, o=o, M=M, N=N: e.tensor_copy(out=o[:M, :N], in_=ps[:M, :N]), reads=[pk], writes=[ok])
            p.dma("pool", PROJ[fc * 128:fc * 128 + M, t0:t0 + N], o[:M, :N], reads=[ok], writes=["PROJ"])
            oi += 1


def ph_outproj(p, XT, Y, CM, W, ga_list, tiles, bias=None):
    kc = CM // 128
    nmod = len(ga_list)
    ga = p.sb([128, nmod, 8], F32, name="ga")
    for m, gd in enumerate(ga_list):
        load_cols(p, ga[:, m, :], gd, "ga")
    if bias is not None:
        b_sb = p.sb([128, 8], F32, name="ob_sb")
        load_cols(p, b_sb[:, :], bias, "ob")
        gb = p.sb([128, nmod, 8], F32, name="gb")
        for m in range(nmod):
            p.op("dve", lambda e, m=m: e.tensor_tensor(out=gb[:, m, :], in0=ga[:, m, :], in1=b_sb[:, :], op=ALU.mult),
                 reads=["ga", "ob"], writes=["gb"])
    wb = load_weight_bf16(p, W, CM, D, "wo")
    xs = [p.sb([128, 8, 512], F32, name=f"xo{i}") for i in range(2)]
    ys = [p.sb([128, kc, 512], BF16, name=f"yo{i}") for i in range(2)]
    pss = [p.ps([128, 512], name=f"pso{i}") for i in range(4)]
    oi = 0
    for ti, (t0, N, mi) in enumerate(tiles):
        x_sb, xk, y_sb, yk = xs[ti % 2], f"xo{ti%2}", ys[ti % 2], f"yo{ti%2}"
        p.dma("sp", x_sb[:, :, :N], fm(XT, t0, N), reads=["XT"], writes=[xk])
        p.dma("sp", y_sb[:, :, :N], fm(Y[0:CM, :], t0, N), reads=["Y"], writes=[yk])
        if bias is not None:
            for c in range(8):
                p.op("act", lambda e, c=c, x_sb=x_sb, N=N, mi=mi: e.activation(out=x_sb[:, c, :N], in_=x_sb[:, c, :N], func=AF.Identity,
                                                                                 bias=gb[:, mi, c:c + 1], scale=1.0),
                     reads=[xk, "gb"], writes=[xk])
        for m in range(8):
            ps, pk = pss[oi % 4], f"pso{oi%4}"
            for c in range(kc):
                p.op("pe", lambda e, ps=ps, c=c, m=m, y_sb=y_sb, N=N: e.matmul(ps[:, :N], lhsT=wb[:, c, m * 128:(m + 1) * 128],
                                                                                rhs=y_sb[:, c, :N], start=(c == 0), stop=(c == kc - 1)),
                     reads=["wo", yk], writes=[pk])
            p.op("dve", lambda e, ps=ps, m=m, x_sb=x_sb, N=N, mi=mi: e.scalar_tensor_tensor(
                out=x_sb[:, m, :N], in0=ps[:, :N], scalar=ga[:, mi, m:m + 1], in1=x_sb[:, m, :N], op0=ALU.mult, op1=ALU.add),
                 reads=[pk, xk, "ga"], writes=[xk])
            oi += 1
        p.dma("pool", fm(XT, t0, N), x_sb[:, :, :N], reads=[xk], writes=["XT"])


def ph_mlp(p, XT, W1, W2, g_d, mods, gf_list, tiles):
    nmod = len(mods)
    NT = 256
    nrm = Norm(p, NT, nmod, g_d, mods)
    gf = p.sb([128, nmod, 8], F32, name="gf")
    for m, gd in enumerate(gf_list):
        load_cols(p, gf[:, m, :], gd, "gf")
    w1 = load_weight_bf16(p, W1, D, 4096, "w1")
    w2 = load_weight_bf16(p, W2, 4096, D, "w2")
    xs = [p.sb([128, 8, NT], F32, name=f"xm{i}") for i in range(2)]
    ht = p.sb([128, 8, NT], BF16, name="htm")
    h1 = p.sb([128, 32, NT], BF16, name="h1")
    rs = [p.sb([128, NT], F32, name=f"r{i}") for i in range(2)]
    pss = [p.ps([128, 512], name=f"psx{i}") for i in range(4)]
    oi = 0
    for ti, (t0, N, mi) in enumerate(tiles):
        x_sb, xk = xs[ti % 2], f"xm{ti%2}"
        p.dma("sp", x_sb[:, :, :N], fm(XT, t0, N), reads=["XT"], writes=[xk])
        nrm.apply(x_sb, xk, N, mi, ht, "htm")
        for hc in range(32):
            ps, pk, r, rk = pss[oi % 4], f"psx{oi%4}", rs[oi % 2], f"r{oi%2}"
            for c in range(8):
                p.op("pe", lambda e, ps=ps, c=c, hc=hc, N=N: e.matmul(ps[:, :N], lhsT=w1[:, c, hc * 128:(hc + 1) * 128],
                                                                       rhs=ht[:, c, :N], start=(c == 0), stop=(c == 7)),
                     reads=["w1", "htm"], writes=[pk])
            p.op("act", lambda e, ps=ps, r=r, N=N: e.activation(out=r[:, :N], in_=ps[:, :N], func=AF.Relu), reads=[pk], writes=[rk])
            p.op("pool", lambda e, r=r, hc=hc, N=N: e.tensor_tensor(out=h1[:, hc, :N], in0=r[:, :N], in1=r[:, :N], op=ALU.mult),
                 reads=[rk], writes=["h1_" + str(hc)])
            oi += 1
        for m in range(8):
            ps, pk = pss[oi % 4], f"psx{oi%4}"
            for hc in range(32):
                p.op("pe", lambda e, ps=ps, hc=hc, m=m, N=N: e.matmul(ps[:, :N], lhsT=w2[:, hc, m * 128:(m + 1) * 128],
                                                                       rhs=h1[:, hc, :N], start=(hc == 0), stop=(hc == 31)),
                     reads=["w2", "h1_" + str(hc)], writes=[pk])
            p.op("dve", lambda e, ps=ps, m=m, x_sb=x_sb, N=N, mi=mi: e.scalar_tensor_tensor(
                out=x_sb[:, m, :N], in0=ps[:, :N], scalar=gf[:, mi, m:m + 1], in1=x_sb[:, m, :N], op0=ALU.mult, op1=ALU.add),
                 reads=[pk, xk, "gf"], writes=[xk])
            oi += 1
        p.dma("pool", fm(XT, t0, N), x_sb[:, :, :N], reads=[xk], writes=["XT"])


def ph_final(p, XT, g_d, tiles, OUT, t_off, IDENT):
    nrm = Norm(p, 512, 1, g_d, [(None, None)])
    ident = p.sb([128, 128], F32, name="ident")
    p.dma("sp", ident[:, :], IDENT, writes=["ident"])
    xs = [p.sb([128, 8, 512], F32, name=f"xf{i}") for i in range(2)]
    hf = p.sb([128, 8, 512], F32, name="hf")
    pst = [p.ps([128, 512], name=f"pst{i}") for i in range(4)]
    ot = [p.sb([128, 1024], F32, name=f"ot{i}") for i in range(2)]
    oi = 0
    k = 0
    for ti, (t0, N, mi) in enumerate(tiles):
        x_sb, xk = xs[ti % 2], f"xf{ti%2}"
        p.dma("sp", x_sb[:, :, :N], fm(XT, t0, N), reads=["XT"], writes=[xk])
        nrm.apply(x_sb, xk, N, 0, hf, "hf")
        for j in range(N // 128):
            o, ok = ot[k % 2], f"ot{k%2}"
            for half in range(2):
                ps, pk = pst[oi % 4], f"pst{oi%4}"
                for cc in range(4):
                    c = half * 4 + cc
                    p.op("pe", lambda e, ps=ps, c=c, cc=cc, j=j: e.transpose(out=ps[:, cc * 128:(cc + 1) * 128],
                                                                            in_=hf[:, c, j * 128:(j + 1) * 128], identity=ident[:, :]),
                         reads=["hf", "ident"], writes=[pk])
                if half == 0:
                    p.op("act", lambda e, ps=ps, o=o: e.copy(out=o[:, 0:512], in_=ps[:, :]), reads=[pk], writes=[ok])
                else:
                    p.op("dve", lambda e, ps=ps, o=o: e.tensor_copy(out=o[:, 512:1024], in_=ps[:, :]), reads=[pk], writes=[ok])
                oi += 1
            tt = t0 + j * 128 - t_off
            p.dma("pool", OUT[tt:tt + 128, :], o[:, :], reads=[ok], writes=["OUT"])
            k += 1


TWO_PI = 2.0 * math.pi


def sin_tmps(p, shape, tag):
    return (p.sb(shape, F32, name=tag + "_a"), p.sb(shape, I32, name=tag + "_i"), p.sb(shape, F32, name=tag + "_f"),
            p.sb(shape, F32, name=tag + "_r"), p.sb(shape, F32, name=tag + "_m"))


def sin_rr(p, out, x, shape, tag, shift=0.0, tmps=None):
    a, ai, af, r, m = tmps if tmps is not None else sin_tmps(p, shape, tag)
    sl = tuple(slice(None) for _ in shape)
    k = tag
    p.op("dve", lambda e: e.tensor_scalar(out=a[sl], in0=x, scalar1=shift, scalar2=1.0 / TWO_PI, op0=ALU.add, op1=ALU.mult),
         reads=[k + "x"], writes=[k + "a"])
    p.op("dve", lambda e: e.tensor_copy(out=ai[sl], in_=a[sl]), reads=[k + "a"], writes=[k + "i"])
    p.op("dve", lambda e: e.tensor_copy(out=af[sl], in_=ai[sl]), reads=[k + "i"], writes=[k + "f"])
    p.op("dve", lambda e: e.tensor_scalar(out=r[sl], in0=x, scalar1=shift, scalar2=None, op0=ALU.add), reads=[k + "x"], writes=[k + "r"])
    p.op("dve", lambda e: e.scalar_tensor_tensor(out=r[sl], in0=af[sl], scalar=-TWO_PI, in1=r[sl], op0=ALU.mult, op1=ALU.add),
         reads=[k + "f", k + "r"], writes=[k + "r"])
    p.op("dve", lambda e: e.tensor_scalar(out=m[sl], in0=r[sl], scalar1=math.pi, scalar2=None, op0=ALU.is_gt), reads=[k + "r"], writes=[k + "m"])
    p.op("dve", lambda e: e.scalar_tensor_tensor(out=r[sl], in0=m[sl], scalar=-TWO_PI, in1=r[sl], op0=ALU.mult, op1=ALU.add),
         reads=[k + "m", k + "r"], writes=[k + "r"])
    p.op("dve", lambda e: e.tensor_scalar(out=m[sl], in0=r[sl], scalar1=-math.pi, scalar2=None, op0=ALU.is_lt), reads=[k + "r"], writes=[k + "m"])
    p.op("dve", lambda e: e.scalar_tensor_tensor(out=r[sl], in0=m[sl], scalar=TWO_PI, in1=r[sl], op0=ALU.mult, op1=ALU.add),
         reads=[k + "m", k + "r"], writes=[k + "r"])
    p.op("dve", lambda e: e.tensor_scalar(out=r[sl], in0=r[sl], scalar1=3.1415925, scalar2=-3.1415925, op0=ALU.min, op1=ALU.max),
         reads=[k + "r"], writes=[k + "r"])
    p.op("act", lambda e: e.activation(out=out, in_=r[sl], func=AF.Sin), reads=[k + "r"], writes=[k + "o"])


def s5_params(p, src3, shape, tag):
    sl = tuple(slice(None) for _ in shape)
    t = {n: p.sb(shape, F32, name=f"{tag}_{n}") for n in
         ("lre", "lim", "ldt", "step", "r", "th", "c", "s", "ar", "ai", "den", "fr", "fi", "t1", "t2")}
    k = tag
    p.dma("sp", t["lre"][sl], src3[0], writes=[k])
    p.dma("sp", t["lim"][sl], src3[1], writes=[k])
    p.dma("sp", t["ldt"][sl], src3[2], writes=[k])

    def dve(fn):
        p.op("dve", fn, reads=[k], writes=[k])

    p.op("act", lambda e: e.activation(out=t["step"][sl], in_=t["ldt"][sl], func=AF.Exp), reads=[k], writes=[k])
    dve(lambda e: e.tensor_tensor(out=t["t1"][sl], in0=t["lre"][sl], in1=t["step"][sl], op=ALU.mult))
    p.op("act", lambda e: e.activation(out=t["r"][sl], in_=t["t1"][sl], func=AF.Exp), reads=[k], writes=[k])
    dve(lambda e: e.tensor_tensor(out=t["th"][sl], in0=t["lim"][sl], in1=t["step"][sl], op=ALU.mult))
    p.op("dve", lambda e: e.tensor_copy(out=t["t2"][sl], in_=t["th"][sl]), reads=[k], writes=[k + "sx", k + "cx"])
    sin_rr(p, t["s"][sl], t["th"][sl], shape, k + "s")
    sin_rr(p, t["c"][sl], t["th"][sl], shape, k + "c", shift=math.pi / 2)
    p.op("dve", lambda e: e.tensor_tensor(out=t["ar"][sl], in0=t["r"][sl], in1=t["c"][sl], op=ALU.mult), reads=[k, k + "co"], writes=[k])
    p.op("dve", lambda e: e.tensor_tensor(out=t["ai"][sl], in0=t["r"][sl], in1=t["s"][sl], op=ALU.mult), reads=[k, k + "so"], writes=[k])
    dve(lambda e: e.tensor_tensor(out=t["den"][sl], in0=t["lre"][sl], in1=t["lre"][sl], op=ALU.mult))
    dve(lambda e: e.tensor_tensor(out=t["t1"][sl], in0=t["lim"][sl], in1=t["lim"][sl], op=ALU.mult))
    dve(lambda e: e.tensor_tensor(out=t["den"][sl], in0=t["den"][sl], in1=t["t1"][sl], op=ALU.add))
    dve(lambda e: e.reciprocal(out=t["den"][sl], in_=t["den"][sl]))
    dve(lambda e: e.tensor_scalar(out=t["t1"][sl], in0=t["ar"][sl], scalar1=-1.0, scalar2=None, op0=ALU.add))
    dve(lambda e: e.tensor_tensor(out=t["fr"][sl], in0=t["t1"][sl], in1=t["lre"][sl], op=ALU.mult))
    dve(lambda e: e.tensor_tensor(out=t["t2"][sl], in0=t["ai"][sl], in1=t["lim"][sl], op=ALU.mult))
    dve(lambda e: e.tensor_tensor(out=t["fr"][sl], in0=t["fr"][sl], in1=t["t2"][sl], op=ALU.add))
    dve(lambda e: e.tensor_tensor(out=t["fr"][sl], in0=t["fr"][sl], in1=t["den"][sl], op=ALU.mult))
    dve(lambda e: e.tensor_tensor(out=t["fi"][sl], in0=t["ai"][sl], in1=t["lre"][sl], op=ALU.mult))
    dve(lambda e: e.tensor_tensor(out=t["t2"][sl], in0=t["t1"][sl], in1=t["lim"][sl], op=ALU.mult))
    dve(lambda e: e.tensor_tensor(out=t["fi"][sl], in0=t["fi"][sl], in1=t["t2"][sl], op=ALU.subtract))
    dve(lambda e: e.tensor_tensor(out=t["fi"][sl], in0=t["fi"][sl], in1=t["den"][sl], op=ALU.mult))
    return t


def windows(T, NCTX, W=512):
    fw_ = [(0, NCTX)] + [(t, min(t + W, T)) for t in range(NCTX, T, W)]
    bw = [(0, NCTX)] + [(max(t - W, NCTX), t) for t in range(T, NCTX, -W)]
    return fw_, bw


def ph_s5_scan(p, PROJ, T, NCTX, LS, BT, CT, DSK, YPRE):
    W = 512
    ls = s5_params(p, LS, [128, 32], "ls")
    bt = p.sb([32, 2, 16, 128], F32, name="bt")
    p.dma("sp", bt[:, 0], BT[0], writes=["bt"])
    p.dma("sp", bt[:, 1], BT[1], writes=["bt"])
    ct = p.sb([128, 2, 16, 32], F32, name="ct")
    p.dma("sp", ct[:, 0], CT[0], writes=["ct"])
    p.dma("sp", ct[:, 1], CT[1], writes=["ct"])
    p.op("dve", lambda e: e.tensor_scalar(out=ct[:, 1], in0=ct[:, 1], scalar1=-1.0, scalar2=None, op0=ALU.mult), reads=["ct"], writes=["ct"])
    dsk = p.sb([32, 16], F32, name="dsk")
    p.dma("sp", dsk[:, :], DSK.rearrange("(g q) -> q g", q=32), writes=["dsk"], allow_slow_non_contiguous=True)
    nsc = p.sb([128, 32], F32, name="nsc")
    p.op("dve", lambda e: e.tensor_scalar(out=nsc[:, :], in0=ls["s"][:, :], scalar1=-1.0, scalar2=None, op0=ALU.mult), reads=["ls", "lsso"], writes=["nsc"])

    F32R = mybir.dt.float32r
    ctr = p.sb([128, 3, 16, 32], F32, name="ctr")
    p.op("dve", lambda e: e.tensor_copy(out=ctr[:, 0].bitcast(F32R), in_=ct[:, 0]), reads=["ct"], writes=["ctr"])
    p.op("dve", lambda e: e.tensor_scalar(out=ctr[:, 1].bitcast(F32R), in0=ct[:, 0], scalar1=-1.0, scalar2=None, op0=ALU.mult), reads=["ct"], writes=["ctr"])
    p.op("dve", lambda e: e.tensor_copy(out=ctr[:, 2].bitcast(F32R), in_=ct[:, 1]), reads=["ct"], writes=["ctr"])
    ones = p.sb([128, W], F32, name="s5ones")
    p.op("pool", lambda e: e.memset(ones[:, :], 1.0), writes=["s5ones"])
    mkd = lambda n: [p.sb([128, W], F32, name=f"{n}{i}") for i in range(2)]
    Ec, Es, rt, Mc, Ms = mkd("Ec"), mkd("Es"), mkd("rt"), mkd("Mc"), mkd("Ms")
    wv = p.sb([128, 4], F32, name="wv")
    wN = [p.sb([128, 2, 3], F32, name=f"wN{i}") for i in range(2)]
    tt = p.sb([128, W], F32, name="ttab")
    u_sb = p.sb([32, T], F32, name="u_sb")
    ysum = p.sb([32, T], F32, name="ysum")
    psA = [p.ps([128, 512], name=f"psA{i}") for i in range(2)]
    psB = [p.ps([128, 512], name=f"psB{i}") for i in range(2)]
    psY = [p.ps([128, 512], name=f"psY{i}") for i in range(2)]
    ta, tb_, tc, td = mkd("ta"), mkd("tb"), mkd("tc"), mkd("td")
    mre, mim = mkd("mre"), mkd("mim")
    gre = [[p.sb([128, W], F32, name=f"gre{i}{j}") for j in range(2)] for i in range(2)]
    gim = [[p.sb([128, W], F32, name=f"gim{i}{j}") for j in range(2)] for i in range(2)]
    q1, q2, q3, q4 = mkd("q1"), mkd("q2"), mkd("q3"), mkd("q4")
    init = [p.sb([128, 4], F32, name=f"init{i}") for i in range(2)]
    ysb = [p.sb([32, W], F32, name=f"ysb{i}") for i in range(2)]
    fwin, bwin = windows(T, NCTX, W)
    assert len(fwin) == len(bwin)
    for gp in range(16):
        p.dma("sp", u_sb[:, :], PROJ[gp * 32:(gp + 1) * 32, :], reads=["PROJ"], writes=["u_sb"])
        p.op("pool", lambda e: e.memset(ysum[:, :], 0.0), writes=["ysum"])
        for d in range(2):
            cix = d * 16 + gp
            cs, sn, rr = ls["c"][:, cix:cix + 1], ls["s"][:, cix:cix + 1], ls["r"][:, cix:cix + 1]
            EC, ES, MC, MS, RT, WN, ek, mk_ = Ec[d], Es[d], Mc[d], Ms[d], rt[d], wN[d], f"E{d}", f"M{d}"
            p.op("dve", lambda e, EC=EC: e.memset(EC[:, 0:1], 1.0), writes=[ek])
            p.op("dve", lambda e, ES=ES: e.memset(ES[:, 0:1], 0.0), writes=[ek])
            p.op("dve", lambda e, cs=cs: e.tensor_copy(out=wv[:, 0:1], in_=cs), reads=["ls", "lsco"], writes=["wv"])
            p.op("dve", lambda e, sn=sn: e.tensor_copy(out=wv[:, 1:2], in_=sn), reads=["ls", "lsso"], writes=["wv"])
            m = 1
            while m < W:
                p.op("dve", lambda e: e.tensor_scalar(out=wv[:, 2:3], in0=wv[:, 1:2], scalar1=-1.0, scalar2=None, op0=ALU.mult), reads=["wv"], writes=["wv"])
                if m == 256:
                    p.op("dve", lambda e, WN=WN: e.tensor_copy(out=WN[:, 0, :], in_=wv[:, 0:3]), reads=["wv"], writes=[f"wN{d}"])
                p.op("dve", lambda e, m=m, EC=EC: e.tensor_scalar(out=tt[:, 0:m], in0=EC[:, 0:m], scalar1=wv[:, 0:1], scalar2=None, op0=ALU.mult), reads=[ek, "wv"], writes=["ttab"])
                p.op("dve", lambda e, m=m, EC=EC, ES=ES: e.scalar_tensor_tensor(out=EC[:, m:2 * m], in0=ES[:, 0:m], scalar=wv[:, 2:3], in1=tt[:, 0:m], op0=ALU.mult, op1=ALU.add),
                     reads=[ek, "wv", "ttab"], writes=[ek])
                p.op("dve", lambda e, m=m, EC=EC: e.tensor_scalar(out=tt[:, 0:m], in0=EC[:, 0:m], scalar1=wv[:, 1:2], scalar2=None, op0=ALU.mult), reads=[ek, "wv"], writes=["ttab"])
                p.op("dve", lambda e, m=m, ES=ES: e.scalar_tensor_tensor(out=ES[:, m:2 * m], in0=ES[:, 0:m], scalar=wv[:, 0:1], in1=tt[:, 0:m], op0=ALU.mult, op1=ALU.add),
                     reads=[ek, "wv", "ttab"], writes=[ek])
                p.op("dve", lambda e: e.tensor_tensor(out=wv[:, 3:4], in0=wv[:, 1:2], in1=wv[:, 1:2], op=ALU.mult), reads=["wv"], writes=["wv"])
                p.op("dve", lambda e: e.scalar_tensor_tensor(out=wv[:, 1:2], in0=wv[:, 1:2], scalar=2.0, in1=wv[:, 0:1], op0=ALU.mult, op1=ALU.mult), reads=["wv"], writes=["wv"])
                p.op("dve", lambda e: e.tensor_tensor(out=wv[:, 0:1], in0=wv[:, 0:1], in1=wv[:, 0:1], op=ALU.mult), reads=["wv"], writes=["wv"])
                p.op("dve", lambda e: e.tensor_tensor(out=wv[:, 0:1], in0=wv[:, 0:1], in1=wv[:, 3:4], op=ALU.subtract), reads=["wv"], writes=["wv"])
                m *= 2
            p.op("dve", lambda e: e.tensor_scalar(out=wv[:, 2:3], in0=wv[:, 1:2], scalar1=-1.0, scalar2=None, op0=ALU.mult), reads=["wv"], writes=["wv"])
            p.op("dve", lambda e, WN=WN: e.tensor_copy(out=WN[:, 1, :], in_=wv[:, 0:3]), reads=["wv"], writes=[f"wN{d}"])
            frc, fic = ls["fr"][:, cix:cix + 1], ls["fi"][:, cix:cix + 1]
            p.op("dve", lambda e, fic=fic, ES=ES: e.tensor_scalar(out=tt[:, :], in0=ES[:, :], scalar1=fic, scalar2=None, op0=ALU.mult), reads=[ek, "ls"], writes=["ttab"])
            p.op("dve", lambda e, frc=frc, EC=EC, MC=MC: e.scalar_tensor_tensor(out=MC[:, :], in0=EC[:, :], scalar=frc, in1=tt[:, :], op0=ALU.mult, op1=ALU.add), reads=[ek, "ls", "ttab"], writes=[mk_])
            p.op("dve", lambda e, fic=fic, EC=EC: e.tensor_scalar(out=tt[:, :], in0=EC[:, :], scalar1=fic, scalar2=None, op0=ALU.mult), reads=[ek, "ls", mk_], writes=["ttab"])
            p.op("dve", lambda e, frc=frc, ES=ES, MS=MS: e.scalar_tensor_tensor(out=MS[:, :], in0=ES[:, :], scalar=frc, in1=tt[:, :], op0=ALU.mult, op1=ALU.subtract), reads=[ek, "ls", "ttab"], writes=[mk_])
            p.op("dve", lambda e, rr=rr, RT=RT: e.tensor_scalar(out=RT[:, :], in0=ones[:, :], scalar1=rr, scalar2=None, op0=ALU.mult), reads=["ls", "s5ones"], writes=[f"rt{d}"])
        prevs = [None, None]
        pending = []
        for wi_ in range(len(fwin)):
            for d in range(2):
                lo, hi = (fwin if d == 0 else bwin)[wi_]
                N = hi - lo
                sl = slice(lo, hi) if d == 0 else slice(hi - 1, lo - 1 if lo > 0 else None, -1)
                EC, ES, MC, MS, RT, WN, ek, mk_ = Ec[d], Es[d], Mc[d], Ms[d], rt[d], wN[d], f"E{d}", f"M{d}"
                b2 = d
                A, B, Yp = psA[b2], psB[b2], psY[b2]
                ak, bk, yk = f"psA{b2}", f"psB{b2}", f"psY{b2}"
                INIT, ik = init[d], f"init{d}"
                p.op("pe", lambda e, A=A, sl=sl, N=N, gp=gp: e.matmul(A[:, :N], lhsT=bt[:, 0, gp, :], rhs=u_sb[:, sl], start=True, stop=True),
                     reads=["bt", "u_sb"], writes=[ak])
                p.op("pe", lambda e, B=B, sl=sl, N=N, gp=gp: e.matmul(B[:, :N], lhsT=bt[:, 1, gp, :], rhs=u_sb[:, sl], start=True, stop=True),
                     reads=["bt", "u_sb"], writes=[bk])
                for fn in pending:
                    fn()
                pending = []
                TA, TB, TC, TD = ta[b2], tb_[b2], tc[b2], td[b2]
                par = wi_ % 2
                MR, MI, GR, GI = mre[b2], mim[b2], gre[b2][par], gim[b2][par]
                PGR, PGI = gre[b2][1 - par], gim[b2][1 - par]
                gk, gik, pgk, pgik = f"gre{b2}{par}", f"gim{b2}{par}", f"gre{b2}{1-par}", f"gim{b2}{1-par}"
                prev = prevs[d]
                if prev is None:
                    p.op("dve", lambda e, INIT=INIT: e.memset(INIT[:, :], 0.0), writes=[ik])
                else:
                    pN = prev
                    wsel = 0 if pN == 256 else 1
                    assert pN in (256, 512)
                    p.op("dve", lambda e, GR=PGR, pN=pN, wsel=wsel, INIT=INIT, WN=WN: e.tensor_scalar(out=INIT[:, 2:3], in0=GR[:, pN - 1:pN], scalar1=WN[:, wsel, 0:1], scalar2=None, op0=ALU.mult),
                         reads=[pgk, f"wN{d}"], writes=[ik])
                    p.op("dve", lambda e, GI=PGI, pN=pN, wsel=wsel, INIT=INIT, WN=WN: e.scalar_tensor_tensor(out=INIT[:, 0:1], in0=GI[:, pN - 1:pN], scalar=WN[:, wsel, 2:3], in1=INIT[:, 2:3], op0=ALU.mult, op1=ALU.add),
                         reads=[pgik, f"wN{d}", ik], writes=[ik])
                    p.op("dve", lambda e, GR=PGR, pN=pN, wsel=wsel, INIT=INIT, WN=WN: e.tensor_scalar(out=INIT[:, 3:4], in0=GR[:, pN - 1:pN], scalar1=WN[:, wsel, 1:2], scalar2=None, op0=ALU.mult),
                         reads=[pgk, f"wN{d}"], writes=[ik])
                    p.op("dve", lambda e, GI=PGI, pN=pN, wsel=wsel, INIT=INIT, WN=WN: e.scalar_tensor_tensor(out=INIT[:, 1:2], in0=GI[:, pN - 1:pN], scalar=WN[:, wsel, 0:1], in1=INIT[:, 3:4], op0=ALU.mult, op1=ALU.add),
                         reads=[pgik, f"wN{d}", ik], writes=[ik])
                p.op("dve", lambda e, A=A, N=N, TA=TA, MC=MC: e.tensor_tensor(out=TA[:, :N], in0=A[:, :N], in1=MC[:, :N], op=ALU.mult), reads=[ak, mk_], writes=[f"ta{b2}"])
                p.op("dve", lambda e, B=B, N=N, TB=TB, MS=MS: e.tensor_tensor(out=TB[:, :N], in0=B[:, :N], in1=MS[:, :N], op=ALU.mult), reads=[bk, mk_], writes=[f"tb{b2}"])
                p.op("dve", lambda e, N=N, TA=TA, TB=TB, MR=MR: e.tensor_tensor(out=MR[:, :N], in0=TA[:, :N], in1=TB[:, :N], op=ALU.add),
                     reads=[f"ta{b2}", f"tb{b2}"], writes=[f"mre{b2}"])
                p.op("dve", lambda e, B=B, N=N, TC=TC, MC=MC: e.tensor_tensor(out=TC[:, :N], in0=B[:, :N], in1=MC[:, :N], op=ALU.mult), reads=[bk, mk_], writes=[f"tc{b2}"])
                p.op("dve", lambda e, A=A, N=N, TD=TD, MS=MS: e.tensor_tensor(out=TD[:, :N], in0=A[:, :N], in1=MS[:, :N], op=ALU.mult), reads=[ak, mk_], writes=[f"td{b2}"])
                p.op("dve", lambda e, N=N, TC=TC, TD=TD, MI=MI: e.tensor_tensor(out=MI[:, :N], in0=TC[:, :N], in1=TD[:, :N], op=ALU.subtract),
                     reads=[f"tc{b2}", f"td{b2}"], writes=[f"mim{b2}"])
                p.op("dve", lambda e, N=N, GR=GR, MR=MR, RT=RT, INIT=INIT: e.tensor_tensor_scan(out=GR[:, :N], data0=RT[:, :N], data1=MR[:, :N], initial=INIT[:, 0:1], op0=ALU.mult, op1=ALU.add),
                     reads=[f"rt{d}", f"mre{b2}", ik], writes=[gk])
                p.op("dve", lambda e, N=N, GI=GI, MI=MI, RT=RT, INIT=INIT: e.tensor_tensor_scan(out=GI[:, :N], data0=RT[:, :N], data1=MI[:, :N], initial=INIT[:, 1:2], op0=ALU.mult, op1=ALU.add),
                     reads=[f"rt{d}", f"mim{b2}", ik], writes=[gik])
                Q1, Q2, Q3, Q4 = q1[b2], q2[b2], q3[b2], q4[b2]
                for (Q, G, Et, qn, gkk) in ((Q1, GR, EC, "q1", gk), (Q2, GI, ES, "q2", gik), (Q3, GR, ES, "q3", gk), (Q4, GI, EC, "q4", gik)):
                    p.op("pool", lambda e, N=N, Q=Q, G=G, Et=Et: e.tensor_tensor(out=Q[:, :N].bitcast(F32R), in0=G[:, :N], in1=Et[:, :N], op=ALU.mult),
                         reads=[gkk, ek], writes=[f"{qn}{b2}"])
                def cstage(Yp=Yp, yk=yk, N=N, gp=gp, b2=b2, sl=sl, Qs=(Q1, Q2, Q3, Q4), YS=ysb[b2]):
                    for k, (Q, ci, qn) in enumerate(((Qs[0], 0, "q1"), (Qs[1], 1, "q2"), (Qs[2], 2, "q3"), (Qs[3], 2, "q4"))):
                        p.op("pe", lambda e, Q=Q, ci=ci, k=k: e.matmul(Yp[:32, :N], lhsT=ctr[:, ci, gp, :].bitcast(F32R), rhs=Q[:, :N].bitcast(F32R),
                                                                         start=(k == 0), stop=(k == 3)), reads=["ctr", f"{qn}{b2}"], writes=[yk])
                    p.op("act", lambda e: e.copy(out=YS[:32, :N], in_=Yp[:32, :N]), reads=[yk], writes=[f"ysb{b2}"])
                    p.op("pool", lambda e: e.tensor_tensor(out=ysum[:, sl], in0=YS[:32, :N], in1=ysum[:, sl], op=ALU.add),
                         reads=[f"ysb{b2}", "ysum"], writes=["ysum"])
                pending.append(cstage)
                prevs[d] = N
        for fn in pending:
            fn()
        pending = []
        p.op("dve", lambda e, gp=gp: e.scalar_tensor_tensor(out=ysum[:, :], in0=u_sb[:, :], scalar=dsk[:, gp:gp + 1], in1=ysum[:, :], op0=ALU.mult, op1=ALU.add),
             reads=["u_sb", "ysum", "dsk"], writes=["ysum"])
        p.dma("pool", YPRE[gp * 32:(gp + 1) * 32, :], ysum[:, :], reads=["ysum"], writes=["YPRE"])


def ph_s5_glu(p, YPRE, T, GW, GB, Y):
    N = 512
    K0 = 0.7978845608028654
    gw = load_weight_bf16(p, GW, 512, 512, "gw", chunk=512)
    gb = p.sb([128, 4], F32, name="gb")
    load_cols(p, gb[:, :], GB, "gb")
    ys = [p.sb([128, 4, N], F32, name=f"yp{i}") for i in range(2)]
    x2 = p.sb([128, 4, N], F32, name="x2")
    g = p.sb([128, 4, N], F32, name="gg")
    gbf = p.sb([128, 4, N], BF16, name="gbf")
    sg = p.sb([128, N], F32, name="sg")
    o = [p.sb([128, N], BF16, name=f"og{i}") for i in range(2)]
    pss = [p.ps([128, 512], name=f"psg{i}") for i in range(2)]
    oi = 0
    for ti, t0 in enumerate(range(0, T, N)):
        n = min(N, T - t0)
        y, yk = ys[ti % 2], f"yp{ti%2}"
        p.dma("sp", y[:, :, :n], fm(YPRE, t0, n), reads=["YPRE"], writes=[yk])
        p.op("pool", lambda e, y=y, n=n: e.tensor_tensor(out=x2[:, :, :n], in0=y[:, :, :n], in1=y[:, :, :n], op=ALU.mult), reads=[yk], writes=["x2"])
        p.op("dve", lambda e, n=n: e.tensor_scalar(out=x2[:, :, :n], in0=x2[:, :, :n], scalar1=0.044715, scalar2=1.0, op0=ALU.mult, op1=ALU.add), reads=["x2"], writes=["x2"])
        p.op("pool", lambda e, y=y, n=n: e.tensor_tensor(out=x2[:, :, :n], in0=x2[:, :, :n], in1=y[:, :, :n], op=ALU.mult), reads=[yk, "x2"], writes=["x2"])
        p.op("act", lambda e, n=n: e.activation(out=x2[:, :, :n], in_=x2[:, :, :n], func=AF.Sigmoid, scale=2.0 * K0), reads=["x2"], writes=["x2"])
        p.op("dve", lambda e, y=y, n=n: e.tensor_tensor(out=g[:, :, :n], in0=x2[:, :, :n], in1=y[:, :, :n], op=ALU.mult), reads=[yk, "x2"], writes=["gg"])
        p.op("act", lambda e, n=n: e.copy(out=gbf[:, :, :n], in_=g[:, :, :n]), reads=["gg"], writes=["gbf"])
        for j in range(4):
            ps, pk, oo, ok = pss[oi % 2], f"psg{oi%2}", o[oi % 2], f"og{oi%2}"
            for c in range(4):
                p.op("pe", lambda e, ps=ps, c=c, j=j, n=n: e.matmul(ps[:, :n], lhsT=gw[:, c, j * 128:(j + 1) * 128], rhs=gbf[:, c, :n], start=(c == 0), stop=(c == 3)),
                     reads=["gw", "gbf"], writes=[pk])
            p.op("act", lambda e, ps=ps, j=j, n=n: e.activation(out=sg[:, :n], in_=ps[:, :n], func=AF.Sigmoid, bias=gb[:, j:j + 1], scale=1.0),
                 reads=[pk, "gb"], writes=["sg"])
            p.op("dve", lambda e, oo=oo, j=j, n=n: e.tensor_tensor(out=oo[:, :n], in0=sg[:, :n], in1=g[:, j, :n], op=ALU.mult), reads=["sg", "gg"], writes=[ok])
            p.dma("pool", Y[j * 128:(j + 1) * 128, t0:t0 + n], oo[:, :n], reads=[ok], writes=["Y"])
            oi += 1


NH = 16


def ph_conv_silu(p, SRC, row0, nrows, T, segs, CW, CB, DST, silu=True):
    nch = nrows // 128
    w = p.sb([128, 3, nch], F32, name="cw")
    for k in range(3):
        load_cols(p, w[:, k, :], CW[k], "cw")
    b = p.sb([128, nch], F32, name="cb")
    load_cols(p, b[:, :], CB, "cw")
    xs = [p.sb([128, T], F32, name=f"cx{i}") for i in range(2)]
    acc = [p.sb([128, T], F32, name=f"ca{i}") for i in range(2)]
    for c in range(nch):
        x, xk, a, ak = xs[c % 2], f"cx{c%2}", acc[c % 2], f"ca{c%2}"
        p.dma("sp", x[:, :], SRC[row0 + c * 128:row0 + (c + 1) * 128, :], reads=["SRC"], writes=[xk])
        p.op("dve", lambda e, x=x, a=a, c=c: e.tensor_scalar(out=a[:, :], in0=x[:, :], scalar1=w[:, 1, c:c + 1], scalar2=b[:, c:c + 1], op0=ALU.mult, op1=ALU.add),
             reads=[xk, "cw"], writes=[ak])
        for (s0, s1) in segs:
            p.op("dve", lambda e, x=x, a=a, c=c, s0=s0, s1=s1: e.scalar_tensor_tensor(out=a[:, s0 + 1:s1], in0=x[:, s0:s1 - 1], scalar=w[:, 0, c:c + 1], in1=a[:, s0 + 1:s1],
                                                                                      op0=ALU.mult, op1=ALU.add), reads=[xk, "cw", ak], writes=[ak])
            p.op("dve", lambda e, x=x, a=a, c=c, s0=s0, s1=s1: e.scalar_tensor_tensor(out=a[:, s0:s1 - 1], in0=x[:, s0 + 1:s1], scalar=w[:, 2, c:c + 1], in1=a[:, s0:s1 - 1],
                                                                                      op0=ALU.mult, op1=ALU.add), reads=[xk, "cw", ak], writes=[ak])
        if silu:
            p.op("act", lambda e, a=a: e.activation(out=a[:, :], in_=a[:, :], func=AF.Silu), reads=[ak], writes=[ak])
        p.dma("pool", DST[c * 128:(c + 1) * 128, :], a[:, :], reads=[ak], writes=["DST"])


def ph_ssd_prep(p, PROJ, T, DTB, ALOG, SC, CF2, ET):
    TB = 1408
    NCB = TB // 128
    dtb = p.sb([32, 1], F32, name="dtb")
    p.dma("sp", dtb[:, :], DTB.rearrange("(p o) -> p o", o=1), writes=["dtb"], allow_slow_non_contiguous=True)
    al = p.sb([32, 1], F32, name="al")
    p.dma("sp", al[:, :], ALOG.rearrange("(p o) -> p o", o=1), writes=["al"], allow_slow_non_contiguous=True)
    p.op("act", lambda e: e.activation(out=al[:, :], in_=al[:, :], func=AF.Exp), reads=["al"], writes=["al"])
    p.op("dve", lambda e: e.tensor_scalar(out=al[:, :], in0=al[:, :], scalar1=-1.0, scalar2=None, op0=ALU.mult), reads=["al"], writes=["al"])
    names = ["dt", "a", "pin", "sin", "cx", "ncx", "npin", "e1", "e2", "e3", "e4", "m0", "m1"]
    t = {n: p.sb([32, TB], F32, name="pp_" + n) for n in names}
    et = p.sb([32, NCB], F32, name="et")

    def A(n):
        return t[n][:, :]

    p.op("pool", lambda e: e.memset(A("m0"), 1.0), writes=["m0"])
    p.op("pool", lambda e: e.memset(A("m1"), 1.0), writes=["m1"])
    m0v = A("m0").rearrange("p (c q) -> p c q", q=128)
    m1v = A("m1").rearrange("p (c q) -> p c q", q=128)
    p.op("pool", lambda e: e.memset(m0v[:, :, 0:1], 0.0), reads=["m0"], writes=["m0"])
    p.op("pool", lambda e: e.memset(m1v[:, :, 127:128], 0.0), reads=["m1"], writes=["m1"])
    for b0 in range(0, T, TB):
        ts = slice(b0, b0 + TB)
        p.dma("sp", A("dt"), PROJ[3584:3616, ts], reads=["PROJ"], writes=["dt"])
        p.op("act", lambda e: e.activation(out=A("dt"), in_=A("dt"), func=AF.Exp, bias=dtb[:, 0:1], scale=1.0), reads=["dt", "dtb"], writes=["dt"])
        p.op("act", lambda e: e.activation(out=A("dt"), in_=A("dt"), func=AF.Ln, bias=1.0, scale=1.0), reads=["dt"], writes=["dt"])
        p.op("dve", lambda e: e.tensor_scalar(out=A("a"), in0=A("dt"), scalar1=al[:, 0:1], scalar2=None, op0=ALU.mult), reads=["dt", "al"], writes=["a"])
        p.op("dve", lambda e: e.tensor_tensor_scan(out=A("pin"), data0=A("m0"), data1=A("a"), initial=0.0, op0=ALU.mult, op1=ALU.add),
             reads=["a", "m0"], writes=["pin"])
        p.op("dve", lambda e: e.tensor_tensor_scan(out=t["sin"][:, ::-1], data0=t["m1"][:, ::-1], data1=t["a"][:, ::-1], initial=0.0, op0=ALU.mult, op1=ALU.add),
             reads=["a", "m1"], writes=["sin"])
        p.dma("pool", SC[0:32, ts], A("dt"), reads=["dt"], writes=["SC"])
        p.op("dve", lambda e: e.tensor_tensor(out=A("e1"), in0=A("sin"), in1=A("a"), op=ALU.subtract), reads=["sin", "a"], writes=["e1"])
        p.op("act", lambda e: e.activation(out=A("e1"), in_=A("e1"), func=AF.Exp), reads=["e1"], writes=["e1"])
        p.op("dve", lambda e: e.tensor_tensor(out=A("e1"), in0=A("e1"), in1=A("dt"), op=ALU.mult), reads=["e1", "dt"], writes=["e1"])
        p.dma("pool", SC[32:48, ts], t["e1"][0:16, :], reads=["e1"], writes=["SC"])
        p.op("dve", lambda e: e.tensor_tensor(out=A("cx"), in0=A("pin"), in1=A("a"), op=ALU.subtract), reads=["pin", "a"], writes=["cx"])
        p.op("act", lambda e: e.activation(out=A("e2"), in_=A("cx"), func=AF.Exp), reads=["cx"], writes=["e2"])
        p.op("dve", lambda e: e.tensor_tensor(out=A("e2"), in0=A("e2"), in1=A("dt"), op=ALU.mult), reads=["e2", "dt"], writes=["e2"])
        p.dma("pool", SC[48:64, ts], t["e2"][16:32, :], reads=["e2"], writes=["SC"])
        p.op("act", lambda e: e.activation(out=A("e3"), in_=A("pin"), func=AF.Exp), reads=["pin"], writes=["e3"])
        p.dma("pool", SC[64:80, ts], t["e3"][0:16, :], reads=["e3"], writes=["SC"])
        p.op("act", lambda e: e.activation(out=A("e4"), in_=A("sin"), func=AF.Exp), reads=["sin"], writes=["e4"])
        p.dma("pool", SC[80:96, ts], t["e4"][16:32, :], reads=["e4"], writes=["SC"])
        p.op("dve", lambda e: e.tensor_scalar(out=A("ncx"), in0=A("cx"), scalar1=-1.0, scalar2=None, op0=ALU.mult), reads=["cx"], writes=["ncx"])
        p.op("dve", lambda e: e.tensor_scalar(out=A("npin"), in0=A("pin"), scalar1=-1.0, scalar2=None, op0=ALU.mult), reads=["pin"], writes=["npin"])
        p.dma("pool", CF2[0, 0:16, ts], t["pin"][0:16, :], reads=["pin"], writes=["CF2"])
        p.dma("pool", CF2[0, 16:32, ts], t["ncx"][16:32, :], reads=["ncx"], writes=["CF2"])
        p.dma("pool", CF2[1, 0:16, ts], t["npin"][0:16, :], reads=["npin"], writes=["CF2"])
        p.dma("pool", CF2[1, 16:32, ts], t["cx"][16:32, :], reads=["cx"], writes=["CF2"])
        pv = A("pin").rearrange("p (c q) -> p c q", q=128)
        p.op("act", lambda e, pv=pv: e.activation(out=et[:, :], in_=pv[:, :, 127], func=AF.Exp), reads=["pin"], writes=["et"])
        c0 = b0 // 128
        p.dma("pool", ET[c0:c0 + NCB, :].rearrange("c j -> j c"), et[:, :], reads=["et"], writes=["ET"], allow_slow_non_contiguous=True)


def ph_ssd_pass(p, d, XBC, T, NCTX, SC, CF2, ET, DSKIP_BC, IDENT, MASKS, YT, YM):
    NCH = T // 128
    nctx = NCTX // 128
    order = list(range(NCH)) if d == 0 else list(range(nctx - 1, -1, -1)) + list(range(NCH - 1, nctx - 1, -1))
    ident = p.sb([128, 128], F32, name="ident")
    p.dma("sp", ident[:, :], IDENT, writes=["ident"])
    identb = p.sb([128, 128], BF16, name="identb")
    p.op("dve", lambda e: e.tensor_copy(out=identb[:, :], in_=ident[:, :]), reads=["ident"], writes=["identb"])
    maskf = p.sb([128, 512], F32, name="maskf")
    p.dma("sp", maskf[:, :], MASKS[d], writes=["maskf"])
    maskb = p.sb([128, 512], BF16, name="maskb")
    p.op("dve", lambda e: e.tensor_copy(out=maskb[:, :], in_=maskf[:, :]), reads=["maskf"], writes=["maskb"])
    etb = p.sb([128, NCH, 32], F32, name="etb")
    p.dma("sp", etb[:, :, :], ET.partition_broadcast(128), reads=["ET"], writes=["etb"])
    dsk = p.sb([128, 16], F32, name="dskb")
    p.dma("sp", dsk[:, :], DSKIP_BC, writes=["dskb"])
    H = p.sb([128, 1024], F32, name="H")
    Hb = p.sb([128, 1024], BF16, name="Hb")
    p.op("pool", lambda e: e.memset(H[:, :], 0.0), writes=["H"])
    p.op("pool", lambda e: e.memset(Hb[:, :], 0.0), writes=["Hb"])
    xin = [p.sb([128, 16, 128], F32, name=f"xin{i}") for i in range(2)]
    sct = [p.sb([96, 128], F32, name=f"sct{i}") for i in range(2)]
    Lc = [p.sb([2, 32, 128], F32, name=f"Lc{i}") for i in range(2)]
    Rc = [p.sb([2, 32, 128], F32, name=f"Rc{i}") for i in range(2)]
    for i in range(2):
        p.op("pool", lambda e, i=i: e.memset(Lc[i][:, :, :], 1.0), writes=[f"Lc{i}"])
        p.op("pool", lambda e, i=i: e.memset(Rc[i][:, :, :], 1.0), writes=[f"Rc{i}"])
    xtok = p.sb([128, 1024], F32, name="xtok")
    btok = p.sb([128, 512], BF16, name="btok")
    bcT = p.sb([128, 8, 128], BF16, name="bcT")
    sc = p.sb([128, 96], F32, name="sc")
    xdt = p.sb([128, 1024], BF16, name="xdt")
    xw = p.sb([128, 1024], BF16, name="xw")
    E = p.sb([128, 512], BF16, name="E")
    GT = p.sb([128, 4, 128], BF16, name="GT")
    ysb = p.sb([128, 256], F32, name="ysb")
    yacc = p.sb([128, 1024], F32, name="yacc")
    yprev = p.sb([128, 1024], F32, name="yprev")
    yfm = [p.sb([128, 8, 128], F32, name=f"yfm{i}") for i in range(2)]
    psT = [p.ps([128, 512], name=f"psT{i}") for i in range(2)]
    psG = p.ps([128, 512], name="psG")
    psD = p.ps([128, 512], name="psD")
    psY = p.ps([128, 512], name="psY")
    psZ = p.ps([128, 512], name="psZ")
    psS = p.ps([128, 512], name="psS")
    ti = 0
    for it, c in enumerate(order):
        t0 = c * 128
        xi, xik = xin[it % 2], f"xin{it%2}"
        st, stk = sct[it % 2], f"sct{it%2}"
        L, Lk, R, Rk = Lc[it % 2], f"Lc{it%2}", Rc[it % 2], f"Rc{it%2}"
        p.dma("sp", xi[:, :, :], fm(XBC, t0, 128), reads=["XBC"], writes=[xik])
        p.dma("sp", st[:, :], SC[:, t0:t0 + 128], reads=["SC"], writes=[stk])
        p.dma("sp", R[0:1, :, :], CF2[0:1, :, t0:t0 + 128], reads=["CF2"], writes=[Rk])
        p.dma("sp", L[1:2, :, :], CF2[1:2, :, t0:t0 + 128], reads=["CF2"], writes=[Lk])
        if d == 1:
            p.dma("sp", yprev[:, :], YT[t0:t0 + 128, :], reads=["YT"], writes=["yprev"])
        for half in range(2):
            ps, pk = psT[ti % 2], f"psT{ti%2}"
            ti += 1
            for cc in range(4):
                p.op("pe", lambda e, ps=ps, cc=cc, half=half, xi=xi: e.transpose(out=ps[:, cc * 128:(cc + 1) * 128], in_=xi[:, half * 4 + cc, :], identity=ident[:, :]),
                     reads=[xik, "ident"], writes=[pk])
            p.op("act", lambda e, ps=ps, half=half: e.copy(out=xtok[:, half * 512:(half + 1) * 512], in_=ps[:, :]), reads=[pk], writes=["xtok"])
        ps, pk = psT[ti % 2], f"psT{ti%2}"
        ti += 1
        for cc in range(4):
            p.op("pe", lambda e, ps=ps, cc=cc, xi=xi: e.transpose(out=ps[:, cc * 128:(cc + 1) * 128], in_=xi[:, 8 + cc, :], identity=ident[:, :]),
                 reads=[xik, "ident"], writes=[pk])
        p.op("act", lambda e, ps=ps: e.copy(out=btok[:, :], in_=ps[:, :]), reads=[pk], writes=["btok"])
        ps, pk = psT[ti % 2], f"psT{ti%2}"
        ti += 1
        p.op("pe", lambda e, ps=ps, st=st: e.transpose(out=ps[:, 0:96], in_=st[:, :], identity=ident[:96, :96]), reads=[stk, "ident"], writes=[pk])
        p.op("dve", lambda e, ps=ps: e.tensor_copy(out=sc[:, :], in_=ps[:, 0:96]), reads=[pk], writes=["sc"])
        p.op("pool", lambda e, xi=xi: e.tensor_copy(out=bcT[:, :, :], in_=xi[:, 8:16, :]), reads=[xik], writes=["bcT"])
        x3 = xtok[:, :].rearrange("p (h q) -> p h q", q=64)
        p.op("dve", lambda e, x3=x3: e.tensor_tensor(out=xdt[:, :].rearrange("p (h q) -> p h q", q=64), in0=x3,
                                                     in1=sc[:, d * 16:d * 16 + 16].unsqueeze(2).to_broadcast([128, 16, 64]), op=ALU.mult),
             reads=["xtok", "sc"], writes=["xdt"])
        p.op("pool", lambda e, x3=x3: e.tensor_tensor(out=xw[:, :].rearrange("p (h q) -> p h q", q=64), in0=x3,
                                                      in1=sc[:, 32 + d * 16:32 + d * 16 + 16].unsqueeze(2).to_broadcast([128, 16, 64]), op=ALU.mult),
             reads=["xtok", "sc"], writes=["xw"])
        for g in range(4):
            p.op("pe", lambda e, g=g: e.matmul(psG[:, 0:128], lhsT=bcT[:, g, :], rhs=bcT[:, 4 + g, :], start=True, stop=True), reads=["bcT"], writes=["psG"])
            p.op("pe", lambda e: e.matmul(psD[:, :], lhsT=identb[:, :], rhs=maskb[:, :], start=True, stop=False), reads=["identb", "maskb"], writes=["psD"])
            for hh in range(4):
                j = d * 16 + g * 4 + hh
                p.op("pe", lambda e, hh=hh, j=j, L=L, R=R: e.matmul(psD[:, hh * 128:(hh + 1) * 128], lhsT=L[:, j, :], rhs=R[:, j, :], start=False, stop=(hh == 3)),
                     reads=[Lk, Rk], writes=["psD"])
            p.op("act", lambda e: e.activation(out=E[:, :], in_=psD[:, :], func=AF.Exp), reads=["psD"], writes=["E"])
            for hh in range(4):
                p.op("dve", lambda e, hh=hh: e.tensor_tensor(out=GT[:, hh, :], in0=psG[:, 0:128], in1=E[:, hh * 128:(hh + 1) * 128], op=ALU.mult),
                     reads=["psG", "E"], writes=["GT" + str(hh)])
            for hh in range(4):
                h = g * 4 + hh
                p.op("pe", lambda e, hh=hh, h=h: e.matmul(psY[:, hh * 64:(hh + 1) * 64], lhsT=GT[:, hh, :], rhs=xdt[:, h * 64:(h + 1) * 64], start=True, stop=True),
                     reads=["GT" + str(hh), "xdt"], writes=["psY"])
            p.op("pe", lambda e, g=g: e.matmul(psZ[:, 0:256], lhsT=bcT[:, 4 + g, :], rhs=Hb[:, g * 256:(g + 1) * 256], start=True, stop=True), reads=["bcT", "Hb"], writes=["psZ"])
            p.op("pe", lambda e, g=g: e.matmul(psS[:, 0:256], lhsT=btok[:, g * 128:(g + 1) * 128], rhs=xw[:, g * 256:(g + 1) * 256], start=True, stop=True),
                 reads=["btok", "xw"], writes=["psS"])
            p.op("act", lambda e: e.copy(out=ysb[:, :], in_=psY[:, 0:256]), reads=["psY"], writes=["ysb"])
            for hh in range(4):
                h = g * 4 + hh
                hs = slice(h * 64, (h + 1) * 64)
                rs = sc[:, 64 + d * 16 + h:64 + d * 16 + h + 1]
                p.op("dve", lambda e, hh=hh, hs=hs, rs=rs: e.scalar_tensor_tensor(out=yacc[:, hs], in0=psZ[:, hh * 64:(hh + 1) * 64], scalar=rs, in1=ysb[:, hh * 64:(hh + 1) * 64],
                                                                                  op0=ALU.mult, op1=ALU.add), reads=["psZ", "ysb", "sc"], writes=["yacc"])
                et = etb[:, c, d * 16 + h:d * 16 + h + 1]
                p.op("dve", lambda e, hh=hh, hs=hs, et=et: e.scalar_tensor_tensor(out=H[:, hs], in0=H[:, hs], scalar=et, in1=psS[:, hh * 64:(hh + 1) * 64], op0=ALU.mult, op1=ALU.add),
                     reads=["H", "psS", "etb"], writes=["H"])
            p.op("act", lambda e, g=g: e.copy(out=Hb[:, g * 256:(g + 1) * 256], in_=H[:, g * 256:(g + 1) * 256]), reads=["H"], writes=["Hb"])
        if d == 0:
            p.op("pool", lambda e, x3=x3: e.tensor_tensor(out=yprev[:, :].rearrange("p (h q) -> p h q", q=64), in0=x3,
                                                          in1=dsk[:, :].unsqueeze(2).to_broadcast([128, 16, 64]), op=ALU.mult),
                 reads=["xtok", "dskb"], writes=["yprev"])
            p.op("pool", lambda e: e.tensor_tensor(out=yacc[:, :], in0=yacc[:, :], in1=yprev[:, :], op=ALU.add), reads=["yacc", "yprev"], writes=["yacc"])
            p.dma("pool", YT[t0:t0 + 128, :], yacc[:, :], reads=["yacc"], writes=["YT"])
        else:
            p.op("pool", lambda e: e.tensor_tensor(out=yacc[:, :], in0=yacc[:, :], in1=yprev[:, :], op=ALU.add), reads=["yacc", "yprev"], writes=["yacc"])
            yf, yfk = yfm[it % 2], f"yfm{it%2}"
            for half in range(2):
                ps, pk = psT[ti % 2], f"psT{ti%2}"
                ti += 1
                for cc in range(4):
                    cch = half * 4 + cc
                    p.op("pe", lambda e, ps=ps, cc=cc, cch=cch: e.transpose(out=ps[:, cc * 128:(cc + 1) * 128], in_=yacc[:, cch * 128:(cch + 1) * 128], identity=ident[:, :]),
                         reads=["yacc", "ident"], writes=[pk])
                p.op("act", lambda e, ps=ps, half=half, yf=yf: e.copy(out=yf[:, half * 4:(half + 1) * 4, :], in_=ps[:, :].rearrange("p (c q) -> p c q", q=128)), reads=[pk], writes=[yfk])
            p.dma("pool", fm(YM, t0, 128), yf[:, :, :], reads=[yfk], writes=["YM"])


def ph_ssd_post(p, YM, PROJ, T, G, Y):
    N = 512
    nrm = Norm(p, N, 1, G, [(None, None)])
    ys = [p.sb([128, 8, N], F32, name=f"ym{i}") for i in range(2)]
    zs = [p.sb([128, 8, N], F32, name=f"zz{i}") for i in range(2)]
    ht = [p.sb([128, 8, N], BF16, name=f"hp{i}") for i in range(2)]
    for ti, t0 in enumerate(range(0, T, N)):
        n = min(N, T - t0)
        y, yk, z, zk, h, hk = ys[ti % 2], f"ym{ti%2}", zs[ti % 2], f"zz{ti%2}", ht[ti % 2], f"hp{ti%2}"
        p.dma("sp", y[:, :, :n], fm(YM, t0, n), reads=["YM"], writes=[yk])
        p.dma("sp", z[:, :, :n], fm(PROJ[512:1536, :], t0, n), reads=["PROJ"], writes=[zk])
        p.op("act", lambda e, z=z, n=n: e.activation(out=z[:, :, :n], in_=z[:, :, :n], func=AF.Silu), reads=[zk], writes=[zk])
        p.op("pool", lambda e, y=y, z=z, n=n: e.tensor_tensor(out=y[:, :, :n], in0=y[:, :, :n], in1=z[:, :, :n], op=ALU.mult), reads=[yk, zk], writes=[yk])
        nrm.apply(y, yk, n, 0, h, hk)
        p.dma("pool", fm(Y[512:1536, :], t0, n), h[:, :, :n], reads=[hk], writes=["Y"])


NFFT = 16384
S = 4


def hy_tables():
    a = np.arange(128, dtype=np.float64)
    ang = 2 * np.pi * np.outer(a, a) / 128.0
    C, Sn = np.cos(ang), np.sin(ang)
    angN = 2 * np.pi * np.outer(a, a) / NFFT
    tabs = np.concatenate([C, -Sn, C, Sn, -Sn, C, np.cos(angN), np.sin(angN), C / NFFT, -Sn / NFFT, -C, -C / NFFT], axis=1)
    return tabs.astype(np.float32)


class FFT:
    def __init__(self, p, TABS, NC=2):
        self.p = p
        F32R = mybir.dt.float32r
        self.tabs = p.sb([128, 1536], F32, name="tabs")
        p.dma("sp", self.tabs[:, :], TABS, writes=["tabs"])
        self.tabs_r = p.sb([128, 1536], F32, name="tabs_r")
        p.op("dve", lambda e: e.tensor_copy(out=self.tabs_r[:, :].bitcast(F32R), in_=self.tabs[:, :]), reads=["tabs"], writes=["tabs"])
        t = self.tabs_r[:, :].bitcast(F32R)
        self.F1 = t[:, 0:256]
        self.CS = t[:, 256:512]
        self.NSC = t[:, 512:768]
        self.C = t[:, 256:384]
        self.Sm = t[:, 384:512]
        self.NS = t[:, 512:640]
        self.TWC = self.tabs[:, 768:896]
        self.TWS = self.tabs[:, 896:1024]
        self.CN = t[:, 1024:1152]
        self.NSN = t[:, 1152:1280]
        self.NegC = t[:, 1280:1408]
        self.NegCN = t[:, 1408:1536]
        self.NC = NC
        self.psA = [p.ps([128, S, 256], name=f"psA{c}") for c in range(NC)]
        self.psXr = [p.ps([128, 512], name=f"psXr{c}") for c in range(NC)]
        self.psXi = [p.ps([128, 512], name=f"psXi{c}") for c in range(NC)]
        self.psy = self.psXr
        mk = lambda n: [p.sb([128, S, 128], F32, name=f"{n}{c}") for c in range(NC)]
        self.t1, self.t2, self.t3, self.t4 = mk("ft1"), mk("ft2"), mk("ft3"), mk("ft4")
        self.u1, self.u2, self.u3, self.u4 = mk("fu1"), mk("fu2"), mk("fu3"), mk("fu4")
        self.Br, self.Bi = mk("Br"), mk("Bi")
        self.Yr, self.Yi = mk("Yr"), mk("Yi")

    def cmul(self, c, ar, ai, ak, br, bi, bk, outr, outi, ok, conj_b=False):
        p = self.p
        F32R = mybir.dt.float32r
        t1, t2, t3, t4 = self.u1[c], self.u2[c], self.u3[c], self.u4[c]
        k1, k2, k3, k4 = f"fu1{c}", f"fu2{c}", f"fu3{c}", f"fu4{c}"
        p.op("dve", lambda e: e.tensor_tensor(out=t1[:, :, :], in0=ar, in1=br, op=ALU.mult), reads=ak + bk, writes=[k1])
        p.op("dve", lambda e: e.tensor_tensor(out=t2[:, :, :], in0=ai, in1=bi, op=ALU.mult), reads=ak + bk, writes=[k2])
        p.op("pool", lambda e: e.tensor_tensor(out=outr.bitcast(F32R), in0=t1[:, :, :], in1=t2[:, :, :], op=(ALU.add if conj_b else ALU.subtract)),
             reads=[k1, k2], writes=ok)
        p.op("dve", lambda e: e.tensor_tensor(out=t3[:, :, :], in0=ai, in1=br, op=ALU.mult), reads=ak + bk, writes=[k3])
        p.op("dve", lambda e: e.tensor_tensor(out=t4[:, :, :], in0=ar, in1=bi, op=ALU.mult), reads=ak + bk, writes=[k4])
        p.op("pool", lambda e: e.tensor_tensor(out=outi.bitcast(F32R), in0=t3[:, :, :], in1=t4[:, :, :], op=(ALU.subtract if conj_b else ALU.add)),
             reads=[k3, k4], writes=ok)

    def cmul_split(self, c, ar, ai, ak, br, bi, bk):
        p = self.p
        F32R = mybir.dt.float32r
        for (t, x, y, kk) in ((self.t1[c], ar, br, f"ft1{c}"), (self.t2[c], ai, bi, f"ft2{c}"), (self.t3[c], ai, br, f"ft3{c}"), (self.t4[c], ar, bi, f"ft4{c}")):
            p.op("dve", lambda e, t=t, x=x, y=y: e.tensor_tensor(out=t[:, :, :].bitcast(F32R), in0=x, in1=y, op=ALU.mult), reads=ak + bk, writes=[kk])

    def _tflat(self, c):
        F32R = mybir.dt.float32r
        return [t[c][:, :, :].rearrange("p s k -> p (s k)").bitcast(F32R) for t in (self.t1, self.t2, self.t3, self.t4)]

    def _tw(self):
        return (self.TWC.unsqueeze(1).to_broadcast([128, S, 128]), self.TWS.unsqueeze(1).to_broadcast([128, S, 128]))

    def _bflat(self, c):
        F32R = mybir.dt.float32r
        return (self.Br[c][:, :, :].rearrange("p s k -> p (s k)").bitcast(F32R), self.Bi[c][:, :, :].rearrange("p s k -> p (s k)").bitcast(F32R))

    def st_f1(self, c, x0, xk, K):
        p, A = self.p, self.psA[c]
        for s in range(S):
            p.op("pe", lambda e, s=s: e.matmul(A[:, s, :], lhsT=x0[:K, s, :].bitcast(mybir.dt.float32r), rhs=self.F1[:K, :], start=True, stop=True),
                 reads=[xk, "tabs"], writes=[f"psA{c}"])

    def st_tw1(self, c):
        A = self.psA[c]
        twc, tws = self._tw()
        self.cmul_split(c, A[:, :, 0:128], A[:, :, 128:256], [f"psA{c}"], twc, tws, ["tabs"])

    def st_f2(self, c):
        p = self.p
        T1, T2, T3, T4 = self._tflat(c)
        Xr, Xi = self.psXr[c], self.psXi[c]
        tk = [f"ft1{c}", f"ft2{c}", f"ft3{c}", f"ft4{c}"]
        for n, (w, t) in enumerate(((self.C, T1), (self.C, T2), (self.Sm, T3), (self.NS, T4))):
            p.op("pe", lambda e, w=w, t=t, n=n: e.matmul(Xr[:, :], lhsT=w, rhs=t, start=(n == 0), stop=(n == 3)), reads=["tabs"] + tk, writes=[f"psXr{c}"])
        for n, (w, t) in enumerate(((self.C, T3), (self.NegC, T4), (self.NS, T1), (self.NS, T2))):
            p.op("pe", lambda e, w=w, t=t, n=n: e.matmul(Xi[:, :], lhsT=w, rhs=t, start=(n == 0), stop=(n == 3)), reads=["tabs"] + tk, writes=[f"psXi{c}"])

    def st_filt(self, c, Hr, Hi, hk):
        Xr = self.psXr[c][:, :].rearrange("p (s k) -> p s k", k=128)
        Xi = self.psXi[c][:, :].rearrange("p (s k) -> p s k", k=128)
        self.cmul(c, Xr, Xi, [f"psXr{c}", f"psXi{c}"], Hr, Hi, [hk], self.Yr[c][:, :, :], self.Yi[c][:, :, :], [f"Y{c}"])

    def st_i1(self, c):
        p, A = self.p, self.psA[c]
        F32R = mybir.dt.float32r
        Yr, Yi = self.Yr[c], self.Yi[c]
        for s in range(S):
            p.op("pe", lambda e, s=s: e.matmul(A[:, s, :], lhsT=Yr[:, s, :].bitcast(F32R), rhs=self.CS, start=True, stop=False), reads=[f"Y{c}", "tabs"], writes=[f"psA{c}"])
            p.op("pe", lambda e, s=s: e.matmul(A[:, s, :], lhsT=Yi[:, s, :].bitcast(F32R), rhs=self.NSC, start=False, stop=True), reads=[f"Y{c}", "tabs"], writes=[f"psA{c}"])

    def st_tw2(self, c):
        A = self.psA[c]
        twc, tws = self._tw()
        self.cmul_split(c, A[:, :, 0:128], A[:, :, 128:256], [f"psA{c}"], twc, tws, ["tabs"])

    def st_i2(self, c, M):
        p = self.p
        T1, T2, T3, T4 = self._tflat(c)
        y = self.psy[c]
        tk = [f"ft1{c}", f"ft2{c}", f"ft3{c}", f"ft4{c}"]
        for n, (w, t) in enumerate(((self.CN, T1), (self.NegCN, T2), (self.NSN, T3), (self.NSN, T4))):
            p.op("pe", lambda e, w=w, t=t, n=n: e.matmul(y[:M, :], lhsT=w[:, :M], rhs=t, start=(n == 0), stop=(n == 3)), reads=["tabs"] + tk, writes=[f"psXr{c}"])


def ph_hy_filter_raw(p, l, FEATS, TL, W1, B1, FQ1, W2, B2, FQ2, W3, DELTAS, FRAW, RINV):
    NT = min(512, l)
    ntile = l // NT
    feats = p.sb([33, l], F32, name="feats")
    p.dma("sp", feats[:, :], FEATS, writes=["feats"])
    w1 = p.sb([33, 64], F32, name="fw1")
    p.dma("sp", w1[:, :], W1, writes=["fw1"])
    w2 = p.sb([64, 64], F32, name="fw2")
    p.dma("sp", w2[:, :], W2, writes=["fw2"])
    w3 = p.sb([64, 4096], F32, name="fw3")
    p.dma("sp", w3[:, :], W3, writes=["fw3"])
    cols = p.sb([64, 6], F32, name="fcols")
    for i, src in enumerate((B1, FQ1, B2, FQ2)):
        p.dma("sp", cols[:, i:i + 1], src.rearrange("(p o) -> p o", o=1), writes=["fcols"], allow_slow_non_contiguous=True)
    p.op("dve", lambda e: e.tensor_tensor(out=cols[:, 4:5], in0=cols[:, 0:1], in1=cols[:, 1:2], op=ALU.mult), reads=["fcols"], writes=["fcols"])
    p.op("dve", lambda e: e.tensor_tensor(out=cols[:, 5:6], in0=cols[:, 2:3], in1=cols[:, 3:4], op=ALU.mult), reads=["fcols"], writes=["fcols"])
    hid2 = p.sb([64, l], F32, name="hid2")
    arg = p.sb([64, NT], F32, name="farg")
    arg2 = p.sb([64, NT], F32, name="farg2")
    hid1 = p.sb([64, NT], F32, name="hid1")
    tm1 = sin_tmps(p, [64, NT], "s1")
    tm2 = sin_tmps(p, [64, NT], "s2")
    ps1 = p.ps([128, 512], name="psf1")
    for ti in range(ntile):
        ts = slice(ti * NT, (ti + 1) * NT)
        p.op("pe", lambda e, ts=ts: e.matmul(ps1[:64, :NT], lhsT=w1[:, :], rhs=feats[:, ts], start=True, stop=True), reads=["fw1", "feats"], writes=["psf1"])
        p.op("dve", lambda e: e.tensor_scalar(out=arg[:, :], in0=ps1[:64, :NT], scalar1=cols[:, 1:2], scalar2=cols[:, 4:5], op0=ALU.mult, op1=ALU.add),
             reads=["psf1", "fcols"], writes=["s1" + "x"])
        sin_rr(p, hid1[:, :], arg[:, :], [64, NT], "s1", tmps=tm1)
        p.op("pe", lambda e: e.matmul(ps1[:64, :NT], lhsT=w2[:, :], rhs=hid1[:, :], start=True, stop=True), reads=["fw2", "s1o"], writes=["psf1"])
        p.op("dve", lambda e: e.tensor_scalar(out=arg2[:, :], in0=ps1[:64, :NT], scalar1=cols[:, 3:4], scalar2=cols[:, 5:6], op0=ALU.mult, op1=ALU.add),
             reads=["psf1", "fcols"], writes=["s2" + "x"])
        sin_rr(p, hid2[:, ts], arg2[:, :], [64, NT], "s2", tmps=tm2)
    tl = p.sb([128, l], F32, name="tl")
    p.dma("sp", tl[:, :], TL.partition_broadcast(128), writes=["tl"])
    dl = p.sb([128, 8], F32, name="ndelta")
    load_cols(p, dl[:, :], DELTAS, "ndelta")
    p.op("dve", lambda e: e.tensor_scalar(out=dl[:, :], in0=dl[:, :], scalar1=-1.0, scalar2=None, op0=ALU.mult), reads=["ndelta"], writes=["ndelta"])
    dec = p.sb([128, l], F32, name="dec")
    sums = p.sb([128, 32, ntile], F32, name="fsums")
    junk = p.sb([128, NT], F32, name="fjunk")
    outs = [p.sb([128, NT], F32, name=f"fo{i}") for i in range(3)]
    pss = [p.ps([128, 512], name=f"psw{i}") for i in range(3)]
    oi = 0
    for cc in range(8):
        p.op("act", lambda e, cc=cc: e.activation(out=dec[:, :], in_=tl[:, :], func=AF.Exp, scale=dl[:, cc:cc + 1]), reads=["tl", "ndelta"], writes=["dec"])
        for od in range(4):
            m = od * 8 + cc
            for ti in range(ntile):
                ts = slice(ti * NT, (ti + 1) * NT)
                ps, pk, o, ok = pss[oi % 3], f"psw{oi%3}", outs[oi % 3], f"fo{oi%3}"
                p.op("pe", lambda e, ps=ps, m=m, ts=ts: e.matmul(ps[:, :NT], lhsT=w3[:, m * 128:(m + 1) * 128], rhs=hid2[:, ts], start=True, stop=True),
                     reads=["fw3", "s2o"], writes=[pk])
                p.op("dve", lambda e, ps=ps, o=o, ts=ts: e.tensor_tensor(out=o[:, :], in0=ps[:, :NT], in1=dec[:, ts], op=ALU.mult), reads=[pk, "dec"], writes=[ok])
                lo = 1 if (od % 2 == 1 and ti == 0) else 0
                p.op("act", lambda e, o=o, m=m, ti=ti, lo=lo: e.activation(out=junk[:, lo:], in_=o[:, lo:], func=AF.Abs, accum_out=sums[:, m, ti:ti + 1]),
                     reads=[ok], writes=["fsums", "fjunk"])
                p.dma("pool", FRAW[m * 128:(m + 1) * 128, ts], o[:, :], reads=[ok], writes=["FRAW"])
                oi += 1
    tot = p.sb([128, 32], F32, name="ftot")
    p.op("dve", lambda e: e.tensor_reduce(out=tot[:, :], in_=sums[:, :, :], axis=AX.X, op=ALU.add), reads=["fsums"], writes=["ftot"])
    rinv = p.sb([128, 2, 8], F32, name="rinv")
    t4 = tot[:, :].rearrange("p (o d c) -> p o d c", o=2, d=2)
    for o_ in range(2):
        p.op("dve", lambda e, o_=o_: e.tensor_tensor(out=rinv[:, o_, :], in0=t4[:, o_, 0, :], in1=t4[:, o_, 1, :], op=ALU.add), reads=["ftot"], writes=["rinv"])
    p.op("dve", lambda e: e.tensor_scalar(out=rinv[:, :, :], in0=rinv[:, :, :], scalar1=1e-6, scalar2=None, op0=ALU.add), reads=["rinv"], writes=["rinv"])
    p.op("dve", lambda e: e.reciprocal(out=rinv[:, :, :], in_=rinv[:, :, :]), reads=["rinv"], writes=["rinv"])
    p.dma("pool", RINV.rearrange("o (c q) -> q o c", q=128), rinv[:, :, :], reads=["rinv"], writes=["RINV"], allow_slow_non_contiguous=True)


def ph_hy_filter_asm(p, l, FRAW, RINV, FILT):
    rinv = p.sb([128, 2, 8], F32, name="rinv2")
    p.dma("sp", rinv[:, :, :], RINV.rearrange("o (c q) -> q o c", q=128), reads=["RINV"], writes=["rinv2"], allow_slow_non_contiguous=True)
    zero = p.sb([128, 4096], F32, name="fzero")
    p.op("pool", lambda e: e.memset(zero[:, :], 0.0), writes=["fzero"])
    f = [p.sb([128, l], F32, name=f"ff{i}") for i in range(2)]
    b = [p.sb([128, l], F32, name="fb0")] * 2
    r = [p.sb([128, l], F32, name="fr0")] * 2
    i = 0
    for o_ in range(2):
        for cc in range(8):
            ff, fk, bb, bk, rr, rk = f[i % 2], f"ff{i%2}", b[0], "fb0", r[0], "fr0"
            rows = slice((o_ * 8 + cc) * 128, (o_ * 8 + cc + 1) * 128)
            mf, mb = (o_ * 2 + 0) * 8 + cc, (o_ * 2 + 1) * 8 + cc
            p.dma("sp", ff[:, :], FRAW[mf * 128:(mf + 1) * 128, :], reads=["FRAW"], writes=[fk])
            p.dma("sp", bb[:, :], FRAW[mb * 128:(mb + 1) * 128, :], reads=["FRAW"], writes=[bk])
            sc = rinv[:, o_, cc:cc + 1]
            p.op("act", lambda e, ff=ff, sc=sc: e.activation(out=ff[:, :], in_=ff[:, :], func=AF.Identity, scale=sc), reads=[fk, "rinv2"], writes=[fk])
            p.op("dve", lambda e, bb=bb, rr=rr, sc=sc: e.tensor_scalar(out=rr[:, :], in0=bb[:, ::-1], scalar1=sc, scalar2=None, op0=ALU.mult), reads=[bk, "rinv2"], writes=[rk])
            p.dma("pool", FILT[rows, 0:l], ff[:, :], reads=[fk], writes=["FILT"])
            p.dma("pool", FILT[rows, NFFT - l + 1:NFFT], rr[:, 0:l - 1], reads=[rk], writes=["FILT"])
            for z0 in range(l, NFFT - l + 1, 4096):
                zn = min(4096, NFFT - l + 1 - z0)
                p.dma("pool", FILT[rows, z0:z0 + zn], zero[:, :zn], reads=["fzero"], writes=["FILT"], allow_slow_non_contiguous=True)
            i += 1


def ph_hy_filter_fft(p, FILT, TABS, HF, groups=None):
    NC = 2
    fft = FFT(p, TABS, NC)
    xs = [p.sb([128, S, 128], F32, name=f"fx{i}") for i in range(2 * NC)]
    hr = [p.sb([128, S, 128], F32, name=f"hro{i}") for i in range(2 * NC)]
    hi = [p.sb([128, S, 128], F32, name=f"hio{i}") for i in range(2 * NC)]
    gl = list(groups if groups is not None else range(2048 // S))
    for b0 in range(0, len(gl), NC):
        batch = gl[b0:b0 + NC]
        ctxs = []
        for c, gi in enumerate(batch):
            bi_ = ((b0 // NC) % 2) * NC + c
            x0, xk = xs[bi_], f"fx{bi_}"
            sig = slice(gi * S, (gi + 1) * S)
            p.dma("sp", x0[:, :, :], FILT[sig, :].rearrange("s (a b) -> a s b", b=128), reads=["FILT"], writes=[xk])
            p.op("act", lambda e, x0=x0: e.copy(out=x0[:, :, :].bitcast(mybir.dt.float32r), in_=x0[:, :, :]), reads=[xk], writes=[xk])
            ctxs.append((c, x0, xk, sig, bi_))
        for (c, x0, xk, sig, bi_) in ctxs:
            fft.st_f1(c, x0, xk, 128)
        for (c, x0, xk, sig, bi_) in ctxs:
            fft.st_tw1(c)
        for (c, x0, xk, sig, bi_) in ctxs:
            fft.st_f2(c)
        for (c, x0, xk, sig, bi_) in ctxs:
            a_, ak, b_, bk = hr[bi_], f"hro{bi_}", hi[bi_], f"hio{bi_}"
            p.op("act", lambda e, a_=a_, c=c: e.copy(out=a_[:, :, :].rearrange("p s k -> p (s k)"), in_=fft.psXr[c][:, :]), reads=[f"psXr{c}"], writes=[ak])
            p.op("act", lambda e, b_=b_, c=c: e.copy(out=b_[:, :, :].rearrange("p s k -> p (s k)"), in_=fft.psXi[c][:, :]), reads=[f"psXi{c}"], writes=[bk])
            p.dma("pool", HF[0, sig].rearrange("s a b -> a s b"), a_[:, :, :], reads=[ak], writes=["HF"])
            p.dma("pool", HF[1, sig].rearrange("s a b -> a s b"), b_[:, :, :], reads=[bk], writes=["HF"])


def ph_hy_conv(p, ZC, t_off, Lsig, HF, HYD, TABS, Y, groups=None):
    K = Lsig // 128
    NC = 2
    F32R = mybir.dt.float32r
    fft = FFT(p, TABS, NC)
    dsk = p.sb([128, 2, 1024], F32, name="hyd")
    p.dma("sp", dsk[:, :, :], HYD.partition_broadcast(128), writes=["hyd"])
    mk = lambda n, dt=F32: [p.sb([64, S, 128], dt, name=f"{n}{i}") for i in range(NC)]
    vs, x1s, x2s, vrs, y1s, tmps = mk("v"), mk("xa"), mk("xb"), mk("vr"), mk("y1"), mk("ctmp")
    yos = mk("yo", BF16)
    for i in range(NC):
        for tl_, nm in ((vrs[i], f"vr{i}"), (y1s[i], f"y1{i}")):
            p.op("pool", lambda e, tl_=tl_: e.memset(tl_[:, :, :], 0.0), writes=[nm])
            p.op("act", lambda e, tl_=tl_: e.copy(out=tl_[:, :, :].bitcast(F32R), in_=tl_[:, :, :]), reads=[nm], writes=[nm])
    H = [[[p.sb([128, S, 128], F32, name=f"H{c}{o}{ri}") for ri in range(2)] for o in range(2)] for c in range(NC)]

    def tb(ap2d):
        return ap2d.rearrange("s (a b) -> a s b", b=128)

    gl = list(groups if groups is not None else range(1024 // S))
    for b0 in range(0, len(gl), NC):
        batch = list(enumerate(gl[b0:b0 + NC]))
        for c, gi in batch:
            c0 = gi * S
            p.dma("sp", vs[c][:K, :, :], tb(ZC[c0:c0 + S, t_off:t_off + Lsig]), reads=["ZC"], writes=[f"v{c}"])
            p.dma("sp", x1s[c][:K, :, :], tb(ZC[1024 + c0:1024 + c0 + S, t_off:t_off + Lsig]), reads=["ZC"], writes=[f"xa{c}"])
            p.dma("sp", x2s[c][:K, :, :], tb(ZC[2048 + c0:2048 + c0 + S, t_off:t_off + Lsig]), reads=["ZC"], writes=[f"xb{c}"])
            for o in range(2):
                for ri in range(2):
                    p.dma("sp", H[c][o][ri][:, :, :], HF[ri, o * 1024 + c0:o * 1024 + c0 + S].rearrange("s a b -> a s b"), reads=["HF"], writes=[f"H{c}{o}"])
            p.op("act", lambda e, c=c: e.copy(out=vrs[c][:K, :, :].bitcast(F32R), in_=vs[c][:K, :, :]), reads=[f"v{c}"], writes=[f"vr{c}"])
        for o in range(2):
            for c, gi in batch:
                src, sk = (vrs[c], f"vr{c}") if o == 0 else (y1s[c], f"y1{c}")
                fft.st_f1(c, src, sk, 64)
            for c, gi in batch:
                fft.st_tw1(c)
            for c, gi in batch:
                fft.st_f2(c)
            for c, gi in batch:
                fft.st_filt(c, H[c][o][0][:, :, :], H[c][o][1][:, :, :], f"H{c}{o}")
            for c, gi in batch:
                fft.st_i1(c)
            for c, gi in batch:
                fft.st_tw2(c)
            for c, gi in batch:
                fft.st_i2(c, 64)
            for c, gi in batch:
                c0 = gi * S
                src, sk = (vs[c], f"v{c}") if o == 0 else (y1s[c], f"y1{c}")
                gate, gk = (x1s[c], f"xa{c}") if o == 0 else (x2s[c], f"xb{c}")
                tmp, tk = tmps[c], f"ctmp{c}"
                dbc = dsk[:64, o, c0:c0 + S].unsqueeze(2).to_broadcast([64, S, 128])
                p.op("dve", lambda e, src=src, dbc=dbc, tmp=tmp: e.tensor_tensor(out=tmp[:K, :, :], in0=src[:K, :, :], in1=dbc[:K], op=ALU.mult), reads=[sk, "hyd"], writes=[tk])
                psy3 = fft.psy[c][:, :].rearrange("p (s k) -> p s k", k=128)
                p.op("dve", lambda e, psy3=psy3, tmp=tmp: e.tensor_tensor(out=tmp[:K, :, :], in0=psy3[:K, :, :], in1=tmp[:K, :, :], op=ALU.add), reads=[f"psXr{c}", tk], writes=[tk])
                if o == 0:
                    p.op("pool", lambda e, gate=gate, tmp=tmp, c=c: e.tensor_tensor(out=y1s[c][:K, :, :].bitcast(F32R), in0=tmp[:K, :, :], in1=gate[:K, :, :], op=ALU.mult),
                         reads=[tk, gk], writes=[f"y1{c}"])
                else:
                    p.op("pool", lambda e, gate=gate, tmp=tmp, c=c: e.tensor_tensor(out=yos[c][:K, :, :], in0=tmp[:K, :, :], in1=gate[:K, :, :], op=ALU.mult),
                         reads=[tk, gk], writes=[f"yo{c}"])
                    p.dma("pool", tb(Y[c0:c0 + S, t_off:t_off + Lsig]), yos[c][:K, :, :], reads=[f"yo{c}"], writes=["Y"])


def hy_ctx_tables(l=256):
    N = 2 * l
    t = np.arange(l, dtype=np.float64)[:, None]
    k = np.arange(N, dtype=np.float64)[None, :]
    ang = 2 * np.pi * t * k / N
    c, s = np.cos(ang), np.sin(ang)
    cb, sb = c.copy(), s.copy()
    cb[0, :] = 0.0
    sb[0, :] = 0.0
    fw = np.stack([c, -s, cb, sb]).reshape(4, l // 128, 128, N // 128, 128).transpose(2, 0, 1, 3, 4)
    iv = np.stack([c.T / N, -s.T / N]).reshape(2, N // 128, 128, l // 128, 128).transpose(2, 0, 1, 3, 4)
    return np.ascontiguousarray(fw).astype(np.float32), np.ascontiguousarray(iv).astype(np.float32)


def ph_hy_ctx_dense(p, ZC, l, FRAW, RINV, HYD, FWT, IVT, IDENT, Y):
    TC, KC = l // 128, (2 * l) // 128
    ident = p.sb([128, 128], F32, name="cid")
    p.dma("sp", ident[:, :], IDENT, writes=["cid"])
    fw = p.sb([128, 4, TC, KC, 128], F32, name="cfw")
    p.dma("sp", fw[:, :, :, :, :], FWT, writes=["cfw"])
    iv = p.sb([128, 2, KC, TC, 128], F32, name="civ")
    p.dma("sp", iv[:, :, :, :, :], IVT, writes=["civ"])
    rinv = p.sb([128, 2, 1024], F32, name="crinv")
    p.dma("sp", rinv[:, :, :], RINV.partition_broadcast(128), reads=["RINV"], writes=["crinv"])
    dsk = p.sb([128, 2, 1024], F32, name="cdsk")
    p.dma("sp", dsk[:, :, :], HYD.partition_broadcast(128), writes=["cdsk"])
    pst = [p.ps([128, 512], name=f"cpt{i}") for i in range(2)]
    psm = [p.ps([128, 512], name=f"cpm{i}") for i in range(4)]
    stg = [p.sb([128, l], F32, name=f"cstg{i}") for i in range(4)]
    fT = p.sb([128, TC, 4096], F32, name="cfT")
    sT = p.sb([128, TC, 3072], F32, name="csT")
    k = 0
    for (SRC, nch_, dst, dk, rk) in ((FRAW, 32, fT, "cfT", "FRAW"), (ZC, 24, sT, "csT", "ZC")):
        for rc in range(0, nch_, 4):
            for tc in range(TC):
                ps, pk = pst[k % 2], f"cpt{k%2}"
                for j in range(4):
                    s_, sk = stg[j], f"cstg{j}"
                    if tc == 0:
                        p.dma("sp", s_[:, :], SRC[(rc + j) * 128:(rc + j + 1) * 128, 0:l], reads=[rk], writes=[sk])
                    p.op("pe", lambda e, ps=ps, j=j, s_=s_, tc=tc: e.transpose(out=ps[:, j * 128:(j + 1) * 128], in_=s_[:, tc * 128:(tc + 1) * 128], identity=ident[:, :]),
                         reads=[sk, "cid"], writes=[pk])
                p.op("act" if k % 2 else "dve", (lambda e, ps=ps, dst=dst, tc=tc, rc=rc: e.copy(out=dst[:, tc, rc * 128:(rc + 4) * 128], in_=ps[:, :])) if k % 2 else
                     (lambda e, ps=ps, dst=dst, tc=tc, rc=rc: e.tensor_copy(out=dst[:, tc, rc * 128:(rc + 4) * 128], in_=ps[:, :])), reads=[pk], writes=[dk])
                k += 1
    Xr = p.sb([128, KC, 1024], F32, name="cXr")
    Xi = p.sb([128, KC, 1024], F32, name="cXi")
    t1 = p.sb([128, 512], F32, name="ct1")
    t2 = p.sb([128, 512], F32, name="ct2")
    y1 = p.sb([128, TC, 1024], F32, name="cy1")
    y2 = p.sb([128, TC, 1024], F32, name="cy2")
    tmp = p.sb([128, 1024], F32, name="ctm")
    Hr = p.sb([128, KC, 1024], F32, name="cHr")
    Hi = p.sb([128, KC, 1024], F32, name="cHi")
    m = 0
    for o in range(2):
        for kc in range(KC):
            if True:
                for cb in range(2):
                    fcol = (o * 2 + 0) * 1024 + cb * 512
                    bcol = (o * 2 + 1) * 1024 + cb * 512
                    for part, (kf, kb, Hd, hk) in enumerate(((0, 2, Hr, "cHr"), (1, 3, Hi, "cHi"))):
                        ps, pk = psm[m % 4], f"cpm{m%4}"
                        n = 0
                        for tc in range(TC):
                            for (kind, col) in ((kf, fcol), (kb, bcol)):
                                p.op("pe", lambda e, ps=ps, kind=kind, tc=tc, kc=kc, col=col, n=n: e.matmul(ps[:, :], lhsT=fw[:, kind, tc, kc, :], rhs=fT[:, tc, col:col + 512],
                                                                                                         start=(n == 0), stop=(n == 2 * TC - 1)), reads=["cfw", "cfT"], writes=[pk])
                                n += 1
                        p.op("dve", lambda e, ps=ps, Hd=Hd, kc=kc, o=o, cb=cb: e.tensor_tensor(out=Hd[:, kc, cb * 512:(cb + 1) * 512], in0=ps[:, :],
                                                                                               in1=rinv[:, o, cb * 512:(cb + 1) * 512], op=ALU.mult), reads=[pk, "crinv"], writes=[hk])
                        m += 1
        sig = (lambda tc, c0: sT[:, tc, c0:c0 + 512]) if o == 0 else (lambda tc, c0: y1[:, tc, c0:c0 + 512])
        sigk = "csT" if o == 0 else "cy1"
        for kc in range(KC):
            for cb in range(2):
                c0 = cb * 512
                psr, prk, psi, pik = psm[0], "cpm0", psm[1], "cpm1"
                for tc in range(TC):
                    p.op("pe", lambda e, tc=tc, kc=kc, c0=c0, sg=sig: e.matmul(psr[:, :], lhsT=fw[:, 0, tc, kc, :], rhs=sg(tc, c0), start=(tc == 0), stop=(tc == TC - 1)),
                         reads=["cfw", sigk], writes=[prk])
                for tc in range(TC):
                    p.op("pe", lambda e, tc=tc, kc=kc, c0=c0, sg=sig: e.matmul(psi[:, :], lhsT=fw[:, 1, tc, kc, :], rhs=sg(tc, c0), start=(tc == 0), stop=(tc == TC - 1)),
                         reads=["cfw", sigk], writes=[pik])
                hs = slice(c0, c0 + 512)
                xs_ = slice(c0, c0 + 512)
                p.op("dve", lambda e, kc=kc, hs=hs: e.tensor_tensor(out=t1[:, :], in0=psr[:, :], in1=Hr[:, kc, hs], op=ALU.mult), reads=[prk, "cHr"], writes=["ct1"])
                p.op("dve", lambda e, kc=kc, hs=hs: e.tensor_tensor(out=t2[:, :], in0=psi[:, :], in1=Hi[:, kc, hs], op=ALU.mult), reads=[pik, "cHi"], writes=["ct2"])
                p.op("pool", lambda e, kc=kc, xs_=xs_: e.tensor_tensor(out=Xr[:, kc, xs_], in0=t1[:, :], in1=t2[:, :], op=ALU.subtract), reads=["ct1", "ct2"], writes=["cXr"])
                p.op("dve", lambda e, kc=kc, hs=hs: e.tensor_tensor(out=t1[:, :], in0=psr[:, :], in1=Hi[:, kc, hs], op=ALU.mult), reads=[prk, "cHi"], writes=["ct1"])
                p.op("dve", lambda e, kc=kc, hs=hs: e.tensor_tensor(out=t2[:, :], in0=psi[:, :], in1=Hr[:, kc, hs], op=ALU.mult), reads=[pik, "cHr"], writes=["ct2"])
                p.op("pool", lambda e, kc=kc, xs_=xs_: e.tensor_tensor(out=Xi[:, kc, xs_], in0=t1[:, :], in1=t2[:, :], op=ALU.add), reads=["ct1", "ct2"], writes=["cXi"])
        dst, dstk = (y1, "cy1") if o == 0 else (y2, "cy2")
        for tc in range(TC):
            for cb in range(2):
                c0 = cb * 512
                ps, pk = psm[2 + cb], f"cpm{2+cb}"
                n = 0
                for kc in range(KC):
                    for (kind, Xs, xk) in ((0, Xr, "cXr"), (1, Xi, "cXi")):
                        p.op("pe", lambda e, ps=ps, kind=kind, kc=kc, tc=tc, Xs=Xs, c0=c0, n=n: e.matmul(ps[:, :], lhsT=iv[:, kind, kc, tc, :], rhs=Xs[:, kc, c0:c0 + 512],
                                                                                                      start=(n == 0), stop=(n == 2 * KC - 1)), reads=["civ", xk], writes=[pk])
                        n += 1
                src = (lambda: sT[:, tc, c0:c0 + 512]) if o == 0 else (lambda: y1[:, tc, c0:c0 + 512])
                gate = sT[:, tc, (1 + o) * 1024 + c0:(1 + o) * 1024 + c0 + 512]
                srcap = src()
                p.op("dve", lambda e, srcap=srcap, o=o, c0=c0: e.tensor_tensor(out=tmp[:, c0:c0 + 512], in0=srcap, in1=dsk[:, o, c0:c0 + 512], op=ALU.mult),
                     reads=[sigk, "cdsk"], writes=["ctm"])
                p.op("dve", lambda e, ps=ps, c0=c0: e.tensor_tensor(out=tmp[:, c0:c0 + 512], in0=ps[:, :], in1=tmp[:, c0:c0 + 512], op=ALU.add), reads=[pk, "ctm"], writes=["ctm"])
                p.op("pool", lambda e, dst=dst, tc=tc, c0=c0, gate=gate: e.tensor_tensor(out=dst[:, tc, c0:c0 + 512], in0=tmp[:, c0:c0 + 512], in1=gate, op=ALU.mult),
                     reads=["ctm", "csT"], writes=[dstk])
    yo = [p.sb([128, l], BF16, name=f"cyo{i}") for i in range(2)]
    k = 0
    for cc in range(8):
        ps, pk = pst[cc % 2], f"cpt{cc%2}"
        for tc in range(TC):
            p.op("pe", lambda e, ps=ps, tc=tc, cc=cc: e.transpose(out=ps[:, tc * 128:(tc + 1) * 128], in_=y2[:, tc, cc * 128:(cc + 1) * 128], identity=ident[:, :]),
                 reads=["cy2", "cid"], writes=[pk])
        o_, ok = yo[cc % 2], f"cyo{cc%2}"
        p.op("act", lambda e, ps=ps, o_=o_: e.copy(out=o_[:, :], in_=ps[:, 0:l]), reads=[pk], writes=[ok])
        p.dma("pool", Y[cc * 128:(cc + 1) * 128, 0:l], o_[:, :], reads=[ok], writes=["Y"])


def s5_layouts(lam_re, lam_im, log_dt, b_re, b_im, c_re, c_im):
    def ls_of(a):
        return np.ascontiguousarray(a.reshape(2, 16, 2, 64).transpose(2, 3, 0, 1).reshape(128, 32))
    ldt = np.broadcast_to(log_dt[:, :, None], (2, 32, 64))
    LS = np.stack([ls_of(lam_re), ls_of(lam_im), ls_of(ldt)]).astype(np.float32)
    BT = np.zeros((2, 32, 16, 128), np.float32)
    CT = np.zeros((2, 128, 16, 32), np.float32)
    for i, (b, c) in enumerate(((b_re, c_re), (b_im, c_im))):
        for gs in range(2):
            bb = b.reshape(16, 2, 64, 16)[:, gs]
            BT[i, gs * 16:(gs + 1) * 16, :, gs * 64:(gs + 1) * 64] = bb.transpose(2, 0, 1)
            cc = c.reshape(16, 2, 16, 64)[:, gs]
            CT[i, gs * 64:(gs + 1) * 64, :, gs * 16:(gs + 1) * 16] = cc.transpose(2, 0, 1)
    return LS, BT, CT


SEQ = 8192
NCTX = 256
T = SEQ + NCTX
DEPTH = 4
NCORES = 4


def ph_init(p, XIN, CIN, PE, XT):
    a = [p.sb([128, 8, 512], F32, name=f"ia{i}") for i in range(2)]
    b = [p.sb([128, 8, 512], F32, name=f"ib{i}") for i in range(2)]
    p.dma("sp", a[0][:, :, :NCTX], fm(CIN, 0, NCTX), writes=["ia0"])
    p.dma("pool", fm(XT, 0, NCTX), a[0][:, :, :NCTX], reads=["ia0"], writes=["XT"])
    for i, t0 in enumerate(range(0, SEQ, 512)):
        k = (i + 1) % 2
        p.dma("sp", a[k][:, :, :], fm(XIN, t0, 512), writes=[f"ia{k}"])
        p.dma("sp", b[k][:, :, :], fm(PE, t0, 512), writes=[f"ib{k}"])
        p.op("dve" if i % 2 else "pool", lambda e, k=k: e.tensor_tensor(out=a[k][:, :, :], in0=a[k][:, :, :], in1=b[k][:, :, :], op=ALU.add),
             reads=[f"ia{k}", f"ib{k}"], writes=[f"ia{k}"])
        p.dma("pool", fm(XT, NCTX + t0, 512), a[k][:, :, :], reads=[f"ia{k}"], writes=["XT"])


def ph_mod(p, CV, MW, MB, M):
    cv = p.sb([128, 8, 2], F32, name="cv")
    for r in range(2):
        p.dma("sp", cv[:, :, r], CV[r].rearrange("(c p) -> p c", p=128), writes=["cv"], allow_slow_non_contiguous=True)
    p.op("act", lambda e: e.activation(out=cv[:, :, :], in_=cv[:, :, :], func=AF.Silu), reads=["cv"], writes=["cv"])
    ws = [p.sb([128, 8, 512], F32, name=f"mw{i}") for i in range(2)]
    bs = [p.sb([2, 512], F32, name=f"mb{i}") for i in range(2)]
    os_ = [p.sb([2, 512], F32, name=f"mo{i}") for i in range(2)]
    pss = [p.ps([128, 512], name=f"psm{i}") for i in range(2)]
    k = 0
    for i in range(DEPTH):
        for n0 in range(0, 6144, 512):
            w, wk, b, bk, o, ok, ps, pk = ws[k % 2], f"mw{k%2}", bs[k % 2], f"mb{k%2}", os_[k % 2], f"mo{k%2}", pss[k % 2], f"psm{k%2}"
            p.dma("sp", w[:, :, :], MW[i, :, n0:n0 + 512].rearrange("(c p) n -> p c n", p=128), writes=[wk])
            p.dma("sp", b[:, :], MB[i, n0:n0 + 512].partition_broadcast(2), writes=[bk])
            for c in range(8):
                p.op("pe", lambda e, ps=ps, w=w, c=c: e.matmul(ps[:2, :], lhsT=cv[:, c, :], rhs=w[:, c, :], start=(c == 0), stop=(c == 7)),
                     reads=["cv", wk], writes=[pk])
            p.op("dve", lambda e, ps=ps, b=b, o=o: e.tensor_tensor(out=o[:, :], in0=ps[:2, :], in1=b[:, :], op=ALU.add), reads=[pk, bk], writes=[ok])
            p.dma("pool", M[i, :, n0:n0 + 512], o[:, :], reads=[ok], writes=["M"])
            k += 1


def token_tiles(N, with_ctx):
    tl = [(0, NCTX, 1)] if with_ctx else []
    return tl + [(NCTX + t0, N, 0) for t0 in range(0, SEQ, N)]


def build_program():
    p = Prog()
    I = {}

    def inp(name, shape, dt=F32):
        I[name] = p.dram(name, shape, dt, "ExternalInput")
        return I[name]

    XIN = inp("xT", [1024, SEQ]); CIN = inp("ctxT", [1024, NCTX]); PE = inp("peT", [1024, SEQ]); CV = inp("cvec", [2, 1024])
    MW = inp("mod_w", [4, 1024, 6144]); MB = inp("mod_b", [4, 6144])
    NMG = inp("norm_mix_g", [4, 1024]); NLG = inp("norm_mlp_g", [4, 1024])
    W1 = inp("mlp_w1", [4, 1024, 4096]); W2 = inp("mlp_w2", [4, 4096, 1024]); FNG = inp("final_norm_g", [1024])
    EIW = inp("ev_in_w", [2, 1024, 3616]); EOW = inp("ev_out_w", [2, 1536, 1024])
    LS = inp("s5_ls", [2, 3, 128, 32]); BT = inp("s5_bt", [2, 2, 32, 16, 128]); CT = inp("s5_ct", [2, 2, 128, 16, 32])
    S5D = inp("s5_d", [2, 512]); GW = inp("s5_glu_w", [2, 512, 512]); GB = inp("s5_glu_b", [2, 512])
    MCW = inp("m2_conv_w", [2, 3, 2048]); MCB = inp("m2_conv_b", [2, 2048]); DTB = inp("m2_dt_bias", [2, 32]); ALOG = inp("m2_a_log", [2, 32])
    M2D = inp("m2_d", [2, 16]); M2G = inp("m2_norm_g", [2, 1024])
    HIW = inp("hy_in_w", [2, 1024, 3072]); HIB = inp("hy_in_b", [2, 3072]); HCW = inp("hy_conv_w", [2, 3, 3072]); HCB = inp("hy_conv_b", [2, 3072])
    HW1 = inp("hy_f_w1", [2, 33, 64]); HB1 = inp("hy_f_b1", [2, 64]); HQ1 = inp("hy_f_freq1", [2, 64])
    HW2 = inp("hy_f_w2", [2, 64, 64]); HB2 = inp("hy_f_b2", [2, 64]); HQ2 = inp("hy_f_freq2", [2, 64]); HW3 = inp("hy_f_w3", [2, 64, 4096])
    HYD = inp("hy_d", [2, 2, 1024]); HOW = inp("hy_out_w", [2, 1024, 1024]); HOB = inp("hy_out_b", [2, 1024])
    IDENT = inp("ident", [128, 128]); MASKS = inp("masks", [2, 128, 512]); TABS = inp("tabs", [128, 1536])
    FEAT_L = inp("feats_l", [33, SEQ]); TL_L = inp("tl_l", [SEQ]); FEAT_C = inp("feats_c", [33, NCTX]); TL_C = inp("tl_c", [NCTX]); DELTAS = inp("deltas", [1024])
    FWT = inp("ctx_fwt", [128, 4, 2, 4, 128]); IVT = inp("ctx_ivt", [128, 2, 4, 2, 128])
    OUT = p.dram("out", [SEQ, 1024], F32, "ExternalOutput")

    XT = p.dram("XT", [1024, T], F32); M = p.dram("M", [4, 2, 6144], F32)
    PROJ = p.dram("PROJ", [3616, T], F32); Y = p.dram("Y", [1536, T], BF16)
    YPRE = p.dram("YPRE", [512, T], F32); XBC = p.dram("XBC", [2048, T], F32)
    SC = p.dram("SC", [96, T], F32); CF2 = p.dram("CF2", [2, 32, T], F32); ET = p.dram("ET", [T // 128, 32], F32)
    YT = p.dram("YT", [T, 1024], F32); YM = p.dram("YM", [1024, T], F32)
    ZC = p.dram("ZC", [3072, T], F32); FRAW = p.dram("FRAW", [4096, SEQ], F32); RINV = p.dram("RINV", [2, 1024], F32)
    FILT = p.dram("FILT", [2048, NFFT], F32); HF = p.dram("HF", [2, 2048, 128, 128], F32)

    ph_init(p, XIN, CIN, PE, XT); p.end()
    ph_mod(p, CV, MW, MB, M); p.end()
    segs = [(0, NCTX), (NCTX, T)]
    for i in range(DEPTH):
        j = i // 2
        ctx_later = i < 2
        sl = lambda k: slice(k * 1024, (k + 1) * 1024)
        mods_a = [(M[i, r, sl(1)], M[i, r, sl(0)]) for r in range(2)]
        mods_f = [(M[i, r, sl(4)], M[i, r, sl(3)]) for r in range(2)]
        ga = [M[i, r, sl(2)] for r in range(2)]
        gf = [M[i, r, sl(5)] for r in range(2)]
        if i % 2 == 0:
            ph_tok_in(p, XT, EIW[j], 3616, NMG[i], mods_a, token_tiles(512, True), PROJ); p.end()
            ph_s5_scan(p, PROJ, T, NCTX, LS[j], BT[j], CT[j], S5D[j], YPRE); p.end()
            ph_s5_glu(p, YPRE, T, GW[j], GB[j], Y); p.end()
            ph_conv_silu(p, PROJ, 1536, 2048, T, segs, MCW[j], MCB[j], XBC); p.end()
            ph_ssd_prep(p, PROJ, T, DTB[j], ALOG[j], SC, CF2, ET); p.end()
            for d in range(2):
                ph_ssd_pass(p, d, XBC, T, NCTX, SC, CF2, ET, M2D[j].partition_broadcast(128), IDENT, MASKS, YT, YM); p.end()
            ph_ssd_post(p, YM, PROJ, T, M2G[j], Y); p.end()
            ph_outproj(p, XT, Y, 1536, EOW[j], ga, token_tiles(512, ctx_later)); p.end()
        else:
            ph_tok_in(p, XT, HIW[j], 3072, NMG[i], mods_a, token_tiles(512, ctx_later), PROJ, bias=HIB[j]); p.end()
            ph_conv_silu(p, PROJ, 0, 3072, T, segs, HCW[j], HCB[j], ZC, silu=False); p.end()
            fargs = (HW1[j], HB1[j], HQ1[j], HW2[j], HB2[j], HQ2[j], HW3[j], DELTAS)
            ph_hy_filter_raw(p, SEQ, FEAT_L, TL_L, *fargs, FRAW, RINV); p.end()
            ph_hy_filter_asm(p, SEQ, FRAW, RINV, FILT); p.end()
            ph_hy_filter_fft(p, FILT, TABS, HF); p.end()
            ph_hy_conv(p, ZC, NCTX, SEQ, HF, HYD[j], TABS, Y); p.end()
            if ctx_later:
                ph_hy_filter_raw(p, NCTX, FEAT_C, TL_C, *fargs, FRAW[:, 0:NCTX], RINV); p.end()
                ph_hy_ctx_dense(p, ZC, NCTX, FRAW[:, 0:NCTX], RINV, HYD[j], FWT, IVT, IDENT, Y); p.end()
            ph_outproj(p, XT, Y, 1024, HOW[j], ga, token_tiles(512, ctx_later), bias=HOB[j]); p.end()
        ph_mlp(p, XT, W1[i], W2[i], NLG[i], mods_f, gf, token_tiles(256, ctx_later)); p.end()
    ph_final(p, XT, FNG, token_tiles(512, False), OUT, NCTX, IDENT); p.end()
    return p


def grid_sincos_np(n, dm):
    rows = n // 64
    quarter = dm // 4
    omega = (1.0 / (np.float32(10000.0) ** (np.arange(quarter, dtype=np.float32) / np.float32(quarter)))).astype(np.float32)
    ang_r = np.arange(rows, dtype=np.float32)[:, None] * omega
    ang_c = np.arange(64, dtype=np.float32)[:, None] * omega
    emb_r = np.concatenate([np.sin(ang_r), np.cos(ang_r)], axis=-1)
    emb_c = np.concatenate([np.sin(ang_c), np.cos(ang_c)], axis=-1)
    half = emb_r.shape[-1]
    pe = np.concatenate([np.broadcast_to(emb_r[:, None, :], (rows, 64, half)), np.broadcast_to(emb_c[None, :, :], (rows, 64, half))], axis=-1)
    return pe.reshape(rows * 64, 2 * half).astype(np.float32)


def feats_tables(l):
    t = np.linspace(0.0, 1.0, l, dtype=np.float32)[:, None]
    bands = np.linspace(1e-4, 15, 16, dtype=np.float32)
    ang = (np.float32(2.0 * math.pi / l)) * np.arange(l, dtype=np.float32)[:, None] * bands
    feats = np.concatenate([t, np.cos(ang), -np.sin(ang)], axis=-1).astype(np.float32)
    return np.ascontiguousarray(feats.T), np.ascontiguousarray(t[:, 0])


def host_inputs(inputs):
    f = lambda a: np.ascontiguousarray(np.asarray(a, dtype=np.float32))
    g = {k: f(v) for k, v in inputs.items()}
    shared = {k: g[k] for k in ("mod_w", "mod_b", "norm_mix_g", "norm_mlp_g", "mlp_w1", "mlp_w2", "final_norm_g", "ev_in_w", "ev_out_w",
                                "s5_d", "s5_glu_w", "s5_glu_b", "m2_conv_w", "m2_conv_b", "m2_d", "m2_norm_g", "hy_in_w", "hy_in_b",
                                "hy_conv_w", "hy_conv_b", "hy_f_w1", "hy_f_b1", "hy_f_freq1", "hy_f_w2", "hy_f_b2", "hy_f_freq2", "hy_f_w3",
                                "hy_d", "hy_out_w", "hy_out_b")}
    shared["m2_dt_bias"] = g["m2_dt_bias"].reshape(2, 32)
    shared["m2_a_log"] = g["m2_a_log"].reshape(2, 32)
    lay = [s5_layouts(g["s5_lam_re"][j], g["s5_lam_im"][j], g["s5_log_dt"][j], g["s5_b_re"][j], g["s5_b_im"][j], g["s5_c_re"][j], g["s5_c_im"][j])
           for j in range(2)]
    shared["s5_ls"] = np.stack([l[0] for l in lay]); shared["s5_bt"] = np.stack([l[1] for l in lay]); shared["s5_ct"] = np.stack([l[2] for l in lay])
    shared["ident"] = np.eye(128, dtype=np.float32)
    sq = np.arange(128)
    mf = np.where(sq[None, :] >= sq[:, None], 0.0, -30000.0).astype(np.float32)
    mb = np.where(sq[None, :] <= sq[:, None], 0.0, -30000.0).astype(np.float32)
    shared["masks"] = np.stack([np.tile(mf, (1, 4)), np.tile(mb, (1, 4))])
    shared["tabs"] = hy_tables()
    shared["ctx_fwt"], shared["ctx_ivt"] = hy_ctx_tables(NCTX)
    shared["feats_l"], shared["tl_l"] = feats_tables(SEQ)
    shared["feats_c"], shared["tl_c"] = feats_tables(NCTX)
    shared["deltas"] = np.abs(np.linspace(math.log(1e-2) / 1.5, math.log(1e-2) / 0.3, 1024, dtype=np.float32)).astype(np.float32)
    shared["peT"] = np.ascontiguousarray(grid_sincos_np(SEQ, 1024).T)
    maps = []
    for b in range(NCORES):
        m = dict(shared)
        m["xT"] = np.ascontiguousarray(g["x"][b].T)
        m["ctxT"] = np.ascontiguousarray(g["ctx"][b].T)
        m["cvec"] = np.stack([g["c"][b], g["c_ctx"]])
        maps.append(m)
    return maps


def kernel(**inputs):
    p = build_program()
    nc = p.build()
    maps = host_inputs(inputs)
    res = run_bass_kernel_spmd(nc, maps, core_ids=list(range(NCORES)))
    return np.stack([np.asarray(r["out"], dtype=np.float32) for r in res.results], axis=0)
```

```python
import contextlib
import math
import numpy as np
import concourse.bass as bass
import concourse.mybir as mybir
from concourse.bass_utils import run_bass_kernel_spmd


F32 = mybir.dt.float32
BF16 = mybir.dt.bfloat16
I32 = mybir.dt.int32
AF = mybir.ActivationFunctionType
ALU = mybir.AluOpType
AX = mybir.AxisListType


class Prog:
    COMPUTE = ("pe", "dve", "act", "pool")
    ENGS = ("pe", "dve", "act", "pool", "sp")
    QS = ("sp", "act", "pool")
    NDS = 8

    def __init__(self):
        self.nc = bass.Bass("TRN2", target_bir_lowering=False)
        nc = self.nc
        self.top = contextlib.ExitStack()
        self.sems = {e: self.top.enter_context(nc.semaphore(f"s_{e}")) for e in self.COMPUTE}
        self.dsems = {q: [self.top.enter_context(nc.semaphore(f"d_{q}{i}")) for i in range(self.NDS)] for q in self.QS}
        self.cnt = {e: 0 for e in self.COMPUTE}
        self.dval = {q: [0] * self.NDS for q in self.QS}
        self.dnext = {q: 0 for q in self.QS}
        self.n_sb = 0
        self.stats = {e: 0 for e in self.ENGS}
        self.stack = None
        self.barrier_tok = []
        self.begin()

    def begin(self):
        self.ops = []
        self.lastw = {}
        self.readers = {}
        self.stack = contextlib.ExitStack()

    def dram(self, name, shape, dt, kind="Internal"):
        return self.nc.dram_tensor(name, list(shape), dt, kind=kind).ap()

    def sb(self, shape, dt=F32, name=None):
        self.n_sb += 1
        return self.stack.enter_context(self.nc.sbuf_tensor(f"{name or 'sb'}_{self.n_sb}", list(shape), dt))

    def ps(self, shape, dt=F32, name=None):
        self.n_sb += 1
        return self.stack.enter_context(self.nc.psum_tensor(f"{name or 'ps'}_{self.n_sb}", list(shape), dt))

    def op(self, eng, fn, reads=(), writes=(), dma=False):
        deps = set()
        for k in reads:
            if k in self.lastw:
                deps.add(self.lastw[k])
        for k in writes:
            if k in self.lastw:
                deps.add(self.lastw[k])
            for r in self.readers.get(k, ()):
                deps.add(r)
        idx = len(self.ops)
        self.ops.append(dict(eng=eng, fn=fn, deps=deps, dma=dma, sig=False))
        for k in reads:
            rl = self.readers.setdefault(k, [])
            if not dma:
                rl[:] = [r for r in rl if self.ops[r]["dma"] or self.ops[r]["eng"] != eng]
            rl.append(idx)
        for k in writes:
            self.lastw[k] = idx
            self.readers[k] = []
        return idx

    def dma(self, q, out, in_, reads=(), writes=(), **kw):
        return self.op(q, lambda e: e.dma_start(out=out, in_=in_, **kw), reads, writes, dma=True)

    def end(self):
        nc = self.nc
        ops = self.ops
        for o in ops:
            for d in o["deps"]:
                ops[d]["sig"] = True
        per = {e: [] for e in self.ENGS}
        for o in ops:
            per[o["eng"]].append(o)
        for e in self.COMPUTE:
            for o in reversed(per[e]):
                if not o["dma"]:
                    o["sig"] = True
                    break
        for o in ops:
            if o["dma"]:
                q = o["eng"]
                k = self.dnext[q]
                self.dnext[q] = (k + 1) % self.NDS
                o["prev"] = (q, k, self.dval[q][k])
                self.dval[q][k] += 16
                o["tok"] = ("d", q, k, self.dval[q][k])
            elif o["sig"]:
                self.cnt[o["eng"]] += 1
                o["tok"] = ("c", o["eng"], self.cnt[o["eng"]])
        for e in self.ENGS:
            self.stats[e] += len(per[e])
        start_tok = self.barrier_tok
        end_tok = [("c", e, self.cnt[e]) for e in self.COMPUTE if self.cnt[e] > 0]
        for q in self.QS:
            for k in range(self.NDS):
                if self.dval[q][k] > 0:
                    end_tok.append(("d", q, k, self.dval[q][k]))
        self.barrier_tok = end_tok
        sems, dsems = self.sems, self.dsems

        def emit(e_name, eng):
            waited = {}

            def wait(t):
                key = t[:-1]
                if waited.get(key, 0) >= t[-1]:
                    return
                waited[key] = t[-1]
                sem = sems[t[1]] if t[0] == "c" else dsems[t[1]][t[2]]
                eng.wait_ge(sem, t[-1])

            for t in start_tok:
                wait(t)
            for o in per[e_name]:
                if o["dma"]:
                    q, k, pv = o["prev"]
                    if pv > 0:
                        wait(("d", q, k, pv))
                for d in sorted(o["deps"]):
                    t = ops[d]["tok"]
                    if t[0] == "c" and t[1] == e_name and e_name == "pe":
                        continue
                    wait(t)
                ins = o["fn"](eng)
                if o["dma"]:
                    t = o["tok"]
                    ins.then_inc(dsems[t[1]][t[2]], 16)
                elif o["sig"]:
                    ins.then_inc(sems[e_name], 1)

        with nc.Block() as block:
            @block.tensor
            def _(eng):
                emit("pe", eng)

            @block.vector
            def _(eng):
                emit("dve", eng)

            @block.scalar
            def _(eng):
                emit("act", eng)

            @block.gpsimd
            def _(eng):
                emit("pool", eng)

            @block.sync
            def _(eng):
                emit("sp", eng)
        self.stack.close()
        self.begin()

    def build(self):
        nc = self.nc
        toks = self.barrier_tok
        sems, dsems = self.sems, self.dsems
        with nc.Block() as block:
            def fin(eng):
                for t in toks:
                    sem = sems[t[1]] if t[0] == "c" else dsems[t[1]][t[2]]
                    eng.wait_ge(sem, t[-1])

            @block.sync
            def _(eng):
                fin(eng)

            @block.gpsimd
            def _(eng):
                fin(eng)
        self.top.close()
        return nc


def run(p, in_maps, n=None, trace=False):
    nc = p.build()
    n = n or len(in_maps)
    return run_bass_kernel_spmd(nc, in_maps, core_ids=list(range(n)), trace=trace)


EPS = 1e-6
D = 1024


def col(ap1d):
    return ap1d.rearrange("(c p) -> p c", p=128)


def load_cols(p, dst, src1d, key, q="sp"):
    p.dma(q, dst, col(src1d), writes=[key], allow_slow_non_contiguous=True)


def load_weight_bf16(p, w_d, K, F, name, q="sp", chunk=1024):
    kc = K // 128
    wb = p.sb([128, kc, F], BF16, name=name)
    stg = [p.sb([128, chunk], F32, name=f"{name}_stg{i}") for i in range(2)]
    i = 0
    for c in range(kc):
        for f0 in range(0, F, chunk):
            fn = min(chunk, F - f0)
            s = stg[i % 2]
            p.dma(q, s[:, :fn], w_d[c * 128:(c + 1) * 128, f0:f0 + fn], writes=[f"{name}_stg{i%2}"])
            if i % 2 == 0:
                p.op("pool", lambda e, s=s, c=c, f0=f0, fn=fn: e.tensor_copy(out=wb[:, c, f0:f0 + fn], in_=s[:, :fn]),
                     reads=[f"{name}_stg{i%2}"], writes=[name])
            else:
                p.op("act", lambda e, s=s, c=c, f0=f0, fn=fn: e.copy(out=wb[:, c, f0:f0 + fn], in_=s[:, :fn]),
                     reads=[f"{name}_stg{i%2}"], writes=[name])
            i += 1
    return wb


class Norm:
    def __init__(self, p, NMAX, nmod, g_d, mods):
        self.p = p
        self.ones = p.sb([128, 128], F32, name="ones")
        p.op("pool", lambda e: e.memset(self.ones[:, :], 1.0), writes=["ones"])
        self.eps = p.sb([128, 1], F32, name="eps")
        p.op("pool", lambda e: e.memset(self.eps[:, :], EPS), writes=["eps"])
        self.sq = p.sb([128, 8, NMAX], F32, name="sq")
        self.tmp = p.sb([128, 8, NMAX], F32, name="ntmp")
        self.rstd = p.sb([128, NMAX], F32, name="rstd")
        self.ps_ss = p.ps([128, 512], name="ps_ss")
        g_sb = p.sb([128, 8], F32, name="g_sb")
        load_cols(p, g_sb[:, :], g_d, "g_sb")
        self.gs = p.sb([128, nmod, 8], F32, name="gs_sb")
        self.sh = p.sb([128, nmod, 8], F32, name="sh_sb")
        sc_sb = p.sb([128, nmod, 8], F32, name="sc_sb")
        for m, (sc_d, sh_d) in enumerate(mods):
            if sc_d is None:
                p.op("pool", lambda e, m=m: e.memset(sc_sb[:, m, :], 0.0), writes=["sc_sb"])
                p.op("pool", lambda e, m=m: e.memset(self.sh[:, m, :], 0.0), writes=["mods"])
            else:
                load_cols(p, sc_sb[:, m, :], sc_d, "sc_sb")
                load_cols(p, self.sh[:, m, :], sh_d, "mods")
        for m in range(nmod):
            p.op("dve", lambda e, m=m: e.scalar_tensor_tensor(out=self.gs[:, m, :], in0=sc_sb[:, m, :], scalar=1.0, in1=g_sb[:, :],
                                                               op0=ALU.add, op1=ALU.mult),
                 reads=["sc_sb", "g_sb"], writes=["mods"])

    def apply(self, x_sb, xk, N, mi, ht, hk):
        p = self.p
        sq, ones, ps_ss, rstd, tmp, eps_t = self.sq, self.ones, self.ps_ss, self.rstd, self.tmp, self.eps
        gs, sh = self.gs[:, mi, :], self.sh[:, mi, :]
        p.op("act", lambda e: e.activation(out=sq[:, :, :N], in_=x_sb[:, :, :N], func=AF.Square), reads=[xk], writes=["sq"])
        for c in range(8):
            p.op("pe", lambda e, c=c: e.matmul(ps_ss[:, :N], lhsT=ones[:, :], rhs=sq[:, c, :N], start=(c == 0), stop=(c == 7)),
                 reads=["sq", "ones"], writes=["ps_ss"])
        p.op("act", lambda e: e.activation(out=rstd[:, :N], in_=ps_ss[:, :N], func=AF.Sqrt, scale=1.0 / D, bias=eps_t[:, 0:1]),
             reads=["ps_ss", "eps"], writes=["rstd"])
        p.op("dve", lambda e: e.reciprocal(out=rstd[:, :N], in_=rstd[:, :N]), reads=["rstd"], writes=["rstd"])
        for c in range(8):
            p.op("dve", lambda e, c=c: e.tensor_tensor(out=tmp[:, c, :N], in0=x_sb[:, c, :N], in1=rstd[:, :N], op=ALU.mult),
                 reads=[xk, "rstd"], writes=["ntmp" + str(c)])
            p.op("act", lambda e, c=c: e.activation(out=ht[:, c, :N], in_=tmp[:, c, :N], func=AF.Identity,
                                                   scale=gs[:, c:c + 1], bias=sh[:, c:c + 1]),
                 reads=["ntmp" + str(c), "mods"], writes=[hk])


def fm(ap2d, t0, N):
    return ap2d[:, t0:t0 + N].rearrange("(c p) n -> p c n", p=128)


def ph_tok_in(p, XT, W, F, g_d, mods, tiles, PROJ, bias=None):
    nmod = len(mods)
    nfc = (F + 127) // 128
    nrm = Norm(p, 512, nmod, g_d, mods)
    if bias is not None:
        b_sb = p.sb([128, nfc], F32, name="b_sb")
        p.op("pool", lambda e: e.memset(b_sb[:, :], 0.0), writes=["b_sb"])
        nfull = F // 128
        p.dma("sp", b_sb[:, :nfull], col(bias[:nfull * 128]), writes=["b_sb"], allow_slow_non_contiguous=True)
        if F % 128:
            p.dma("sp", b_sb[:F % 128, nfull:nfull + 1], bias[nfull * 128:F].rearrange("(p o) -> p o", o=1), writes=["b_sb"],
                  allow_slow_non_contiguous=True)
    wb = load_weight_bf16(p, W, D, F, "wb")
    xs = [p.sb([128, 8, 512], F32, name=f"xt{i}") for i in range(2)]
    hts = [p.sb([128, 8, 512], BF16, name=f"ht{i}") for i in range(2)]
    pss = [p.ps([128, 512], name=f"psm{i}") for i in range(4)]
    outs = [p.sb([128, 512], F32, name=f"o{i}") for i in range(4)]
    oi = 0
    for ti, (t0, N, mi) in enumerate(tiles):
        x_sb, xk, ht, hk = xs[ti % 2], f"xt{ti%2}", hts[ti % 2], f"ht{ti%2}"
        p.dma("sp", x_sb[:, :, :N], fm(XT, t0, N), reads=["XT"], writes=[xk])
        nrm.apply(x_sb, xk, N, mi, ht, hk)
        for fc in range(nfc):
            M = min(128, F - fc * 128)
            ps, pk, o, ok = pss[oi % 4], f"psm{oi%4}", outs[oi % 4], f"o{oi%4}"
            for c in range(8):
                p.op("pe", lambda e, ps=ps, c=c, fc=fc, M=M, ht=ht, N=N: e.matmul(ps[:M, :N], lhsT=wb[:, c, fc * 128:fc * 128 + M],
                                                                                  rhs=ht[:, c, :N], start=(c == 0), stop=(c == 7)),
                     reads=["wb", hk], writes=[pk])
            if bias is not None:
                p.op("act", lambda e, ps=ps, o=o, M=M, N=N, fc=fc: e.activation(out=o[:M, :N], in_=ps[:M, :N], func=AF.Identity,
                                                                                 bias=b_sb[:M, fc:fc + 1], scale=1.0),
                     reads=[pk, "b_sb"], writes=[ok])
            elif oi % 2 == 0:
                p.op("act", lambda e, ps=ps, o=o, M=M, N=N: e.copy(out=o[:M, :N], in_=ps[:M, :N]), reads=[pk], writes=[ok])
            else:
                p.op("dve", lambda e, ps=ps, o=o, M=M, N=N: e.tensor_copy(out=o[:M, :N], in_=ps[:M, :N]), reads=[pk], writes=[ok])
            p.dma("pool", PROJ[fc * 128:fc * 128 + M, t0:t0 + N], o[:M, :N], reads=[ok], writes=["PROJ"])
            oi += 1


def ph_outproj(p, XT, Y, CM, W, ga_list, tiles, bias=None):
    kc = CM // 128
    nmod = len(ga_list)
    ga = p.sb([128, nmod, 8], F32, name="ga")
    for m, gd in enumerate(ga_list):
        load_cols(p, ga[:, m, :], gd, "ga")
    if bias is not None:
        b_sb = p.sb([128, 8], F32, name="ob_sb")
        load_cols(p, b_sb[:, :], bias, "ob")
        gb = p.sb([128, nmod, 8], F32, name="gb")
        for m in range(nmod):
            p.op("dve", lambda e, m=m: e.tensor_tensor(out=gb[:, m, :], in0=ga[:, m, :], in1=b_sb[:, :], op=ALU.mult),
                 reads=["ga", "ob"], writes=["gb"])
    wb = load_weight_bf16(p, W, CM, D, "wo")
    xs = [p.sb([128, 8, 512], F32, name=f"xo{i}") for i in range(2)]
    ys = [p.sb([128, kc, 512], BF16, name=f"yo{i}") for i in range(2)]
    pss = [p.ps([128, 512], name=f"pso{i}") for i in range(4)]
    oi = 0
    for ti, (t0, N, mi) in enumerate(tiles):
        x_sb, xk, y_sb, yk = xs[ti % 2], f"xo{ti%2}", ys[ti % 2], f"yo{ti%2}"
        p.dma("sp", x_sb[:, :, :N], fm(XT, t0, N), reads=["XT"], writes=[xk])
        p.dma("sp", y_sb[:, :, :N], fm(Y[0:CM, :], t0, N), reads=["Y"], writes=[yk])
        if bias is not None:
            for c in range(8):
                p.op("act", lambda e, c=c, x_sb=x_sb, N=N, mi=mi: e.activation(out=x_sb[:, c, :N], in_=x_sb[:, c, :N], func=AF.Identity,
                                                                                 bias=gb[:, mi, c:c + 1], scale=1.0),
                     reads=[xk, "gb"], writes=[xk])
        for m in range(8):
            ps, pk = pss[oi % 4], f"pso{oi%4}"
            for c in range(kc):
                p.op("pe", lambda e, ps=ps, c=c, m=m, y_sb=y_sb, N=N: e.matmul(ps[:, :N], lhsT=wb[:, c, m * 128:(m + 1) * 128],
                                                                                rhs=y_sb[:, c, :N], start=(c == 0), stop=(c == kc - 1)),
                     reads=["wo", yk], writes=[pk])
            p.op("dve", lambda e, ps=ps, m=m, x_sb=x_sb, N=N, mi=mi: e.scalar_tensor_tensor(
                out=x_sb[:, m, :N], in0=ps[:, :N], scalar=ga[:, mi, m:m + 1], in1=x_sb[:, m, :N], op0=ALU.mult, op1=ALU.add),
                 reads=[pk, xk, "ga"], writes=[xk])
            oi += 1
        p.dma("pool", fm(XT, t0, N), x_sb[:, :, :N], reads=[xk], writes=["XT"])


def ph_mlp(p, XT, W1, W2, g_d, mods, gf_list, tiles):
    nmod = len(mods)
    NT = 256
    nrm = Norm(p, NT, nmod, g_d, mods)
    gf = p.sb([128, nmod, 8], F32, name="gf")
    for m, gd in enumerate(gf_list):
        load_cols(p, gf[:, m, :], gd, "gf")
    w1 = load_weight_bf16(p, W1, D, 4096, "w1")
    w2 = load_weight_bf16(p, W2, 4096, D, "w2")
    xs = [p.sb([128, 8, NT], F32, name=f"xm{i}") for i in range(2)]
    ht = p.sb([128, 8, NT], BF16, name="htm")
    h1 = p.sb([128, 32, NT], BF16, name="h1")
    rs = [p.sb([128, NT], F32, name=f"r{i}") for i in range(2)]
    pss = [p.ps([128, 512], name=f"psx{i}") for i in range(4)]
    oi = 0
    for ti, (t0, N, mi) in enumerate(tiles):
        x_sb, xk = xs[ti % 2], f"xm{ti%2}"
        p.dma("sp", x_sb[:, :, :N], fm(XT, t0, N), reads=["XT"], writes=[xk])
        nrm.apply(x_sb, xk, N, mi, ht, "htm")
        for hc in range(32):
            ps, pk, r, rk = pss[oi % 4], f"psx{oi%4}", rs[oi % 2], f"r{oi%2}"
            for c in range(8):
                p.op("pe", lambda e, ps=ps, c=c, hc=hc, N=N: e.matmul(ps[:, :N], lhsT=w1[:, c, hc * 128:(hc + 1) * 128],
                                                                       rhs=ht[:, c, :N], start=(c == 0), stop=(c == 7)),
                     reads=["w1", "htm"], writes=[pk])
            p.op("act", lambda e, ps=ps, r=r, N=N: e.activation(out=r[:, :N], in_=ps[:, :N], func=AF.Relu), reads=[pk], writes=[rk])
            p.op("pool", lambda e, r=r, hc=hc, N=N: e.tensor_tensor(out=h1[:, hc, :N], in0=r[:, :N], in1=r[:, :N], op=ALU.mult),
                 reads=[rk], writes=["h1_" + str(hc)])
            oi += 1
        for m in range(8):
            ps, pk = pss[oi % 4], f"psx{oi%4}"
            for hc in range(32):
                p.op("pe", lambda e, ps=ps, hc=hc, m=m, N=N: e.matmul(ps[:, :N], lhsT=w2[:, hc, m * 128:(m + 1) * 128],
                                                                       rhs=h1[:, hc, :N], start=(hc == 0), stop=(hc == 31)),
                     reads=["w2", "h1_" + str(hc)], writes=[pk])
            p.op("dve", lambda e, ps=ps, m=m, x_sb=x_sb, N=N, mi=mi: e.scalar_tensor_tensor(
                out=x_sb[:, m, :N], in0=ps[:, :N], scalar=gf[:, mi, m:m + 1], in1=x_sb[:, m, :N], op0=ALU.mult, op1=ALU.add),
                 reads=[pk, xk, "gf"], writes=[xk])
            oi += 1
        p.dma("pool", fm(XT, t0, N), x_sb[:, :, :N], reads=[xk], writes=["XT"])


def ph_final(p, XT, g_d, tiles, OUT, t_off, IDENT):
    nrm = Norm(p, 512, 1, g_d, [(None, None)])
    ident = p.sb([128, 128], F32, name="ident")
    p.dma("sp", ident[:, :], IDENT, writes=["ident"])
    xs = [p.sb([128, 8, 512], F32, name=f"xf{i}") for i in range(2)]
    hf = p.sb([128, 8, 512], F32, name="hf")
    pst = [p.ps([128, 512], name=f"pst{i}") for i in range(4)]
    ot = [p.sb([128, 1024], F32, name=f"ot{i}") for i in range(2)]
    oi = 0
    k = 0
    for ti, (t0, N, mi) in enumerate(tiles):
        x_sb, xk = xs[ti % 2], f"xf{ti%2}"
        p.dma("sp", x_sb[:, :, :N], fm(XT, t0, N), reads=["XT"], writes=[xk])
        nrm.apply(x_sb, xk, N, 0, hf, "hf")
        for j in range(N // 128):
            o, ok = ot[k % 2], f"ot{k%2}"
            for half in range(2):
                ps, pk = pst[oi % 4], f"pst{oi%4}"
                for cc in range(4):
                    c = half * 4 + cc
                    p.op("pe", lambda e, ps=ps, c=c, cc=cc, j=j: e.transpose(out=ps[:, cc * 128:(cc + 1) * 128],
                                                                            in_=hf[:, c, j * 128:(j + 1) * 128], identity=ident[:, :]),
                         reads=["hf", "ident"], writes=[pk])
                if half == 0:
                    p.op("act", lambda e, ps=ps, o=o: e.copy(out=o[:, 0:512], in_=ps[:, :]), reads=[pk], writes=[ok])
                else:
                    p.op("dve", lambda e, ps=ps, o=o: e.tensor_copy(out=o[:, 512:1024], in_=ps[:, :]), reads=[pk], writes=[ok])
                oi += 1
            tt = t0 + j * 128 - t_off
            p.dma("pool", OUT[tt:tt + 128, :], o[:, :], reads=[ok], writes=["OUT"])
            k += 1


TWO_PI = 2.0 * math.pi


def sin_tmps(p, shape, tag):
    return (p.sb(shape, F32, name=tag + "_a"), p.sb(shape, I32, name=tag + "_i"), p.sb(shape, F32, name=tag + "_f"),
            p.sb(shape, F32, name=tag + "_r"), p.sb(shape, F32, name=tag + "_m"))


def sin_rr(p, out, x, shape, tag, shift=0.0, tmps=None):
    a, ai, af, r, m = tmps if tmps is not None else sin_tmps(p, shape, tag)
    sl = tuple(slice(None) for _ in shape)
    k = tag
    p.op("dve", lambda e: e.tensor_scalar(out=a[sl], in0=x, scalar1=shift, scalar2=1.0 / TWO_PI, op0=ALU.add, op1=ALU.mult),
         reads=[k + "x"], writes=[k + "a"])
    p.op("dve", lambda e: e.tensor_copy(out=ai[sl], in_=a[sl]), reads=[k + "a"], writes=[k + "i"])
    p.op("dve", lambda e: e.tensor_copy(out=af[sl], in_=ai[sl]), reads=[k + "i"], writes=[k + "f"])
    p.op("dve", lambda e: e.tensor_scalar(out=r[sl], in0=x, scalar1=shift, scalar2=None, op0=ALU.add), reads=[k + "x"], writes=[k + "r"])
    p.op("dve", lambda e: e.scalar_tensor_tensor(out=r[sl], in0=af[sl], scalar=-TWO_PI, in1=r[sl], op0=ALU.mult, op1=ALU.add),
         reads=[k + "f", k + "r"], writes=[k + "r"])
    p.op("dve", lambda e: e.tensor_scalar(out=m[sl], in0=r[sl], scalar1=math.pi, scalar2=None, op0=ALU.is_gt), reads=[k + "r"], writes=[k + "m"])
    p.op("dve", lambda e: e.scalar_tensor_tensor(out=r[sl], in0=m[sl], scalar=-TWO_PI, in1=r[sl], op0=ALU.mult, op1=ALU.add),
         reads=[k + "m", k + "r"], writes=[k + "r"])
    p.op("dve", lambda e: e.tensor_scalar(out=m[sl], in0=r[sl], scalar1=-math.pi, scalar2=None, op0=ALU.is_lt), reads=[k + "r"], writes=[k + "m"])
    p.op("dve", lambda e: e.scalar_tensor_tensor(out=r[sl], in0=m[sl], scalar=TWO_PI, in1=r[sl], op0=ALU.mult, op1=ALU.add),
         reads=[k + "m", k + "r"], writes=[k + "r"])
    p.op("dve", lambda e: e.tensor_scalar(out=r[sl], in0=r[sl], scalar1=3.1415925, scalar2=-3.1415925, op0=ALU.min, op1=ALU.max),
         reads=[k + "r"], writes=[k + "r"])
    p.op("act", lambda e: e.activation(out=out, in_=r[sl], func=AF.Sin), reads=[k + "r"], writes=[k + "o"])


def s5_params(p, src3, shape, tag):
    sl = tuple(slice(None) for _ in shape)
    t = {n: p.sb(shape, F32, name=f"{tag}_{n}") for n in
         ("lre", "lim", "ldt", "step", "r", "th", "c", "s", "ar", "ai", "den", "fr", "fi", "t1", "t2")}
    k = tag
    p.dma("sp", t["lre"][sl], src3[0], writes=[k])
    p.dma("sp", t["lim"][sl], src3[1], writes=[k])
    p.dma("sp", t["ldt"][sl], src3[2], writes=[k])

    def dve(fn):
        p.op("dve", fn, reads=[k], writes=[k])

    p.op("act", lambda e: e.activation(out=t["step"][sl], in_=t["ldt"][sl], func=AF.Exp), reads=[k], writes=[k])
    dve(lambda e: e.tensor_tensor(out=t["t1"][sl], in0=t["lre"][sl], in1=t["step"][sl], op=ALU.mult))
    p.op("act", lambda e: e.activation(out=t["r"][sl], in_=t["t1"][sl], func=AF.Exp), reads=[k], writes=[k])
    dve(lambda e: e.tensor_tensor(out=t["th"][sl], in0=t["lim"][sl], in1=t["step"][sl], op=ALU.mult))
    p.op("dve", lambda e: e.tensor_copy(out=t["t2"][sl], in_=t["th"][sl]), reads=[k], writes=[k + "sx", k + "cx"])
    sin_rr(p, t["s"][sl], t["th"][sl], shape, k + "s")
    sin_rr(p, t["c"][sl], t["th"][sl], shape, k + "c", shift=math.pi / 2)
    p.op("dve", lambda e: e.tensor_tensor(out=t["ar"][sl], in0=t["r"][sl], in1=t["c"][sl], op=ALU.mult), reads=[k, k + "co"], writes=[k])
    p.op("dve", lambda e: e.tensor_tensor(out=t["ai"][sl], in0=t["r"][sl], in1=t["s"][sl], op=ALU.mult), reads=[k, k + "so"], writes=[k])
    dve(lambda e: e.tensor_tensor(out=t["den"][sl], in0=t["lre"][sl], in1=t["lre"][sl], op=ALU.mult))
    dve(lambda e: e.tensor_tensor(out=t["t1"][sl], in0=t["lim"][sl], in1=t["lim"][sl], op=ALU.mult))
    dve(lambda e: e.tensor_tensor(out=t["den"][sl], in0=t["den"][sl], in1=t["t1"][sl], op=ALU.add))
    dve(lambda e: e.reciprocal(out=t["den"][sl], in_=t["den"][sl]))
    dve(lambda e: e.tensor_scalar(out=t["t1"][sl], in0=t["ar"][sl], scalar1=-1.0, scalar2=None, op0=ALU.add))
    dve(lambda e: e.tensor_tensor(out=t["fr"][sl], in0=t["t1"][sl], in1=t["lre"][sl], op=ALU.mult))
    dve(lambda e: e.tensor_tensor(out=t["t2"][sl], in0=t["ai"][sl], in1=t["lim"][sl], op=ALU.mult))
    dve(lambda e: e.tensor_tensor(out=t["fr"][sl], in0=t["fr"][sl], in1=t["t2"][sl], op=ALU.add))
    dve(lambda e: e.tensor_tensor(out=t["fr"][sl], in0=t["fr"][sl], in1=t["den"][sl], op=ALU.mult))
    dve(lambda e: e.tensor_tensor(out=t["fi"][sl], in0=t["ai"][sl], in1=t["lre"][sl], op=ALU.mult))
    dve(lambda e: e.tensor_tensor(out=t["t2"][sl], in0=t["t1"][sl], in1=t["lim"][sl], op=ALU.mult))
    dve(lambda e: e.tensor_tensor(out=t["fi"][sl], in0=t["fi"][sl], in1=t["t2"][sl], op=ALU.subtract))
    dve(lambda e: e.tensor_tensor(out=t["fi"][sl], in0=t["fi"][sl], in1=t["den"][sl], op=ALU.mult))
    return t


def windows(T, NCTX, W=512):
    fw_ = [(0, NCTX)] + [(t, min(t + W, T)) for t in range(NCTX, T, W)]
    bw = [(0, NCTX)] + [(max(t - W, NCTX), t) for t in range(T, NCTX, -W)]
    return fw_, bw


def ph_s5_scan(p, PROJ, T, NCTX, LS, BT, CT, DSK, YPRE):
    W = 512
    ls = s5_params(p, LS, [128, 32], "ls")
    bt = p.sb([32, 2, 16, 128], F32, name="bt")
    p.dma("sp", bt[:, 0], BT[0], writes=["bt"])
    p.dma("sp", bt[:, 1], BT[1], writes=["bt"])
    ct = p.sb([128, 2, 16, 32], F32, name="ct")
    p.dma("sp", ct[:, 0], CT[0], writes=["ct"])
    p.dma("sp", ct[:, 1], CT[1], writes=["ct"])
    p.op("dve", lambda e: e.tensor_scalar(out=ct[:, 1], in0=ct[:, 1], scalar1=-1.0, scalar2=None, op0=ALU.mult), reads=["ct"], writes=["ct"])
    dsk = p.sb([32, 16], F32, name="dsk")
    p.dma("sp", dsk[:, :], DSK.rearrange("(g q) -> q g", q=32), writes=["dsk"], allow_slow_non_contiguous=True)
    nsc = p.sb([128, 32], F32, name="nsc")
    p.op("dve", lambda e: e.tensor_scalar(out=nsc[:, :], in0=ls["s"][:, :], scalar1=-1.0, scalar2=None, op0=ALU.mult), reads=["ls", "lsso"], writes=["nsc"])

    F32R = mybir.dt.float32r
    ctr = p.sb([128, 3, 16, 32], F32, name="ctr")
    p.op("dve", lambda e: e.tensor_copy(out=ctr[:, 0].bitcast(F32R), in_=ct[:, 0]), reads=["ct"], writes=["ctr"])
    p.op("dve", lambda e: e.tensor_scalar(out=ctr[:, 1].bitcast(F32R), in0=ct[:, 0], scalar1=-1.0, scalar2=None, op0=ALU.mult), reads=["ct"], writes=["ctr"])
    p.op("dve", lambda e: e.tensor_copy(out=ctr[:, 2].bitcast(F32R), in_=ct[:, 1]), reads=["ct"], writes=["ctr"])
    ones = p.sb([128, W], F32, name="s5ones")
    p.op("pool", lambda e: e.memset(ones[:, :], 1.0), writes=["s5ones"])
    mkd = lambda n: [p.sb([128, W], F32, name=f"{n}{i}") for i in range(2)]
    Ec, Es, rt, Mc, Ms = mkd("Ec"), mkd("Es"), mkd("rt"), mkd("Mc"), mkd("Ms")
    wv = p.sb([128, 4], F32, name="wv")
    wN = [p.sb([128, 2, 3], F32, name=f"wN{i}") for i in range(2)]
    tt = p.sb([128, W], F32, name="ttab")
    u_sb = p.sb([32, T], F32, name="u_sb")
    ysum = p.sb([32, T], F32, name="ysum")
    psA = [p.ps([128, 512], name=f"psA{i}") for i in range(2)]
    psB = [p.ps([128, 512], name=f"psB{i}") for i in range(2)]
    psY = [p.ps([128, 512], name=f"psY{i}") for i in range(2)]
    ta, tb_, tc, td = mkd("ta"), mkd("tb"), mkd("tc"), mkd("td")
    mre, mim = mkd("mre"), mkd("mim")
    gre = [[p.sb([128, W], F32, name=f"gre{i}{j}") for j in range(2)] for i in range(2)]
    gim = [[p.sb([128, W], F32, name=f"gim{i}{j}") for j in range(2)] for i in range(2)]
    q1, q2, q3, q4 = mkd("q1"), mkd("q2"), mkd("q3"), mkd("q4")
    init = [p.sb([128, 4], F32, name=f"init{i}") for i in range(2)]
    ysb = [p.sb([32, W], F32, name=f"ysb{i}") for i in range(2)]
    fwin, bwin = windows(T, NCTX, W)
    assert len(fwin) == len(bwin)
    for gp in range(16):
        p.dma("sp", u_sb[:, :], PROJ[gp * 32:(gp + 1) * 32, :], reads=["PROJ"], writes=["u_sb"])
        p.op("pool", lambda e: e.memset(ysum[:, :], 0.0), writes=["ysum"])
        for d in range(2):
            cix = d * 16 + gp
            cs, sn, rr = ls["c"][:, cix:cix + 1], ls["s"][:, cix:cix + 1], ls["r"][:, cix:cix + 1]
            EC, ES, MC, MS, RT, WN, ek, mk_ = Ec[d], Es[d], Mc[d], Ms[d], rt[d], wN[d], f"E{d}", f"M{d}"
            p.op("dve", lambda e, EC=EC: e.memset(EC[:, 0:1], 1.0), writes=[ek])
            p.op("dve", lambda e, ES=ES: e.memset(ES[:, 0:1], 0.0), writes=[ek])
            p.op("dve", lambda e, cs=cs: e.tensor_copy(out=wv[:, 0:1], in_=cs), reads=["ls", "lsco"], writes=["wv"])
            p.op("dve", lambda e, sn=sn: e.tensor_copy(out=wv[:, 1:2], in_=sn), reads=["ls", "lsso"], writes=["wv"])
            m = 1
            while m < W:
                p.op("dve", lambda e: e.tensor_scalar(out=wv[:, 2:3], in0=wv[:, 1:2], scalar1=-1.0, scalar2=None, op0=ALU.mult), reads=["wv"], writes=["wv"])
                if m == 256:
                    p.op("dve", lambda e, WN=WN: e.tensor_copy(out=WN[:, 0, :], in_=wv[:, 0:3]), reads=["wv"], writes=[f"wN{d}"])
                p.op("dve", lambda e, m=m, EC=EC: e.tensor_scalar(out=tt[:, 0:m], in0=EC[:, 0:m], scalar1=wv[:, 0:1], scalar2=None, op0=ALU.mult), reads=[ek, "wv"], writes=["ttab"])
                p.op("dve", lambda e, m=m, EC=EC, ES=ES: e.scalar_tensor_tensor(out=EC[:, m:2 * m], in0=ES[:, 0:m], scalar=wv[:, 2:3], in1=tt[:, 0:m], op0=ALU.mult, op1=ALU.add),
                     reads=[ek, "wv", "ttab"], writes=[ek])
                p.op("dve", lambda e, m=m, EC=EC: e.tensor_scalar(out=tt[:, 0:m], in0=EC[:, 0:m], scalar1=wv[:, 1:2], scalar2=None, op0=ALU.mult), reads=[ek, "wv"], writes=["ttab"])
                p.op("dve", lambda e, m=m, ES=ES: e.scalar_tensor_tensor(out=ES[:, m:2 * m], in0=ES[:, 0:m], scalar=wv[:, 0:1], in1=tt[:, 0:m], op0=ALU.mult, op1=ALU.add),
                     reads=[ek, "wv", "ttab"], writes=[ek])
                p.op("dve", lambda e: e.tensor_tensor(out=wv[:, 3:4], in0=wv[:, 1:2], in1=wv[:, 1:2], op=ALU.mult), reads=["wv"], writes=["wv"])
                p.op("dve", lambda e: e.scalar_tensor_tensor(out=wv[:, 1:2], in0=wv[:, 1:2], scalar=2.0, in1=wv[:, 0:1], op0=ALU.mult, op1=ALU.mult), reads=["wv"], writes=["wv"])
                p.op("dve", lambda e: e.tensor_tensor(out=wv[:, 0:1], in0=wv[:, 0:1], in1=wv[:, 0:1], op=ALU.mult), reads=["wv"], writes=["wv"])
                p.op("dve", lambda e: e.tensor_tensor(out=wv[:, 0:1], in0=wv[:, 0:1], in1=wv[:, 3:4], op=ALU.subtract), reads=["wv"], writes=["wv"])
                m *= 2
            p.op("dve", lambda e: e.tensor_scalar(out=wv[:, 2:3], in0=wv[:, 1:2], scalar1=-1.0, scalar2=None, op0=ALU.mult), reads=["wv"], writes=["wv"])
            p.op("dve", lambda e, WN=WN: e.tensor_copy(out=WN[:, 1, :], in_=wv[:, 0:3]), reads=["wv"], writes=[f"wN{d}"])
            frc, fic = ls["fr"][:, cix:cix + 1], ls["fi"][:, cix:cix + 1]
            p.op("dve", lambda e, fic=fic, ES=ES: e.tensor_scalar(out=tt[:, :], in0=ES[:, :], scalar1=fic, scalar2=None, op0=ALU.mult), reads=[ek, "ls"], writes=["ttab"])
            p.op("dve", lambda e, frc=frc, EC=EC, MC=MC: e.scalar_tensor_tensor(out=MC[:, :], in0=EC[:, :], scalar=frc, in1=tt[:, :], op0=ALU.mult, op1=ALU.add), reads=[ek, "ls", "ttab"], writes=[mk_])
            p.op("dve", lambda e, fic=fic, EC=EC: e.tensor_scalar(out=tt[:, :], in0=EC[:, :], scalar1=fic, scalar2=None, op0=ALU.mult), reads=[ek, "ls", mk_], writes=["ttab"])
            p.op("dve", lambda e, frc=frc, ES=ES, MS=MS: e.scalar_tensor_tensor(out=MS[:, :], in0=ES[:, :], scalar=frc, in1=tt[:, :], op0=ALU.mult, op1=ALU.subtract), reads=[ek, "ls", "ttab"], writes=[mk_])
            p.op("dve", lambda e, rr=rr, RT=RT: e.tensor_scalar(out=RT[:, :], in0=ones[:, :], scalar1=rr, scalar2=None, op0=ALU.mult), reads=["ls", "s5ones"], writes=[f"rt{d}"])
        prevs = [None, None]
        pending = []
        for wi_ in range(len(fwin)):
            for d in range(2):
                lo, hi = (fwin if d == 0 else bwin)[wi_]
                N = hi - lo
                sl = slice(lo, hi) if d == 0 else slice(hi - 1, lo - 1 if lo > 0 else None, -1)
                p.op("pe", lambda e, A=psA[d], sl=sl, N=N, gp=gp: e.matmul(A[:, :N], lhsT=bt[:, 0, gp, :], rhs=u_sb[:, sl], start=True, stop=True),
                     reads=["bt", "u_sb"], writes=[f"psA{d}"])
                p.op("pe", lambda e, B=psB[d], sl=sl, N=N, gp=gp: e.matmul(B[:, :N], lhsT=bt[:, 1, gp, :], rhs=u_sb[:, sl], start=True, stop=True),
                     reads=["bt", "u_sb"], writes=[f"psB{d}"])
            for fn in pending:
                fn()
            pending = []
            for d in range(2):
                lo, hi = (fwin if d == 0 else bwin)[wi_]
                N = hi - lo
                sl = slice(lo, hi) if d == 0 else slice(hi - 1, lo - 1 if lo > 0 else None, -1)
                EC, ES, MC, MS, RT, WN, ek, mk_ = Ec[d], Es[d], Mc[d], Ms[d], rt[d], wN[d], f"E{d}", f"M{d}"
                b2 = d
                A, B, Yp = psA[b2], psB[b2], psY[b2]
                ak, bk, yk = f"psA{b2}", f"psB{b2}", f"psY{b2}"
                INIT, ik = init[d], f"init{d}"
                TA, TB, TC, TD = ta[b2], tb_[b2], tc[b2], td[b2]
                par = wi_ % 2
                MR, MI, GR, GI = mre[b2], mim[b2], gre[b2][par], gim[b2][par]
                PGR, PGI = gre[b2][1 - par], gim[b2][1 - par]
                gk, gik, pgk, pgik = f"gre{b2}{par}", f"gim{b2}{par}", f"gre{b2}{1-par}", f"gim{b2}{1-par}"
                prev = prevs[d]
                if prev is None:
                    p.op("dve", lambda e, INIT=INIT: e.memset(INIT[:, :], 0.0), writes=[ik])
                else:
                    pN = prev
                    wsel = 0 if pN == 256 else 1
                    assert pN in (256, 512)
                    p.op("dve", lambda e, GR=PGR, pN=pN, wsel=wsel, INIT=INIT, WN=WN: e.tensor_scalar(out=INIT[:, 2:3], in0=GR[:, pN - 1:pN], scalar1=WN[:, wsel, 0:1], scalar2=None, op0=ALU.mult),
                         reads=[pgk, f"wN{d}"], writes=[ik])
                    p.op("dve", lambda e, GI=PGI, pN=pN, wsel=wsel, INIT=INIT, WN=WN: e.scalar_tensor_tensor(out=INIT[:, 0:1], in0=GI[:, pN - 1:pN], scalar=WN[:, wsel, 2:3], in1=INIT[:, 2:3], op0=ALU.mult, op1=ALU.add),
                         reads=[pgik, f"wN{d}", ik], writes=[ik])
                    p.op("dve", lambda e, GR=PGR, pN=pN, wsel=wsel, INIT=INIT, WN=WN: e.tensor_scalar(out=INIT[:, 3:4], in0=GR[:, pN - 1:pN], scalar1=WN[:, wsel, 1:2], scalar2=None, op0=ALU.mult),
                         reads=[pgk, f"wN{d}"], writes=[ik])
                    p.op("dve", lambda e, GI=PGI, pN=pN, wsel=wsel, INIT=INIT, WN=WN: e.scalar_tensor_tensor(out=INIT[:, 1:2], in0=GI[:, pN - 1:pN], scalar=WN[:, wsel, 0:1], in1=INIT[:, 3:4], op0=ALU.mult, op1=ALU.add),
                         reads=[pgik, f"wN{d}", ik], writes=[ik])
                p.op("dve", lambda e, A=A, N=N, TA=TA, MC=MC: e.tensor_tensor(out=TA[:, :N], in0=A[:, :N], in1=MC[:, :N], op=ALU.mult), reads=[ak, mk_], writes=[f"ta{b2}"])
                p.op("dve", lambda e, B=B, N=N, TB=TB, MS=MS: e.tensor_tensor(out=TB[:, :N], in0=B[:, :N], in1=MS[:, :N], op=ALU.mult), reads=[bk, mk_], writes=[f"tb{b2}"])
                p.op("dve", lambda e, N=N, TA=TA, TB=TB, MR=MR: e.tensor_tensor(out=MR[:, :N], in0=TA[:, :N], in1=TB[:, :N], op=ALU.add),
                     reads=[f"ta{b2}", f"tb{b2}"], writes=[f"mre{b2}"])
                p.op("dve", lambda e, B=B, N=N, TC=TC, MC=MC: e.tensor_tensor(out=TC[:, :N], in0=B[:, :N], in1=MC[:, :N], op=ALU.mult), reads=[bk, mk_], writes=[f"tc{b2}"])
                p.op("dve", lambda e, A=A, N=N, TD=TD, MS=MS: e.tensor_tensor(out=TD[:, :N], in0=A[:, :N], in1=MS[:, :N], op=ALU.mult), reads=[ak, mk_], writes=[f"td{b2}"])
                p.op("dve", lambda e, N=N, TC=TC, TD=TD, MI=MI: e.tensor_tensor(out=MI[:, :N], in0=TC[:, :N], in1=TD[:, :N], op=ALU.subtract),
                     reads=[f"tc{b2}", f"td{b2}"], writes=[f"mim{b2}"])
                p.op("dve", lambda e, N=N, GR=GR, MR=MR, RT=RT, INIT=INIT: e.tensor_tensor_scan(out=GR[:, :N], data0=RT[:, :N], data1=MR[:, :N], initial=INIT[:, 0:1], op0=ALU.mult, op1=ALU.add),
                     reads=[f"rt{d}", f"mre{b2}", ik], writes=[gk])
                p.op("dve", lambda e, N=N, GI=GI, MI=MI, RT=RT, INIT=INIT: e.tensor_tensor_scan(out=GI[:, :N], data0=RT[:, :N], data1=MI[:, :N], initial=INIT[:, 1:2], op0=ALU.mult, op1=ALU.add),
                     reads=[f"rt{d}", f"mim{b2}", ik], writes=[gik])
                Q1, Q2, Q3, Q4 = q1[b2], q2[b2], q3[b2], q4[b2]
                for (Q, G, Et, qn, gkk) in ((Q1, GR, EC, "q1", gk), (Q2, GI, ES, "q2", gik), (Q3, GR, ES, "q3", gk), (Q4, GI, EC, "q4", gik)):
                    p.op("pool", lambda e, N=N, Q=Q, G=G, Et=Et: e.tensor_tensor(out=Q[:, :N].bitcast(F32R), in0=G[:, :N], in1=Et[:, :N], op=ALU.mult),
                         reads=[gkk, ek], writes=[f"{qn}{b2}"])
                def cstage(Yp=Yp, yk=yk, N=N, gp=gp, b2=b2, sl=sl, Qs=(Q1, Q2, Q3, Q4), YS=ysb[b2]):
                    for k, (Q, ci, qn) in enumerate(((Qs[0], 0, "q1"), (Qs[1], 1, "q2"), (Qs[2], 2, "q3"), (Qs[3], 2, "q4"))):
                        p.op("pe", lambda e, Q=Q, ci=ci, k=k: e.matmul(Yp[:32, :N], lhsT=ctr[:, ci, gp, :].bitcast(F32R), rhs=Q[:, :N].bitcast(F32R),
                                                                         start=(k == 0), stop=(k == 3)), reads=["ctr", f"{qn}{b2}"], writes=[yk])
                    p.op("act", lambda e: e.copy(out=YS[:32, :N], in_=Yp[:32, :N]), reads=[yk], writes=[f"ysb{b2}"])
                    p.op("pool", lambda e: e.tensor_tensor(out=ysum[:, sl], in0=YS[:32, :N], in1=ysum[:, sl], op=ALU.add),
                         reads=[f"ysb{b2}", "ysum"], writes=["ysum"])
                pending.append(cstage)
                prevs[d] = N
        for fn in pending:
            fn()
        pending = []
        p.op("dve", lambda e, gp=gp: e.scalar_tensor_tensor(out=ysum[:, :], in0=u_sb[:, :], scalar=dsk[:, gp:gp + 1], in1=ysum[:, :], op0=ALU.mult, op1=ALU.add),
             reads=["u_sb", "ysum", "dsk"], writes=["ysum"])
        p.dma("pool", YPRE[gp * 32:(gp + 1) * 32, :], ysum[:, :], reads=["ysum"], writes=["YPRE"])


def ph_s5_glu(p, YPRE, T, GW, GB, Y):
    N = 512
    K0 = 0.7978845608028654
    gw = load_weight_bf16(p, GW, 512, 512, "gw", chunk=512)
    gb = p.sb([128, 4], F32, name="gb")
    load_cols(p, gb[:, :], GB, "gb")
    ys = [p.sb([128, 4, N], F32, name=f"yp{i}") for i in range(2)]
    x2 = p.sb([128, 4, N], F32, name="x2")
    g = p.sb([128, 4, N], F32, name="gg")
    gbf = p.sb([128, 4, N], BF16, name="gbf")
    sg = p.sb([128, N], F32, name="sg")
    o = [p.sb([128, N], BF16, name=f"og{i}") for i in range(2)]
    pss = [p.ps([128, 512], name=f"psg{i}") for i in range(2)]
    oi = 0
    for ti, t0 in enumerate(range(0, T, N)):
        n = min(N, T - t0)
        y, yk = ys[ti % 2], f"yp{ti%2}"
        p.dma("sp", y[:, :, :n], fm(YPRE, t0, n), reads=["YPRE"], writes=[yk])
        p.op("pool", lambda e, y=y, n=n: e.tensor_tensor(out=x2[:, :, :n], in0=y[:, :, :n], in1=y[:, :, :n], op=ALU.mult), reads=[yk], writes=["x2"])
        p.op("dve", lambda e, n=n: e.tensor_scalar(out=x2[:, :, :n], in0=x2[:, :, :n], scalar1=0.044715, scalar2=1.0, op0=ALU.mult, op1=ALU.add), reads=["x2"], writes=["x2"])
        p.op("pool", lambda e, y=y, n=n: e.tensor_tensor(out=x2[:, :, :n], in0=x2[:, :, :n], in1=y[:, :, :n], op=ALU.mult), reads=[yk, "x2"], writes=["x2"])
        p.op("act", lambda e, n=n: e.activation(out=x2[:, :, :n], in_=x2[:, :, :n], func=AF.Sigmoid, scale=2.0 * K0), reads=["x2"], writes=["x2"])
        p.op("dve", lambda e, y=y, n=n: e.tensor_tensor(out=g[:, :, :n], in0=x2[:, :, :n], in1=y[:, :, :n], op=ALU.mult), reads=[yk, "x2"], writes=["gg"])
        p.op("act", lambda e, n=n: e.copy(out=gbf[:, :, :n], in_=g[:, :, :n]), reads=["gg"], writes=["gbf"])
        for j in range(4):
            ps, pk, oo, ok = pss[oi % 2], f"psg{oi%2}", o[oi % 2], f"og{oi%2}"
            for c in range(4):
                p.op("pe", lambda e, ps=ps, c=c, j=j, n=n: e.matmul(ps[:, :n], lhsT=gw[:, c, j * 128:(j + 1) * 128], rhs=gbf[:, c, :n], start=(c == 0), stop=(c == 3)),
                     reads=["gw", "gbf"], writes=[pk])
            p.op("act", lambda e, ps=ps, j=j, n=n: e.activation(out=sg[:, :n], in_=ps[:, :n], func=AF.Sigmoid, bias=gb[:, j:j + 1], scale=1.0),
                 reads=[pk, "gb"], writes=["sg"])
            p.op("dve", lambda e, oo=oo, j=j, n=n: e.tensor_tensor(out=oo[:, :n], in0=sg[:, :n], in1=g[:, j, :n], op=ALU.mult), reads=["sg", "gg"], writes=[ok])
            p.dma("pool", Y[j * 128:(j + 1) * 128, t0:t0 + n], oo[:, :n], reads=[ok], writes=["Y"])
            oi += 1


NH = 16


def ph_conv_silu(p, SRC, row0, nrows, T, segs, CW, CB, DST, silu=True):
    nch = nrows // 128
    w = p.sb([128, 3, nch], F32, name="cw")
    for k in range(3):
        load_cols(p, w[:, k, :], CW[k], "cw")
    b = p.sb([128, nch], F32, name="cb")
    load_cols(p, b[:, :], CB, "cw")
    xs = [p.sb([128, T], F32, name=f"cx{i}") for i in range(2)]
    acc = [p.sb([128, T], F32, name=f"ca{i}") for i in range(2)]
    for c in range(nch):
        x, xk, a, ak = xs[c % 2], f"cx{c%2}", acc[c % 2], f"ca{c%2}"
        p.dma("sp", x[:, :], SRC[row0 + c * 128:row0 + (c + 1) * 128, :], reads=["SRC"], writes=[xk])
        p.op("dve", lambda e, x=x, a=a, c=c: e.tensor_scalar(out=a[:, :], in0=x[:, :], scalar1=w[:, 1, c:c + 1], scalar2=b[:, c:c + 1], op0=ALU.mult, op1=ALU.add),
             reads=[xk, "cw"], writes=[ak])
        for (s0, s1) in segs:
            p.op("dve", lambda e, x=x, a=a, c=c, s0=s0, s1=s1: e.scalar_tensor_tensor(out=a[:, s0 + 1:s1], in0=x[:, s0:s1 - 1], scalar=w[:, 0, c:c + 1], in1=a[:, s0 + 1:s1],
                                                                                      op0=ALU.mult, op1=ALU.add), reads=[xk, "cw", ak], writes=[ak])
            p.op("dve", lambda e, x=x, a=a, c=c, s0=s0, s1=s1: e.scalar_tensor_tensor(out=a[:, s0:s1 - 1], in0=x[:, s0 + 1:s1], scalar=w[:, 2, c:c + 1], in1=a[:, s0:s1 - 1],
                                                                                      op0=ALU.mult, op1=ALU.add), reads=[xk, "cw", ak], writes=[ak])
        if silu:
            p.op("act", lambda e, a=a: e.activation(out=a[:, :], in_=a[:, :], func=AF.Silu), reads=[ak], writes=[ak])
        p.dma("pool", DST[c * 128:(c + 1) * 128, :], a[:, :], reads=[ak], writes=["DST"])


def ph_ssd_prep(p, PROJ, T, DTB, ALOG, SC, CF2, ET):
    TB = 1408
    NCB = TB // 128
    dtb = p.sb([32, 1], F32, name="dtb")
    p.dma("sp", dtb[:, :], DTB.rearrange("(p o) -> p o", o=1), writes=["dtb"], allow_slow_non_contiguous=True)
    al = p.sb([32, 1], F32, name="al")
    p.dma("sp", al[:, :], ALOG.rearrange("(p o) -> p o", o=1), writes=["al"], allow_slow_non_contiguous=True)
    p.op("act", lambda e: e.activation(out=al[:, :], in_=al[:, :], func=AF.Exp), reads=["al"], writes=["al"])
    p.op("dve", lambda e: e.tensor_scalar(out=al[:, :], in0=al[:, :], scalar1=-1.0, scalar2=None, op0=ALU.mult), reads=["al"], writes=["al"])
    names = ["dt", "a", "pin", "sin", "cx", "ncx", "npin", "e1", "e2", "e3", "e4", "m0", "m1"]
    t = {n: p.sb([32, TB], F32, name="pp_" + n) for n in names}
    et = p.sb([32, NCB], F32, name="et")

    def A(n):
        return t[n][:, :]

    p.op("pool", lambda e: e.memset(A("m0"), 1.0), writes=["m0"])
    p.op("pool", lambda e: e.memset(A("m1"), 1.0), writes=["m1"])
    m0v = A("m0").rearrange("p (c q) -> p c q", q=128)
    m1v = A("m1").rearrange("p (c q) -> p c q", q=128)
    p.op("pool", lambda e: e.memset(m0v[:, :, 0:1], 0.0), reads=["m0"], writes=["m0"])
    p.op("pool", lambda e: e.memset(m1v[:, :, 127:128], 0.0), reads=["m1"], writes=["m1"])
    for b0 in range(0, T, TB):
        ts = slice(b0, b0 + TB)
        p.dma("sp", A("dt"), PROJ[3584:3616, ts], reads=["PROJ"], writes=["dt"])
        p.op("act", lambda e: e.activation(out=A("dt"), in_=A("dt"), func=AF.Exp, bias=dtb[:, 0:1], scale=1.0), reads=["dt", "dtb"], writes=["dt"])
        p.op("act", lambda e: e.activation(out=A("dt"), in_=A("dt"), func=AF.Ln, bias=1.0, scale=1.0), reads=["dt"], writes=["dt"])
        p.op("dve", lambda e: e.tensor_scalar(out=A("a"), in0=A("dt"), scalar1=al[:, 0:1], scalar2=None, op0=ALU.mult), reads=["dt", "al"], writes=["a"])
        p.op("dve", lambda e: e.tensor_tensor_scan(out=A("pin"), data0=A("m0"), data1=A("a"), initial=0.0, op0=ALU.mult, op1=ALU.add),
             reads=["a", "m0"], writes=["pin"])
        p.op("dve", lambda e: e.tensor_tensor_scan(out=t["sin"][:, ::-1], data0=t["m1"][:, ::-1], data1=t["a"][:, ::-1], initial=0.0, op0=ALU.mult, op1=ALU.add),
             reads=["a", "m1"], writes=["sin"])
        p.dma("pool", SC[0:32, ts], A("dt"), reads=["dt"], writes=["SC"])
        p.op("dve", lambda e: e.tensor_tensor(out=A("e1"), in0=A("sin"), in1=A("a"), op=ALU.subtract), reads=["sin", "a"], writes=["e1"])
        p.op("act", lambda e: e.activation(out=A("e1"), in_=A("e1"), func=AF.Exp), reads=["e1"], writes=["e1"])
        p.op("dve", lambda e: e.tensor_tensor(out=A("e1"), in0=A("e1"), in1=A("dt"), op=ALU.mult), reads=["e1", "dt"], writes=["e1"])
        p.dma("pool", SC[32:48, ts], t["e1"][0:16, :], reads=["e1"], writes=["SC"])
        p.op("dve", lambda e: e.tensor_tensor(out=A("cx"), in0=A("pin"), in1=A("a"), op=ALU.subtract), reads=["pin", "a"], writes=["cx"])
        p.op("act", lambda e: e.activation(out=A("e2"), in_=A("cx"), func=AF.Exp), reads=["cx"], writes=["e2"])
        p.op("dve", lambda e: e.tensor_tensor(out=A("e2"), in0=A("e2"), in1=A("dt"), op=ALU.mult), reads=["e2", "dt"], writes=["e2"])
        p.dma("pool", SC[48:64, ts], t["e2"][16:32, :], reads=["e2"], writes=["SC"])
        p.op("act", lambda e: e.activation(out=A("e3"), in_=A("pin"), func=AF.Exp), reads=["pin"], writes=["e3"])
        p.dma("pool", SC[64:80, ts], t["e3"][0:16, :], reads=["e3"], writes=["SC"])
        p.op("act", lambda e: e.activation(out=A("e4"), in_=A("sin"), func=AF.Exp), reads=["sin"], writes=["e4"])
        p.dma("pool", SC[80:96, ts], t["e4"][16:32, :], reads=["e4"], writes=["SC"])
        p.op("dve", lambda e: e.tensor_scalar(out=A("ncx"), in0=A("cx"), scalar1=-1.0, scalar2=None, op0=ALU.mult), reads=["cx"], writes=["ncx"])
        p.op("dve", lambda e: e.tensor_scalar(out=A("npin"), in0=A("pin"), scalar1=-1.0, scalar2=None, op0=ALU.mult), reads=["pin"], writes=["npin"])
        p.dma("pool", CF2[0, 0:16, ts], t["pin"][0:16, :], reads=["pin"], writes=["CF2"])
        p.dma("pool", CF2[0, 16:32, ts], t["ncx"][16:32, :], reads=["ncx"], writes=["CF2"])
        p.dma("pool", CF2[1, 0:16, ts], t["npin"][0:16, :], reads=["npin"], writes=["CF2"])
        p.dma("pool", CF2[1, 16:32, ts], t["cx"][16:32, :], reads=["cx"], writes=["CF2"])
        pv = A("pin").rearrange("p (c q) -> p c q", q=128)
        p.op("act", lambda e, pv=pv: e.activation(out=et[:, :], in_=pv[:, :, 127], func=AF.Exp), reads=["pin"], writes=["et"])
        c0 = b0 // 128
        p.dma("pool", ET[c0:c0 + NCB, :].rearrange("c j -> j c"), et[:, :], reads=["et"], writes=["ET"], allow_slow_non_contiguous=True)


def ph_ssd_pass(p, d, XBC, T, NCTX, SC, CF2, ET, DSKIP_BC, IDENT, MASKS, YT, YM):
    NCH = T // 128
    nctx = NCTX // 128
    order = list(range(NCH)) if d == 0 else list(range(nctx - 1, -1, -1)) + list(range(NCH - 1, nctx - 1, -1))
    ident = p.sb([128, 128], F32, name="ident")
    p.dma("sp", ident[:, :], IDENT, writes=["ident"])
    identb = p.sb([128, 128], BF16, name="identb")
    p.op("dve", lambda e: e.tensor_copy(out=identb[:, :], in_=ident[:, :]), reads=["ident"], writes=["identb"])
    maskf = p.sb([128, 512], F32, name="maskf")
    p.dma("sp", maskf[:, :], MASKS[d], writes=["maskf"])
    maskb = p.sb([128, 512], BF16, name="maskb")
    p.op("dve", lambda e: e.tensor_copy(out=maskb[:, :], in_=maskf[:, :]), reads=["maskf"], writes=["maskb"])
    etb = p.sb([128, NCH, 32], F32, name="etb")
    p.dma("sp", etb[:, :, :], ET.partition_broadcast(128), reads=["ET"], writes=["etb"])
    dsk = p.sb([128, 16], F32, name="dskb")
    p.dma("sp", dsk[:, :], DSKIP_BC, writes=["dskb"])
    H = p.sb([128, 1024], F32, name="H")
    Hb = p.sb([128, 1024], BF16, name="Hb")
    p.op("pool", lambda e: e.memset(H[:, :], 0.0), writes=["H"])
    p.op("pool", lambda e: e.memset(Hb[:, :], 0.0), writes=["Hb"])
    xin = [p.sb([128, 16, 128], F32, name=f"xin{i}") for i in range(2)]
    sct = [p.sb([96, 128], F32, name=f"sct{i}") for i in range(2)]
    Lc = [p.sb([2, 32, 128], F32, name=f"Lc{i}") for i in range(2)]
    Rc = [p.sb([2, 32, 128], F32, name=f"Rc{i}") for i in range(2)]
    for i in range(2):
        p.op("pool", lambda e, i=i: e.memset(Lc[i][:, :, :], 1.0), writes=[f"Lc{i}"])
        p.op("pool", lambda e, i=i: e.memset(Rc[i][:, :, :], 1.0), writes=[f"Rc{i}"])
    xtok = p.sb([128, 1024], F32, name="xtok")
    btok = p.sb([128, 512], BF16, name="btok")
    bcT = p.sb([128, 8, 128], BF16, name="bcT")
    sc = p.sb([128, 96], F32, name="sc")
    xdt = p.sb([128, 1024], BF16, name="xdt")
    xw = p.sb([128, 1024], BF16, name="xw")
    Es_ = [p.sb([128, 512], BF16, name=f"E{i}") for i in range(2)]
    GTs = [p.sb([128, 4, 128], BF16, name=f"GT{i}") for i in range(2)]
    ysb = p.sb([128, 256], F32, name="ysb")
    yacc = p.sb([128, 1024], F32, name="yacc")
    yprev = p.sb([128, 1024], F32, name="yprev")
    yfm = [p.sb([128, 8, 128], F32, name=f"yfm{i}") for i in range(2)]
    psT = [p.ps([128, 512], name=f"psT{i}") for i in range(2)]
    psG = p.ps([128, 512], name="psG")
    psDs = [p.ps([128, 512], name=f"psD{i}") for i in range(2)]
    psY = p.ps([128, 512], name="psY")
    psZ = p.ps([128, 512], name="psZ")
    psS = p.ps([128, 512], name="psS")
    ti = 0
    for it, c in enumerate(order):
        t0 = c * 128
        xi, xik = xin[it % 2], f"xin{it%2}"
        st, stk = sct[it % 2], f"sct{it%2}"
        L, Lk, R, Rk = Lc[it % 2], f"Lc{it%2}", Rc[it % 2], f"Rc{it%2}"
        p.dma("sp", xi[:, :, :], fm(XBC, t0, 128), reads=["XBC"], writes=[xik])
        p.dma("sp", st[:, :], SC[:, t0:t0 + 128], reads=["SC"], writes=[stk])
        p.dma("sp", R[0:1, :, :], CF2[0:1, :, t0:t0 + 128], reads=["CF2"], writes=[Rk])
        p.dma("sp", L[1:2, :, :], CF2[1:2, :, t0:t0 + 128], reads=["CF2"], writes=[Lk])
        if d == 1:
            p.dma("sp", yprev[:, :], YT[t0:t0 + 128, :], reads=["YT"], writes=["yprev"])
        for half in range(2):
            ps, pk = psT[ti % 2], f"psT{ti%2}"
            ti += 1
            for cc in range(4):
                p.op("pe", lambda e, ps=ps, cc=cc, half=half, xi=xi: e.transpose(out=ps[:, cc * 128:(cc + 1) * 128], in_=xi[:, half * 4 + cc, :], identity=ident[:, :]),
                     reads=[xik, "ident"], writes=[pk])
            p.op("act", lambda e, ps=ps, half=half: e.copy(out=xtok[:, half * 512:(half + 1) * 512], in_=ps[:, :]), reads=[pk], writes=["xtok"])
        ps, pk = psT[ti % 2], f"psT{ti%2}"
        ti += 1
        for cc in range(4):
            p.op("pe", lambda e, ps=ps, cc=cc, xi=xi: e.transpose(out=ps[:, cc * 128:(cc + 1) * 128], in_=xi[:, 8 + cc, :], identity=ident[:, :]),
                 reads=[xik, "ident"], writes=[pk])
        p.op("act", lambda e, ps=ps: e.copy(out=btok[:, :], in_=ps[:, :]), reads=[pk], writes=["btok"])
        ps, pk = psT[ti % 2], f"psT{ti%2}"
        ti += 1
        p.op("pe", lambda e, ps=ps, st=st: e.transpose(out=ps[:, 0:96], in_=st[:, :], identity=ident[:96, :96]), reads=[stk, "ident"], writes=[pk])
        p.op("dve", lambda e, ps=ps: e.tensor_copy(out=sc[:, :], in_=ps[:, 0:96]), reads=[pk], writes=["sc"])
        p.op("pool", lambda e, xi=xi: e.tensor_copy(out=bcT[:, :, :], in_=xi[:, 8:16, :]), reads=[xik], writes=["bcT"])
        x3 = xtok[:, :].rearrange("p (h q) -> p h q", q=64)
        p.op("dve", lambda e, x3=x3: e.tensor_tensor(out=xdt[:, :].rearrange("p (h q) -> p h q", q=64), in0=x3,
                                                     in1=sc[:, d * 16:d * 16 + 16].unsqueeze(2).to_broadcast([128, 16, 64]), op=ALU.mult),
             reads=["xtok", "sc"], writes=["xdt"])
        p.op("pool", lambda e, x3=x3: e.tensor_tensor(out=xw[:, :].rearrange("p (h q) -> p h q", q=64), in0=x3,
                                                      in1=sc[:, 32 + d * 16:32 + d * 16 + 16].unsqueeze(2).to_broadcast([128, 16, 64]), op=ALU.mult),
             reads=["xtok", "sc"], writes=["xw"])
        def stage1(g):
            gs = g % 2
            p.op("pe", lambda e, g=g, gs=gs: e.matmul(psG[:, gs * 128:(gs + 1) * 128], lhsT=bcT[:, g, :], rhs=bcT[:, 4 + g, :], start=True, stop=True),
                 reads=["bcT"], writes=[f"psG{gs}"])
            PD = psDs[gs]
            p.op("pe", lambda e, PD=PD: e.matmul(PD[:, :], lhsT=identb[:, :], rhs=maskb[:, :], start=True, stop=False), reads=["identb", "maskb"], writes=[f"psD{gs}"])
            for hh in range(4):
                j = d * 16 + g * 4 + hh
                p.op("pe", lambda e, hh=hh, j=j, L=L, R=R, PD=PD: e.matmul(PD[:, hh * 128:(hh + 1) * 128], lhsT=L[:, j, :], rhs=R[:, j, :], start=False, stop=(hh == 3)),
                     reads=[Lk, Rk], writes=[f"psD{gs}"])

        def stage2(g):
            gs = g % 2
            PD, EE, GG = psDs[gs], Es_[gs], GTs[gs]
            p.op("act", lambda e, PD=PD, EE=EE: e.activation(out=EE[:, :], in_=PD[:, :], func=AF.Exp), reads=[f"psD{gs}"], writes=[f"E{gs}"])
            for hh in range(4):
                p.op("dve", lambda e, hh=hh, gs=gs, EE=EE, GG=GG: e.tensor_tensor(out=GG[:, hh, :], in0=psG[:, gs * 128:(gs + 1) * 128], in1=EE[:, hh * 128:(hh + 1) * 128], op=ALU.mult),
                     reads=[f"psG{gs}", f"E{gs}"], writes=[f"GT{gs}{hh}"])

        def stage3(g):
            gs = g % 2
            GG = GTs[gs]
            for hh in range(4):
                h = g * 4 + hh
                p.op("pe", lambda e, hh=hh, h=h, GG=GG: e.matmul(psY[:, hh * 64:(hh + 1) * 64], lhsT=GG[:, hh, :], rhs=xdt[:, h * 64:(h + 1) * 64], start=True, stop=True),
                     reads=[f"GT{gs}{hh}", "xdt"], writes=["psY"])
            p.op("pe", lambda e, g=g: e.matmul(psZ[:, 0:256], lhsT=bcT[:, 4 + g, :], rhs=Hb[:, g * 256:(g + 1) * 256], start=True, stop=True), reads=["bcT", "Hb"], writes=["psZ"])
            p.op("pe", lambda e, g=g: e.matmul(psS[:, 0:256], lhsT=btok[:, g * 128:(g + 1) * 128], rhs=xw[:, g * 256:(g + 1) * 256], start=True, stop=True),
                 reads=["btok", "xw"], writes=["psS"])
            p.op("act", lambda e: e.copy(out=ysb[:, :], in_=psY[:, 0:256]), reads=["psY"], writes=["ysb"])
            for hh in range(4):
                h = g * 4 + hh
                hs = slice(h * 64, (h + 1) * 64)
                rs = sc[:, 64 + d * 16 + h:64 + d * 16 + h + 1]
                p.op("dve", lambda e, hh=hh, hs=hs, rs=rs: e.scalar_tensor_tensor(out=yacc[:, hs], in0=psZ[:, hh * 64:(hh + 1) * 64], scalar=rs, in1=ysb[:, hh * 64:(hh + 1) * 64],
                                                                                  op0=ALU.mult, op1=ALU.add), reads=["psZ", "ysb", "sc"], writes=["yacc"])
                et = etb[:, c, d * 16 + h:d * 16 + h + 1]
                p.op("dve", lambda e, hh=hh, hs=hs, et=et: e.scalar_tensor_tensor(out=H[:, hs], in0=H[:, hs], scalar=et, in1=psS[:, hh * 64:(hh + 1) * 64], op0=ALU.mult, op1=ALU.add),
                     reads=["H", "psS", "etb"], writes=["H"])
            p.op("act", lambda e, g=g: e.copy(out=Hb[:, g * 256:(g + 1) * 256], in_=H[:, g * 256:(g + 1) * 256]), reads=["H"], writes=["Hb"])

        stage1(0)
        for g in range(4):
            if g + 1 < 4:
                stage1(g + 1)
            stage2(g)
            stage3(g)
        if d == 0:
            p.op("pool", lambda e, x3=x3: e.tensor_tensor(out=yprev[:, :].rearrange("p (h q) -> p h q", q=64), in0=x3,
                                                          in1=dsk[:, :].unsqueeze(2).to_broadcast([128, 16, 64]), op=ALU.mult),
                 reads=["xtok", "dskb"], writes=["yprev"])
            p.op("pool", lambda e: e.tensor_tensor(out=yacc[:, :], in0=yacc[:, :], in1=yprev[:, :], op=ALU.add), reads=["yacc", "yprev"], writes=["yacc"])
            p.dma("pool", YT[t0:t0 + 128, :], yacc[:, :], reads=["yacc"], writes=["YT"])
        else:
            p.op("pool", lambda e: e.tensor_tensor(out=yacc[:, :], in0=yacc[:, :], in1=yprev[:, :], op=ALU.add), reads=["yacc", "yprev"], writes=["yacc"])
            yf, yfk = yfm[it % 2], f"yfm{it%2}"
            for half in range(2):
                ps, pk = psT[ti % 2], f"psT{ti%2}"
                ti += 1
                for cc in range(4):
                    cch = half * 4 + cc
                    p.op("pe", lambda e, ps=ps, cc=cc, cch=cch: e.transpose(out=ps[:, cc * 128:(cc + 1) * 128], in_=yacc[:, cch * 128:(cch + 1) * 128], identity=ident[:, :]),
                         reads=["yacc", "ident"], writes=[pk])
                p.op("act", lambda e, ps=ps, half=half, yf=yf: e.copy(out=yf[:, half * 4:(half + 1) * 4, :], in_=ps[:, :].rearrange("p (c q) -> p c q", q=128)), reads=[pk], writes=[yfk])
            p.dma("pool", fm(YM, t0, 128), yf[:, :, :], reads=[yfk], writes=["YM"])


def ph_ssd_post(p, YM, PROJ, T, G, Y):
    N = 512
    nrm = Norm(p, N, 1, G, [(None, None)])
    ys = [p.sb([128, 8, N], F32, name=f"ym{i}") for i in range(2)]
    zs = [p.sb([128, 8, N], F32, name=f"zz{i}") for i in range(2)]
    ht = [p.sb([128, 8, N], BF16, name=f"hp{i}") for i in range(2)]
    for ti, t0 in enumerate(range(0, T, N)):
        n = min(N, T - t0)
        y, yk, z, zk, h, hk = ys[ti % 2], f"ym{ti%2}", zs[ti % 2], f"zz{ti%2}", ht[ti % 2], f"hp{ti%2}"
        p.dma("sp", y[:, :, :n], fm(YM, t0, n), reads=["YM"], writes=[yk])
        p.dma("sp", z[:, :, :n], fm(PROJ[512:1536, :], t0, n), reads=["PROJ"], writes=[zk])
        p.op("act", lambda e, z=z, n=n: e.activation(out=z[:, :, :n], in_=z[:, :, :n], func=AF.Silu), reads=[zk], writes=[zk])
        p.op("pool", lambda e, y=y, z=z, n=n: e.tensor_tensor(out=y[:, :, :n], in0=y[:, :, :n], in1=z[:, :, :n], op=ALU.mult), reads=[yk, zk], writes=[yk])
        nrm.apply(y, yk, n, 0, h, hk)
        p.dma("pool", fm(Y[512:1536, :], t0, n), h[:, :, :n], reads=[hk], writes=["Y"])


NFFT = 16384
S = 4


def hy_tables():
    a = np.arange(128, dtype=np.float64)
    ang = 2 * np.pi * np.outer(a, a) / 128.0
    C, Sn = np.cos(ang), np.sin(ang)
    angN = 2 * np.pi * np.outer(a, a) / NFFT
    tabs = np.concatenate([C, -Sn, C, Sn, -Sn, C, np.cos(angN), np.sin(angN), C / NFFT, -Sn / NFFT, -C, -C / NFFT], axis=1)
    return tabs.astype(np.float32)


class FFT:
    def __init__(self, p, TABS, NC=2):
        self.p = p
        F32R = mybir.dt.float32r
        self.tabs = p.sb([128, 1536], F32, name="tabs")
        p.dma("sp", self.tabs[:, :], TABS, writes=["tabs"])
        self.tabs_r = p.sb([128, 1536], F32, name="tabs_r")
        p.op("dve", lambda e: e.tensor_copy(out=self.tabs_r[:, :].bitcast(F32R), in_=self.tabs[:, :]), reads=["tabs"], writes=["tabs"])
        t = self.tabs_r[:, :].bitcast(F32R)
        self.F1 = t[:, 0:256]
        self.CS = t[:, 256:512]
        self.NSC = t[:, 512:768]
        self.C = t[:, 256:384]
        self.Sm = t[:, 384:512]
        self.NS = t[:, 512:640]
        self.TWC = self.tabs[:, 768:896]
        self.TWS = self.tabs[:, 896:1024]
        self.CN = t[:, 1024:1152]
        self.NSN = t[:, 1152:1280]
        self.NegC = t[:, 1280:1408]
        self.NegCN = t[:, 1408:1536]
        self.NC = NC
        self.psA = [p.ps([128, S, 256], name=f"psA{c}") for c in range(NC)]
        self.psXr = [p.ps([128, 512], name=f"psXr{c}") for c in range(NC)]
        self.psXi = [p.ps([128, 512], name=f"psXi{c}") for c in range(NC)]
        self.psy = self.psXr
        mk = lambda n: [p.sb([128, S, 128], F32, name=f"{n}{c}") for c in range(NC)]
        self.t1, self.t2, self.t3, self.t4 = mk("ft1"), mk("ft2"), mk("ft3"), mk("ft4")
        self.u1, self.u2, self.u3, self.u4 = mk("fu1"), mk("fu2"), mk("fu3"), mk("fu4")
        self.Br, self.Bi = mk("Br"), mk("Bi")
        self.Yr, self.Yi = mk("Yr"), mk("Yi")

    def cmul(self, c, ar, ai, ak, br, bi, bk, outr, outi, ok, conj_b=False):
        p = self.p
        F32R = mybir.dt.float32r
        t1, t2, t3, t4 = self.u1[c], self.u2[c], self.u3[c], self.u4[c]
        k1, k2, k3, k4 = f"fu1{c}", f"fu2{c}", f"fu3{c}", f"fu4{c}"
        p.op("dve", lambda e: e.tensor_tensor(out=t1[:, :, :], in0=ar, in1=br, op=ALU.mult), reads=ak + bk, writes=[k1])
        p.op("dve", lambda e: e.tensor_tensor(out=t2[:, :, :], in0=ai, in1=bi, op=ALU.mult), reads=ak + bk, writes=[k2])
        p.op("pool", lambda e: e.tensor_tensor(out=outr.bitcast(F32R), in0=t1[:, :, :], in1=t2[:, :, :], op=(ALU.add if conj_b else ALU.subtract)),
             reads=[k1, k2], writes=ok)
        p.op("dve", lambda e: e.tensor_tensor(out=t3[:, :, :], in0=ai, in1=br, op=ALU.mult), reads=ak + bk, writes=[k3])
        p.op("dve", lambda e: e.tensor_tensor(out=t4[:, :, :], in0=ar, in1=bi, op=ALU.mult), reads=ak + bk, writes=[k4])
        p.op("pool", lambda e: e.tensor_tensor(out=outi.bitcast(F32R), in0=t3[:, :, :], in1=t4[:, :, :], op=(ALU.subtract if conj_b else ALU.add)),
             reads=[k3, k4], writes=ok)

    def cmul_split(self, c, ar, ai, ak, br, bi, bk):
        p = self.p
        F32R = mybir.dt.float32r
        for (t, x, y, kk) in ((self.t1[c], ar, br, f"ft1{c}"), (self.t2[c], ai, bi, f"ft2{c}"), (self.t3[c], ai, br, f"ft3{c}"), (self.t4[c], ar, bi, f"ft4{c}")):
            p.op("dve", lambda e, t=t, x=x, y=y: e.tensor_tensor(out=t[:, :, :].bitcast(F32R), in0=x, in1=y, op=ALU.mult), reads=ak + bk, writes=[kk])

    def _tflat(self, c):
        F32R = mybir.dt.float32r
        return [t[c][:, :, :].rearrange("p s k -> p (s k)").bitcast(F32R) for t in (self.t1, self.t2, self.t3, self.t4)]

    def _tw(self):
        return (self.TWC.unsqueeze(1).to_broadcast([128, S, 128]), self.TWS.unsqueeze(1).to_broadcast([128, S, 128]))

    def _bflat(self, c):
        F32R = mybir.dt.float32r
        return (self.Br[c][:, :, :].rearrange("p s k -> p (s k)").bitcast(F32R), self.Bi[c][:, :, :].rearrange("p s k -> p (s k)").bitcast(F32R))

    def st_f1(self, c, x0, xk, K):
        p, A = self.p, self.psA[c]
        for s in range(S):
            p.op("pe", lambda e, s=s: e.matmul(A[:, s, :], lhsT=x0[:K, s, :].bitcast(mybir.dt.float32r), rhs=self.F1[:K, :], start=True, stop=True),
                 reads=[xk, "tabs"], writes=[f"psA{c}"])

    def st_tw1(self, c):
        A = self.psA[c]
        twc, tws = self._tw()
        self.cmul_split(c, A[:, :, 0:128], A[:, :, 128:256], [f"psA{c}"], twc, tws, ["tabs"])

    def st_f2(self, c):
        p = self.p
        T1, T2, T3, T4 = self._tflat(c)
        Xr, Xi = self.psXr[c], self.psXi[c]
        tk = [f"ft1{c}", f"ft2{c}", f"ft3{c}", f"ft4{c}"]
        for n, (w, t) in enumerate(((self.C, T1), (self.C, T2), (self.Sm, T3), (self.NS, T4))):
            p.op("pe", lambda e, w=w, t=t, n=n: e.matmul(Xr[:, :], lhsT=w, rhs=t, start=(n == 0), stop=(n == 3)), reads=["tabs"] + tk, writes=[f"psXr{c}"])
        for n, (w, t) in enumerate(((self.C, T3), (self.NegC, T4), (self.NS, T1), (self.NS, T2))):
            p.op("pe", lambda e, w=w, t=t, n=n: e.matmul(Xi[:, :], lhsT=w, rhs=t, start=(n == 0), stop=(n == 3)), reads=["tabs"] + tk, writes=[f"psXi{c}"])

    def st_filt(self, c, Hr, Hi, hk):
        Xr = self.psXr[c][:, :].rearrange("p (s k) -> p s k", k=128)
        Xi = self.psXi[c][:, :].rearrange("p (s k) -> p s k", k=128)
        self.cmul(c, Xr, Xi, [f"psXr{c}", f"psXi{c}"], Hr, Hi, [hk], self.Yr[c][:, :, :], self.Yi[c][:, :, :], [f"Y{c}"])

    def st_i1(self, c):
        p, A = self.p, self.psA[c]
        F32R = mybir.dt.float32r
        Yr, Yi = self.Yr[c], self.Yi[c]
        for s in range(S):
            p.op("pe", lambda e, s=s: e.matmul(A[:, s, :], lhsT=Yr[:, s, :].bitcast(F32R), rhs=self.CS, start=True, stop=False), reads=[f"Y{c}", "tabs"], writes=[f"psA{c}"])
            p.op("pe", lambda e, s=s: e.matmul(A[:, s, :], lhsT=Yi[:, s, :].bitcast(F32R), rhs=self.NSC, start=False, stop=True), reads=[f"Y{c}", "tabs"], writes=[f"psA{c}"])

    def st_tw2(self, c):
        A = self.psA[c]
        twc, tws = self._tw()
        self.cmul_split(c, A[:, :, 0:128], A[:, :, 128:256], [f"psA{c}"], twc, tws, ["tabs"])

    def st_i2(self, c, M):
        p = self.p
        T1, T2, T3, T4 = self._tflat(c)
        y = self.psy[c]
        tk = [f"ft1{c}", f"ft2{c}", f"ft3{c}", f"ft4{c}"]
        for n, (w, t) in enumerate(((self.CN, T1), (self.NegCN, T2), (self.NSN, T3), (self.NSN, T4))):
            p.op("pe", lambda e, w=w, t=t, n=n: e.matmul(y[:M, :], lhsT=w[:, :M], rhs=t, start=(n == 0), stop=(n == 3)), reads=["tabs"] + tk, writes=[f"psXr{c}"])


def ph_hy_filter_raw(p, l, FEATS, TL, W1, B1, FQ1, W2, B2, FQ2, W3, DELTAS, FRAW, RINV):
    NT = min(512, l)
    ntile = l // NT
    feats = p.sb([33, l], F32, name="feats")
    p.dma("sp", feats[:, :], FEATS, writes=["feats"])
    w1 = p.sb([33, 64], F32, name="fw1")
    p.dma("sp", w1[:, :], W1, writes=["fw1"])
    w2 = p.sb([64, 64], F32, name="fw2")
    p.dma("sp", w2[:, :], W2, writes=["fw2"])
    w3 = p.sb([64, 4096], F32, name="fw3")
    p.dma("sp", w3[:, :], W3, writes=["fw3"])
    cols = p.sb([64, 6], F32, name="fcols")
    for i, src in enumerate((B1, FQ1, B2, FQ2)):
        p.dma("sp", cols[:, i:i + 1], src.rearrange("(p o) -> p o", o=1), writes=["fcols"], allow_slow_non_contiguous=True)
    p.op("dve", lambda e: e.tensor_tensor(out=cols[:, 4:5], in0=cols[:, 0:1], in1=cols[:, 1:2], op=ALU.mult), reads=["fcols"], writes=["fcols"])
    p.op("dve", lambda e: e.tensor_tensor(out=cols[:, 5:6], in0=cols[:, 2:3], in1=cols[:, 3:4], op=ALU.mult), reads=["fcols"], writes=["fcols"])
    hid2 = p.sb([64, l], F32, name="hid2")
    arg = p.sb([64, NT], F32, name="farg")
    arg2 = p.sb([64, NT], F32, name="farg2")
    hid1 = p.sb([64, NT], F32, name="hid1")
    tm1 = sin_tmps(p, [64, NT], "s1")
    tm2 = sin_tmps(p, [64, NT], "s2")
    ps1 = p.ps([128, 512], name="psf1")
    for ti in range(ntile):
        ts = slice(ti * NT, (ti + 1) * NT)
        p.op("pe", lambda e, ts=ts: e.matmul(ps1[:64, :NT], lhsT=w1[:, :], rhs=feats[:, ts], start=True, stop=True), reads=["fw1", "feats"], writes=["psf1"])
        p.op("dve", lambda e: e.tensor_scalar(out=arg[:, :], in0=ps1[:64, :NT], scalar1=cols[:, 1:2], scalar2=cols[:, 4:5], op0=ALU.mult, op1=ALU.add),
             reads=["psf1", "fcols"], writes=["s1" + "x"])
        sin_rr(p, hid1[:, :], arg[:, :], [64, NT], "s1", tmps=tm1)
        p.op("pe", lambda e: e.matmul(ps1[:64, :NT], lhsT=w2[:, :], rhs=hid1[:, :], start=True, stop=True), reads=["fw2", "s1o"], writes=["psf1"])
        p.op("dve", lambda e: e.tensor_scalar(out=arg2[:, :], in0=ps1[:64, :NT], scalar1=cols[:, 3:4], scalar2=cols[:, 5:6], op0=ALU.mult, op1=ALU.add),
             reads=["psf1", "fcols"], writes=["s2" + "x"])
        sin_rr(p, hid2[:, ts], arg2[:, :], [64, NT], "s2", tmps=tm2)
    tl = p.sb([128, l], F32, name="tl")
    p.dma("sp", tl[:, :], TL.partition_broadcast(128), writes=["tl"])
    dl = p.sb([128, 8], F32, name="ndelta")
    load_cols(p, dl[:, :], DELTAS, "ndelta")
    p.op("dve", lambda e: e.tensor_scalar(out=dl[:, :], in0=dl[:, :], scalar1=-1.0, scalar2=None, op0=ALU.mult), reads=["ndelta"], writes=["ndelta"])
    dec = p.sb([128, l], F32, name="dec")
    sums = p.sb([128, 32, ntile], F32, name="fsums")
    junk = p.sb([128, NT], F32, name="fjunk")
    outs = [p.sb([128, NT], F32, name=f"fo{i}") for i in range(3)]
    pss = [p.ps([128, 512], name=f"psw{i}") for i in range(3)]
    oi = 0
    for cc in range(8):
        p.op("act", lambda e, cc=cc: e.activation(out=dec[:, :], in_=tl[:, :], func=AF.Exp, scale=dl[:, cc:cc + 1]), reads=["tl", "ndelta"], writes=["dec"])
        for od in range(4):
            m = od * 8 + cc
            for ti in range(ntile):
                ts = slice(ti * NT, (ti + 1) * NT)
                ps, pk, o, ok = pss[oi % 3], f"psw{oi%3}", outs[oi % 3], f"fo{oi%3}"
                p.op("pe", lambda e, ps=ps, m=m, ts=ts: e.matmul(ps[:, :NT], lhsT=w3[:, m * 128:(m + 1) * 128], rhs=hid2[:, ts], start=True, stop=True),
                     reads=["fw3", "s2o"], writes=[pk])
                p.op("dve", lambda e, ps=ps, o=o, ts=ts: e.tensor_tensor(out=o[:, :], in0=ps[:, :NT], in1=dec[:, ts], op=ALU.mult), reads=[pk, "dec"], writes=[ok])
                lo = 1 if (od % 2 == 1 and ti == 0) else 0
                p.op("act", lambda e, o=o, m=m, ti=ti, lo=lo: e.activation(out=junk[:, lo:], in_=o[:, lo:], func=AF.Abs, accum_out=sums[:, m, ti:ti + 1]),
                     reads=[ok], writes=["fsums", "fjunk"])
                p.dma("pool", FRAW[m * 128:(m + 1) * 128, ts], o[:, :], reads=[ok], writes=["FRAW"])
                oi += 1
    tot = p.sb([128, 32], F32, name="ftot")
    p.op("dve", lambda e: e.tensor_reduce(out=tot[:, :], in_=sums[:, :, :], axis=AX.X, op=ALU.add), reads=["fsums"], writes=["ftot"])
    rinv = p.sb([128, 2, 8], F32, name="rinv")
    t4 = tot[:, :].rearrange("p (o d c) -> p o d c", o=2, d=2)
    for o_ in range(2):
        p.op("dve", lambda e, o_=o_: e.tensor_tensor(out=rinv[:, o_, :], in0=t4[:, o_, 0, :], in1=t4[:, o_, 1, :], op=ALU.add), reads=["ftot"], writes=["rinv"])
    p.op("dve", lambda e: e.tensor_scalar(out=rinv[:, :, :], in0=rinv[:, :, :], scalar1=1e-6, scalar2=None, op0=ALU.add), reads=["rinv"], writes=["rinv"])
    p.op("dve", lambda e: e.reciprocal(out=rinv[:, :, :], in_=rinv[:, :, :]), reads=["rinv"], writes=["rinv"])
    p.dma("pool", RINV.rearrange("o (c q) -> q o c", q=128), rinv[:, :, :], reads=["rinv"], writes=["RINV"], allow_slow_non_contiguous=True)


def ph_hy_filter_asm(p, l, FRAW, RINV, FILT):
    rinv = p.sb([128, 2, 8], F32, name="rinv2")
    p.dma("sp", rinv[:, :, :], RINV.rearrange("o (c q) -> q o c", q=128), reads=["RINV"], writes=["rinv2"], allow_slow_non_contiguous=True)
    zero = p.sb([128, 4096], F32, name="fzero")
    p.op("pool", lambda e: e.memset(zero[:, :], 0.0), writes=["fzero"])
    f = [p.sb([128, l], F32, name=f"ff{i}") for i in range(2)]
    b = [p.sb([128, l], F32, name="fb0")] * 2
    r = [p.sb([128, l], F32, name="fr0")] * 2
    i = 0
    for o_ in range(2):
        for cc in range(8):
            ff, fk, bb, bk, rr, rk = f[i % 2], f"ff{i%2}", b[0], "fb0", r[0], "fr0"
            rows = slice((o_ * 8 + cc) * 128, (o_ * 8 + cc + 1) * 128)
            mf, mb = (o_ * 2 + 0) * 8 + cc, (o_ * 2 + 1) * 8 + cc
            p.dma("sp", ff[:, :], FRAW[mf * 128:(mf + 1) * 128, :], reads=["FRAW"], writes=[fk])
            p.dma("sp", bb[:, :], FRAW[mb * 128:(mb + 1) * 128, :], reads=["FRAW"], writes=[bk])
            sc = rinv[:, o_, cc:cc + 1]
            p.op("act", lambda e, ff=ff, sc=sc: e.activation(out=ff[:, :], in_=ff[:, :], func=AF.Identity, scale=sc), reads=[fk, "rinv2"], writes=[fk])
            p.op("dve", lambda e, bb=bb, rr=rr, sc=sc: e.tensor_scalar(out=rr[:, :], in0=bb[:, ::-1], scalar1=sc, scalar2=None, op0=ALU.mult), reads=[bk, "rinv2"], writes=[rk])
            p.dma("pool", FILT[rows, 0:l], ff[:, :], reads=[fk], writes=["FILT"])
            p.dma("pool", FILT[rows, NFFT - l + 1:NFFT], rr[:, 0:l - 1], reads=[rk], writes=["FILT"])
            for z0 in range(l, NFFT - l + 1, 4096):
                zn = min(4096, NFFT - l + 1 - z0)
                p.dma("pool", FILT[rows, z0:z0 + zn], zero[:, :zn], reads=["fzero"], writes=["FILT"], allow_slow_non_contiguous=True)
            i += 1


def ph_hy_filter_fft(p, FILT, TABS, HF, groups=None):
    NC = 2
    fft = FFT(p, TABS, NC)
    xs = [p.sb([128, S, 128], F32, name=f"fx{i}") for i in range(2 * NC)]
    hr = [p.sb([128, S, 128], F32, name=f"hro{i}") for i in range(2 * NC)]
    hi = [p.sb([128, S, 128], F32, name=f"hio{i}") for i in range(2 * NC)]
    gl = list(groups if groups is not None else range(2048 // S))
    for b0 in range(0, len(gl), NC):
        batch = gl[b0:b0 + NC]
        ctxs = []
        for c, gi in enumerate(batch):
            bi_ = ((b0 // NC) % 2) * NC + c
            x0, xk = xs[bi_], f"fx{bi_}"
            sig = slice(gi * S, (gi + 1) * S)
            p.dma("sp", x0[:, :, :], FILT[sig, :].rearrange("s (a b) -> a s b", b=128), reads=["FILT"], writes=[xk])
            p.op("act", lambda e, x0=x0: e.copy(out=x0[:, :, :].bitcast(mybir.dt.float32r), in_=x0[:, :, :]), reads=[xk], writes=[xk])
            ctxs.append((c, x0, xk, sig, bi_))
        for (c, x0, xk, sig, bi_) in ctxs:
            fft.st_f1(c, x0, xk, 128)
        for (c, x0, xk, sig, bi_) in ctxs:
            fft.st_tw1(c)
        for (c, x0, xk, sig, bi_) in ctxs:
            fft.st_f2(c)
        for (c, x0, xk, sig, bi_) in ctxs:
            a_, ak, b_, bk = hr[bi_], f"hro{bi_}", hi[bi_], f"hio{bi_}"
            p.op("act", lambda e, a_=a_, c=c: e.copy(out=a_[:, :, :].rearrange("p s k -> p (s k)"), in_=fft.psXr[c][:, :]), reads=[f"psXr{c}"], writes=[ak])
            p.op("act", lambda e, b_=b_, c=c: e.copy(out=b_[:, :, :].rearrange("p s k -> p (s k)"), in_=fft.psXi[c][:, :]), reads=[f"psXi{c}"], writes=[bk])
            p.dma("pool", HF[0, sig].rearrange("s a b -> a s b"), a_[:, :, :], reads=[ak], writes=["HF"])
            p.dma("pool", HF[1, sig].rearrange("s a b -> a s b"), b_[:, :, :], reads=[bk], writes=["HF"])


def ph_hy_conv(p, ZC, t_off, Lsig, HF, HYD, TABS, Y, groups=None):
    K = Lsig // 128
    NC = 2
    F32R = mybir.dt.float32r
    fft = FFT(p, TABS, NC)
    dsk = p.sb([128, 2, 1024], F32, name="hyd")
    p.dma("sp", dsk[:, :, :], HYD.partition_broadcast(128), writes=["hyd"])
    mk = lambda n, dt=F32: [p.sb([64, S, 128], dt, name=f"{n}{i}") for i in range(NC)]
    vs, x1s, x2s, vrs, y1s, tmps = mk("v"), mk("xa"), mk("xb"), mk("vr"), mk("y1"), mk("ctmp")
    yos = mk("yo", BF16)
    for i in range(NC):
        for tl_, nm in ((vrs[i], f"vr{i}"), (y1s[i], f"y1{i}")):
            p.op("pool", lambda e, tl_=tl_: e.memset(tl_[:, :, :], 0.0), writes=[nm])
            p.op("act", lambda e, tl_=tl_: e.copy(out=tl_[:, :, :].bitcast(F32R), in_=tl_[:, :, :]), reads=[nm], writes=[nm])
    H = [[[p.sb([128, S, 128], F32, name=f"H{c}{o}{ri}") for ri in range(2)] for o in range(2)] for c in range(NC)]

    def tb(ap2d):
        return ap2d.rearrange("s (a b) -> a s b", b=128)

    gl = list(groups if groups is not None else range(1024 // S))
    for b0 in range(0, len(gl), NC):
        batch = list(enumerate(gl[b0:b0 + NC]))
        for c, gi in batch:
            c0 = gi * S
            p.dma("sp", vs[c][:K, :, :], tb(ZC[c0:c0 + S, t_off:t_off + Lsig]), reads=["ZC"], writes=[f"v{c}"])
            p.dma("sp", x1s[c][:K, :, :], tb(ZC[1024 + c0:1024 + c0 + S, t_off:t_off + Lsig]), reads=["ZC"], writes=[f"xa{c}"])
            p.dma("sp", x2s[c][:K, :, :], tb(ZC[2048 + c0:2048 + c0 + S, t_off:t_off + Lsig]), reads=["ZC"], writes=[f"xb{c}"])
            for o in range(2):
                for ri in range(2):
                    p.dma("sp", H[c][o][ri][:, :, :], HF[ri, o * 1024 + c0:o * 1024 + c0 + S].rearrange("s a b -> a s b"), reads=["HF"], writes=[f"H{c}{o}"])
            p.op("act", lambda e, c=c: e.copy(out=vrs[c][:K, :, :].bitcast(F32R), in_=vs[c][:K, :, :]), reads=[f"v{c}"], writes=[f"vr{c}"])
        for o in range(2):
            for c, gi in batch:
                src, sk = (vrs[c], f"vr{c}") if o == 0 else (y1s[c], f"y1{c}")
                fft.st_f1(c, src, sk, 64)
            for c, gi in batch:
                fft.st_tw1(c)
            for c, gi in batch:
                fft.st_f2(c)
            for c, gi in batch:
                fft.st_filt(c, H[c][o][0][:, :, :], H[c][o][1][:, :, :], f"H{c}{o}")
            for c, gi in batch:
                fft.st_i1(c)
            for c, gi in batch:
                fft.st_tw2(c)
            for c, gi in batch:
                fft.st_i2(c, 64)
            for c, gi in batch:
                c0 = gi * S
                src, sk = (vs[c], f"v{c}") if o == 0 else (y1s[c], f"y1{c}")
                gate, gk = (x1s[c], f"xa{c}") if o == 0 else (x2s[c], f"xb{c}")
                tmp, tk = tmps[c], f"ctmp{c}"
                dbc = dsk[:64, o, c0:c0 + S].unsqueeze(2).to_broadcast([64, S, 128])
                p.op("dve", lambda e, src=src, dbc=dbc, tmp=tmp: e.tensor_tensor(out=tmp[:K, :, :], in0=src[:K, :, :], in1=dbc[:K], op=ALU.mult), reads=[sk, "hyd"], writes=[tk])
                psy3 = fft.psy[c][:, :].rearrange("p (s k) -> p s k", k=128)
                p.op("dve", lambda e, psy3=psy3, tmp=tmp: e.tensor_tensor(out=tmp[:K, :, :], in0=psy3[:K, :, :], in1=tmp[:K, :, :], op=ALU.add), reads=[f"psXr{c}", tk], writes=[tk])
                if o == 0:
                    p.op("pool", lambda e, gate=gate, tmp=tmp, c=c: e.tensor_tensor(out=y1s[c][:K, :, :].bitcast(F32R), in0=tmp[:K, :, :], in1=gate[:K, :, :], op=ALU.mult),
                         reads=[tk, gk], writes=[f"y1{c}"])
                else:
                    p.op("pool", lambda e, gate=gate, tmp=tmp, c=c: e.tensor_tensor(out=yos[c][:K, :, :], in0=tmp[:K, :, :], in1=gate[:K, :, :], op=ALU.mult),
                         reads=[tk, gk], writes=[f"yo{c}"])
                    p.dma("pool", tb(Y[c0:c0 + S, t_off:t_off + Lsig]), yos[c][:K, :, :], reads=[f"yo{c}"], writes=["Y"])


def hy_ctx_tables(l=256):
    N = 2 * l
    t = np.arange(l, dtype=np.float64)[:, None]
    k = np.arange(N, dtype=np.float64)[None, :]
    ang = 2 * np.pi * t * k / N
    c, s = np.cos(ang), np.sin(ang)
    cb, sb = c.copy(), s.copy()
    cb[0, :] = 0.0
    sb[0, :] = 0.0
    fw = np.stack([c, -s, cb, sb]).reshape(4, l // 128, 128, N // 128, 128).transpose(2, 0, 1, 3, 4)
    iv = np.stack([c.T / N, -s.T / N]).reshape(2, N // 128, 128, l // 128, 128).transpose(2, 0, 1, 3, 4)
    return np.ascontiguousarray(fw).astype(np.float32), np.ascontiguousarray(iv).astype(np.float32)


def ph_hy_ctx_dense(p, ZC, l, FRAW, RINV, HYD, FWT, IVT, IDENT, Y):
    TC, KC = l // 128, (2 * l) // 128
    ident = p.sb([128, 128], F32, name="cid")
    p.dma("sp", ident[:, :], IDENT, writes=["cid"])
    fw = p.sb([128, 4, TC, KC, 128], F32, name="cfw")
    p.dma("sp", fw[:, :, :, :, :], FWT, writes=["cfw"])
    iv = p.sb([128, 2, KC, TC, 128], F32, name="civ")
    p.dma("sp", iv[:, :, :, :, :], IVT, writes=["civ"])
    rinv = p.sb([128, 2, 1024], F32, name="crinv")
    p.dma("sp", rinv[:, :, :], RINV.partition_broadcast(128), reads=["RINV"], writes=["crinv"])
    dsk = p.sb([128, 2, 1024], F32, name="cdsk")
    p.dma("sp", dsk[:, :, :], HYD.partition_broadcast(128), writes=["cdsk"])
    pst = [p.ps([128, 512], name=f"cpt{i}") for i in range(2)]
    psm = [p.ps([128, 512], name=f"cpm{i}") for i in range(4)]
    stg = [p.sb([128, l], F32, name=f"cstg{i}") for i in range(4)]
    fT = p.sb([128, TC, 4096], F32, name="cfT")
    sT = p.sb([128, TC, 3072], F32, name="csT")
    k = 0
    for (SRC, nch_, dst, dk, rk) in ((FRAW, 32, fT, "cfT", "FRAW"), (ZC, 24, sT, "csT", "ZC")):
        for rc in range(0, nch_, 4):
            for tc in range(TC):
                ps, pk = pst[k % 2], f"cpt{k%2}"
                for j in range(4):
                    s_, sk = stg[j], f"cstg{j}"
                    if tc == 0:
                        p.dma("sp", s_[:, :], SRC[(rc + j) * 128:(rc + j + 1) * 128, 0:l], reads=[rk], writes=[sk])
                    p.op("pe", lambda e, ps=ps, j=j, s_=s_, tc=tc: e.transpose(out=ps[:, j * 128:(j + 1) * 128], in_=s_[:, tc * 128:(tc + 1) * 128], identity=ident[:, :]),
                         reads=[sk, "cid"], writes=[pk])
                p.op("act" if k % 2 else "dve", (lambda e, ps=ps, dst=dst, tc=tc, rc=rc: e.copy(out=dst[:, tc, rc * 128:(rc + 4) * 128], in_=ps[:, :])) if k % 2 else
                     (lambda e, ps=ps, dst=dst, tc=tc, rc=rc: e.tensor_copy(out=dst[:, tc, rc * 128:(rc + 4) * 128], in_=ps[:, :])), reads=[pk], writes=[dk])
                k += 1
    Xr = p.sb([128, KC, 1024], F32, name="cXr")
    Xi = p.sb([128, KC, 1024], F32, name="cXi")
    t1 = p.sb([128, 512], F32, name="ct1")
    t2 = p.sb([128, 512], F32, name="ct2")
    y1 = p.sb([128, TC, 1024], F32, name="cy1")
    y2 = p.sb([128, TC, 1024], F32, name="cy2")
    tmp = p.sb([128, 1024], F32, name="ctm")
    Hr = p.sb([128, KC, 1024], F32, name="cHr")
    Hi = p.sb([128, KC, 1024], F32, name="cHi")
    m = 0
    for o in range(2):
        for kc in range(KC):
            if True:
                for cb in range(2):
                    fcol = (o * 2 + 0) * 1024 + cb * 512
                    bcol = (o * 2 + 1) * 1024 + cb * 512
                    for part, (kf, kb, Hd, hk) in enumerate(((0, 2, Hr, "cHr"), (1, 3, Hi, "cHi"))):
                        ps, pk = psm[m % 4], f"cpm{m%4}"
                        n = 0
                        for tc in range(TC):
                            for (kind, col) in ((kf, fcol), (kb, bcol)):
                                p.op("pe", lambda e, ps=ps, kind=kind, tc=tc, kc=kc, col=col, n=n: e.matmul(ps[:, :], lhsT=fw[:, kind, tc, kc, :], rhs=fT[:, tc, col:col + 512],
                                                                                                         start=(n == 0), stop=(n == 2 * TC - 1)), reads=["cfw", "cfT"], writes=[pk])
                                n += 1
                        p.op("dve", lambda e, ps=ps, Hd=Hd, kc=kc, o=o, cb=cb: e.tensor_tensor(out=Hd[:, kc, cb * 512:(cb + 1) * 512], in0=ps[:, :],
                                                                                               in1=rinv[:, o, cb * 512:(cb + 1) * 512], op=ALU.mult), reads=[pk, "crinv"], writes=[hk])
                        m += 1
        sig = (lambda tc, c0: sT[:, tc, c0:c0 + 512]) if o == 0 else (lambda tc, c0: y1[:, tc, c0:c0 + 512])
        sigk = "csT" if o == 0 else "cy1"
        for kc in range(KC):
            for cb in range(2):
                c0 = cb * 512
                psr, prk, psi, pik = psm[0], "cpm0", psm[1], "cpm1"
                for tc in range(TC):
                    p.op("pe", lambda e, tc=tc, kc=kc, c0=c0, sg=sig: e.matmul(psr[:, :], lhsT=fw[:, 0, tc, kc, :], rhs=sg(tc, c0), start=(tc == 0), stop=(tc == TC - 1)),
                         reads=["cfw", sigk], writes=[prk])
                for tc in range(TC):
                    p.op("pe", lambda e, tc=tc, kc=kc, c0=c0, sg=sig: e.matmul(psi[:, :], lhsT=fw[:, 1, tc, kc, :], rhs=sg(tc, c0), start=(tc == 0), stop=(tc == TC - 1)),
                         reads=["cfw", sigk], writes=[pik])
                hs = slice(c0, c0 + 512)
                xs_ = slice(c0, c0 + 512)
                p.op("dve", lambda e, kc=kc, hs=hs: e.tensor_tensor(out=t1[:, :], in0=psr[:, :], in1=Hr[:, kc, hs], op=ALU.mult), reads=[prk, "cHr"], writes=["ct1"])
                p.op("dve", lambda e, kc=kc, hs=hs: e.tensor_tensor(out=t2[:, :], in0=psi[:, :], in1=Hi[:, kc, hs], op=ALU.mult), reads=[pik, "cHi"], writes=["ct2"])
                p.op("pool", lambda e, kc=kc, xs_=xs_: e.tensor_tensor(out=Xr[:, kc, xs_], in0=t1[:, :], in1=t2[:, :], op=ALU.subtract), reads=["ct1", "ct2"], writes=["cXr"])
                p.op("dve", lambda e, kc=kc, hs=hs: e.tensor_tensor(out=t1[:, :], in0=psr[:, :], in1=Hi[:, kc, hs], op=ALU.mult), reads=[prk, "cHi"], writes=["ct1"])
                p.op("dve", lambda e, kc=kc, hs=hs: e.tensor_tensor(out=t2[:, :], in0=psi[:, :], in1=Hr[:, kc, hs], op=ALU.mult), reads=[pik, "cHr"], writes=["ct2"])
                p.op("pool", lambda e, kc=kc, xs_=xs_: e.tensor_tensor(out=Xi[:, kc, xs_], in0=t1[:, :], in1=t2[:, :], op=ALU.add), reads=["ct1", "ct2"], writes=["cXi"])
        dst, dstk = (y1, "cy1") if o == 0 else (y2, "cy2")
        for tc in range(TC):
            for cb in range(2):
                c0 = cb * 512
                ps, pk = psm[2 + cb], f"cpm{2+cb}"
                n = 0
                for kc in range(KC):
                    for (kind, Xs, xk) in ((0, Xr, "cXr"), (1, Xi, "cXi")):
                        p.op("pe", lambda e, ps=ps, kind=kind, kc=kc, tc=tc, Xs=Xs, c0=c0, n=n: e.matmul(ps[:, :], lhsT=iv[:, kind, kc, tc, :], rhs=Xs[:, kc, c0:c0 + 512],
                                                                                                      start=(n == 0), stop=(n == 2 * KC - 1)), reads=["civ", xk], writes=[pk])
                        n += 1
                src = (lambda: sT[:, tc, c0:c0 + 512]) if o == 0 else (lambda: y1[:, tc, c0:c0 + 512])
                gate = sT[:, tc, (1 + o) * 1024 + c0:(1 + o) * 1024 + c0 + 512]
                srcap = src()
                p.op("dve", lambda e, srcap=srcap, o=o, c0=c0: e.tensor_tensor(out=tmp[:, c0:c0 + 512], in0=srcap, in1=dsk[:, o, c0:c0 + 512], op=ALU.mult),
                     reads=[sigk, "cdsk"], writes=["ctm"])
                p.op("dve", lambda e, ps=ps, c0=c0: e.tensor_tensor(out=tmp[:, c0:c0 + 512], in0=ps[:, :], in1=tmp[:, c0:c0 + 512], op=ALU.add), reads=[pk, "ctm"], writes=["ctm"])
                p.op("pool", lambda e, dst=dst, tc=tc, c0=c0, gate=gate: e.tensor_tensor(out=dst[:, tc, c0:c0 + 512], in0=tmp[:, c0:c0 + 512], in1=gate, op=ALU.mult),
                     reads=["ctm", "csT"], writes=[dstk])
    yo = [p.sb([128, l], BF16, name=f"cyo{i}") for i in range(2)]
    k = 0
    for cc in range(8):
        ps, pk = pst[cc % 2], f"cpt{cc%2}"
        for tc in range(TC):
            p.op("pe", lambda e, ps=ps, tc=tc, cc=cc: e.transpose(out=ps[:, tc * 128:(tc + 1) * 128], in_=y2[:, tc, cc * 128:(cc + 1) * 128], identity=ident[:, :]),
                 reads=["cy2", "cid"], writes=[pk])
        o_, ok = yo[cc % 2], f"cyo{cc%2}"
        p.op("act", lambda e, ps=ps, o_=o_: e.copy(out=o_[:, :], in_=ps[:, 0:l]), reads=[pk], writes=[ok])
        p.dma("pool", Y[cc * 128:(cc + 1) * 128, 0:l], o_[:, :], reads=[ok], writes=["Y"])


def s5_layouts(lam_re, lam_im, log_dt, b_re, b_im, c_re, c_im):
    def ls_of(a):
        return np.ascontiguousarray(a.reshape(2, 16, 2, 64).transpose(2, 3, 0, 1).reshape(128, 32))
    ldt = np.broadcast_to(log_dt[:, :, None], (2, 32, 64))
    LS = np.stack([ls_of(lam_re), ls_of(lam_im), ls_of(ldt)]).astype(np.float32)
    BT = np.zeros((2, 32, 16, 128), np.float32)
    CT = np.zeros((2, 128, 16, 32), np.float32)
    for i, (b, c) in enumerate(((b_re, c_re), (b_im, c_im))):
        for gs in range(2):
            bb = b.reshape(16, 2, 64, 16)[:, gs]
            BT[i, gs * 16:(gs + 1) * 16, :, gs * 64:(gs + 1) * 64] = bb.transpose(2, 0, 1)
            cc = c.reshape(16, 2, 16, 64)[:, gs]
            CT[i, gs * 64:(gs + 1) * 64, :, gs * 16:(gs + 1) * 16] = cc.transpose(2, 0, 1)
    return LS, BT, CT


SEQ = 8192
NCTX = 256
T = SEQ + NCTX
DEPTH = 4
NCORES = 4


def ph_init(p, XIN, CIN, PE, XT):
    a = [p.sb([128, 8, 512], F32, name=f"ia{i}") for i in range(2)]
    b = [p.sb([128, 8, 512], F32, name=f"ib{i}") for i in range(2)]
    p.dma("sp", a[0][:, :, :NCTX], fm(CIN, 0, NCTX), writes=["ia0"])
    p.dma("pool", fm(XT, 0, NCTX), a[0][:, :, :NCTX], reads=["ia0"], writes=["XT"])
    for i, t0 in enumerate(range(0, SEQ, 512)):
        k = (i + 1) % 2
        p.dma("sp", a[k][:, :, :], fm(XIN, t0, 512), writes=[f"ia{k}"])
        p.dma("sp", b[k][:, :, :], fm(PE, t0, 512), writes=[f"ib{k}"])
        p.op("dve" if i % 2 else "pool", lambda e, k=k: e.tensor_tensor(out=a[k][:, :, :], in0=a[k][:, :, :], in1=b[k][:, :, :], op=ALU.add),
             reads=[f"ia{k}", f"ib{k}"], writes=[f"ia{k}"])
        p.dma("pool", fm(XT, NCTX + t0, 512), a[k][:, :, :], reads=[f"ia{k}"], writes=["XT"])


def ph_mod(p, CV, MW, MB, M):
    cv = p.sb([128, 8, 2], F32, name="cv")
    for r in range(2):
        p.dma("sp", cv[:, :, r], CV[r].rearrange("(c p) -> p c", p=128), writes=["cv"], allow_slow_non_contiguous=True)
    p.op("act", lambda e: e.activation(out=cv[:, :, :], in_=cv[:, :, :], func=AF.Silu), reads=["cv"], writes=["cv"])
    ws = [p.sb([128, 8, 512], F32, name=f"mw{i}") for i in range(2)]
    bs = [p.sb([2, 512], F32, name=f"mb{i}") for i in range(2)]
    os_ = [p.sb([2, 512], F32, name=f"mo{i}") for i in range(2)]
    pss = [p.ps([128, 512], name=f"psm{i}") for i in range(2)]
    k = 0
    for i in range(DEPTH):
        for n0 in range(0, 6144, 512):
            w, wk, b, bk, o, ok, ps, pk = ws[k % 2], f"mw{k%2}", bs[k % 2], f"mb{k%2}", os_[k % 2], f"mo{k%2}", pss[k % 2], f"psm{k%2}"
            p.dma("sp", w[:, :, :], MW[i, :, n0:n0 + 512].rearrange("(c p) n -> p c n", p=128), writes=[wk])
            p.dma("sp", b[:, :], MB[i, n0:n0 + 512].partition_broadcast(2), writes=[bk])
            for c in range(8):
                p.op("pe", lambda e, ps=ps, w=w, c=c: e.matmul(ps[:2, :], lhsT=cv[:, c, :], rhs=w[:, c, :], start=(c == 0), stop=(c == 7)),
                     reads=["cv", wk], writes=[pk])
            p.op("dve", lambda e, ps=ps, b=b, o=o: e.tensor_tensor(out=o[:, :], in0=ps[:2, :], in1=b[:, :], op=ALU.add), reads=[pk, bk], writes=[ok])
            p.dma("pool", M[i, :, n0:n0 + 512], o[:, :], reads=[ok], writes=["M"])
            k += 1


def token_tiles(N, with_ctx):
    tl = [(0, NCTX, 1)] if with_ctx else []
    return tl + [(NCTX + t0, N, 0) for t0 in range(0, SEQ, N)]


def build_program():
    p = Prog()
    I = {}

    def inp(name, shape, dt=F32):
        I[name] = p.dram(name, shape, dt, "ExternalInput")
        return I[name]

    XIN = inp("xT", [1024, SEQ]); CIN = inp("ctxT", [1024, NCTX]); PE = inp("peT", [1024, SEQ]); CV = inp("cvec", [2, 1024])
    MW = inp("mod_w", [4, 1024, 6144]); MB = inp("mod_b", [4, 6144])
    NMG = inp("norm_mix_g", [4, 1024]); NLG = inp("norm_mlp_g", [4, 1024])
    W1 = inp("mlp_w1", [4, 1024, 4096]); W2 = inp("mlp_w2", [4, 4096, 1024]); FNG = inp("final_norm_g", [1024])
    EIW = inp("ev_in_w", [2, 1024, 3616]); EOW = inp("ev_out_w", [2, 1536, 1024])
    LS = inp("s5_ls", [2, 3, 128, 32]); BT = inp("s5_bt", [2, 2, 32, 16, 128]); CT = inp("s5_ct", [2, 2, 128, 16, 32])
    S5D = inp("s5_d", [2, 512]); GW = inp("s5_glu_w", [2, 512, 512]); GB = inp("s5_glu_b", [2, 512])
    MCW = inp("m2_conv_w", [2, 3, 2048]); MCB = inp("m2_conv_b", [2, 2048]); DTB = inp("m2_dt_bias", [2, 32]); ALOG = inp("m2_a_log", [2, 32])
    M2D = inp("m2_d", [2, 16]); M2G = inp("m2_norm_g", [2, 1024])
    HIW = inp("hy_in_w", [2, 1024, 3072]); HIB = inp("hy_in_b", [2, 3072]); HCW = inp("hy_conv_w", [2, 3, 3072]); HCB = inp("hy_conv_b", [2, 3072])
    HW1 = inp("hy_f_w1", [2, 33, 64]); HB1 = inp("hy_f_b1", [2, 64]); HQ1 = inp("hy_f_freq1", [2, 64])
    HW2 = inp("hy_f_w2", [2, 64, 64]); HB2 = inp("hy_f_b2", [2, 64]); HQ2 = inp("hy_f_freq2", [2, 64]); HW3 = inp("hy_f_w3", [2, 64, 4096])
    HYD = inp("hy_d", [2, 2, 1024]); HOW = inp("hy_out_w", [2, 1024, 1024]); HOB = inp("hy_out_b", [2, 1024])
    IDENT = inp("ident", [128, 128]); MASKS = inp("masks", [2, 128, 512]); TABS = inp("tabs", [128, 1536])
    FEAT_L = inp("feats_l", [33, SEQ]); TL_L = inp("tl_l", [SEQ]); FEAT_C = inp("feats_c", [33, NCTX]); TL_C = inp("tl_c", [NCTX]); DELTAS = inp("deltas", [1024])
    FWT = inp("ctx_fwt", [128, 4, 2, 4, 128]); IVT = inp("ctx_ivt", [128, 2, 4, 2, 128])
    OUT = p.dram("out", [SEQ, 1024], F32, "ExternalOutput")

    XT = p.dram("XT", [1024, T], F32); M = p.dram("M", [4, 2, 6144], F32)
    PROJ = p.dram("PROJ", [3616, T], F32); Y = p.dram("Y", [1536, T], BF16)
    YPRE = p.dram("YPRE", [512, T], F32); XBC = p.dram("XBC", [2048, T], F32)
    SC = p.dram("SC", [96, T], F32); CF2 = p.dram("CF2", [2, 32, T], F32); ET = p.dram("ET", [T // 128, 32], F32)
    YT = p.dram("YT", [T, 1024], F32); YM = p.dram("YM", [1024, T], F32)
    ZC = p.dram("ZC", [3072, T], F32); FRAW = p.dram("FRAW", [4096, SEQ], F32); RINV = p.dram("RINV", [2, 1024], F32)
    FILT = p.dram("FILT", [2048, NFFT], F32); HF = p.dram("HF", [2, 2048, 128, 128], F32)

    ph_init(p, XIN, CIN, PE, XT); p.end()
    ph_mod(p, CV, MW, MB, M); p.end()
    segs = [(0, NCTX), (NCTX, T)]
    for i in range(DEPTH):
        j = i // 2
        ctx_later = i < 2
        sl = lambda k: slice(k * 1024, (k + 1) * 1024)
        mods_a = [(M[i, r, sl(1)], M[i, r, sl(0)]) for r in range(2)]
        mods_f = [(M[i, r, sl(4)], M[i, r, sl(3)]) for r in range(2)]
        ga = [M[i, r, sl(2)] for r in range(2)]
        gf = [M[i, r, sl(5)] for r in range(2)]
        if i % 2 == 0:
            ph_tok_in(p, XT, EIW[j], 3616, NMG[i], mods_a, token_tiles(512, True), PROJ); p.end()
            ph_s5_scan(p, PROJ, T, NCTX, LS[j], BT[j], CT[j], S5D[j], YPRE); p.end()
            ph_s5_glu(p, YPRE, T, GW[j], GB[j], Y); p.end()
            ph_conv_silu(p, PROJ, 1536, 2048, T, segs, MCW[j], MCB[j], XBC); p.end()
            ph_ssd_prep(p, PROJ, T, DTB[j], ALOG[j], SC, CF2, ET); p.end()
            for d in range(2):
                ph_ssd_pass(p, d, XBC, T, NCTX, SC, CF2, ET, M2D[j].partition_broadcast(128), IDENT, MASKS, YT, YM); p.end()
            ph_ssd_post(p, YM, PROJ, T, M2G[j], Y); p.end()
            ph_outproj(p, XT, Y, 1536, EOW[j], ga, token_tiles(512, ctx_later)); p.end()
        else:
            ph_tok_in(p, XT, HIW[j], 3072, NMG[i], mods_a, token_tiles(512, ctx_later), PROJ, bias=HIB[j]); p.end()
            ph_conv_silu(p, PROJ, 0, 3072, T, segs, HCW[j], HCB[j], ZC, silu=False); p.end()
            fargs = (HW1[j], HB1[j], HQ1[j], HW2[j], HB2[j], HQ2[j], HW3[j], DELTAS)
            ph_hy_filter_raw(p, SEQ, FEAT_L, TL_L, *fargs, FRAW, RINV); p.end()
            ph_hy_filter_asm(p, SEQ, FRAW, RINV, FILT); p.end()
            ph_hy_filter_fft(p, FILT, TABS, HF); p.end()
            ph_hy_conv(p, ZC, NCTX, SEQ, HF, HYD[j], TABS, Y); p.end()
            if ctx_later:
                ph_hy_filter_raw(p, NCTX, FEAT_C, TL_C, *fargs, FRAW[:, 0:NCTX], RINV); p.end()
                ph_hy_ctx_dense(p, ZC, NCTX, FRAW[:, 0:NCTX], RINV, HYD[j], FWT, IVT, IDENT, Y); p.end()
            ph_outproj(p, XT, Y, 1024, HOW[j], ga, token_tiles(512, ctx_later), bias=HOB[j]); p.end()
        ph_mlp(p, XT, W1[i], W2[i], NLG[i], mods_f, gf, token_tiles(256, ctx_later)); p.end()
    ph_final(p, XT, FNG, token_tiles(512, False), OUT, NCTX, IDENT); p.end()
    return p


def grid_sincos_np(n, dm):
    rows = n // 64
    quarter = dm // 4
    omega = (1.0 / (np.float32(10000.0) ** (np.arange(quarter, dtype=np.float32) / np.float32(quarter)))).astype(np.float32)
    ang_r = np.arange(rows, dtype=np.float32)[:, None] * omega
    ang_c = np.arange(64, dtype=np.float32)[:, None] * omega
    emb_r = np.concatenate([np.sin(ang_r), np.cos(ang_r)], axis=-1)
    emb_c = np.concatenate([np.sin(ang_c), np.cos(ang_c)], axis=-1)
    half = emb_r.shape[-1]
    pe = np.concatenate([np.broadcast_to(emb_r[:, None, :], (rows, 64, half)), np.broadcast_to(emb_c[None, :, :], (rows, 64, half))], axis=-1)
    return pe.reshape(rows * 64, 2 * half).astype(np.float32)


def feats_tables(l):
    t = np.linspace(0.0, 1.0, l, dtype=np.float32)[:, None]
    bands = np.linspace(1e-4, 15, 16, dtype=np.float32)
    ang = (np.float32(2.0 * math.pi / l)) * np.arange(l, dtype=np.float32)[:, None] * bands
    feats = np.concatenate([t, np.cos(ang), -np.sin(ang)], axis=-1).astype(np.float32)
    return np.ascontiguousarray(feats.T), np.ascontiguousarray(t[:, 0])


def host_inputs(inputs):
    f = lambda a: np.ascontiguousarray(np.asarray(a, dtype=np.float32))
    g = {k: f(v) for k, v in inputs.items()}
    shared = {k: g[k] for k in ("mod_w", "mod_b", "norm_mix_g", "norm_mlp_g", "mlp_w1", "mlp_w2", "final_norm_g", "ev_in_w", "ev_out_w",
                                "s5_d", "s5_glu_w", "s5_glu_b", "m2_conv_w", "m2_conv_b", "m2_d", "m2_norm_g", "hy_in_w", "hy_in_b",
                                "hy_conv_w", "hy_conv_b", "hy_f_w1", "hy_f_b1", "hy_f_freq1", "hy_f_w2", "hy_f_b2", "hy_f_freq2", "hy_f_w3",
                                "hy_d", "hy_out_w", "hy_out_b")}
    shared["m2_dt_bias"] = g["m2_dt_bias"].reshape(2, 32)
    shared["m2_a_log"] = g["m2_a_log"].reshape(2, 32)
    lay = [s5_layouts(g["s5_lam_re"][j], g["s5_lam_im"][j], g["s5_log_dt"][j], g["s5_b_re"][j], g["s5_b_im"][j], g["s5_c_re"][j], g["s5_c_im"][j])
           for j in range(2)]
    shared["s5_ls"] = np.stack([l[0] for l in lay]); shared["s5_bt"] = np.stack([l[1] for l in lay]); shared["s5_ct"] = np.stack([l[2] for l in lay])
    shared["ident"] = np.eye(128, dtype=np.float32)
    sq = np.arange(128)
    mf = np.where(sq[None, :] >= sq[:, None], 0.0, -30000.0).astype(np.float32)
    mb = np.where(sq[None, :] <= sq[:, None], 0.0, -30000.0).astype(np.float32)
    shared["masks"] = np.stack([np.tile(mf, (1, 4)), np.tile(mb, (1, 4))])
    shared["tabs"] = hy_tables()
    shared["ctx_fwt"], shared["ctx_ivt"] = hy_ctx_tables(NCTX)
    shared["feats_l"], shared["tl_l"] = feats_tables(SEQ)
    shared["feats_c"], shared["tl_c"] = feats_tables(NCTX)
    shared["deltas"] = np.abs(np.linspace(math.log(1e-2) / 1.5, math.log(1e-2) / 0.3, 1024, dtype=np.float32)).astype(np.float32)
    shared["peT"] = np.ascontiguousarray(grid_sincos_np(SEQ, 1024).T)
    maps = []
    for b in range(NCORES):
        m = dict(shared)
        m["xT"] = np.ascontiguousarray(g["x"][b].T)
        m["ctxT"] = np.ascontiguousarray(g["ctx"][b].T)
        m["cvec"] = np.stack([g["c"][b], g["c_ctx"]])
        maps.append(m)
    return maps


def kernel(**inputs):
    p = build_program()
    nc = p.build()
    maps = host_inputs(inputs)
    res = run_bass_kernel_spmd(nc, maps, core_ids=list(range(NCORES)))
    return np.stack([np.asarray(r["out"], dtype=np.float32) for r in res.results], axis=0)
```

```python
import contextlib
import math
import numpy as np
import concourse.bass as bass
import concourse.mybir as mybir
from concourse.bass_utils import run_bass_kernel_spmd


F32 = mybir.dt.float32
BF16 = mybir.dt.bfloat16
I32 = mybir.dt.int32
AF = mybir.ActivationFunctionType
ALU = mybir.AluOpType
AX = mybir.AxisListType


class Prog:
    COMPUTE = ("pe", "dve", "act", "pool")
    ENGS = ("pe", "dve", "act", "pool", "sp")
    QS = ("sp", "act", "pool")
    NDS = 8

    def __init__(self):
        self.nc = bass.Bass("TRN2", target_bir_lowering=False)
        nc = self.nc
        self.top = contextlib.ExitStack()
        self.sems = {e: self.top.enter_context(nc.semaphore(f"s_{e}")) for e in self.COMPUTE}
        self.dsems = {q: [self.top.enter_context(nc.semaphore(f"d_{q}{i}")) for i in range(self.NDS)] for q in self.QS}
        self.cnt = {e: 0 for e in self.COMPUTE}
        self.dval = {q: [0] * self.NDS for q in self.QS}
        self.dnext = {q: 0 for q in self.QS}
        self.n_sb = 0
        self.stats = {e: 0 for e in self.ENGS}
        self.stack = None
        self.barrier_tok = []
        self.begin()

    def begin(self):
        self.ops = []
        self.lastw = {}
        self.readers = {}
        self.stack = contextlib.ExitStack()

    def dram(self, name, shape, dt, kind="Internal"):
        return self.nc.dram_tensor(name, list(shape), dt, kind=kind).ap()

    def sb(self, shape, dt=F32, name=None):
        self.n_sb += 1
        return self.stack.enter_context(self.nc.sbuf_tensor(f"{name or 'sb'}_{self.n_sb}", list(shape), dt))

    def ps(self, shape, dt=F32, name=None):
        self.n_sb += 1
        return self.stack.enter_context(self.nc.psum_tensor(f"{name or 'ps'}_{self.n_sb}", list(shape), dt))

    def op(self, eng, fn, reads=(), writes=(), dma=False):
        deps = set()
        for k in reads:
            if k in self.lastw:
                deps.add(self.lastw[k])
        for k in writes:
            if k in self.lastw:
                deps.add(self.lastw[k])
            for r in self.readers.get(k, ()):
                deps.add(r)
        idx = len(self.ops)
        self.ops.append(dict(eng=eng, fn=fn, deps=deps, dma=dma, sig=False))
        for k in reads:
            rl = self.readers.setdefault(k, [])
            if not dma:
                rl[:] = [r for r in rl if self.ops[r]["dma"] or self.ops[r]["eng"] != eng]
            rl.append(idx)
        for k in writes:
            self.lastw[k] = idx
            self.readers[k] = []
        return idx

    def dma(self, q, out, in_, reads=(), writes=(), **kw):
        return self.op(q, lambda e: e.dma_start(out=out, in_=in_, **kw), reads, writes, dma=True)

    def end(self):
        nc = self.nc
        ops = self.ops
        for o in ops:
            for d in o["deps"]:
                ops[d]["sig"] = True
        per = {e: [] for e in self.ENGS}
        for o in ops:
            per[o["eng"]].append(o)
        for e in self.COMPUTE:
            for o in reversed(per[e]):
                if not o["dma"]:
                    o["sig"] = True
                    break
        for o in ops:
            if o["dma"]:
                q = o["eng"]
                k = self.dnext[q]
                self.dnext[q] = (k + 1) % self.NDS
                o["prev"] = (q, k, self.dval[q][k])
                self.dval[q][k] += 16
                o["tok"] = ("d", q, k, self.dval[q][k])
            elif o["sig"]:
                self.cnt[o["eng"]] += 1
                o["tok"] = ("c", o["eng"], self.cnt[o["eng"]])
        for e in self.ENGS:
            self.stats[e] += len(per[e])
        start_tok = self.barrier_tok
        end_tok = [("c", e, self.cnt[e]) for e in self.COMPUTE if self.cnt[e] > 0]
        for q in self.QS:
            for k in range(self.NDS):
                if self.dval[q][k] > 0:
                    end_tok.append(("d", q, k, self.dval[q][k]))
        self.barrier_tok = end_tok
        sems, dsems = self.sems, self.dsems

        def emit(e_name, eng):
            waited = {}

            def wait(t):
                key = t[:-1]
                if waited.get(key, 0) >= t[-1]:
                    return
                waited[key] = t[-1]
                sem = sems[t[1]] if t[0] == "c" else dsems[t[1]][t[2]]
                eng.wait_ge(sem, t[-1])

            for t in start_tok:
                wait(t)
            for o in per[e_name]:
                if o["dma"]:
                    q, k, pv = o["prev"]
                    if pv > 0:
                        wait(("d", q, k, pv))
                for d in sorted(o["deps"]):
                    t = ops[d]["tok"]
                    if t[0] == "c" and t[1] == e_name and e_name == "pe":
                        continue
                    wait(t)
                ins = o["fn"](eng)
                if o["dma"]:
                    t = o["tok"]
                    ins.then_inc(dsems[t[1]][t[2]], 16)
                elif o["sig"]:
                    ins.then_inc(sems[e_name], 1)

        with nc.Block() as block:
            @block.tensor
            def _(eng):
                emit("pe", eng)

            @block.vector
            def _(eng):
                emit("dve", eng)

            @block.scalar
            def _(eng):
                emit("act", eng)

            @block.gpsimd
            def _(eng):
                emit("pool", eng)

            @block.sync
            def _(eng):
                emit("sp", eng)
        self.stack.close()
        self.begin()

    def build(self):
        nc = self.nc
        toks = self.barrier_tok
        sems, dsems = self.sems, self.dsems
        with nc.Block() as block:
            def fin(eng):
                for t in toks:
                    sem = sems[t[1]] if t[0] == "c" else dsems[t[1]][t[2]]
                    eng.wait_ge(sem, t[-1])

            @block.sync
            def _(eng):
                fin(eng)

            @block.gpsimd
            def _(eng):
                fin(eng)
        self.top.close()
        return nc


def run(p, in_maps, n=None, trace=False):
    nc = p.build()
    n = n or len(in_maps)
    return run_bass_kernel_spmd(nc, in_maps, core_ids=list(range(n)), trace=trace)


EPS = 1e-6
D = 1024


def col(ap1d):
    return ap1d.rearrange("(c p) -> p c", p=128)


def load_cols(p, dst, src1d, key, q="sp"):
    p.dma(q, dst, col(src1d), writes=[key], allow_slow_non_contiguous=True)


def load_weight_bf16(p, w_d, K, F, name, q="sp", chunk=1024):
    kc = K // 128
    wb = p.sb([128, kc, F], BF16, name=name)
    stg = [p.sb([128, chunk], F32, name=f"{name}_stg{i}") for i in range(2)]
    i = 0
    for b, f0 in enumerate(range(0, F, chunk)):
        fn = min(chunk, F - f0)
        for c in range(kc):
            s = stg[i % 2]
            p.dma(q, s[:, :fn], w_d[c * 128:(c + 1) * 128, f0:f0 + fn], writes=[f"{name}_stg{i%2}"])
            if i % 2 == 0:
                p.op("pool", lambda e, s=s, c=c, f0=f0, fn=fn: e.tensor_copy(out=wb[:, c, f0:f0 + fn], in_=s[:, :fn]),
                     reads=[f"{name}_stg{i%2}"], writes=[f"{name}_b{b}"])
            else:
                p.op("act", lambda e, s=s, c=c, f0=f0, fn=fn: e.copy(out=wb[:, c, f0:f0 + fn], in_=s[:, :fn]),
                     reads=[f"{name}_stg{i%2}"], writes=[f"{name}_b{b}"])
            i += 1
    return wb


def wkey(name, col, chunk=1024):
    return f"{name}_b{col // chunk}"


class Norm:
    def __init__(self, p, NMAX, nmod, g_d, mods):
        self.p = p
        ones0 = p.sb([128, 128], F32, name="ones0")
        p.op("pool", lambda e: e.memset(ones0[:, :], 1.0), writes=["ones0"])
        self.ones = p.sb([128, 128], F32, name="ones")
        p.op("pool", lambda e: e.tensor_copy(out=self.ones[:, :].bitcast(mybir.dt.float32r), in_=ones0[:, :]), reads=["ones0"], writes=["ones"])
        self.eps = p.sb([128, 1], F32, name="eps")
        p.op("pool", lambda e: e.memset(self.eps[:, :], EPS), writes=["eps"])
        self.sq = p.sb([128, 8, NMAX], F32, name="sq")
        self.tmp = p.sb([128, 8, NMAX], F32, name="ntmp")
        self.rstd = p.sb([128, NMAX], F32, name="rstd")
        self.ps_ss = p.ps([128, 512], name="ps_ss")
        g_sb = p.sb([128, 8], F32, name="g_sb")
        load_cols(p, g_sb[:, :], g_d, "g_sb")
        self.gs = p.sb([128, nmod, 8], F32, name="gs_sb")
        self.sh = p.sb([128, nmod, 8], F32, name="sh_sb")
        sc_sb = p.sb([128, nmod, 8], F32, name="sc_sb")
        for m, (sc_d, sh_d) in enumerate(mods):
            if sc_d is None:
                p.op("pool", lambda e, m=m: e.memset(sc_sb[:, m, :], 0.0), writes=["sc_sb"])
                p.op("pool", lambda e, m=m: e.memset(self.sh[:, m, :], 0.0), writes=["mods"])
            else:
                load_cols(p, sc_sb[:, m, :], sc_d, "sc_sb")
                load_cols(p, self.sh[:, m, :], sh_d, "mods")
        for m in range(nmod):
            p.op("dve", lambda e, m=m: e.scalar_tensor_tensor(out=self.gs[:, m, :], in0=sc_sb[:, m, :], scalar=1.0, in1=g_sb[:, :],
                                                               op0=ALU.add, op1=ALU.mult),
                 reads=["sc_sb", "g_sb"], writes=["mods"])

    def apply(self, x_sb, xk, N, mi, ht, hk):
        p = self.p
        sq, ones, ps_ss, rstd, tmp, eps_t = self.sq, self.ones, self.ps_ss, self.rstd, self.tmp, self.eps
        gs, sh = self.gs[:, mi, :], self.sh[:, mi, :]
        F32R = mybir.dt.float32r
        p.op("act", lambda e: e.activation(out=sq[:, :, :N].bitcast(F32R), in_=x_sb[:, :, :N], func=AF.Square), reads=[xk], writes=["sq"])
        for c in range(8):
            p.op("pe", lambda e, c=c: e.matmul(ps_ss[:, :N], lhsT=ones[:, :].bitcast(F32R), rhs=sq[:, c, :N].bitcast(F32R), start=(c == 0), stop=(c == 7)),
                 reads=["sq", "ones"], writes=["ps_ss"])
        p.op("act", lambda e: e.activation(out=rstd[:, :N], in_=ps_ss[:, :N], func=AF.Sqrt, scale=1.0 / D, bias=eps_t[:, 0:1]),
             reads=["ps_ss", "eps"], writes=["rstd"])
        p.op("dve", lambda e: e.reciprocal(out=rstd[:, :N], in_=rstd[:, :N]), reads=["rstd"], writes=["rstd"])
        for c in range(8):
            p.op("dve", lambda e, c=c: e.tensor_tensor(out=tmp[:, c, :N], in0=x_sb[:, c, :N], in1=rstd[:, :N], op=ALU.mult),
                 reads=[xk, "rstd"], writes=["ntmp" + str(c)])
            p.op("act", lambda e, c=c: e.activation(out=ht[:, c, :N], in_=tmp[:, c, :N], func=AF.Identity,
                                                   scale=gs[:, c:c + 1], bias=sh[:, c:c + 1]),
                 reads=["ntmp" + str(c), "mods"], writes=[hk])


def fm(ap2d, t0, N):
    return ap2d[:, t0:t0 + N].rearrange("(c p) n -> p c n", p=128)


def ph_tok_in(p, XT, W, F, g_d, mods, tiles, PROJ, bias=None):
    nmod = len(mods)
    nfc = (F + 127) // 128
    nrm = Norm(p, 512, nmod, g_d, mods)
    if bias is not None:
        b_sb = p.sb([128, nfc], F32, name="b_sb")
        p.op("pool", lambda e: e.memset(b_sb[:, :], 0.0), writes=["b_sb"])
        nfull = F // 128
        p.dma("sp", b_sb[:, :nfull], col(bias[:nfull * 128]), writes=["b_sb"], allow_slow_non_contiguous=True)
        if F % 128:
            p.dma("sp", b_sb[:F % 128, nfull:nfull + 1], bias[nfull * 128:F].rearrange("(p o) -> p o", o=1), writes=["b_sb"],
                  allow_slow_non_contiguous=True)
    xs = [p.sb([128, 8, 512], F32, name=f"xt{i}") for i in range(2)]
    p.dma("sp", xs[0][:, :, :tiles[0][1]], fm(XT, tiles[0][0], tiles[0][1]), reads=["XT"], writes=["xt0"])
    wb = load_weight_bf16(p, W, D, F, "wb")
    hts = [p.sb([128, 8, 512], BF16, name=f"ht{i}") for i in range(2)]
    pss = [p.ps([128, 512], name=f"psm{i}") for i in range(4)]
    outs = [p.sb([128, 512], F32, name=f"o{i}") for i in range(4)]
    oi = 0
    for ti, (t0, N, mi) in enumerate(tiles):
        x_sb, xk, ht, hk = xs[ti % 2], f"xt{ti%2}", hts[ti % 2], f"ht{ti%2}"
        if ti > 0:
            p.dma("sp", x_sb[:, :, :N], fm(XT, t0, N), reads=["XT"], writes=[xk])
        nrm.apply(x_sb, xk, N, mi, ht, hk)
        for fc in range(nfc):
            M = min(128, F - fc * 128)
            ps, pk, o, ok = pss[oi % 4], f"psm{oi%4}", outs[oi % 4], f"o{oi%4}"
            for c in range(8):
                p.op("pe", lambda e, ps=ps, c=c, fc=fc, M=M, ht=ht, N=N: e.matmul(ps[:M, :N], lhsT=wb[:, c, fc * 128:fc * 128 + M],
                                                                                  rhs=ht[:, c, :N], start=(c == 0), stop=(c == 7)),
                     reads=[wkey("wb", fc * 128), hk], writes=[pk])
            if bias is not None:
                p.op("act", lambda e, ps=ps, o=o, M=M, N=N, fc=fc: e.activation(out=o[:M, :N], in_=ps[:M, :N], func=AF.Identity,
                                                                                 bias=b_sb[:M, fc:fc + 1], scale=1.0),
                     reads=[pk, "b_sb"], writes=[ok])
            elif oi % 2 == 0:
                p.op("act", lambda e, ps=ps, o=o, M=M, N=N: e.copy(out=o[:M, :N], in_=ps[:M, :N]), reads=[pk], writes=[ok])
            else:
                p.op("dve", lambda e, ps=ps, o=o, M=M, N=N: e.tensor_copy(out=o[:M, :N], in_=ps[:M, :N]), reads=[pk], writes=[ok])
            p.dma("pool", PROJ[fc * 128:fc * 128 + M, t0:t0 + N], o[:M, :N], reads=[ok], writes=["PROJ"])
            oi += 1


def ph_outproj(p, XT, Y, CM, W, ga_list, tiles, bias=None):
    kc = CM // 128
    nmod = len(ga_list)
    ga = p.sb([128, nmod, 8], F32, name="ga")
    for m, gd in enumerate(ga_list):
        load_cols(p, ga[:, m, :], gd, "ga")
    if bias is not None:
        b_sb = p.sb([128, 8], F32, name="ob_sb")
        load_cols(p, b_sb[:, :], bias, "ob")
        gb = p.sb([128, nmod, 8], F32, name="gb")
        for m in range(nmod):
            p.op("dve", lambda e, m=m: e.tensor_tensor(out=gb[:, m, :], in0=ga[:, m, :], in1=b_sb[:, :], op=ALU.mult),
                 reads=["ga", "ob"], writes=["gb"])
    xs = [p.sb([128, 8, 512], F32, name=f"xo{i}") for i in range(2)]
    ys = [p.sb([128, kc, 512], BF16, name=f"yo{i}") for i in range(2)]
    p.dma("sp", xs[0][:, :, :tiles[0][1]], fm(XT, tiles[0][0], tiles[0][1]), reads=["XT"], writes=["xo0"])
    p.dma("sp", ys[0][:, :, :tiles[0][1]], fm(Y[0:CM, :], tiles[0][0], tiles[0][1]), reads=["Y"], writes=["yo0"])
    wb = load_weight_bf16(p, W, CM, D, "wo")
    pss = [p.ps([128, 512], name=f"pso{i}") for i in range(4)]
    oi = 0
    for ti, (t0, N, mi) in enumerate(tiles):
        x_sb, xk, y_sb, yk = xs[ti % 2], f"xo{ti%2}", ys[ti % 2], f"yo{ti%2}"
        if ti > 0:
            p.dma("sp", x_sb[:, :, :N], fm(XT, t0, N), reads=["XT"], writes=[xk])
            p.dma("sp", y_sb[:, :, :N], fm(Y[0:CM, :], t0, N), reads=["Y"], writes=[yk])
        if bias is not None:
            for c in range(8):
                p.op("act", lambda e, c=c, x_sb=x_sb, N=N, mi=mi: e.activation(out=x_sb[:, c, :N], in_=x_sb[:, c, :N], func=AF.Identity,
                                                                                 bias=gb[:, mi, c:c + 1], scale=1.0),
                     reads=[xk, "gb"], writes=[xk])
        for m in range(8):
            ps, pk = pss[oi % 4], f"pso{oi%4}"
            for c in range(kc):
                p.op("pe", lambda e, ps=ps, c=c, m=m, y_sb=y_sb, N=N: e.matmul(ps[:, :N], lhsT=wb[:, c, m * 128:(m + 1) * 128],
                                                                                rhs=y_sb[:, c, :N], start=(c == 0), stop=(c == kc - 1)),
                     reads=["wo_b0", yk], writes=[pk])
            p.op("dve", lambda e, ps=ps, m=m, x_sb=x_sb, N=N, mi=mi: e.scalar_tensor_tensor(
                out=x_sb[:, m, :N], in0=ps[:, :N], scalar=ga[:, mi, m:m + 1], in1=x_sb[:, m, :N], op0=ALU.mult, op1=ALU.add),
                 reads=[pk, xk, "ga"], writes=[xk])
            oi += 1
        p.dma("pool", fm(XT, t0, N), x_sb[:, :, :N], reads=[xk], writes=["XT"])


def ph_mlp(p, XT, W1, W2, g_d, mods, gf_list, tiles):
    nmod = len(mods)
    NT = 256
    nrm = Norm(p, NT, nmod, g_d, mods)
    gf = p.sb([128, nmod, 8], F32, name="gf")
    for m, gd in enumerate(gf_list):
        load_cols(p, gf[:, m, :], gd, "gf")
    xs = [p.sb([128, 8, NT], F32, name=f"xm{i}") for i in range(2)]
    p.dma("sp", xs[0][:, :, :tiles[0][1]], fm(XT, tiles[0][0], tiles[0][1]), reads=["XT"], writes=["xm0"])
    w1 = load_weight_bf16(p, W1, D, 4096, "w1")
    w2 = load_weight_bf16(p, W2, 4096, D, "w2")
    ht = p.sb([128, 8, NT], BF16, name="htm")
    h1 = p.sb([128, 32, NT], BF16, name="h1")
    rs = [p.sb([128, NT], F32, name=f"r{i}") for i in range(2)]
    pss = [p.ps([128, 512], name=f"psx{i}") for i in range(4)]
    oi = 0
    for ti, (t0, N, mi) in enumerate(tiles):
        x_sb, xk = xs[ti % 2], f"xm{ti%2}"
        if ti > 0:
            p.dma("sp", x_sb[:, :, :N], fm(XT, t0, N), reads=["XT"], writes=[xk])
        nrm.apply(x_sb, xk, N, mi, ht, "htm")
        for hc in range(32):
            ps, pk, r, rk = pss[oi % 4], f"psx{oi%4}", rs[oi % 2], f"r{oi%2}"
            for c in range(8):
                p.op("pe", lambda e, ps=ps, c=c, hc=hc, N=N: e.matmul(ps[:, :N], lhsT=w1[:, c, hc * 128:(hc + 1) * 128],
                                                                       rhs=ht[:, c, :N], start=(c == 0), stop=(c == 7)),
                     reads=[wkey("w1", hc * 128), "htm"], writes=[pk])
            p.op("act", lambda e, ps=ps, r=r, N=N: e.activation(out=r[:, :N], in_=ps[:, :N], func=AF.Relu), reads=[pk], writes=[rk])
            p.op("pool", lambda e, r=r, hc=hc, N=N: e.tensor_tensor(out=h1[:, hc, :N], in0=r[:, :N], in1=r[:, :N], op=ALU.mult),
                 reads=[rk], writes=["h1_" + str(hc)])
            oi += 1
        for m in range(8):
            ps, pk = pss[oi % 4], f"psx{oi%4}"
            for hc in range(32):
                p.op("pe", lambda e, ps=ps, hc=hc, m=m, N=N: e.matmul(ps[:, :N], lhsT=w2[:, hc, m * 128:(m + 1) * 128],
                                                                       rhs=h1[:, hc, :N], start=(hc == 0), stop=(hc == 31)),
                     reads=["w2_b0", "h1_" + str(hc)], writes=[pk])
            p.op("dve", lambda e, ps=ps, m=m, x_sb=x_sb, N=N, mi=mi: e.scalar_tensor_tensor(
                out=x_sb[:, m, :N], in0=ps[:, :N], scalar=gf[:, mi, m:m + 1], in1=x_sb[:, m, :N], op0=ALU.mult, op1=ALU.add),
                 reads=[pk, xk, "gf"], writes=[xk])
            oi += 1
        p.dma("pool", fm(XT, t0, N), x_sb[:, :, :N], reads=[xk], writes=["XT"])


def ph_final(p, XT, g_d, tiles, OUT, t_off, IDENT):
    nrm = Norm(p, 512, 1, g_d, [(None, None)])
    ident = p.sb([128, 128], F32, name="ident")
    p.dma("sp", ident[:, :], IDENT, writes=["ident"])
    xs = [p.sb([128, 8, 512], F32, name=f"xf{i}") for i in range(2)]
    hf = p.sb([128, 8, 512], F32, name="hf")
    pst = [p.ps([128, 512], name=f"pst{i}") for i in range(4)]
    ot = [p.sb([128, 1024], F32, name=f"ot{i}") for i in range(2)]
    oi = 0
    k = 0
    for ti, (t0, N, mi) in enumerate(tiles):
        x_sb, xk = xs[ti % 2], f"xf{ti%2}"
        p.dma("sp", x_sb[:, :, :N], fm(XT, t0, N), reads=["XT"], writes=[xk])
        nrm.apply(x_sb, xk, N, 0, hf, "hf")
        for j in range(N // 128):
            o, ok = ot[k % 2], f"ot{k%2}"
            for half in range(2):
                ps, pk = pst[oi % 4], f"pst{oi%4}"
                for cc in range(4):
                    c = half * 4 + cc
                    p.op("pe", lambda e, ps=ps, c=c, cc=cc, j=j: e.transpose(out=ps[:, cc * 128:(cc + 1) * 128],
                                                                            in_=hf[:, c, j * 128:(j + 1) * 128], identity=ident[:, :]),
                         reads=["hf", "ident"], writes=[pk])
                if half == 0:
                    p.op("act", lambda e, ps=ps, o=o: e.copy(out=o[:, 0:512], in_=ps[:, :]), reads=[pk], writes=[ok])
                else:
                    p.op("dve", lambda e, ps=ps, o=o: e.tensor_copy(out=o[:, 512:1024], in_=ps[:, :]), reads=[pk], writes=[ok])
                oi += 1
            tt = t0 + j * 128 - t_off
            p.dma("pool", OUT[tt:tt + 128, :], o[:, :], reads=[ok], writes=["OUT"])
            k += 1


TWO_PI = 2.0 * math.pi


def sin_tmps(p, shape, tag):
    return (p.sb(shape, F32, name=tag + "_a"), p.sb(shape, I32, name=tag + "_i"), p.sb(shape, F32, name=tag + "_f"),
            p.sb(shape, F32, name=tag + "_r"), p.sb(shape, F32, name=tag + "_m"))


def sin_rr(p, out, x, shape, tag, shift=0.0, tmps=None):
    a, ai, af, r, m = tmps if tmps is not None else sin_tmps(p, shape, tag)
    sl = tuple(slice(None) for _ in shape)
    k = tag
    p.op("dve", lambda e: e.tensor_scalar(out=a[sl], in0=x, scalar1=shift, scalar2=1.0 / TWO_PI, op0=ALU.add, op1=ALU.mult),
         reads=[k + "x"], writes=[k + "a"])
    p.op("dve", lambda e: e.tensor_copy(out=ai[sl], in_=a[sl]), reads=[k + "a"], writes=[k + "i"])
    p.op("dve", lambda e: e.tensor_copy(out=af[sl], in_=ai[sl]), reads=[k + "i"], writes=[k + "f"])
    p.op("dve", lambda e: e.tensor_scalar(out=r[sl], in0=x, scalar1=shift, scalar2=None, op0=ALU.add), reads=[k + "x"], writes=[k + "r"])
    p.op("dve", lambda e: e.scalar_tensor_tensor(out=r[sl], in0=af[sl], scalar=-TWO_PI, in1=r[sl], op0=ALU.mult, op1=ALU.add),
         reads=[k + "f", k + "r"], writes=[k + "r"])
    p.op("dve", lambda e: e.tensor_scalar(out=m[sl], in0=r[sl], scalar1=math.pi, scalar2=None, op0=ALU.is_gt), reads=[k + "r"], writes=[k + "m"])
    p.op("dve", lambda e: e.scalar_tensor_tensor(out=r[sl], in0=m[sl], scalar=-TWO_PI, in1=r[sl], op0=ALU.mult, op1=ALU.add),
         reads=[k + "m", k + "r"], writes=[k + "r"])
    p.op("dve", lambda e: e.tensor_scalar(out=m[sl], in0=r[sl], scalar1=-math.pi, scalar2=None, op0=ALU.is_lt), reads=[k + "r"], writes=[k + "m"])
    p.op("dve", lambda e: e.scalar_tensor_tensor(out=r[sl], in0=m[sl], scalar=TWO_PI, in1=r[sl], op0=ALU.mult, op1=ALU.add),
         reads=[k + "m", k + "r"], writes=[k + "r"])
    p.op("dve", lambda e: e.tensor_scalar(out=r[sl], in0=r[sl], scalar1=3.1415925, scalar2=-3.1415925, op0=ALU.min, op1=ALU.max),
         reads=[k + "r"], writes=[k + "r"])
    p.op("act", lambda e: e.activation(out=out, in_=r[sl], func=AF.Sin), reads=[k + "r"], writes=[k + "o"])


def s5_params(p, src3, shape, tag):
    sl = tuple(slice(None) for _ in shape)
    t = {n: p.sb(shape, F32, name=f"{tag}_{n}") for n in
         ("lre", "lim", "ldt", "step", "r", "th", "c", "s", "ar", "ai", "den", "fr", "fi", "t1", "t2")}
    k = tag
    p.dma("sp", t["lre"][sl], src3[0], writes=[k])
    p.dma("sp", t["lim"][sl], src3[1], writes=[k])
    p.dma("sp", t["ldt"][sl], src3[2], writes=[k])

    def dve(fn):
        p.op("dve", fn, reads=[k], writes=[k])

    p.op("act", lambda e: e.activation(out=t["step"][sl], in_=t["ldt"][sl], func=AF.Exp), reads=[k], writes=[k])
    dve(lambda e: e.tensor_tensor(out=t["t1"][sl], in0=t["lre"][sl], in1=t["step"][sl], op=ALU.mult))
    p.op("act", lambda e: e.activation(out=t["r"][sl], in_=t["t1"][sl], func=AF.Exp), reads=[k], writes=[k])
    dve(lambda e: e.tensor_tensor(out=t["th"][sl], in0=t["lim"][sl], in1=t["step"][sl], op=ALU.mult))
    p.op("dve", lambda e: e.tensor_copy(out=t["t2"][sl], in_=t["th"][sl]), reads=[k], writes=[k + "sx", k + "cx"])
    sin_rr(p, t["s"][sl], t["th"][sl], shape, k + "s")
    sin_rr(p, t["c"][sl], t["th"][sl], shape, k + "c", shift=math.pi / 2)
    p.op("dve", lambda e: e.tensor_tensor(out=t["ar"][sl], in0=t["r"][sl], in1=t["c"][sl], op=ALU.mult), reads=[k, k + "co"], writes=[k])
    p.op("dve", lambda e: e.tensor_tensor(out=t["ai"][sl], in0=t["r"][sl], in1=t["s"][sl], op=ALU.mult), reads=[k, k + "so"], writes=[k])
    dve(lambda e: e.tensor_tensor(out=t["den"][sl], in0=t["lre"][sl], in1=t["lre"][sl], op=ALU.mult))
    dve(lambda e: e.tensor_tensor(out=t["t1"][sl], in0=t["lim"][sl], in1=t["lim"][sl], op=ALU.mult))
    dve(lambda e: e.tensor_tensor(out=t["den"][sl], in0=t["den"][sl], in1=t["t1"][sl], op=ALU.add))
    dve(lambda e: e.reciprocal(out=t["den"][sl], in_=t["den"][sl]))
    dve(lambda e: e.tensor_scalar(out=t["t1"][sl], in0=t["ar"][sl], scalar1=-1.0, scalar2=None, op0=ALU.add))
    dve(lambda e: e.tensor_tensor(out=t["fr"][sl], in0=t["t1"][sl], in1=t["lre"][sl], op=ALU.mult))
    dve(lambda e: e.tensor_tensor(out=t["t2"][sl], in0=t["ai"][sl], in1=t["lim"][sl], op=ALU.mult))
    dve(lambda e: e.tensor_tensor(out=t["fr"][sl], in0=t["fr"][sl], in1=t["t2"][sl], op=ALU.add))
    dve(lambda e: e.tensor_tensor(out=t["fr"][sl], in0=t["fr"][sl], in1=t["den"][sl], op=ALU.mult))
    dve(lambda e: e.tensor_tensor(out=t["fi"][sl], in0=t["ai"][sl], in1=t["lre"][sl], op=ALU.mult))
    dve(lambda e: e.tensor_tensor(out=t["t2"][sl], in0=t["t1"][sl], in1=t["lim"][sl], op=ALU.mult))
    dve(lambda e: e.tensor_tensor(out=t["fi"][sl], in0=t["fi"][sl], in1=t["t2"][sl], op=ALU.subtract))
    dve(lambda e: e.tensor_tensor(out=t["fi"][sl], in0=t["fi"][sl], in1=t["den"][sl], op=ALU.mult))
    return t


def windows(T, NCTX, W=512):
    fw_ = [(0, NCTX)] + [(t, min(t + W, T)) for t in range(NCTX, T, W)]
    bw = [(0, NCTX)] + [(max(t - W, NCTX), t) for t in range(T, NCTX, -W)]
    return fw_, bw


def ph_s5_scan(p, PROJ, T, NCTX, LS, BT, CT, DSK, YPRE):
    W = 512
    ls = s5_params(p, LS, [128, 32], "ls")
    bt = p.sb([32, 2, 16, 128], F32, name="bt")
    p.dma("sp", bt[:, 0], BT[0], writes=["bt"])
    p.dma("sp", bt[:, 1], BT[1], writes=["bt"])
    ct = p.sb([128, 2, 16, 32], F32, name="ct")
    p.dma("sp", ct[:, 0], CT[0], writes=["ct"])
    p.dma("sp", ct[:, 1], CT[1], writes=["ct"])
    p.op("dve", lambda e: e.tensor_scalar(out=ct[:, 1], in0=ct[:, 1], scalar1=-1.0, scalar2=None, op0=ALU.mult), reads=["ct"], writes=["ct"])
    dsk = p.sb([32, 16], F32, name="dsk")
    p.dma("sp", dsk[:, :], DSK.rearrange("(g q) -> q g", q=32), writes=["dsk"], allow_slow_non_contiguous=True)
    nsc = p.sb([128, 32], F32, name="nsc")
    p.op("dve", lambda e: e.tensor_scalar(out=nsc[:, :], in0=ls["s"][:, :], scalar1=-1.0, scalar2=None, op0=ALU.mult), reads=["ls", "lsso"], writes=["nsc"])

    F32R = mybir.dt.float32r
    ctr = p.sb([128, 3, 16, 32], F32, name="ctr")
    p.op("dve", lambda e: e.tensor_copy(out=ctr[:, 0].bitcast(F32R), in_=ct[:, 0]), reads=["ct"], writes=["ctr"])
    p.op("dve", lambda e: e.tensor_scalar(out=ctr[:, 1].bitcast(F32R), in0=ct[:, 0], scalar1=-1.0, scalar2=None, op0=ALU.mult), reads=["ct"], writes=["ctr"])
    p.op("dve", lambda e: e.tensor_copy(out=ctr[:, 2].bitcast(F32R), in_=ct[:, 1]), reads=["ct"], writes=["ctr"])
    ones = p.sb([128, W], F32, name="s5ones")
    p.op("pool", lambda e: e.memset(ones[:, :], 1.0), writes=["s5ones"])
    mkd = lambda n: [p.sb([128, W], F32, name=f"{n}{i}") for i in range(2)]
    Ec, Es, rt, Mc, Ms = mkd("Ec"), mkd("Es"), mkd("rt"), mkd("Mc"), mkd("Ms")
    wv = p.sb([128, 4], F32, name="wv")
    wN = [p.sb([128, 2, 3], F32, name=f"wN{i}") for i in range(2)]
    tt = p.sb([128, W], F32, name="ttab")
    u_sb = p.sb([32, T], F32, name="u_sb")
    ysum = p.sb([32, T], F32, name="ysum")
    psA = [p.ps([128, 512], name=f"psA{i}") for i in range(2)]
    psB = [p.ps([128, 512], name=f"psB{i}") for i in range(2)]
    psY = [p.ps([128, 512], name=f"psY{i}") for i in range(2)]
    ta, tb_, tc, td = mkd("ta"), mkd("tb"), mkd("tc"), mkd("td")
    mre, mim = mkd("mre"), mkd("mim")
    gre = [[p.sb([128, W], F32, name=f"gre{i}{j}") for j in range(2)] for i in range(2)]
    gim = [[p.sb([128, W], F32, name=f"gim{i}{j}") for j in range(2)] for i in range(2)]
    q1, q2, q3, q4 = mkd("q1"), mkd("q2"), mkd("q3"), mkd("q4")
    init = [p.sb([128, 4], F32, name=f"init{i}") for i in range(2)]
    ysb = [p.sb([32, W], F32, name=f"ysb{i}") for i in range(2)]
    fwin, bwin = windows(T, NCTX, W)
    assert len(fwin) == len(bwin)
    for gp in range(16):
        p.dma("sp", u_sb[:, :], PROJ[gp * 32:(gp + 1) * 32, :], reads=["PROJ"], writes=["u_sb"])
        p.op("pool", lambda e: e.memset(ysum[:, :], 0.0), writes=["ysum"])
        for d in range(2):
            cix = d * 16 + gp
            cs, sn, rr = ls["c"][:, cix:cix + 1], ls["s"][:, cix:cix + 1], ls["r"][:, cix:cix + 1]
            EC, ES, MC, MS, RT, WN, ek, mk_ = Ec[d], Es[d], Mc[d], Ms[d], rt[d], wN[d], f"E{d}", f"M{d}"
            p.op("dve", lambda e, EC=EC: e.memset(EC[:, 0:1], 1.0), writes=[ek])
            p.op("dve", lambda e, ES=ES: e.memset(ES[:, 0:1], 0.0), writes=[ek])
            p.op("dve", lambda e, cs=cs: e.tensor_copy(out=wv[:, 0:1], in_=cs), reads=["ls", "lsco"], writes=["wv"])
            p.op("dve", lambda e, sn=sn: e.tensor_copy(out=wv[:, 1:2], in_=sn), reads=["ls", "lsso"], writes=["wv"])
            m = 1
            while m < W:
                p.op("dve", lambda e: e.tensor_scalar(out=wv[:, 2:3], in0=wv[:, 1:2], scalar1=-1.0, scalar2=None, op0=ALU.mult), reads=["wv"], writes=["wv"])
                if m == 256:
                    p.op("dve", lambda e, WN=WN: e.tensor_copy(out=WN[:, 0, :], in_=wv[:, 0:3]), reads=["wv"], writes=[f"wN{d}"])
                p.op("dve", lambda e, m=m, EC=EC: e.tensor_scalar(out=tt[:, 0:m], in0=EC[:, 0:m], scalar1=wv[:, 0:1], scalar2=None, op0=ALU.mult), reads=[ek, "wv"], writes=["ttab"])
                p.op("dve", lambda e, m=m, EC=EC, ES=ES: e.scalar_tensor_tensor(out=EC[:, m:2 * m], in0=ES[:, 0:m], scalar=wv[:, 2:3], in1=tt[:, 0:m], op0=ALU.mult, op1=ALU.add),
                     reads=[ek, "wv", "ttab"], writes=[ek])
                p.op("dve", lambda e, m=m, EC=EC: e.tensor_scalar(out=tt[:, 0:m], in0=EC[:, 0:m], scalar1=wv[:, 1:2], scalar2=None, op0=ALU.mult), reads=[ek, "wv"], writes=["ttab"])
                p.op("dve", lambda e, m=m, ES=ES: e.scalar_tensor_tensor(out=ES[:, m:2 * m], in0=ES[:, 0:m], scalar=wv[:, 0:1], in1=tt[:, 0:m], op0=ALU.mult, op1=ALU.add),
                     reads=[ek, "wv", "ttab"], writes=[ek])
                p.op("dve", lambda e: e.tensor_tensor(out=wv[:, 3:4], in0=wv[:, 1:2], in1=wv[:, 1:2], op=ALU.mult), reads=["wv"], writes=["wv"])
                p.op("dve", lambda e: e.scalar_tensor_tensor(out=wv[:, 1:2], in0=wv[:, 1:2], scalar=2.0, in1=wv[:, 0:1], op0=ALU.mult, op1=ALU.mult), reads=["wv"], writes=["wv"])
                p.op("dve", lambda e: e.tensor_tensor(out=wv[:, 0:1], in0=wv[:, 0:1], in1=wv[:, 0:1], op=ALU.mult), reads=["wv"], writes=["wv"])
                p.op("dve", lambda e: e.tensor_tensor(out=wv[:, 0:1], in0=wv[:, 0:1], in1=wv[:, 3:4], op=ALU.subtract), reads=["wv"], writes=["wv"])
                m *= 2
            p.op("dve", lambda e: e.tensor_scalar(out=wv[:, 2:3], in0=wv[:, 1:2], scalar1=-1.0, scalar2=None, op0=ALU.mult), reads=["wv"], writes=["wv"])
            p.op("dve", lambda e, WN=WN: e.tensor_copy(out=WN[:, 1, :], in_=wv[:, 0:3]), reads=["wv"], writes=[f"wN{d}"])
            frc, fic = ls["fr"][:, cix:cix + 1], ls["fi"][:, cix:cix + 1]
            p.op("dve", lambda e, fic=fic, ES=ES: e.tensor_scalar(out=tt[:, :], in0=ES[:, :], scalar1=fic, scalar2=None, op0=ALU.mult), reads=[ek, "ls"], writes=["ttab"])
            p.op("dve", lambda e, frc=frc, EC=EC, MC=MC: e.scalar_tensor_tensor(out=MC[:, :], in0=EC[:, :], scalar=frc, in1=tt[:, :], op0=ALU.mult, op1=ALU.add), reads=[ek, "ls", "ttab"], writes=[mk_])
            p.op("dve", lambda e, fic=fic, EC=EC: e.tensor_scalar(out=tt[:, :], in0=EC[:, :], scalar1=fic, scalar2=None, op0=ALU.mult), reads=[ek, "ls", mk_], writes=["ttab"])
            p.op("dve", lambda e, frc=frc, ES=ES, MS=MS: e.scalar_tensor_tensor(out=MS[:, :], in0=ES[:, :], scalar=frc, in1=tt[:, :], op0=ALU.mult, op1=ALU.subtract), reads=[ek, "ls", "ttab"], writes=[mk_])
            p.op("dve", lambda e, rr=rr, RT=RT: e.tensor_scalar(out=RT[:, :], in0=ones[:, :], scalar1=rr, scalar2=None, op0=ALU.mult), reads=["ls", "s5ones"], writes=[f"rt{d}"])
        prevs = [None, None]
        pending = []
        for wi_ in range(len(fwin)):
            for d in range(2):
                lo, hi = (fwin if d == 0 else bwin)[wi_]
                N = hi - lo
                sl = slice(lo, hi) if d == 0 else slice(hi - 1, lo - 1 if lo > 0 else None, -1)
                p.op("pe", lambda e, A=psA[d], sl=sl, N=N, gp=gp: e.matmul(A[:, :N], lhsT=bt[:, 0, gp, :], rhs=u_sb[:, sl], start=True, stop=True),
                     reads=["bt", "u_sb"], writes=[f"psA{d}"])
                p.op("pe", lambda e, B=psB[d], sl=sl, N=N, gp=gp: e.matmul(B[:, :N], lhsT=bt[:, 1, gp, :], rhs=u_sb[:, sl], start=True, stop=True),
                     reads=["bt", "u_sb"], writes=[f"psB{d}"])
            for fn in pending:
                fn()
            pending = []
            for d in range(2):
                lo, hi = (fwin if d == 0 else bwin)[wi_]
                N = hi - lo
                sl = slice(lo, hi) if d == 0 else slice(hi - 1, lo - 1 if lo > 0 else None, -1)
                EC, ES, MC, MS, RT, WN, ek, mk_ = Ec[d], Es[d], Mc[d], Ms[d], rt[d], wN[d], f"E{d}", f"M{d}"
                b2 = d
                A, B, Yp = psA[b2], psB[b2], psY[b2]
                ak, bk, yk = f"psA{b2}", f"psB{b2}", f"psY{b2}"
                INIT, ik = init[d], f"init{d}"
                TA, TB, TC, TD = ta[b2], tb_[b2], tc[b2], td[b2]
                par = wi_ % 2
                MR, MI, GR, GI = mre[b2], mim[b2], gre[b2][par], gim[b2][par]
                PGR, PGI = gre[b2][1 - par], gim[b2][1 - par]
                gk, gik, pgk, pgik = f"gre{b2}{par}", f"gim{b2}{par}", f"gre{b2}{1-par}", f"gim{b2}{1-par}"
                prev = prevs[d]
                if prev is None:
                    p.op("dve", lambda e, INIT=INIT: e.memset(INIT[:, :], 0.0), writes=[ik])
                else:
                    pN = prev
                    wsel = 0 if pN == 256 else 1
                    assert pN in (256, 512)
                    p.op("dve", lambda e, GR=PGR, pN=pN, wsel=wsel, INIT=INIT, WN=WN: e.tensor_scalar(out=INIT[:, 2:3], in0=GR[:, pN - 1:pN], scalar1=WN[:, wsel, 0:1], scalar2=None, op0=ALU.mult),
                         reads=[pgk, f"wN{d}"], writes=[ik])
                    p.op("dve", lambda e, GI=PGI, pN=pN, wsel=wsel, INIT=INIT, WN=WN: e.scalar_tensor_tensor(out=INIT[:, 0:1], in0=GI[:, pN - 1:pN], scalar=WN[:, wsel, 2:3], in1=INIT[:, 2:3], op0=ALU.mult, op1=ALU.add),
                         reads=[pgik, f"wN{d}", ik], writes=[ik])
                    p.op("dve", lambda e, GR=PGR, pN=pN, wsel=wsel, INIT=INIT, WN=WN: e.tensor_scalar(out=INIT[:, 3:4], in0=GR[:, pN - 1:pN], scalar1=WN[:, wsel, 1:2], scalar2=None, op0=ALU.mult),
                         reads=[pgk, f"wN{d}"], writes=[ik])
                    p.op("dve", lambda e, GI=PGI, pN=pN, wsel=wsel, INIT=INIT, WN=WN: e.scalar_tensor_tensor(out=INIT[:, 1:2], in0=GI[:, pN - 1:pN], scalar=WN[:, wsel, 0:1], in1=INIT[:, 3:4], op0=ALU.mult, op1=ALU.add),
                         reads=[pgik, f"wN{d}", ik], writes=[ik])
                p.op("dve", lambda e, A=A, N=N, TA=TA, MC=MC: e.tensor_tensor(out=TA[:, :N], in0=A[:, :N], in1=MC[:, :N], op=ALU.mult), reads=[ak, mk_], writes=[f"ta{b2}"])
                p.op("dve", lambda e, B=B, N=N, TB=TB, MS=MS: e.tensor_tensor(out=TB[:, :N], in0=B[:, :N], in1=MS[:, :N], op=ALU.mult), reads=[bk, mk_], writes=[f"tb{b2}"])
                p.op("dve", lambda e, N=N, TA=TA, TB=TB, MR=MR: e.tensor_tensor(out=MR[:, :N], in0=TA[:, :N], in1=TB[:, :N], op=ALU.add),
                     reads=[f"ta{b2}", f"tb{b2}"], writes=[f"mre{b2}"])
                p.op("dve", lambda e, B=B, N=N, TC=TC, MC=MC: e.tensor_tensor(out=TC[:, :N], in0=B[:, :N], in1=MC[:, :N], op=ALU.mult), reads=[bk, mk_], writes=[f"tc{b2}"])
                p.op("dve", lambda e, A=A, N=N, TD=TD, MS=MS: e.tensor_tensor(out=TD[:, :N], in0=A[:, :N], in1=MS[:, :N], op=ALU.mult), reads=[ak, mk_], writes=[f"td{b2}"])
                p.op("dve", lambda e, N=N, TC=TC, TD=TD, MI=MI: e.tensor_tensor(out=MI[:, :N], in0=TC[:, :N], in1=TD[:, :N], op=ALU.subtract),
                     reads=[f"tc{b2}", f"td{b2}"], writes=[f"mim{b2}"])
                p.op("dve", lambda e, N=N, GR=GR, MR=MR, RT=RT, INIT=INIT: e.tensor_tensor_scan(out=GR[:, :N], data0=RT[:, :N], data1=MR[:, :N], initial=INIT[:, 0:1], op0=ALU.mult, op1=ALU.add),
                     reads=[f"rt{d}", f"mre{b2}", ik], writes=[gk])
                p.op("dve", lambda e, N=N, GI=GI, MI=MI, RT=RT, INIT=INIT: e.tensor_tensor_scan(out=GI[:, :N], data0=RT[:, :N], data1=MI[:, :N], initial=INIT[:, 1:2], op0=ALU.mult, op1=ALU.add),
                     reads=[f"rt{d}", f"mim{b2}", ik], writes=[gik])
                Q1, Q2, Q3, Q4 = q1[b2], q2[b2], q3[b2], q4[b2]
                for (Q, G, Et, qn, gkk) in ((Q1, GR, EC, "q1", gk), (Q2, GI, ES, "q2", gik), (Q3, GR, ES, "q3", gk), (Q4, GI, EC, "q4", gik)):
                    p.op("pool", lambda e, N=N, Q=Q, G=G, Et=Et: e.tensor_tensor(out=Q[:, :N].bitcast(F32R), in0=G[:, :N], in1=Et[:, :N], op=ALU.mult),
                         reads=[gkk, ek], writes=[f"{qn}{b2}"])
                def cstage(Yp=Yp, yk=yk, N=N, gp=gp, b2=b2, sl=sl, Qs=(Q1, Q2, Q3, Q4), YS=ysb[b2]):
                    for k, (Q, ci, qn) in enumerate(((Qs[0], 0, "q1"), (Qs[1], 1, "q2"), (Qs[2], 2, "q3"), (Qs[3], 2, "q4"))):
                        p.op("pe", lambda e, Q=Q, ci=ci, k=k: e.matmul(Yp[:32, :N], lhsT=ctr[:, ci, gp, :].bitcast(F32R), rhs=Q[:, :N].bitcast(F32R),
                                                                         start=(k == 0), stop=(k == 3)), reads=["ctr", f"{qn}{b2}"], writes=[yk])
                    p.op("act", lambda e: e.copy(out=YS[:32, :N], in_=Yp[:32, :N]), reads=[yk], writes=[f"ysb{b2}"])
                    p.op("pool", lambda e: e.tensor_tensor(out=ysum[:, sl], in0=YS[:32, :N], in1=ysum[:, sl], op=ALU.add),
                         reads=[f"ysb{b2}", "ysum"], writes=["ysum"])
                pending.append(cstage)
                prevs[d] = N
        for fn in pending:
            fn()
        pending = []
        p.op("dve", lambda e, gp=gp: e.scalar_tensor_tensor(out=ysum[:, :], in0=u_sb[:, :], scalar=dsk[:, gp:gp + 1], in1=ysum[:, :], op0=ALU.mult, op1=ALU.add),
             reads=["u_sb", "ysum", "dsk"], writes=["ysum"])
        p.dma("pool", YPRE[gp * 32:(gp + 1) * 32, :], ysum[:, :], reads=["ysum"], writes=["YPRE"])


def ph_s5_glu(p, YPRE, T, GW, GB, Y):
    N = 512
    K0 = 0.7978845608028654
    gw = load_weight_bf16(p, GW, 512, 512, "gw", chunk=512)
    gb = p.sb([128, 4], F32, name="gb")
    load_cols(p, gb[:, :], GB, "gb")
    ys = [p.sb([128, 4, N], F32, name=f"yp{i}") for i in range(2)]
    x2 = p.sb([128, 4, N], F32, name="x2")
    g = p.sb([128, 4, N], F32, name="gg")
    gbf = p.sb([128, 4, N], BF16, name="gbf")
    sg = p.sb([128, N], F32, name="sg")
    o = [p.sb([128, N], BF16, name=f"og{i}") for i in range(2)]
    pss = [p.ps([128, 512], name=f"psg{i}") for i in range(2)]
    oi = 0
    for ti, t0 in enumerate(range(0, T, N)):
        n = min(N, T - t0)
        y, yk = ys[ti % 2], f"yp{ti%2}"
        p.dma("sp", y[:, :, :n], fm(YPRE, t0, n), reads=["YPRE"], writes=[yk])
        p.op("pool", lambda e, y=y, n=n: e.tensor_tensor(out=x2[:, :, :n], in0=y[:, :, :n], in1=y[:, :, :n], op=ALU.mult), reads=[yk], writes=["x2"])
        p.op("dve", lambda e, n=n: e.tensor_scalar(out=x2[:, :, :n], in0=x2[:, :, :n], scalar1=0.044715, scalar2=1.0, op0=ALU.mult, op1=ALU.add), reads=["x2"], writes=["x2"])
        p.op("pool", lambda e, y=y, n=n: e.tensor_tensor(out=x2[:, :, :n], in0=x2[:, :, :n], in1=y[:, :, :n], op=ALU.mult), reads=[yk, "x2"], writes=["x2"])
        p.op("act", lambda e, n=n: e.activation(out=x2[:, :, :n], in_=x2[:, :, :n], func=AF.Sigmoid, scale=2.0 * K0), reads=["x2"], writes=["x2"])
        p.op("dve", lambda e, y=y, n=n: e.tensor_tensor(out=g[:, :, :n], in0=x2[:, :, :n], in1=y[:, :, :n], op=ALU.mult), reads=[yk, "x2"], writes=["gg"])
        p.op("act", lambda e, n=n: e.copy(out=gbf[:, :, :n], in_=g[:, :, :n]), reads=["gg"], writes=["gbf"])
        for j in range(4):
            ps, pk, oo, ok = pss[oi % 2], f"psg{oi%2}", o[oi % 2], f"og{oi%2}"
            for c in range(4):
                p.op("pe", lambda e, ps=ps, c=c, j=j, n=n: e.matmul(ps[:, :n], lhsT=gw[:, c, j * 128:(j + 1) * 128], rhs=gbf[:, c, :n], start=(c == 0), stop=(c == 3)),
                     reads=[f"gw_b{(j * 128) // 512}", "gbf"], writes=[pk])
            p.op("act", lambda e, ps=ps, j=j, n=n: e.activation(out=sg[:, :n], in_=ps[:, :n], func=AF.Sigmoid, bias=gb[:, j:j + 1], scale=1.0),
                 reads=[pk, "gb"], writes=["sg"])
            p.op("dve", lambda e, oo=oo, j=j, n=n: e.tensor_tensor(out=oo[:, :n], in0=sg[:, :n], in1=g[:, j, :n], op=ALU.mult), reads=["sg", "gg"], writes=[ok])
            p.dma("pool", Y[j * 128:(j + 1) * 128, t0:t0 + n], oo[:, :n], reads=[ok], writes=["Y"])
            oi += 1


NH = 16


def ph_conv_silu(p, SRC, row0, nrows, T, segs, CW, CB, DST, silu=True):
    nch = nrows // 128
    w = p.sb([128, 3, nch], F32, name="cw")
    for k in range(3):
        load_cols(p, w[:, k, :], CW[k], "cw")
    b = p.sb([128, nch], F32, name="cb")
    load_cols(p, b[:, :], CB, "cw")
    xs = [p.sb([128, T], F32, name=f"cx{i}") for i in range(2)]
    acc = [p.sb([128, T], F32, name=f"ca{i}") for i in range(2)]
    for c in range(nch):
        x, xk, a, ak = xs[c % 2], f"cx{c%2}", acc[c % 2], f"ca{c%2}"
        p.dma("sp", x[:, :], SRC[row0 + c * 128:row0 + (c + 1) * 128, :], reads=["SRC"], writes=[xk])
        p.op("dve", lambda e, x=x, a=a, c=c: e.tensor_scalar(out=a[:, :], in0=x[:, :], scalar1=w[:, 1, c:c + 1], scalar2=b[:, c:c + 1], op0=ALU.mult, op1=ALU.add),
             reads=[xk, "cw"], writes=[ak])
        for (s0, s1) in segs:
            p.op("dve", lambda e, x=x, a=a, c=c, s0=s0, s1=s1: e.scalar_tensor_tensor(out=a[:, s0 + 1:s1], in0=x[:, s0:s1 - 1], scalar=w[:, 0, c:c + 1], in1=a[:, s0 + 1:s1],
                                                                                      op0=ALU.mult, op1=ALU.add), reads=[xk, "cw", ak], writes=[ak])
            p.op("dve", lambda e, x=x, a=a, c=c, s0=s0, s1=s1: e.scalar_tensor_tensor(out=a[:, s0:s1 - 1], in0=x[:, s0 + 1:s1], scalar=w[:, 2, c:c + 1], in1=a[:, s0:s1 - 1],
                                                                                      op0=ALU.mult, op1=ALU.add), reads=[xk, "cw", ak], writes=[ak])
        if silu:
            p.op("act", lambda e, a=a: e.activation(out=a[:, :], in_=a[:, :], func=AF.Silu), reads=[ak], writes=[ak])
        p.dma("pool", DST[c * 128:(c + 1) * 128, :], a[:, :], reads=[ak], writes=["DST"])


def ph_ssd_prep(p, PROJ, T, DTB, ALOG, SC, CF2, ET):
    TB = 1408
    NCB = TB // 128
    dtb = p.sb([32, 1], F32, name="dtb")
    p.dma("sp", dtb[:, :], DTB.rearrange("(p o) -> p o", o=1), writes=["dtb"], allow_slow_non_contiguous=True)
    al = p.sb([32, 1], F32, name="al")
    p.dma("sp", al[:, :], ALOG.rearrange("(p o) -> p o", o=1), writes=["al"], allow_slow_non_contiguous=True)
    p.op("act", lambda e: e.activation(out=al[:, :], in_=al[:, :], func=AF.Exp), reads=["al"], writes=["al"])
    p.op("dve", lambda e: e.tensor_scalar(out=al[:, :], in0=al[:, :], scalar1=-1.0, scalar2=None, op0=ALU.mult), reads=["al"], writes=["al"])
    names = ["dt", "a", "pin", "sin", "cx", "ncx", "npin", "e1", "e2", "e3", "e4", "m0", "m1"]
    t = {n: p.sb([32, TB], F32, name="pp_" + n) for n in names}
    et = p.sb([32, NCB], F32, name="et")

    def A(n):
        return t[n][:, :]

    p.op("pool", lambda e: e.memset(A("m0"), 1.0), writes=["m0"])
    p.op("pool", lambda e: e.memset(A("m1"), 1.0), writes=["m1"])
    m0v = A("m0").rearrange("p (c q) -> p c q", q=128)
    m1v = A("m1").rearrange("p (c q) -> p c q", q=128)
    p.op("pool", lambda e: e.memset(m0v[:, :, 0:1], 0.0), reads=["m0"], writes=["m0"])
    p.op("pool", lambda e: e.memset(m1v[:, :, 127:128], 0.0), reads=["m1"], writes=["m1"])
    for b0 in range(0, T, TB):
        ts = slice(b0, b0 + TB)
        p.dma("sp", A("dt"), PROJ[3584:3616, ts], reads=["PROJ"], writes=["dt"])
        p.op("act", lambda e: e.activation(out=A("dt"), in_=A("dt"), func=AF.Exp, bias=dtb[:, 0:1], scale=1.0), reads=["dt", "dtb"], writes=["dt"])
        p.op("act", lambda e: e.activation(out=A("dt"), in_=A("dt"), func=AF.Ln, bias=1.0, scale=1.0), reads=["dt"], writes=["dt"])
        p.op("dve", lambda e: e.tensor_scalar(out=A("a"), in0=A("dt"), scalar1=al[:, 0:1], scalar2=None, op0=ALU.mult), reads=["dt", "al"], writes=["a"])
        p.op("dve", lambda e: e.tensor_tensor_scan(out=A("pin"), data0=A("m0"), data1=A("a"), initial=0.0, op0=ALU.mult, op1=ALU.add),
             reads=["a", "m0"], writes=["pin"])
        p.op("dve", lambda e: e.tensor_tensor_scan(out=t["sin"][:, ::-1], data0=t["m1"][:, ::-1], data1=t["a"][:, ::-1], initial=0.0, op0=ALU.mult, op1=ALU.add),
             reads=["a", "m1"], writes=["sin"])
        p.dma("pool", SC[0:32, ts], A("dt"), reads=["dt"], writes=["SC"])
        p.op("dve", lambda e: e.tensor_tensor(out=A("e1"), in0=A("sin"), in1=A("a"), op=ALU.subtract), reads=["sin", "a"], writes=["e1"])
        p.op("act", lambda e: e.activation(out=A("e1"), in_=A("e1"), func=AF.Exp), reads=["e1"], writes=["e1"])
        p.op("dve", lambda e: e.tensor_tensor(out=A("e1"), in0=A("e1"), in1=A("dt"), op=ALU.mult), reads=["e1", "dt"], writes=["e1"])
        p.dma("pool", SC[32:48, ts], t["e1"][0:16, :], reads=["e1"], writes=["SC"])
        p.op("dve", lambda e: e.tensor_tensor(out=A("cx"), in0=A("pin"), in1=A("a"), op=ALU.subtract), reads=["pin", "a"], writes=["cx"])
        p.op("act", lambda e: e.activation(out=A("e2"), in_=A("cx"), func=AF.Exp), reads=["cx"], writes=["e2"])
        p.op("dve", lambda e: e.tensor_tensor(out=A("e2"), in0=A("e2"), in1=A("dt"), op=ALU.mult), reads=["e2", "dt"], writes=["e2"])
        p.dma("pool", SC[48:64, ts], t["e2"][16:32, :], reads=["e2"], writes=["SC"])
        p.op("act", lambda e: e.activation(out=A("e3"), in_=A("pin"), func=AF.Exp), reads=["pin"], writes=["e3"])
        p.dma("pool", SC[64:80, ts], t["e3"][0:16, :], reads=["e3"], writes=["SC"])
        p.op("act", lambda e: e.activation(out=A("e4"), in_=A("sin"), func=AF.Exp), reads=["sin"], writes=["e4"])
        p.dma("pool", SC[80:96, ts], t["e4"][16:32, :], reads=["e4"], writes=["SC"])
        p.op("dve", lambda e: e.tensor_scalar(out=A("ncx"), in0=A("cx"), scalar1=-1.0, scalar2=None, op0=ALU.mult), reads=["cx"], writes=["ncx"])
        p.op("dve", lambda e: e.tensor_scalar(out=A("npin"), in0=A("pin"), scalar1=-1.0, scalar2=None, op0=ALU.mult), reads=["pin"], writes=["npin"])
        p.dma("pool", CF2[0, 0:16, ts], t["pin"][0:16, :], reads=["pin"], writes=["CF2"])
        p.dma("pool", CF2[0, 16:32, ts], t["ncx"][16:32, :], reads=["ncx"], writes=["CF2"])
        p.dma("pool", CF2[1, 0:16, ts], t["npin"][0:16, :], reads=["npin"], writes=["CF2"])
        p.dma("pool", CF2[1, 16:32, ts], t["cx"][16:32, :], reads=["cx"], writes=["CF2"])
        pv = A("pin").rearrange("p (c q) -> p c q", q=128)
        p.op("act", lambda e, pv=pv: e.activation(out=et[:, :], in_=pv[:, :, 127], func=AF.Exp), reads=["pin"], writes=["et"])
        c0 = b0 // 128
        p.dma("pool", ET[c0:c0 + NCB, :].rearrange("c j -> j c"), et[:, :], reads=["et"], writes=["ET"], allow_slow_non_contiguous=True)


def ph_ssd_pass(p, d, XBC, T, NCTX, SC, CF2, ET, DSKIP_BC, IDENT, MASKS, YT, YM):
    NCH = T // 128
    nctx = NCTX // 128
    order = list(range(NCH)) if d == 0 else list(range(nctx - 1, -1, -1)) + list(range(NCH - 1, nctx - 1, -1))
    ident = p.sb([128, 128], F32, name="ident")
    p.dma("sp", ident[:, :], IDENT, writes=["ident"])
    identb = p.sb([128, 128], BF16, name="identb")
    p.op("dve", lambda e: e.tensor_copy(out=identb[:, :], in_=ident[:, :]), reads=["ident"], writes=["identb"])
    maskf = p.sb([128, 512], F32, name="maskf")
    p.dma("sp", maskf[:, :], MASKS[d], writes=["maskf"])
    maskb = p.sb([128, 512], BF16, name="maskb")
    p.op("dve", lambda e: e.tensor_copy(out=maskb[:, :], in_=maskf[:, :]), reads=["maskf"], writes=["maskb"])
    etb = p.sb([128, NCH, 32], F32, name="etb")
    p.dma("sp", etb[:, :, :], ET.partition_broadcast(128), reads=["ET"], writes=["etb"])
    dsk = p.sb([128, 16], F32, name="dskb")
    p.dma("sp", dsk[:, :], DSKIP_BC, writes=["dskb"])
    H = p.sb([128, 1024], F32, name="H")
    Hb = p.sb([128, 1024], BF16, name="Hb")
    p.op("pool", lambda e: e.memset(H[:, :], 0.0), writes=["H"])
    p.op("pool", lambda e: e.memset(Hb[:, :], 0.0), writes=["Hb"])
    xin = [p.sb([128, 16, 128], F32, name=f"xin{i}") for i in range(2)]
    sct = [p.sb([96, 128], F32, name=f"sct{i}") for i in range(2)]
    Lc = [p.sb([2, 32, 128], F32, name=f"Lc{i}") for i in range(2)]
    Rc = [p.sb([2, 32, 128], F32, name=f"Rc{i}") for i in range(2)]
    for i in range(2):
        p.op("pool", lambda e, i=i: e.memset(Lc[i][:, :, :], 1.0), writes=[f"Lc{i}"])
        p.op("pool", lambda e, i=i: e.memset(Rc[i][:, :, :], 1.0), writes=[f"Rc{i}"])
    xtok = p.sb([128, 1024], F32, name="xtok")
    btok = p.sb([128, 512], BF16, name="btok")
    bcT = p.sb([128, 8, 128], BF16, name="bcT")
    sc = p.sb([128, 96], F32, name="sc")
    xdt = p.sb([128, 1024], BF16, name="xdt")
    xw = p.sb([128, 1024], BF16, name="xw")
    Es_ = [p.sb([128, 512], BF16, name=f"E{i}") for i in range(2)]
    GTs = [p.sb([128, 4, 128], BF16, name=f"GT{i}") for i in range(2)]
    ysb = p.sb([128, 256], F32, name="ysb")
    yacc = p.sb([128, 1024], F32, name="yacc")
    yprev = p.sb([128, 1024], F32, name="yprev")
    yfm = [p.sb([128, 8, 128], F32, name=f"yfm{i}") for i in range(2)]
    psT = [p.ps([128, 512], name=f"psT{i}") for i in range(2)]
    psG = p.ps([128, 512], name="psG")
    psDs = [p.ps([128, 512], name=f"psD{i}") for i in range(2)]
    psY = p.ps([128, 512], name="psY")
    psZ = p.ps([128, 512], name="psZ")
    psS = p.ps([128, 512], name="psS")
    ti = 0
    for it, c in enumerate(order):
        t0 = c * 128
        xi, xik = xin[it % 2], f"xin{it%2}"
        st, stk = sct[it % 2], f"sct{it%2}"
        L, Lk, R, Rk = Lc[it % 2], f"Lc{it%2}", Rc[it % 2], f"Rc{it%2}"
        p.dma("sp", xi[:, :, :], fm(XBC, t0, 128), reads=["XBC"], writes=[xik])
        p.dma("sp", st[:, :], SC[:, t0:t0 + 128], reads=["SC"], writes=[stk])
        p.dma("sp", R[0:1, :, :], CF2[0:1, :, t0:t0 + 128], reads=["CF2"], writes=[Rk])
        p.dma("sp", L[1:2, :, :], CF2[1:2, :, t0:t0 + 128], reads=["CF2"], writes=[Lk])
        if d == 1:
            p.dma("sp", yprev[:, :], YT[t0:t0 + 128, :], reads=["YT"], writes=["yprev"])
        for half in range(2):
            ps, pk = psT[ti % 2], f"psT{ti%2}"
            ti += 1
            for cc in range(4):
                p.op("pe", lambda e, ps=ps, cc=cc, half=half, xi=xi: e.transpose(out=ps[:, cc * 128:(cc + 1) * 128], in_=xi[:, half * 4 + cc, :], identity=ident[:, :]),
                     reads=[xik, "ident"], writes=[pk])
            p.op("act", lambda e, ps=ps, half=half: e.copy(out=xtok[:, half * 512:(half + 1) * 512], in_=ps[:, :]), reads=[pk], writes=["xtok"])
        ps, pk = psT[ti % 2], f"psT{ti%2}"
        ti += 1
        for cc in range(4):
            p.op("pe", lambda e, ps=ps, cc=cc, xi=xi: e.transpose(out=ps[:, cc * 128:(cc + 1) * 128], in_=xi[:, 8 + cc, :], identity=ident[:, :]),
                 reads=[xik, "ident"], writes=[pk])
        p.op("act", lambda e, ps=ps: e.copy(out=btok[:, :], in_=ps[:, :]), reads=[pk], writes=["btok"])
        ps, pk = psT[ti % 2], f"psT{ti%2}"
        ti += 1
        p.op("pe", lambda e, ps=ps, st=st: e.transpose(out=ps[:, 0:96], in_=st[:, :], identity=ident[:96, :96]), reads=[stk, "ident"], writes=[pk])
        p.op("dve", lambda e, ps=ps: e.tensor_copy(out=sc[:, :], in_=ps[:, 0:96]), reads=[pk], writes=["sc"])
        p.op("pool", lambda e, xi=xi: e.tensor_copy(out=bcT[:, :, :], in_=xi[:, 8:16, :]), reads=[xik], writes=["bcT"])
        x3 = xtok[:, :].rearrange("p (h q) -> p h q", q=64)
        p.op("dve", lambda e, x3=x3: e.tensor_tensor(out=xdt[:, :].rearrange("p (h q) -> p h q", q=64), in0=x3,
                                                     in1=sc[:, d * 16:d * 16 + 16].unsqueeze(2).to_broadcast([128, 16, 64]), op=ALU.mult),
             reads=["xtok", "sc"], writes=["xdt"])
        p.op("pool", lambda e, x3=x3: e.tensor_tensor(out=xw[:, :].rearrange("p (h q) -> p h q", q=64), in0=x3,
                                                      in1=sc[:, 32 + d * 16:32 + d * 16 + 16].unsqueeze(2).to_broadcast([128, 16, 64]), op=ALU.mult),
             reads=["xtok", "sc"], writes=["xw"])
        def stage1(g):
            gs = g % 2
            p.op("pe", lambda e, g=g, gs=gs: e.matmul(psG[:, gs * 128:(gs + 1) * 128], lhsT=bcT[:, g, :], rhs=bcT[:, 4 + g, :], start=True, stop=True),
                 reads=["bcT"], writes=[f"psG{gs}"])
            PD = psDs[gs]
            p.op("pe", lambda e, PD=PD: e.matmul(PD[:, :], lhsT=identb[:, :], rhs=maskb[:, :], start=True, stop=False), reads=["identb", "maskb"], writes=[f"psD{gs}"])
            for hh in range(4):
                j = d * 16 + g * 4 + hh
                p.op("pe", lambda e, hh=hh, j=j, L=L, R=R, PD=PD: e.matmul(PD[:, hh * 128:(hh + 1) * 128], lhsT=L[:, j, :], rhs=R[:, j, :], start=False, stop=(hh == 3)),
                     reads=[Lk, Rk], writes=[f"psD{gs}"])

        def stage2(g):
            gs = g % 2
            PD, EE, GG = psDs[gs], Es_[gs], GTs[gs]
            p.op("act", lambda e, PD=PD, EE=EE: e.activation(out=EE[:, :], in_=PD[:, :], func=AF.Exp), reads=[f"psD{gs}"], writes=[f"E{gs}"])
            for hh in range(4):
                p.op("dve", lambda e, hh=hh, gs=gs, EE=EE, GG=GG: e.tensor_tensor(out=GG[:, hh, :], in0=psG[:, gs * 128:(gs + 1) * 128], in1=EE[:, hh * 128:(hh + 1) * 128], op=ALU.mult),
                     reads=[f"psG{gs}", f"E{gs}"], writes=[f"GT{gs}{hh}"])

        def stage3(g):
            gs = g % 2
            GG = GTs[gs]
            for hh in range(4):
                h = g * 4 + hh
                p.op("pe", lambda e, hh=hh, h=h, GG=GG: e.matmul(psY[:, hh * 64:(hh + 1) * 64], lhsT=GG[:, hh, :], rhs=xdt[:, h * 64:(h + 1) * 64], start=True, stop=True),
                     reads=[f"GT{gs}{hh}", "xdt"], writes=["psY"])
            p.op("pe", lambda e, g=g: e.matmul(psZ[:, 0:256], lhsT=bcT[:, 4 + g, :], rhs=Hb[:, g * 256:(g + 1) * 256], start=True, stop=True), reads=["bcT", "Hb"], writes=["psZ"])
            p.op("pe", lambda e, g=g: e.matmul(psS[:, 0:256], lhsT=btok[:, g * 128:(g + 1) * 128], rhs=xw[:, g * 256:(g + 1) * 256], start=True, stop=True),
                 reads=["btok", "xw"], writes=["psS"])
            p.op("act", lambda e: e.copy(out=ysb[:, :], in_=psY[:, 0:256]), reads=["psY"], writes=["ysb"])
            for hh in range(4):
                h = g * 4 + hh
                hs = slice(h * 64, (h + 1) * 64)
                rs = sc[:, 64 + d * 16 + h:64 + d * 16 + h + 1]
                p.op("dve", lambda e, hh=hh, hs=hs, rs=rs: e.scalar_tensor_tensor(out=yacc[:, hs], in0=psZ[:, hh * 64:(hh + 1) * 64], scalar=rs, in1=ysb[:, hh * 64:(hh + 1) * 64],
                                                                                  op0=ALU.mult, op1=ALU.add), reads=["psZ", "ysb", "sc"], writes=["yacc"])
                et = etb[:, c, d * 16 + h:d * 16 + h + 1]
                p.op("dve", lambda e, hh=hh, hs=hs, et=et: e.scalar_tensor_tensor(out=H[:, hs], in0=H[:, hs], scalar=et, in1=psS[:, hh * 64:(hh + 1) * 64], op0=ALU.mult, op1=ALU.add),
                     reads=["H", "psS", "etb"], writes=["H"])
            p.op("act", lambda e, g=g: e.copy(out=Hb[:, g * 256:(g + 1) * 256], in_=H[:, g * 256:(g + 1) * 256]), reads=["H"], writes=["Hb"])

        stage1(0)
        for g in range(4):
            if g + 1 < 4:
                stage1(g + 1)
            stage2(g)
            stage3(g)
        if d == 0:
            p.op("pool", lambda e, x3=x3: e.tensor_tensor(out=yprev[:, :].rearrange("p (h q) -> p h q", q=64), in0=x3,
                                                          in1=dsk[:, :].unsqueeze(2).to_broadcast([128, 16, 64]), op=ALU.mult),
                 reads=["xtok", "dskb"], writes=["yprev"])
            p.op("pool", lambda e: e.tensor_tensor(out=yacc[:, :], in0=yacc[:, :], in1=yprev[:, :], op=ALU.add), reads=["yacc", "yprev"], writes=["yacc"])
            p.dma("pool", YT[t0:t0 + 128, :], yacc[:, :], reads=["yacc"], writes=["YT"])
        else:
            p.op("pool", lambda e: e.tensor_tensor(out=yacc[:, :], in0=yacc[:, :], in1=yprev[:, :], op=ALU.add), reads=["yacc", "yprev"], writes=["yacc"])
            yf, yfk = yfm[it % 2], f"yfm{it%2}"
            for half in range(2):
                ps, pk = psT[ti % 2], f"psT{ti%2}"
                ti += 1
                for cc in range(4):
                    cch = half * 4 + cc
                    p.op("pe", lambda e, ps=ps, cc=cc, cch=cch: e.transpose(out=ps[:, cc * 128:(cc + 1) * 128], in_=yacc[:, cch * 128:(cch + 1) * 128], identity=ident[:, :]),
                         reads=["yacc", "ident"], writes=[pk])
                p.op("act", lambda e, ps=ps, half=half, yf=yf: e.copy(out=yf[:, half * 4:(half + 1) * 4, :], in_=ps[:, :].rearrange("p (c q) -> p c q", q=128)), reads=[pk], writes=[yfk])
            p.dma("pool", fm(YM, t0, 128), yf[:, :, :], reads=[yfk], writes=["YM"])


def ph_ssd_post(p, YM, PROJ, T, G, Y):
    N = 512
    nrm = Norm(p, N, 1, G, [(None, None)])
    ys = [p.sb([128, 8, N], F32, name=f"ym{i}") for i in range(2)]
    zs = [p.sb([128, 8, N], F32, name=f"zz{i}") for i in range(2)]
    ht = [p.sb([128, 8, N], BF16, name=f"hp{i}") for i in range(2)]
    for ti, t0 in enumerate(range(0, T, N)):
        n = min(N, T - t0)
        y, yk, z, zk, h, hk = ys[ti % 2], f"ym{ti%2}", zs[ti % 2], f"zz{ti%2}", ht[ti % 2], f"hp{ti%2}"
        p.dma("sp", y[:, :, :n], fm(YM, t0, n), reads=["YM"], writes=[yk])
        p.dma("sp", z[:, :, :n], fm(PROJ[512:1536, :], t0, n), reads=["PROJ"], writes=[zk])
        p.op("act", lambda e, z=z, n=n: e.activation(out=z[:, :, :n], in_=z[:, :, :n], func=AF.Silu), reads=[zk], writes=[zk])
        p.op("pool", lambda e, y=y, z=z, n=n: e.tensor_tensor(out=y[:, :, :n], in0=y[:, :, :n], in1=z[:, :, :n], op=ALU.mult), reads=[yk, zk], writes=[yk])
        nrm.apply(y, yk, n, 0, h, hk)
        p.dma("pool", fm(Y[512:1536, :], t0, n), h[:, :, :n], reads=[hk], writes=["Y"])


NFFT = 16384
S = 4


def hy_tables():
    a = np.arange(128, dtype=np.float64)
    ang = 2 * np.pi * np.outer(a, a) / 128.0
    C, Sn = np.cos(ang), np.sin(ang)
    angN = 2 * np.pi * np.outer(a, a) / NFFT
    tabs = np.concatenate([C, -Sn, C, Sn, -Sn, C, np.cos(angN), np.sin(angN), C / NFFT, -Sn / NFFT, -C, -C / NFFT], axis=1)
    return tabs.astype(np.float32)


class FFT:
    def __init__(self, p, TABS, NC=2):
        self.p = p
        F32R = mybir.dt.float32r
        self.tabs = p.sb([128, 1536], F32, name="tabs")
        p.dma("sp", self.tabs[:, :], TABS, writes=["tabs"])
        self.tabs_r = p.sb([128, 1536], F32, name="tabs_r")
        p.op("dve", lambda e: e.tensor_copy(out=self.tabs_r[:, :].bitcast(F32R), in_=self.tabs[:, :]), reads=["tabs"], writes=["tabs"])
        t = self.tabs_r[:, :].bitcast(F32R)
        self.F1 = t[:, 0:256]
        self.CS = t[:, 256:512]
        self.NSC = t[:, 512:768]
        self.C = t[:, 256:384]
        self.Sm = t[:, 384:512]
        self.NS = t[:, 512:640]
        self.TWC = self.tabs[:, 768:896]
        self.TWS = self.tabs[:, 896:1024]
        self.CN = t[:, 1024:1152]
        self.NSN = t[:, 1152:1280]
        self.NegC = t[:, 1280:1408]
        self.NegCN = t[:, 1408:1536]
        self.NC = NC
        self.psA = [p.ps([128, S, 256], name=f"psA{c}") for c in range(NC)]
        self.psXr = [p.ps([128, 512], name=f"psXr{c}") for c in range(NC)]
        self.psXi = [p.ps([128, 512], name=f"psXi{c}") for c in range(NC)]
        self.psy = self.psXr
        mk = lambda n: [p.sb([128, S, 128], F32, name=f"{n}{c}") for c in range(NC)]
        self.t1, self.t2, self.t3, self.t4 = mk("ft1"), mk("ft2"), mk("ft3"), mk("ft4")
        self.u1, self.u2, self.u3, self.u4 = mk("fu1"), mk("fu2"), mk("fu3"), mk("fu4")
        self.Br, self.Bi = mk("Br"), mk("Bi")
        self.Yr, self.Yi = mk("Yr"), mk("Yi")

    def cmul(self, c, ar, ai, ak, br, bi, bk, outr, outi, ok, conj_b=False):
        p = self.p
        F32R = mybir.dt.float32r
        t1, t2, t3, t4 = self.u1[c], self.u2[c], self.u3[c], self.u4[c]
        k1, k2, k3, k4 = f"fu1{c}", f"fu2{c}", f"fu3{c}", f"fu4{c}"
        p.op("dve", lambda e: e.tensor_tensor(out=t1[:, :, :], in0=ar, in1=br, op=ALU.mult), reads=ak + bk, writes=[k1])
        p.op("dve", lambda e: e.tensor_tensor(out=t2[:, :, :], in0=ai, in1=bi, op=ALU.mult), reads=ak + bk, writes=[k2])
        p.op("pool", lambda e: e.tensor_tensor(out=outr.bitcast(F32R), in0=t1[:, :, :], in1=t2[:, :, :], op=(ALU.add if conj_b else ALU.subtract)),
             reads=[k1, k2], writes=ok)
        p.op("dve", lambda e: e.tensor_tensor(out=t3[:, :, :], in0=ai, in1=br, op=ALU.mult), reads=ak + bk, writes=[k3])
        p.op("dve", lambda e: e.tensor_tensor(out=t4[:, :, :], in0=ar, in1=bi, op=ALU.mult), reads=ak + bk, writes=[k4])
        p.op("pool", lambda e: e.tensor_tensor(out=outi.bitcast(F32R), in0=t3[:, :, :], in1=t4[:, :, :], op=(ALU.subtract if conj_b else ALU.add)),
             reads=[k3, k4], writes=ok)

    def cmul_split(self, c, ar, ai, ak, br, bi, bk):
        p = self.p
        F32R = mybir.dt.float32r
        for (t, x, y, kk) in ((self.t1[c], ar, br, f"ft1{c}"), (self.t2[c], ai, bi, f"ft2{c}"), (self.t3[c], ai, br, f"ft3{c}"), (self.t4[c], ar, bi, f"ft4{c}")):
            p.op("dve", lambda e, t=t, x=x, y=y: e.tensor_tensor(out=t[:, :, :].bitcast(F32R), in0=x, in1=y, op=ALU.mult), reads=ak + bk, writes=[kk])

    def _tflat(self, c):
        F32R = mybir.dt.float32r
        return [t[c][:, :, :].rearrange("p s k -> p (s k)").bitcast(F32R) for t in (self.t1, self.t2, self.t3, self.t4)]

    def _tw(self):
        return (self.TWC.unsqueeze(1).to_broadcast([128, S, 128]), self.TWS.unsqueeze(1).to_broadcast([128, S, 128]))

    def _bflat(self, c):
        F32R = mybir.dt.float32r
        return (self.Br[c][:, :, :].rearrange("p s k -> p (s k)").bitcast(F32R), self.Bi[c][:, :, :].rearrange("p s k -> p (s k)").bitcast(F32R))

    def st_f1(self, c, x0, xk, K):
        p, A = self.p, self.psA[c]
        for s in range(S):
            p.op("pe", lambda e, s=s: e.matmul(A[:, s, :], lhsT=x0[:K, s, :].bitcast(mybir.dt.float32r), rhs=self.F1[:K, :], start=True, stop=True),
                 reads=[xk, "tabs"], writes=[f"psA{c}"])

    def st_tw1(self, c):
        A = self.psA[c]
        twc, tws = self._tw()
        self.cmul_split(c, A[:, :, 0:128], A[:, :, 128:256], [f"psA{c}"], twc, tws, ["tabs"])

    def st_f2(self, c):
        p = self.p
        T1, T2, T3, T4 = self._tflat(c)
        Xr, Xi = self.psXr[c], self.psXi[c]
        tk = [f"ft1{c}", f"ft2{c}", f"ft3{c}", f"ft4{c}"]
        for n, (w, t) in enumerate(((self.C, T1), (self.C, T2), (self.Sm, T3), (self.NS, T4))):
            p.op("pe", lambda e, w=w, t=t, n=n: e.matmul(Xr[:, :], lhsT=w, rhs=t, start=(n == 0), stop=(n == 3)), reads=["tabs"] + tk, writes=[f"psXr{c}"])
        for n, (w, t) in enumerate(((self.C, T3), (self.NegC, T4), (self.NS, T1), (self.NS, T2))):
            p.op("pe", lambda e, w=w, t=t, n=n: e.matmul(Xi[:, :], lhsT=w, rhs=t, start=(n == 0), stop=(n == 3)), reads=["tabs"] + tk, writes=[f"psXi{c}"])

    def st_filt(self, c, Hr, Hi, hk):
        Xr = self.psXr[c][:, :].rearrange("p (s k) -> p s k", k=128)
        Xi = self.psXi[c][:, :].rearrange("p (s k) -> p s k", k=128)
        self.cmul(c, Xr, Xi, [f"psXr{c}", f"psXi{c}"], Hr, Hi, [hk], self.Yr[c][:, :, :], self.Yi[c][:, :, :], [f"Y{c}"])

    def st_i1(self, c):
        p, A = self.p, self.psA[c]
        F32R = mybir.dt.float32r
        Yr, Yi = self.Yr[c], self.Yi[c]
        for s in range(S):
            p.op("pe", lambda e, s=s: e.matmul(A[:, s, :], lhsT=Yr[:, s, :].bitcast(F32R), rhs=self.CS, start=True, stop=False), reads=[f"Y{c}", "tabs"], writes=[f"psA{c}"])
            p.op("pe", lambda e, s=s: e.matmul(A[:, s, :], lhsT=Yi[:, s, :].bitcast(F32R), rhs=self.NSC, start=False, stop=True), reads=[f"Y{c}", "tabs"], writes=[f"psA{c}"])

    def st_tw2(self, c):
        A = self.psA[c]
        twc, tws = self._tw()
        self.cmul_split(c, A[:, :, 0:128], A[:, :, 128:256], [f"psA{c}"], twc, tws, ["tabs"])

    def st_i2(self, c, M):
        p = self.p
        T1, T2, T3, T4 = self._tflat(c)
        y = self.psy[c]
        tk = [f"ft1{c}", f"ft2{c}", f"ft3{c}", f"ft4{c}"]
        for n, (w, t) in enumerate(((self.CN, T1), (self.NegCN, T2), (self.NSN, T3), (self.NSN, T4))):
            p.op("pe", lambda e, w=w, t=t, n=n: e.matmul(y[:M, :], lhsT=w[:, :M], rhs=t, start=(n == 0), stop=(n == 3)), reads=["tabs"] + tk, writes=[f"psXr{c}"])


def ph_hy_filter_raw(p, l, FEATS, TL, W1, B1, FQ1, W2, B2, FQ2, W3, DELTAS, FRAW, RINV):
    NT = min(512, l)
    ntile = l // NT
    feats = p.sb([33, l], F32, name="feats")
    p.dma("sp", feats[:, :], FEATS, writes=["feats"])
    w1 = p.sb([33, 64], F32, name="fw1")
    p.dma("sp", w1[:, :], W1, writes=["fw1"])
    w2 = p.sb([64, 64], F32, name="fw2")
    p.dma("sp", w2[:, :], W2, writes=["fw2"])
    w3 = p.sb([64, 4096], F32, name="fw3")
    p.dma("sp", w3[:, :], W3, writes=["fw3"])
    cols = p.sb([64, 6], F32, name="fcols")
    for i, src in enumerate((B1, FQ1, B2, FQ2)):
        p.dma("sp", cols[:, i:i + 1], src.rearrange("(p o) -> p o", o=1), writes=["fcols"], allow_slow_non_contiguous=True)
    p.op("dve", lambda e: e.tensor_tensor(out=cols[:, 4:5], in0=cols[:, 0:1], in1=cols[:, 1:2], op=ALU.mult), reads=["fcols"], writes=["fcols"])
    p.op("dve", lambda e: e.tensor_tensor(out=cols[:, 5:6], in0=cols[:, 2:3], in1=cols[:, 3:4], op=ALU.mult), reads=["fcols"], writes=["fcols"])
    hid2 = p.sb([64, l], F32, name="hid2")
    arg = p.sb([64, NT], F32, name="farg")
    arg2 = p.sb([64, NT], F32, name="farg2")
    hid1 = p.sb([64, NT], F32, name="hid1")
    tm1 = sin_tmps(p, [64, NT], "s1")
    tm2 = sin_tmps(p, [64, NT], "s2")
    ps1 = p.ps([128, 512], name="psf1")
    for ti in range(ntile):
        ts = slice(ti * NT, (ti + 1) * NT)
        p.op("pe", lambda e, ts=ts: e.matmul(ps1[:64, :NT], lhsT=w1[:, :], rhs=feats[:, ts], start=True, stop=True), reads=["fw1", "feats"], writes=["psf1"])
        p.op("dve", lambda e: e.tensor_scalar(out=arg[:, :], in0=ps1[:64, :NT], scalar1=cols[:, 1:2], scalar2=cols[:, 4:5], op0=ALU.mult, op1=ALU.add),
             reads=["psf1", "fcols"], writes=["s1" + "x"])
        sin_rr(p, hid1[:, :], arg[:, :], [64, NT], "s1", tmps=tm1)
        p.op("pe", lambda e: e.matmul(ps1[:64, :NT], lhsT=w2[:, :], rhs=hid1[:, :], start=True, stop=True), reads=["fw2", "s1o"], writes=["psf1"])
        p.op("dve", lambda e: e.tensor_scalar(out=arg2[:, :], in0=ps1[:64, :NT], scalar1=cols[:, 3:4], scalar2=cols[:, 5:6], op0=ALU.mult, op1=ALU.add),
             reads=["psf1", "fcols"], writes=["s2" + "x"])
        sin_rr(p, hid2[:, ts], arg2[:, :], [64, NT], "s2", tmps=tm2)
    tl = p.sb([128, l], F32, name="tl")
    p.dma("sp", tl[:, :], TL.partition_broadcast(128), writes=["tl"])
    dl = p.sb([128, 8], F32, name="ndelta")
    load_cols(p, dl[:, :], DELTAS, "ndelta")
    p.op("dve", lambda e: e.tensor_scalar(out=dl[:, :], in0=dl[:, :], scalar1=-1.0, scalar2=None, op0=ALU.mult), reads=["ndelta"], writes=["ndelta"])
    dec = p.sb([128, l], F32, name="dec")
    sums = p.sb([128, 32, ntile], F32, name="fsums")
    junk = p.sb([128, NT], F32, name="fjunk")
    outs = [p.sb([128, NT], F32, name=f"fo{i}") for i in range(3)]
    pss = [p.ps([128, 512], name=f"psw{i}") for i in range(3)]
    oi = 0
    for cc in range(8):
        p.op("act", lambda e, cc=cc: e.activation(out=dec[:, :], in_=tl[:, :], func=AF.Exp, scale=dl[:, cc:cc + 1]), reads=["tl", "ndelta"], writes=["dec"])
        for od in range(4):
            m = od * 8 + cc
            for ti in range(ntile):
                ts = slice(ti * NT, (ti + 1) * NT)
                ps, pk, o, ok = pss[oi % 3], f"psw{oi%3}", outs[oi % 3], f"fo{oi%3}"
                p.op("pe", lambda e, ps=ps, m=m, ts=ts: e.matmul(ps[:, :NT], lhsT=w3[:, m * 128:(m + 1) * 128], rhs=hid2[:, ts], start=True, stop=True),
                     reads=["fw3", "s2o"], writes=[pk])
                p.op("dve", lambda e, ps=ps, o=o, ts=ts: e.tensor_tensor(out=o[:, :], in0=ps[:, :NT], in1=dec[:, ts], op=ALU.mult), reads=[pk, "dec"], writes=[ok])
                lo = 1 if (od % 2 == 1 and ti == 0) else 0
                p.op("act", lambda e, o=o, m=m, ti=ti, lo=lo: e.activation(out=junk[:, lo:], in_=o[:, lo:], func=AF.Abs, accum_out=sums[:, m, ti:ti + 1]),
                     reads=[ok], writes=["fsums", "fjunk"])
                p.dma("pool", FRAW[m * 128:(m + 1) * 128, ts], o[:, :], reads=[ok], writes=["FRAW"])
                oi += 1
    tot = p.sb([128, 32], F32, name="ftot")
    p.op("dve", lambda e: e.tensor_reduce(out=tot[:, :], in_=sums[:, :, :], axis=AX.X, op=ALU.add), reads=["fsums"], writes=["ftot"])
    rinv = p.sb([128, 2, 8], F32, name="rinv")
    t4 = tot[:, :].rearrange("p (o d c) -> p o d c", o=2, d=2)
    for o_ in range(2):
        p.op("dve", lambda e, o_=o_: e.tensor_tensor(out=rinv[:, o_, :], in0=t4[:, o_, 0, :], in1=t4[:, o_, 1, :], op=ALU.add), reads=["ftot"], writes=["rinv"])
    p.op("dve", lambda e: e.tensor_scalar(out=rinv[:, :, :], in0=rinv[:, :, :], scalar1=1e-6, scalar2=None, op0=ALU.add), reads=["rinv"], writes=["rinv"])
    p.op("dve", lambda e: e.reciprocal(out=rinv[:, :, :], in_=rinv[:, :, :]), reads=["rinv"], writes=["rinv"])
    p.dma("pool", RINV.rearrange("o (c q) -> q o c", q=128), rinv[:, :, :], reads=["rinv"], writes=["RINV"], allow_slow_non_contiguous=True)


def ph_hy_filter_asm(p, l, FRAW, RINV, FILT):
    rinv = p.sb([128, 2, 8], F32, name="rinv2")
    p.dma("sp", rinv[:, :, :], RINV.rearrange("o (c q) -> q o c", q=128), reads=["RINV"], writes=["rinv2"], allow_slow_non_contiguous=True)
    zero = p.sb([128, 4096], F32, name="fzero")
    p.op("pool", lambda e: e.memset(zero[:, :], 0.0), writes=["fzero"])
    f = [p.sb([128, l], F32, name=f"ff{i}") for i in range(2)]
    b = [p.sb([128, l], F32, name="fb0")] * 2
    r = [p.sb([128, l], F32, name="fr0")] * 2
    i = 0
    for o_ in range(2):
        for cc in range(8):
            ff, fk, bb, bk, rr, rk = f[i % 2], f"ff{i%2}", b[0], "fb0", r[0], "fr0"
            rows = slice((o_ * 8 + cc) * 128, (o_ * 8 + cc + 1) * 128)
            mf, mb = (o_ * 2 + 0) * 8 + cc, (o_ * 2 + 1) * 8 + cc
            p.dma("sp", ff[:, :], FRAW[mf * 128:(mf + 1) * 128, :], reads=["FRAW"], writes=[fk])
            p.dma("sp", bb[:, :], FRAW[mb * 128:(mb + 1) * 128, :], reads=["FRAW"], writes=[bk])
            sc = rinv[:, o_, cc:cc + 1]
            p.op("act", lambda e, ff=ff, sc=sc: e.activation(out=ff[:, :], in_=ff[:, :], func=AF.Identity, scale=sc), reads=[fk, "rinv2"], writes=[fk])
            p.op("dve", lambda e, bb=bb, rr=rr, sc=sc: e.tensor_scalar(out=rr[:, :], in0=bb[:, ::-1], scalar1=sc, scalar2=None, op0=ALU.mult), reads=[bk, "rinv2"], writes=[rk])
            p.dma("pool", FILT[rows, 0:l], ff[:, :], reads=[fk], writes=["FILT"])
            p.dma("pool", FILT[rows, NFFT - l + 1:NFFT], rr[:, 0:l - 1], reads=[rk], writes=["FILT"])
            for z0 in range(l, NFFT - l + 1, 4096):
                zn = min(4096, NFFT - l + 1 - z0)
                p.dma("pool", FILT[rows, z0:z0 + zn], zero[:, :zn], reads=["fzero"], writes=["FILT"], allow_slow_non_contiguous=True)
            i += 1


def ph_hy_filter_fft(p, FILT, TABS, HF, groups=None):
    NC = 2
    fft = FFT(p, TABS, NC)
    xs = [p.sb([128, S, 128], F32, name=f"fx{i}") for i in range(2 * NC)]
    hr = [p.sb([128, S, 128], F32, name=f"hro{i}") for i in range(2 * NC)]
    hi = [p.sb([128, S, 128], F32, name=f"hio{i}") for i in range(2 * NC)]
    gl = list(groups if groups is not None else range(2048 // S))
    for b0 in range(0, len(gl), NC):
        batch = gl[b0:b0 + NC]
        ctxs = []
        for c, gi in enumerate(batch):
            bi_ = ((b0 // NC) % 2) * NC + c
            x0, xk = xs[bi_], f"fx{bi_}"
            sig = slice(gi * S, (gi + 1) * S)
            p.dma("sp", x0[:, :, :], FILT[sig, :].rearrange("s (a b) -> a s b", b=128), reads=["FILT"], writes=[xk])
            p.op("act", lambda e, x0=x0: e.copy(out=x0[:, :, :].bitcast(mybir.dt.float32r), in_=x0[:, :, :]), reads=[xk], writes=[xk])
            ctxs.append((c, x0, xk, sig, bi_))
        for (c, x0, xk, sig, bi_) in ctxs:
            fft.st_f1(c, x0, xk, 128)
        for (c, x0, xk, sig, bi_) in ctxs:
            fft.st_tw1(c)
        for (c, x0, xk, sig, bi_) in ctxs:
            fft.st_f2(c)
        for (c, x0, xk, sig, bi_) in ctxs:
            a_, ak, b_, bk = hr[bi_], f"hro{bi_}", hi[bi_], f"hio{bi_}"
            p.op("act", lambda e, a_=a_, c=c: e.copy(out=a_[:, :, :].rearrange("p s k -> p (s k)"), in_=fft.psXr[c][:, :]), reads=[f"psXr{c}"], writes=[ak])
            p.op("act", lambda e, b_=b_, c=c: e.copy(out=b_[:, :, :].rearrange("p s k -> p (s k)"), in_=fft.psXi[c][:, :]), reads=[f"psXi{c}"], writes=[bk])
            p.dma("pool", HF[0, sig].rearrange("s a b -> a s b"), a_[:, :, :], reads=[ak], writes=["HF"])
            p.dma("pool", HF[1, sig].rearrange("s a b -> a s b"), b_[:, :, :], reads=[bk], writes=["HF"])


def ph_hy_conv(p, ZC, t_off, Lsig, HF, HYD, TABS, Y, groups=None):
    K = Lsig // 128
    NC = 2
    F32R = mybir.dt.float32r
    fft = FFT(p, TABS, NC)
    dsk = p.sb([128, 2, 1024], F32, name="hyd")
    p.dma("sp", dsk[:, :, :], HYD.partition_broadcast(128), writes=["hyd"])
    mk = lambda n, dt=F32: [p.sb([64, S, 128], dt, name=f"{n}{i}") for i in range(NC)]
    vs, x1s, x2s, vrs, y1s, tmps = mk("v"), mk("xa"), mk("xb"), mk("vr"), mk("y1"), mk("ctmp")
    yos = mk("yo", BF16)
    for i in range(NC):
        for tl_, nm in ((vrs[i], f"vr{i}"), (y1s[i], f"y1{i}")):
            p.op("pool", lambda e, tl_=tl_: e.memset(tl_[:, :, :], 0.0), writes=[nm])
            p.op("act", lambda e, tl_=tl_: e.copy(out=tl_[:, :, :].bitcast(F32R), in_=tl_[:, :, :]), reads=[nm], writes=[nm])
    H = [[[p.sb([128, S, 128], F32, name=f"H{c}{o}{ri}") for ri in range(2)] for o in range(2)] for c in range(NC)]

    def tb(ap2d):
        return ap2d.rearrange("s (a b) -> a s b", b=128)

    gl = list(groups if groups is not None else range(1024 // S))
    for b0 in range(0, len(gl), NC):
        batch = list(enumerate(gl[b0:b0 + NC]))
        for c, gi in batch:
            c0 = gi * S
            p.dma("sp", vs[c][:K, :, :], tb(ZC[c0:c0 + S, t_off:t_off + Lsig]), reads=["ZC"], writes=[f"v{c}"])
            p.dma("sp", x1s[c][:K, :, :], tb(ZC[1024 + c0:1024 + c0 + S, t_off:t_off + Lsig]), reads=["ZC"], writes=[f"xa{c}"])
            p.dma("sp", x2s[c][:K, :, :], tb(ZC[2048 + c0:2048 + c0 + S, t_off:t_off + Lsig]), reads=["ZC"], writes=[f"xb{c}"])
            for o in range(2):
                for ri in range(2):
                    p.dma("sp", H[c][o][ri][:, :, :], HF[ri, o * 1024 + c0:o * 1024 + c0 + S].rearrange("s a b -> a s b"), reads=["HF"], writes=[f"H{c}{o}"])
            p.op("act", lambda e, c=c: e.copy(out=vrs[c][:K, :, :].bitcast(F32R), in_=vs[c][:K, :, :]), reads=[f"v{c}"], writes=[f"vr{c}"])
        for o in range(2):
            for c, gi in batch:
                src, sk = (vrs[c], f"vr{c}") if o == 0 else (y1s[c], f"y1{c}")
                fft.st_f1(c, src, sk, 64)
            for c, gi in batch:
                fft.st_tw1(c)
            for c, gi in batch:
                fft.st_f2(c)
            for c, gi in batch:
                fft.st_filt(c, H[c][o][0][:, :, :], H[c][o][1][:, :, :], f"H{c}{o}")
            for c, gi in batch:
                fft.st_i1(c)
            for c, gi in batch:
                fft.st_tw2(c)
            for c, gi in batch:
                fft.st_i2(c, 64)
            for c, gi in batch:
                c0 = gi * S
                src, sk = (vs[c], f"v{c}") if o == 0 else (y1s[c], f"y1{c}")
                gate, gk = (x1s[c], f"xa{c}") if o == 0 else (x2s[c], f"xb{c}")
                tmp, tk = tmps[c], f"ctmp{c}"
                dbc = dsk[:64, o, c0:c0 + S].unsqueeze(2).to_broadcast([64, S, 128])
                p.op("dve", lambda e, src=src, dbc=dbc, tmp=tmp: e.tensor_tensor(out=tmp[:K, :, :], in0=src[:K, :, :], in1=dbc[:K], op=ALU.mult), reads=[sk, "hyd"], writes=[tk])
                psy3 = fft.psy[c][:, :].rearrange("p (s k) -> p s k", k=128)
                p.op("dve", lambda e, psy3=psy3, tmp=tmp: e.tensor_tensor(out=tmp[:K, :, :], in0=psy3[:K, :, :], in1=tmp[:K, :, :], op=ALU.add), reads=[f"psXr{c}", tk], writes=[tk])
                if o == 0:
                    p.op("pool", lambda e, gate=gate, tmp=tmp, c=c: e.tensor_tensor(out=y1s[c][:K, :, :].bitcast(F32R), in0=tmp[:K, :, :], in1=gate[:K, :, :], op=ALU.mult),
                         reads=[tk, gk], writes=[f"y1{c}"])
                else:
                    p.op("pool", lambda e, gate=gate, tmp=tmp, c=c: e.tensor_tensor(out=yos[c][:K, :, :], in0=tmp[:K, :, :], in1=gate[:K, :, :], op=ALU.mult),
                         reads=[tk, gk], writes=[f"yo{c}"])
                    p.dma("pool", tb(Y[c0:c0 + S, t_off:t_off + Lsig]), yos[c][:K, :, :], reads=[f"yo{c}"], writes=["Y"])


def hy_ctx_tables(l=256):
    N = 2 * l
    t = np.arange(l, dtype=np.float64)[:, None]
    k = np.arange(N, dtype=np.float64)[None, :]
    ang = 2 * np.pi * t * k / N
    c, s = np.cos(ang), np.sin(ang)
    cb, sb = c.copy(), s.copy()
    cb[0, :] = 0.0
    sb[0, :] = 0.0
    fw = np.stack([c, -s, cb, sb]).reshape(4, l // 128, 128, N // 128, 128).transpose(2, 0, 1, 3, 4)
    iv = np.stack([c.T / N, -s.T / N]).reshape(2, N // 128, 128, l // 128, 128).transpose(2, 0, 1, 3, 4)
    return np.ascontiguousarray(fw).astype(np.float32), np.ascontiguousarray(iv).astype(np.float32)


def ph_hy_ctx_dense(p, ZC, l, FRAW, RINV, HYD, FWT, IVT, IDENT, Y):
    TC, KC = l // 128, (2 * l) // 128
    ident = p.sb([128, 128], F32, name="cid")
    p.dma("sp", ident[:, :], IDENT, writes=["cid"])
    fw = p.sb([128, 4, TC, KC, 128], F32, name="cfw")
    p.dma("sp", fw[:, :, :, :, :], FWT, writes=["cfw"])
    iv = p.sb([128, 2, KC, TC, 128], F32, name="civ")
    p.dma("sp", iv[:, :, :, :, :], IVT, writes=["civ"])
    rinv = p.sb([128, 2, 1024], F32, name="crinv")
    p.dma("sp", rinv[:, :, :], RINV.partition_broadcast(128), reads=["RINV"], writes=["crinv"])
    dsk = p.sb([128, 2, 1024], F32, name="cdsk")
    p.dma("sp", dsk[:, :, :], HYD.partition_broadcast(128), writes=["cdsk"])
    pst = [p.ps([128, 512], name=f"cpt{i}") for i in range(2)]
    psm = [p.ps([128, 512], name=f"cpm{i}") for i in range(4)]
    stg = [p.sb([128, l], F32, name=f"cstg{i}") for i in range(4)]
    fT = p.sb([128, TC, 4096], F32, name="cfT")
    sT = p.sb([128, TC, 3072], F32, name="csT")
    k = 0
    for (SRC, nch_, dst, dk, rk) in ((FRAW, 32, fT, "cfT", "FRAW"), (ZC, 24, sT, "csT", "ZC")):
        for rc in range(0, nch_, 4):
            for tc in range(TC):
                ps, pk = pst[k % 2], f"cpt{k%2}"
                for j in range(4):
                    s_, sk = stg[j], f"cstg{j}"
                    if tc == 0:
                        p.dma("sp", s_[:, :], SRC[(rc + j) * 128:(rc + j + 1) * 128, 0:l], reads=[rk], writes=[sk])
                    p.op("pe", lambda e, ps=ps, j=j, s_=s_, tc=tc: e.transpose(out=ps[:, j * 128:(j + 1) * 128], in_=s_[:, tc * 128:(tc + 1) * 128], identity=ident[:, :]),
                         reads=[sk, "cid"], writes=[pk])
                p.op("act" if k % 2 else "dve", (lambda e, ps=ps, dst=dst, tc=tc, rc=rc: e.copy(out=dst[:, tc, rc * 128:(rc + 4) * 128], in_=ps[:, :])) if k % 2 else
                     (lambda e, ps=ps, dst=dst, tc=tc, rc=rc: e.tensor_copy(out=dst[:, tc, rc * 128:(rc + 4) * 128], in_=ps[:, :])), reads=[pk], writes=[dk])
                k += 1
    Xr = p.sb([128, KC, 1024], F32, name="cXr")
    Xi = p.sb([128, KC, 1024], F32, name="cXi")
    t1 = p.sb([128, 512], F32, name="ct1")
    t2 = p.sb([128, 512], F32, name="ct2")
    y1 = p.sb([128, TC, 1024], F32, name="cy1")
    y2 = p.sb([128, TC, 1024], F32, name="cy2")
    tmp = p.sb([128, 1024], F32, name="ctm")
    Hr = p.sb([128, KC, 1024], F32, name="cHr")
    Hi = p.sb([128, KC, 1024], F32, name="cHi")
    m = 0
    for o in range(2):
        for kc in range(KC):
            if True:
                for cb in range(2):
                    fcol = (o * 2 + 0) * 1024 + cb * 512
                    bcol = (o * 2 + 1) * 1024 + cb * 512
                    for part, (kf, kb, Hd, hk) in enumerate(((0, 2, Hr, "cHr"), (1, 3, Hi, "cHi"))):
                        ps, pk = psm[m % 4], f"cpm{m%4}"
                        n = 0
                        for tc in range(TC):
                            for (kind, col) in ((kf, fcol), (kb, bcol)):
                                p.op("pe", lambda e, ps=ps, kind=kind, tc=tc, kc=kc, col=col, n=n: e.matmul(ps[:, :], lhsT=fw[:, kind, tc, kc, :], rhs=fT[:, tc, col:col + 512],
                                                                                                         start=(n == 0), stop=(n == 2 * TC - 1)), reads=["cfw", "cfT"], writes=[pk])
                                n += 1
                        p.op("dve", lambda e, ps=ps, Hd=Hd, kc=kc, o=o, cb=cb: e.tensor_tensor(out=Hd[:, kc, cb * 512:(cb + 1) * 512], in0=ps[:, :],
                                                                                               in1=rinv[:, o, cb * 512:(cb + 1) * 512], op=ALU.mult), reads=[pk, "crinv"], writes=[hk])
                        m += 1
        sig = (lambda tc, c0: sT[:, tc, c0:c0 + 512]) if o == 0 else (lambda tc, c0: y1[:, tc, c0:c0 + 512])
        sigk = "csT" if o == 0 else "cy1"
        for kc in range(KC):
            for cb in range(2):
                c0 = cb * 512
                psr, prk, psi, pik = psm[0], "cpm0", psm[1], "cpm1"
                for tc in range(TC):
                    p.op("pe", lambda e, tc=tc, kc=kc, c0=c0, sg=sig: e.matmul(psr[:, :], lhsT=fw[:, 0, tc, kc, :], rhs=sg(tc, c0), start=(tc == 0), stop=(tc == TC - 1)),
                         reads=["cfw", sigk], writes=[prk])
                for tc in range(TC):
                    p.op("pe", lambda e, tc=tc, kc=kc, c0=c0, sg=sig: e.matmul(psi[:, :], lhsT=fw[:, 1, tc, kc, :], rhs=sg(tc, c0), start=(tc == 0), stop=(tc == TC - 1)),
                         reads=["cfw", sigk], writes=[pik])
                hs = slice(c0, c0 + 512)
                xs_ = slice(c0, c0 + 512)
                p.op("dve", lambda e, kc=kc, hs=hs: e.tensor_tensor(out=t1[:, :], in0=psr[:, :], in1=Hr[:, kc, hs], op=ALU.mult), reads=[prk, "cHr"], writes=["ct1"])
                p.op("dve", lambda e, kc=kc, hs=hs: e.tensor_tensor(out=t2[:, :], in0=psi[:, :], in1=Hi[:, kc, hs], op=ALU.mult), reads=[pik, "cHi"], writes=["ct2"])
                p.op("pool", lambda e, kc=kc, xs_=xs_: e.tensor_tensor(out=Xr[:, kc, xs_], in0=t1[:, :], in1=t2[:, :], op=ALU.subtract), reads=["ct1", "ct2"], writes=["cXr"])
                p.op("dve", lambda e, kc=kc, hs=hs: e.tensor_tensor(out=t1[:, :], in0=psr[:, :], in1=Hi[:, kc, hs], op=ALU.mult), reads=[prk, "cHi"], writes=["ct1"])
                p.op("dve", lambda e, kc=kc, hs=hs: e.tensor_tensor(out=t2[:, :], in0=psi[:, :], in1=Hr[:, kc, hs], op=ALU.mult), reads=[pik, "cHr"], writes=["ct2"])
                p.op("pool", lambda e, kc=kc, xs_=xs_: e.tensor_tensor(out=Xi[:, kc, xs_], in0=t1[:, :], in1=t2[:, :], op=ALU.add), reads=["ct1", "ct2"], writes=["cXi"])
        dst, dstk = (y1, "cy1") if o == 0 else (y2, "cy2")
        for tc in range(TC):
            for cb in range(2):
                c0 = cb * 512
                ps, pk = psm[2 + cb], f"cpm{2+cb}"
                n = 0
                for kc in range(KC):
                    for (kind, Xs, xk) in ((0, Xr, "cXr"), (1, Xi, "cXi")):
                        p.op("pe", lambda e, ps=ps, kind=kind, kc=kc, tc=tc, Xs=Xs, c0=c0, n=n: e.matmul(ps[:, :], lhsT=iv[:, kind, kc, tc, :], rhs=Xs[:, kc, c0:c0 + 512],
                                                                                                      start=(n == 0), stop=(n == 2 * KC - 1)), reads=["civ", xk], writes=[pk])
                        n += 1
                src = (lambda: sT[:, tc, c0:c0 + 512]) if o == 0 else (lambda: y1[:, tc, c0:c0 + 512])
                gate = sT[:, tc, (1 + o) * 1024 + c0:(1 + o) * 1024 + c0 + 512]
                srcap = src()
                p.op("dve", lambda e, srcap=srcap, o=o, c0=c0: e.tensor_tensor(out=tmp[:, c0:c0 + 512], in0=srcap, in1=dsk[:, o, c0:c0 + 512], op=ALU.mult),
                     reads=[sigk, "cdsk"], writes=["ctm"])
                p.op("dve", lambda e, ps=ps, c0=c0: e.tensor_tensor(out=tmp[:, c0:c0 + 512], in0=ps[:, :], in1=tmp[:, c0:c0 + 512], op=ALU.add), reads=[pk, "ctm"], writes=["ctm"])
                p.op("pool", lambda e, dst=dst, tc=tc, c0=c0, gate=gate: e.tensor_tensor(out=dst[:, tc, c0:c0 + 512], in0=tmp[:, c0:c0 + 512], in1=gate, op=ALU.mult),
                     reads=["ctm", "csT"], writes=[dstk])
    yo = [p.sb([128, l], BF16, name=f"cyo{i}") for i in range(2)]
    k = 0
    for cc in range(8):
        ps, pk = pst[cc % 2], f"cpt{cc%2}"
        for tc in range(TC):
            p.op("pe", lambda e, ps=ps, tc=tc, cc=cc: e.transpose(out=ps[:, tc * 128:(tc + 1) * 128], in_=y2[:, tc, cc * 128:(cc + 1) * 128], identity=ident[:, :]),
                 reads=["cy2", "cid"], writes=[pk])
        o_, ok = yo[cc % 2], f"cyo{cc%2}"
        p.op("act", lambda e, ps=ps, o_=o_: e.copy(out=o_[:, :], in_=ps[:, 0:l]), reads=[pk], writes=[ok])
        p.dma("pool", Y[cc * 128:(cc + 1) * 128, 0:l], o_[:, :], reads=[ok], writes=["Y"])


def s5_layouts(lam_re, lam_im, log_dt, b_re, b_im, c_re, c_im):
    def ls_of(a):
        return np.ascontiguousarray(a.reshape(2, 16, 2, 64).transpose(2, 3, 0, 1).reshape(128, 32))
    ldt = np.broadcast_to(log_dt[:, :, None], (2, 32, 64))
    LS = np.stack([ls_of(lam_re), ls_of(lam_im), ls_of(ldt)]).astype(np.float32)
    BT = np.zeros((2, 32, 16, 128), np.float32)
    CT = np.zeros((2, 128, 16, 32), np.float32)
    for i, (b, c) in enumerate(((b_re, c_re), (b_im, c_im))):
        for gs in range(2):
            bb = b.reshape(16, 2, 64, 16)[:, gs]
            BT[i, gs * 16:(gs + 1) * 16, :, gs * 64:(gs + 1) * 64] = bb.transpose(2, 0, 1)
            cc = c.reshape(16, 2, 16, 64)[:, gs]
            CT[i, gs * 64:(gs + 1) * 64, :, gs * 16:(gs + 1) * 16] = cc.transpose(2, 0, 1)
    return LS, BT, CT


SEQ = 8192
NCTX = 256
T = SEQ + NCTX
DEPTH = 4
NCORES = 4


def ph_init(p, XIN, CIN, PE, XT):
    a = [p.sb([128, 8, 512], F32, name=f"ia{i}") for i in range(2)]
    b = [p.sb([128, 8, 512], F32, name=f"ib{i}") for i in range(2)]
    p.dma("sp", a[0][:, :, :NCTX], fm(CIN, 0, NCTX), writes=["ia0"])
    p.dma("pool", fm(XT, 0, NCTX), a[0][:, :, :NCTX], reads=["ia0"], writes=["XT"])
    for i, t0 in enumerate(range(0, SEQ, 512)):
        k = (i + 1) % 2
        p.dma("sp", a[k][:, :, :], fm(XIN, t0, 512), writes=[f"ia{k}"])
        p.dma("sp", b[k][:, :, :], fm(PE, t0, 512), writes=[f"ib{k}"])
        p.op("dve" if i % 2 else "pool", lambda e, k=k: e.tensor_tensor(out=a[k][:, :, :], in0=a[k][:, :, :], in1=b[k][:, :, :], op=ALU.add),
             reads=[f"ia{k}", f"ib{k}"], writes=[f"ia{k}"])
        p.dma("pool", fm(XT, NCTX + t0, 512), a[k][:, :, :], reads=[f"ia{k}"], writes=["XT"])


def ph_mod(p, CV, MW, MB, M):
    cv = p.sb([128, 8, 2], F32, name="cv")
    for r in range(2):
        p.dma("sp", cv[:, :, r], CV[r].rearrange("(c p) -> p c", p=128), writes=["cv"], allow_slow_non_contiguous=True)
    p.op("act", lambda e: e.activation(out=cv[:, :, :], in_=cv[:, :, :], func=AF.Silu), reads=["cv"], writes=["cv"])
    ws = [p.sb([128, 8, 512], F32, name=f"mw{i}") for i in range(2)]
    bs = [p.sb([2, 512], F32, name=f"mb{i}") for i in range(2)]
    os_ = [p.sb([2, 512], F32, name=f"mo{i}") for i in range(2)]
    pss = [p.ps([128, 512], name=f"psm{i}") for i in range(2)]
    k = 0
    for i in range(DEPTH):
        for n0 in range(0, 6144, 512):
            w, wk, b, bk, o, ok, ps, pk = ws[k % 2], f"mw{k%2}", bs[k % 2], f"mb{k%2}", os_[k % 2], f"mo{k%2}", pss[k % 2], f"psm{k%2}"
            p.dma("sp", w[:, :, :], MW[i, :, n0:n0 + 512].rearrange("(c p) n -> p c n", p=128), writes=[wk])
            p.dma("sp", b[:, :], MB[i, n0:n0 + 512].partition_broadcast(2), writes=[bk])
            for c in range(8):
                p.op("pe", lambda e, ps=ps, w=w, c=c: e.matmul(ps[:2, :], lhsT=cv[:, c, :], rhs=w[:, c, :], start=(c == 0), stop=(c == 7)),
                     reads=["cv", wk], writes=[pk])
            p.op("dve", lambda e, ps=ps, b=b, o=o: e.tensor_tensor(out=o[:, :], in0=ps[:2, :], in1=b[:, :], op=ALU.add), reads=[pk, bk], writes=[ok])
            p.dma("pool", M[i, :, n0:n0 + 512], o[:, :], reads=[ok], writes=["M"])
            k += 1


def token_tiles(N, with_ctx):
    tl = [(0, NCTX, 1)] if with_ctx else []
    return tl + [(NCTX + t0, N, 0) for t0 in range(0, SEQ, N)]


def build_program():
    p = Prog()
    I = {}

    def inp(name, shape, dt=F32):
        I[name] = p.dram(name, shape, dt, "ExternalInput")
        return I[name]

    XIN = inp("xT", [1024, SEQ]); CIN = inp("ctxT", [1024, NCTX]); PE = inp("peT", [1024, SEQ]); CV = inp("cvec", [2, 1024])
    MW = inp("mod_w", [4, 1024, 6144]); MB = inp("mod_b", [4, 6144])
    NMG = inp("norm_mix_g", [4, 1024]); NLG = inp("norm_mlp_g", [4, 1024])
    W1 = inp("mlp_w1", [4, 1024, 4096]); W2 = inp("mlp_w2", [4, 4096, 1024]); FNG = inp("final_norm_g", [1024])
    EIW = inp("ev_in_w", [2, 1024, 3616]); EOW = inp("ev_out_w", [2, 1536, 1024])
    LS = inp("s5_ls", [2, 3, 128, 32]); BT = inp("s5_bt", [2, 2, 32, 16, 128]); CT = inp("s5_ct", [2, 2, 128, 16, 32])
    S5D = inp("s5_d", [2, 512]); GW = inp("s5_glu_w", [2, 512, 512]); GB = inp("s5_glu_b", [2, 512])
    MCW = inp("m2_conv_w", [2, 3, 2048]); MCB = inp("m2_conv_b", [2, 2048]); DTB = inp("m2_dt_bias", [2, 32]); ALOG = inp("m2_a_log", [2, 32])
    M2D = inp("m2_d", [2, 16]); M2G = inp("m2_norm_g", [2, 1024])
    HIW = inp("hy_in_w", [2, 1024, 3072]); HIB = inp("hy_in_b", [2, 3072]); HCW = inp("hy_conv_w", [2, 3, 3072]); HCB = inp("hy_conv_b", [2, 3072])
    HW1 = inp("hy_f_w1", [2, 33, 64]); HB1 = inp("hy_f_b1", [2, 64]); HQ1 = inp("hy_f_freq1", [2, 64])
    HW2 = inp("hy_f_w2", [2, 64, 64]); HB2 = inp("hy_f_b2", [2, 64]); HQ2 = inp("hy_f_freq2", [2, 64]); HW3 = inp("hy_f_w3", [2, 64, 4096])
    HYD = inp("hy_d", [2, 2, 1024]); HOW = inp("hy_out_w", [2, 1024, 1024]); HOB = inp("hy_out_b", [2, 1024])
    IDENT = inp("ident", [128, 128]); MASKS = inp("masks", [2, 128, 512]); TABS = inp("tabs", [128, 1536])
    FEAT_L = inp("feats_l", [33, SEQ]); TL_L = inp("tl_l", [SEQ]); FEAT_C = inp("feats_c", [33, NCTX]); TL_C = inp("tl_c", [NCTX]); DELTAS = inp("deltas", [1024])
    FWT = inp("ctx_fwt", [128, 4, 2, 4, 128]); IVT = inp("ctx_ivt", [128, 2, 4, 2, 128])
    OUT = p.dram("out", [SEQ, 1024], F32, "ExternalOutput")

    XT = p.dram("XT", [1024, T], F32); M = p.dram("M", [4, 2, 6144], F32)
    PROJ = p.dram("PROJ", [3616, T], F32); Y = p.dram("Y", [1536, T], BF16)
    YPRE = p.dram("YPRE", [512, T], F32); XBC = p.dram("XBC", [2048, T], F32)
    SC = p.dram("SC", [96, T], F32); CF2 = p.dram("CF2", [2, 32, T], F32); ET = p.dram("ET", [T // 128, 32], F32)
    YT = p.dram("YT", [T, 1024], F32); YM = p.dram("YM", [1024, T], F32)
    ZC = p.dram("ZC", [3072, T], F32); FRAW = p.dram("FRAW", [4096, SEQ], F32); RINV = p.dram("RINV", [2, 1024], F32)
    FILT = p.dram("FILT", [2048, NFFT], F32); HF = p.dram("HF", [2, 2048, 128, 128], F32)

    ph_init(p, XIN, CIN, PE, XT); p.end()
    ph_mod(p, CV, MW, MB, M); p.end()
    segs = [(0, NCTX), (NCTX, T)]
    for i in range(DEPTH):
        j = i // 2
        ctx_later = i < 2
        sl = lambda k: slice(k * 1024, (k + 1) * 1024)
        mods_a = [(M[i, r, sl(1)], M[i, r, sl(0)]) for r in range(2)]
        mods_f = [(M[i, r, sl(4)], M[i, r, sl(3)]) for r in range(2)]
        ga = [M[i, r, sl(2)] for r in range(2)]
        gf = [M[i, r, sl(5)] for r in range(2)]
        if i % 2 == 0:
            ph_tok_in(p, XT, EIW[j], 3616, NMG[i], mods_a, token_tiles(512, True), PROJ); p.end()
            ph_s5_scan(p, PROJ, T, NCTX, LS[j], BT[j], CT[j], S5D[j], YPRE); p.end()
            ph_s5_glu(p, YPRE, T, GW[j], GB[j], Y); p.end()
            ph_conv_silu(p, PROJ, 1536, 2048, T, segs, MCW[j], MCB[j], XBC); p.end()
            ph_ssd_prep(p, PROJ, T, DTB[j], ALOG[j], SC, CF2, ET); p.end()
            for d in range(2):
                ph_ssd_pass(p, d, XBC, T, NCTX, SC, CF2, ET, M2D[j].partition_broadcast(128), IDENT, MASKS, YT, YM); p.end()
            ph_ssd_post(p, YM, PROJ, T, M2G[j], Y); p.end()
            ph_outproj(p, XT, Y, 1536, EOW[j], ga, token_tiles(512, ctx_later)); p.end()
        else:
            ph_tok_in(p, XT, HIW[j], 3072, NMG[i], mods_a, token_tiles(512, ctx_later), PROJ, bias=HIB[j]); p.end()
            ph_conv_silu(p, PROJ, 0, 3072, T, segs, HCW[j], HCB[j], ZC, silu=False); p.end()
            fargs = (HW1[j], HB1[j], HQ1[j], HW2[j], HB2[j], HQ2[j], HW3[j], DELTAS)
            ph_hy_filter_raw(p, SEQ, FEAT_L, TL_L, *fargs, FRAW, RINV); p.end()
            ph_hy_filter_asm(p, SEQ, FRAW, RINV, FILT); p.end()
            ph_hy_filter_fft(p, FILT, TABS, HF); p.end()
            ph_hy_conv(p, ZC, NCTX, SEQ, HF, HYD[j], TABS, Y); p.end()
            if ctx_later:
                ph_hy_filter_raw(p, NCTX, FEAT_C, TL_C, *fargs, FRAW[:, 0:NCTX], RINV); p.end()
                ph_hy_ctx_dense(p, ZC, NCTX, FRAW[:, 0:NCTX], RINV, HYD[j], FWT, IVT, IDENT, Y); p.end()
            ph_outproj(p, XT, Y, 1024, HOW[j], ga, token_tiles(512, ctx_later), bias=HOB[j]); p.end()
        ph_mlp(p, XT, W1[i], W2[i], NLG[i], mods_f, gf, token_tiles(256, ctx_later)); p.end()
    ph_final(p, XT, FNG, token_tiles(512, False), OUT, NCTX, IDENT); p.end()
    return p


def grid_sincos_np(n, dm):
    rows = n // 64
    quarter = dm // 4
    omega = (1.0 / (np.float32(10000.0) ** (np.arange(quarter, dtype=np.float32) / np.float32(quarter)))).astype(np.float32)
    ang_r = np.arange(rows, dtype=np.float32)[:, None] * omega
    ang_c = np.arange(64, dtype=np.float32)[:, None] * omega
    emb_r = np.concatenate([np.sin(ang_r), np.cos(ang_r)], axis=-1)
    emb_c = np.concatenate([np.sin(ang_c), np.cos(ang_c)], axis=-1)
    half = emb_r.shape[-1]
    pe = np.concatenate([np.broadcast_to(emb_r[:, None, :], (rows, 64, half)), np.broadcast_to(emb_c[None, :, :], (rows, 64, half))], axis=-1)
    return pe.reshape(rows * 64, 2 * half).astype(np.float32)


def feats_tables(l):
    t = np.linspace(0.0, 1.0, l, dtype=np.float32)[:, None]
    bands = np.linspace(1e-4, 15, 16, dtype=np.float32)
    ang = (np.float32(2.0 * math.pi / l)) * np.arange(l, dtype=np.float32)[:, None] * bands
    feats = np.concatenate([t, np.cos(ang), -np.sin(ang)], axis=-1).astype(np.float32)
    return np.ascontiguousarray(feats.T), np.ascontiguousarray(t[:, 0])


def host_inputs(inputs):
    f = lambda a: np.ascontiguousarray(np.asarray(a, dtype=np.float32))
    g = {k: f(v) for k, v in inputs.items()}
    shared = {k: g[k] for k in ("mod_w", "mod_b", "norm_mix_g", "norm_mlp_g", "mlp_w1", "mlp_w2", "final_norm_g", "ev_in_w", "ev_out_w",
                                "s5_d", "s5_glu_w", "s5_glu_b", "m2_conv_w", "m2_conv_b", "m2_d", "m2_norm_g", "hy_in_w", "hy_in_b",
                                "hy_conv_w", "hy_conv_b", "hy_f_w1", "hy_f_b1", "hy_f_freq1", "hy_f_w2", "hy_f_b2", "hy_f_freq2", "hy_f_w3",
                                "hy_d", "hy_out_w", "hy_out_b")}
    shared["m2_dt_bias"] = g["m2_dt_bias"].reshape(2, 32)
    shared["m2_a_log"] = g["m2_a_log"].reshape(2, 32)
    lay = [s5_layouts(g["s5_lam_re"][j], g["s5_lam_im"][j], g["s5_log_dt"][j], g["s5_b_re"][j], g["s5_b_im"][j], g["s5_c_re"][j], g["s5_c_im"][j])
           for j in range(2)]
    shared["s5_ls"] = np.stack([l[0] for l in lay]); shared["s5_bt"] = np.stack([l[1] for l in lay]); shared["s5_ct"] = np.stack([l[2] for l in lay])
    shared["ident"] = np.eye(128, dtype=np.float32)
    sq = np.arange(128)
    mf = np.where(sq[None, :] >= sq[:, None], 0.0, -30000.0).astype(np.float32)
    mb = np.where(sq[None, :] <= sq[:, None], 0.0, -30000.0).astype(np.float32)
    shared["masks"] = np.stack([np.tile(mf, (1, 4)), np.tile(mb, (1, 4))])
    shared["tabs"] = hy_tables()
    shared["ctx_fwt"], shared["ctx_ivt"] = hy_ctx_tables(NCTX)
    shared["feats_l"], shared["tl_l"] = feats_tables(SEQ)
    shared["feats_c"], shared["tl_c"] = feats_tables(NCTX)
    shared["deltas"] = np.abs(np.linspace(math.log(1e-2) / 1.5, math.log(1e-2) / 0.3, 1024, dtype=np.float32)).astype(np.float32)
    shared["peT"] = np.ascontiguousarray(grid_sincos_np(SEQ, 1024).T)
    maps = []
    for b in range(NCORES):
        m = dict(shared)
        m["xT"] = np.ascontiguousarray(g["x"][b].T)
        m["ctxT"] = np.ascontiguousarray(g["ctx"][b].T)
        m["cvec"] = np.stack([g["c"][b], g["c_ctx"]])
        maps.append(m)
    return maps


def kernel(**inputs):
    p = build_program()
    nc = p.build()
    maps = host_inputs(inputs)
    res = run_bass_kernel_spmd(nc, maps, core_ids=list(range(NCORES)))
    return np.stack([np.asarray(r["out"], dtype=np.float32) for r in res.results], axis=0)
```
